# Optimizing a Trainium2 kernel written in Bass

```python
import math
import jax, jax.numpy as jnp
from jax import lax
import numpy as np

D_MODEL = 1024
BATCH = 8
SEQ = 4096
DEPTH = 1
DEC_BATCH = 128
DEC_SEQ = 8
PAST_LEN = 16384
PAGE_SIZE = 128

HEAD_DIM = 64
A_HEADS = 16
A_KV_HEADS = 4
A_GROUP = A_HEADS // A_KV_HEADS
A_WIDTH = A_HEADS * HEAD_DIM
KV_WIDTH = A_KV_HEADS * HEAD_DIM
WINDOW = 128
BLOCK = WINDOW
ATTN_SCALE = HEAD_DIM ** -0.5
N_BUCKETS = 32
MAX_DISTANCE = 128
B_HEADS = 16
B_WIDTH = B_HEADS * HEAD_DIM
DECAY_LORA = 64
A_LORA = 64
SHIFT_W = 3 * B_WIDTH + DECAY_LORA + A_LORA
GN_EPS = 64e-5
NORM_EPS = 1e-6
IN_SIZES = (A_WIDTH, KV_WIDTH, KV_WIDTH, A_WIDTH, SHIFT_W, B_WIDTH, D_MODEL, D_MODEL)
IN_COLS = 2 * A_WIDTH + 2 * KV_WIDTH + SHIFT_W + B_WIDTH + 2 * D_MODEL
SHIFT_SIZES = (B_WIDTH, B_WIDTH, B_WIDTH, DECAY_LORA, A_LORA)

kernel_name = "hybrid_swa_sink_rwkv7_gated_merge_step"


def _split(p, sizes):
    out, start = [], 0
    for n in sizes:
        out.append(p[..., start:start + n])
        start += n
    return out


def rmsnorm(x, g):
    xf = x.astype(jnp.float32)
    y = xf * lax.rsqrt(jnp.mean(xf * xf, axis=-1, keepdims=True) + NORM_EPS)
    return (y * g.astype(jnp.float32)).astype(x.dtype)


def t5_bucket(dist):
    max_exact = N_BUCKETS // 2
    d = jnp.maximum(dist, 1).astype(jnp.float32)
    large = max_exact + (jnp.log(d / max_exact) / math.log(MAX_DISTANCE / max_exact)
                         * (N_BUCKETS - max_exact)).astype(jnp.int32)
    large = jnp.minimum(large, N_BUCKETS - 1)
    return jnp.where(dist < max_exact, dist, large)


def window_bias(rel_bias, q_pos, k_pos):
    dist = q_pos[:, None] - k_pos[None, :]
    valid = (dist >= 0) & (dist <= WINDOW)
    bias = rel_bias.astype(jnp.float32)[t5_bucket(jnp.maximum(dist, 0))]
    bias = jnp.moveaxis(bias, -1, 0).reshape(A_KV_HEADS, A_GROUP, q_pos.shape[0], k_pos.shape[0])
    return bias, valid


def sink_softmax(s, sinks):
    sk = sinks.astype(jnp.float32)[..., None]
    m = jnp.maximum(jnp.max(s, axis=-1), sk)
    p = jnp.exp(s - m[..., None])
    denom = jnp.sum(p, axis=-1) + jnp.exp(sk - m)
    return p / denom[..., None]


def swa_banded(q, k, v, rel_bias, sinks):
    Bn, T = q.shape[0], q.shape[1]
    nb = T // BLOCK
    qb = q.reshape(Bn, nb, BLOCK, A_KV_HEADS, A_GROUP, HEAD_DIM)
    kb = k.reshape(Bn, nb, BLOCK, A_KV_HEADS, HEAD_DIM)
    vb = v.reshape(Bn, nb, BLOCK, A_KV_HEADS, HEAD_DIM)
    kband = jnp.concatenate([jnp.concatenate([jnp.zeros_like(kb[:, :1]), kb[:, :-1]], 1), kb], 2)
    vband = jnp.concatenate([jnp.concatenate([jnp.zeros_like(vb[:, :1]), vb[:, :-1]], 1), vb], 2)
    k_pos = jnp.arange(2 * BLOCK)
    bias, valid = window_bias(rel_bias, BLOCK + jnp.arange(BLOCK), k_pos)
    no_prev = (jnp.arange(nb)[:, None, None] == 0) & (k_pos[None, None, :] < BLOCK)
    valid_b = valid[None] & ~no_prev

    def one_sequence(args):
        qs, ks, vs = args
        s = jnp.einsum('nqhgd,nkhd->nhgqk', qs, ks, preferred_element_type=jnp.float32) * ATTN_SCALE + bias
        s = jnp.where(valid_b[:, None, None], s, -jnp.inf)
        p = sink_softmax(s, sinks)
        return jnp.einsum('nhgqk,nkhd->nqhgd', p.astype(vs.dtype), vs)

    o = lax.map(one_sequence, (qb, kband, vband))
    return o.reshape(Bn, T, A_WIDTH)


def swa_with_buffer(q, k, v, k_buf, v_buf, rel_bias, sinks):
    Bn, S = q.shape[0], q.shape[1]
    L = k_buf.shape[1]
    kc = jnp.concatenate([k_buf.astype(k.dtype), k], 1)
    vc = jnp.concatenate([v_buf.astype(v.dtype), v], 1)
    bias, valid = window_bias(rel_bias, L + jnp.arange(S), jnp.arange(L + S))
    s = jnp.einsum('bqhgd,bkhd->bhgqk', q, kc, preferred_element_type=jnp.float32) * ATTN_SCALE + bias
    s = jnp.where(valid, s, -jnp.inf)
    p = sink_softmax(s, sinks)
    o = jnp.einsum('bhgqk,bkhd->bqhgd', p.astype(vc.dtype), vc)
    return o.reshape(Bn, S, A_WIDTH), kc[:, S:], vc[:, S:]


def wkv_scan(r, w, k, v, a, b, S0):
    xs = tuple(jnp.moveaxis(t, 1, 0) for t in (r, w, k, v, a, b))

    def step(S, inp):
        rt, wt, kt, vt, at, bt = inp
        sa = jnp.einsum('bhvk,bhk->bhv', S, at)
        S = S * wt[:, :, None, :] + sa[..., None] * bt[:, :, None, :] + vt[..., None] * kt[:, :, None, :]
        return S, jnp.einsum('bhvk,bhk->bhv', S, rt)

    S_T, ys = lax.scan(step, S0.astype(jnp.float32), xs)
    return jnp.moveaxis(ys, 0, 1), S_T


def rwkv_time_mix(ps, shift0, wkv0, mu, w0, w2, a0, a2, k_k, k_a, r_k, lnx_g, lnx_b):
    Bn, T, _ = ps.shape
    prev = jnp.concatenate([shift0[:, None].astype(ps.dtype), ps[:, :-1]], 1)
    z = ps + (prev - ps) * mu
    r, k, v, wl, al = _split(z, SHIFT_SIZES)
    w_raw = -jax.nn.softplus(-(w0 + jnp.tanh(wl) @ w2)) - 0.5
    decay = jnp.exp(-jnp.exp(w_raw.astype(jnp.float32)))
    a = jax.nn.sigmoid((a0 + al @ a2).astype(jnp.float32))

    def heads(t):
        return t.reshape(Bn, T, B_HEADS, HEAD_DIM).astype(jnp.float32)

    r, k, v, a, decay = heads(r), heads(k), heads(v), heads(a), heads(decay)
    kk = k * k_k.reshape(B_HEADS, HEAD_DIM).astype(jnp.float32)
    kk = kk / jnp.maximum(jnp.sqrt(jnp.sum(kk * kk, axis=-1, keepdims=True)), 1e-12)
    k = k * (1.0 + (a - 1.0) * k_a.reshape(B_HEADS, HEAD_DIM).astype(jnp.float32))
    y, wkv_T = wkv_scan(r, decay, k, v, -kk, kk * a, wkv0)
    mean = jnp.mean(y, axis=-1, keepdims=True)
    var = jnp.mean(jnp.square(y - mean), axis=-1, keepdims=True)
    y = (y - mean) * lax.rsqrt(var + GN_EPS) * lnx_g.reshape(B_HEADS, HEAD_DIM).astype(jnp.float32) \
        + lnx_b.reshape(B_HEADS, HEAD_DIM).astype(jnp.float32)
    y = y + jnp.sum(r * k * r_k.astype(jnp.float32), axis=-1, keepdims=True) * v
    return y.reshape(Bn, T, B_WIDTH).astype(ps.dtype), wkv_T


def mixer_layer(x, k_buf, v_buf, wkv0, shift0, rel_bias, norm_g, w_in, sinks, mu, w0, w2, a0, a2,
                k_k, k_a, r_k, lnx_g, lnx_b, w_out_a, w_out_b, w_o):
    Bn, T, _ = x.shape
    h = rmsnorm(x, norm_g)
    p = jnp.einsum('btd,dc->btc', h, w_in)
    q, k, v, ga, ps, gb, ma, mb = _split(p, IN_SIZES)
    q = q.reshape(Bn, T, A_KV_HEADS, A_GROUP, HEAD_DIM)
    k = k.reshape(Bn, T, A_KV_HEADS, HEAD_DIM)
    v = v.reshape(Bn, T, A_KV_HEADS, HEAD_DIM)
    sinks_g = sinks.reshape(A_KV_HEADS, A_GROUP)
    if k_buf is None:
        o = swa_banded(q, k, v, rel_bias, sinks_g)
        keep = min(WINDOW, T)
        new_k, new_v = k[:, T - keep:], v[:, T - keep:]
        shift0 = jnp.zeros((Bn, SHIFT_W), ps.dtype)
        wkv0 = jnp.zeros((Bn, B_HEADS, HEAD_DIM, HEAD_DIM), jnp.float32)
    else:
        o, new_k, new_v = swa_with_buffer(q, k, v, k_buf, v_buf, rel_bias, sinks_g)
    ya = jnp.einsum('btc,cd->btd', o * jax.nn.silu(ga), w_out_a)
    yb_raw, wkv_T = rwkv_time_mix(ps, shift0, wkv0, mu, w0, w2, a0, a2, k_k, k_a, r_k, lnx_g, lnx_b)
    yb = jnp.einsum('btc,cd->btd', yb_raw * jax.nn.silu(gb), w_out_b)
    merged = jax.nn.sigmoid(ma) * ya + jax.nn.sigmoid(mb) * yb
    out = x + jnp.einsum('btd,de->bte', merged, w_o)
    return out, new_k, new_v, wkv_T, ps[:, -1]


def setup_inputs(seed: int = 0) -> dict:
    key = jax.random.key(seed)
    ks = jax.random.split(key, 24)

    def nrm(k, shape, scale):
        return jax.random.normal(k, shape, jnp.float32) * scale

    win = min(WINDOW, PAST_LEN)
    return {
        "x_prompt": nrm(ks[0], (BATCH, SEQ, D_MODEL), 1.0),
        "x_sample": nrm(ks[1], (DEC_BATCH, DEC_SEQ, D_MODEL), 1.0),
        "cache_k_win": nrm(ks[2], (DEPTH, DEC_BATCH, win, A_KV_HEADS, HEAD_DIM), 1.0),
        "cache_v_win": nrm(ks[3], (DEPTH, DEC_BATCH, win, A_KV_HEADS, HEAD_DIM), 1.0),
        "state_wkv": nrm(ks[4], (DEPTH, DEC_BATCH, B_HEADS, HEAD_DIM, HEAD_DIM), 0.3),
        "state_shift": nrm(ks[5], (DEPTH, DEC_BATCH, SHIFT_W), 1.0),
        "rel_bias": nrm(ks[6], (N_BUCKETS, A_HEADS), 0.3),
        "norm_g": 1.0 + nrm(ks[7], (DEPTH, D_MODEL), 0.02),
        "w_in": nrm(ks[8], (DEPTH, D_MODEL, IN_COLS), D_MODEL ** -0.5),
        "attn_sinks": nrm(ks[9], (DEPTH, A_HEADS), 0.5),
        "shift_mu": jax.random.uniform(ks[10], (DEPTH, SHIFT_W), jnp.float32),
        "rwkv_w0": jax.random.uniform(ks[11], (DEPTH, B_WIDTH), jnp.float32, -4.0, 1.0),
        "rwkv_w2": nrm(ks[12], (DEPTH, DECAY_LORA, B_WIDTH), 0.1 * DECAY_LORA ** -0.5),
        "rwkv_a0": nrm(ks[13], (DEPTH, B_WIDTH), 0.3),
        "rwkv_a2": nrm(ks[14], (DEPTH, A_LORA, B_WIDTH), 0.1 * A_LORA ** -0.5),
        "rwkv_k_k": 0.85 + nrm(ks[15], (DEPTH, B_WIDTH), 0.05),
        "rwkv_k_a": 1.0 + nrm(ks[16], (DEPTH, B_WIDTH), 0.05),
        "rwkv_r_k": nrm(ks[17], (DEPTH, B_HEADS, HEAD_DIM), 0.1),
        "lnx_g": 1.0 + nrm(ks[18], (DEPTH, B_WIDTH), 0.02),
        "lnx_b": nrm(ks[19], (DEPTH, B_WIDTH), 0.02),
        "w_out_a": nrm(ks[20], (DEPTH, A_WIDTH, D_MODEL), A_WIDTH ** -0.5),
        "w_out_b": nrm(ks[21], (DEPTH, B_WIDTH, D_MODEL), B_WIDTH ** -0.5),
        "w_o": nrm(ks[22], (DEPTH, D_MODEL, D_MODEL), D_MODEL ** -0.5),
        "final_g": 1.0 + nrm(ks[23], (D_MODEL,), 0.02),
    }


def reference(x_prompt, x_sample, cache_k_win, cache_v_win, state_wkv, state_shift, rel_bias, norm_g,
              w_in, attn_sinks, shift_mu, rwkv_w0, rwkv_w2, rwkv_a0, rwkv_a2, rwkv_k_k, rwkv_k_a,
              rwkv_r_k, lnx_g, lnx_b, w_out_a, w_out_b, w_o, final_g):
    hp, hs = x_prompt, x_sample
    pk, pv, pw, psh, sk, sv, sw, ssh = [], [], [], [], [], [], [], []
    for l in range(DEPTH):
        lw = (norm_g[l], w_in[l], attn_sinks[l], shift_mu[l], rwkv_w0[l], rwkv_w2[l], rwkv_a0[l],
              rwkv_a2[l], rwkv_k_k[l], rwkv_k_a[l], rwkv_r_k[l], lnx_g[l], lnx_b[l],
              w_out_a[l], w_out_b[l], w_o[l])
        hp, k1, v1, s1, t1 = mixer_layer(hp, None, None, None, None, rel_bias, *lw)
        hs, k2, v2, s2, t2 = mixer_layer(hs, cache_k_win[l], cache_v_win[l], state_wkv[l],
                                         state_shift[l], rel_bias, *lw)
        pk.append(k1); pv.append(v1); pw.append(s1); psh.append(t1)
        sk.append(k2); sv.append(v2); sw.append(s2); ssh.append(t2)
    y_prompt = rmsnorm(hp, final_g)
    y_sample = rmsnorm(hs, final_g)
    prompt_k_win, prompt_v_win = jnp.stack(pk), jnp.stack(pv)
    prompt_wkv, prompt_shift = jnp.stack(pw), jnp.stack(psh)
    sample_k_win, sample_v_win = jnp.stack(sk), jnp.stack(sv)
    sample_wkv, sample_shift = jnp.stack(sw), jnp.stack(ssh)
    return (y_prompt, y_sample, prompt_k_win, prompt_v_win, prompt_wkv, prompt_shift,
            sample_k_win, sample_v_win, sample_wkv, sample_shift)
```

```python
import math
from contextlib import ExitStack

import numpy as np
import concourse.bass as bass
import concourse.mybir as mybir
from concourse.bass_utils import run_bass_kernel_spmd

F32 = mybir.dt.float32
BF16 = mybir.dt.bfloat16
ALU = mybir.AluOpType
AF = mybir.ActivationFunctionType
AX = mybir.AxisListType

NCORES = 8
D = 1024
NT = 32
NEG = -30000.0
CDEC = math.exp(-0.5)
ARENA = 45056


class _Buf:
    __slots__ = ("last_write", "readers", "dsem")

    def __init__(self):
        self.last_write = None
        self.readers = {}
        self.dsem = None


class Sched:
    ENGS = ("pe", "act", "dve", "pool", "sp")

    def __init__(self):
        self.q = {e: [] for e in self.ENGS}
        self.cnt = {e: 0 for e in self.ENGS}
        self.seen = {e: {} for e in self.ENGS}
        self.dma_sems = {}
        self.bufs = {}
        self.cap = None
        self.caps = {}

    def buf(self, name):
        b = self.bufs.get(name)
        if b is None:
            b = self.bufs[name] = _Buf()
        return b

    def _deps(self, reads, writes):
        deps = {}

        def add(tok):
            if tok is not None and deps.get(tok[0], 0) < tok[1]:
                deps[tok[0]] = tok[1]
        for r in reads:
            add(self.buf(r).last_write)
        for w in writes:
            b = self.buf(w)
            add(b.last_write)
            for k, v in b.readers.items():
                add((k, v))
        return deps

    def _waits(self, e, deps):
        for k, v in deps.items():
            if k[0] == "dma":
                v = self.dma_sems[k]
            if k == ("eng", "pe") and e == "pe":
                continue
            if DBG.get("nosame") and k == ("eng", e):
                continue
            if self.seen[e].get(k, 0) >= v:
                continue
            self.seen[e][k] = v
            self.q[e].append(("wait", k, v))

    def _post(self, tok, reads, writes):
        for w in writes:
            b = self.buf(w)
            b.last_write = tok
            b.readers = {}
        for r in reads:
            if r not in writes:
                self.buf(r).readers[tok[0]] = tok[1]

    @staticmethod
    def _excl(reads, writes):
        pr = [r for r in reads if r.startswith("pb")]
        if pr:
            writes = list(writes) + [r for r in pr if r not in writes]
        return reads, writes

    def mark(self, name):
        if name is None:
            self.cap = None
        else:
            self.cap = self.caps.setdefault(name, [])

    def _emit_item(self, it):
        if it[0] == "op":
            self.op(*it[1:])
        else:
            self.dma(*it[1:-1], home=it[-1])

    def replay(self, *names):
        cap, self.cap = self.cap, None
        for n in names:
            for it in self.caps.pop(n, []):
                self._emit_item(it)
        self.cap = cap

    def interleave(self, na, nb):
        cap, self.cap = self.cap, None
        a, b = self.caps.pop(na, []), self.caps.pop(nb, [])
        for i in range(max(len(a), len(b))):
            if i < len(a):
                self._emit_item(a[i])
            if i < len(b):
                self._emit_item(b[i])
        self.cap = cap

    def op(self, e, fn, reads=(), writes=()):
        if self.cap is not None:
            self.cap.append(("op", e, fn, tuple(reads), tuple(writes)))
            return
        reads, writes = self._excl(reads, writes)
        self._waits(e, self._deps(reads, writes))
        self.cnt[e] += 1
        tok = (("eng", e), self.cnt[e])
        self.q[e].append(("ins", fn, tok))
        self._post(tok, reads, writes)

    def dma(self, e, fn, reads=(), writes=(), home=None):
        if self.cap is not None:
            self.cap.append(("dma", e, fn, tuple(reads), tuple(writes), home))
            return
        self._waits(e, self._deps(reads, writes))
        hb = self.buf(home if home is not None else (list(writes) + list(reads))[0])
        if hb.dsem is None:
            hb.dsem = ("dma", len(self.dma_sems))
            self.dma_sems[hb.dsem] = 0
        self.dma_sems[hb.dsem] += 16
        tok = (hb.dsem, self.dma_sems[hb.dsem])
        self.q[e].append(("dma", fn, tok))
        self._post(tok, reads, writes)

    def barrier(self):
        deps = {("eng", en): self.cnt[en] for en in self.ENGS if self.cnt[en]}
        for k, v in self.dma_sems.items():
            deps[k] = v
        for e in self.ENGS:
            for k, v in deps.items():
                if k == ("eng", e) or self.seen[e].get(k, 0) >= v:
                    continue
                self.seen[e][k] = v
                self.q[e].append(("wait", k, v))
        for e in self.ENGS:
            if e == "sp":
                continue
            self.cnt[e] += 1
            self.q[e].append(("ins", lambda eng: eng.nop(), (("eng", e), self.cnt[e])))
        deps = {("eng", en): self.cnt[en] for en in self.ENGS if self.cnt[en]}
        for e in self.ENGS:
            for k, v in deps.items():
                if k == ("eng", e) or self.seen[e].get(k, 0) >= v:
                    continue
                self.seen[e][k] = v
                self.q[e].append(("wait", k, v))

    def emit(self, nc, stack):
        semh = {}
        for e in self.ENGS:
            semh[("eng", e)] = stack.enter_context(nc.semaphore("s_" + e))
        for k in self.dma_sems:
            semh[k] = stack.enter_context(nc.semaphore("d_%d" % k[1]))
        block = stack.enter_context(nc.Block())
        q = self.q

        def run(eng, items):
            for it in items:
                if it[0] == "wait":
                    eng.wait_ge(semh[it[1]], it[2])
                elif it[0] == "ins":
                    it[1](eng).then_inc(semh[it[2][0]], 1)
                else:
                    it[1](eng).then_inc(semh[it[2][0]], 16)

        @block.tensor
        def _(eng):
            run(eng, q["pe"])

        @block.scalar
        def _(eng):
            run(eng, q["act"])

        @block.vector
        def _(eng):
            run(eng, q["dve"])

        @block.gpsimd
        def _(eng):
            run(eng, q["pool"])

        @block.sync
        def _(eng):
            run(eng, q["sp"])


def _t5_bucket_onehot():
    d = np.arange(129)
    dd = np.maximum(d, 1).astype(np.float32)
    large = 16 + (np.log(dd / np.float32(16)) / np.float32(math.log(128 / 16)) * np.float32(16)).astype(np.int32)
    large = np.minimum(large, 31)
    bkt = np.where(d < 16, d, large)
    e = np.zeros((32, 129), np.float32)
    e[bkt, d] = 1.0
    return e


class _NullSched:
    def op(self, *a, **k):
        pass

    def dma(self, *a, **k):
        pass


class Ctx:
    pass


DBG = {}


def build(phases=("p1", "p2a", "p2b", "p3a", "p2d", "p3b"), debug=False, ntiles=NT):
    nc = bass.Bass("TRN2", target_bir_lowering=False)
    S = Sched()
    C = Ctx()
    C.nc, C.S, C.debug, C.ntiles = nc, S, debug, ntiles
    C.phases = tuple(phases)

    def din(name, shape):
        return nc.dram_tensor(name, list(shape), F32, kind="ExternalInput").ap()

    def dout(name, shape):
        return nc.dram_tensor(name, list(shape), F32, kind="ExternalOutput").ap()

    def dscr(name, shape, dt=F32):
        if debug and dt == F32:
            return nc.dram_tensor(name, list(shape), dt, kind="ExternalOutput").ap()
        return nc.dram_tensor(name, list(shape), dt).ap()

    I = C.I = {}
    for name, shape in [
        ("xp", (4096, D)), ("xs", (128, D)), ("ck", (16, 128, 256)), ("cv", (16, 128, 256)),
        ("swkv", (16, 16, 64, 64)), ("sshift", (16, 3200)), ("rel_bias", (32, 16)),
        ("onehot", (32, 129)), ("norm_g", (D,)), ("w_in", (D, 8832)), ("sinks", (16,)),
        ("mu", (3200,)), ("w0", (D,)), ("w2", (64, D)), ("a0", (D,)), ("a2", (64, D)),
        ("k_k", (D,)), ("k_a", (D,)), ("r_k", (D,)), ("lnx_g", (D,)), ("lnx_b", (D,)),
        ("w_out_a", (D, D)), ("w_out_b", (D, D)), ("w_o", (D, D)), ("final_g", (D,)),
    ]:
        I[name] = din(name, shape)
    O = C.O = {}
    for name, shape in [
        ("yp", (4096, D)), ("ys", (128, D)), ("pk", (128, 256)), ("pv", (128, 256)),
        ("pw", (16, 64, 64)), ("psh", (3200,)), ("sk", (16, 128, 256)), ("sv", (16, 128, 256)),
        ("sw", (16, 16, 64, 64)), ("ssh", (16, 3200)),
    ]:
        O[name] = dout(name, shape)
    T = C.T = {}
    T["extd"] = dscr("extd", (16, 128, 384))
    T["ya"] = dscr("ya", (NT + 1, 128, D))
    T["yb"] = dscr("yb", (NT + 1, 128, D))
    T["z"] = dscr("z", (NT + 1, 128, 3200))
    T["sgb"] = dscr("sgb", (NT + 1, 128, D))
    T["carry"] = dscr("carry", (NT + 1, 3200))
    T["s6"] = dscr("s6", (6, 128, D))
    T["sy"] = dscr("sy", (128, D))
    T["sextra"] = dscr("sextra", (128, 2 * D + 16))
    T["wbf_in"] = dscr("wbf_in", (D, 6272), BF16)
    T["wbf_ob"] = dscr("wbf_ob", (D, D), BF16)
    T["wbf_o"] = dscr("wbf_o", (D, D), BF16)
    if debug:
        T["dMTp"] = dscr("dMTp", (128, 16, 128))
        T["dMTc"] = dscr("dMTc", (128, 16, 128))
        T["dMTs"] = dscr("dMTs", (128, 16, 128))
        T["dt1"] = dscr("dt1", (NT + 1, 128, D))
        T["dy"] = dscr("dy", (NT + 1, 128, D))

    with ExitStack() as st:
        arena = st.enter_context(nc.sbuf_tensor("arena", [128, ARENA], F32))
        psum = st.enter_context(nc.psum_tensor("psum", [128, 8, 512], F32))
        C.arena, C.psum = arena, psum
        C.apos = 0

        def alloc(shape, dt=F32):
            n = 1
            for s_ in shape[1:]:
                n *= s_
            words = n if dt == F32 else (n + 1) // 2
            words = (words + 7) // 8 * 8
            off = C.apos
            C.apos += words
            assert C.apos <= ARENA, ("SBUF arena overflow", C.apos)
            v = arena[0:shape[0], off:off + words]
            if dt != F32:
                v = v.bitcast(dt)
            v = v[:, 0:n]
            if len(shape) == 2:
                return v
            names = " ".join("a%d" % i for i in range(len(shape) - 1))
            kw = {"a%d" % i: shape[i + 1] for i in range(len(shape) - 1)}
            return v.rearrange("p (%s) -> p %s" % (names, names), **kw)
        C.alloc = alloc

        def reset():
            C.apos = 0
        C.reset = reset

        def pbank(i, dt=F32):
            v = psum[:, i, :]
            if dt != F32:
                v = v.bitcast(dt)
            return v
        C.pbank = pbank
        C.rot = 0

        for ph in phases:
            PHASES[ph](C)
            S.barrier()
        S.emit(nc, st)
    return nc


def _common_consts(C, init=True):
    S, alloc, I = C.S, C.alloc, C.I
    C.identf = alloc([128, 128])
    C.ident = alloc([128, 128], BF16)
    C.gbc = alloc([128, D])
    if not init:
        S = _NullSched()
    S.op("pool", lambda e: e.memset(C.identf, 0.0), writes=["identf"])
    S.op("pool", lambda e: e.affine_select(out=C.identf, in_=C.identf, pattern=[[-1, 128]],
                                           compare_op=ALU.not_equal, fill=1.0, base=0,
                                           channel_multiplier=1),
         reads=["identf"], writes=["identf"])
    S.op("dve", lambda e: e.tensor_copy(out=C.ident, in_=C.identf), reads=["identf"], writes=["ident"])
    S.dma("sp", lambda e: e.dma_start(out=C.gbc, in_=I["norm_g"].partition_broadcast(128)), writes=["gbc"])
    C.xt = [alloc([128, D]), alloc([128, D]), alloc([128, D])]
    C.junk = alloc([128, D], BF16)
    C.ss = alloc([128, 1])
    C.rstd = alloc([128, 1])
    C.xsb = alloc([128, D], BF16)
    C.hT = alloc([128, 8, 128], BF16)


def _x_src(C, t):
    return C.I["xs"] if t == NT else C.I["xp"][t * 128:(t + 1) * 128, :]


def _load_x(C, t, slot):
    C.S.dma("sp", lambda e: e.dma_start(out=C.xt[slot], in_=_x_src(C, t)), writes=["xt%d" % slot])


def _norm_pre(C, slot):
    S = C.S
    xt = C.xt[slot]
    xn = "xt%d" % slot
    S.op("pool", lambda e: e.memset(C.ss, 0.0), writes=["ss"])
    S.op("act", lambda e: e.activation(out=C.junk, in_=xt, func=AF.Square, accum_out=C.ss),
         reads=[xn, "ss"], writes=["junk", "ss"])
    S.op("dve", lambda e: e.tensor_scalar(out=C.rstd, in0=C.ss, scalar1=1.0 / D, scalar2=1e-6,
                                          op0=ALU.mult, op1=ALU.add), reads=["ss"], writes=["rstd"])
    S.op("act", lambda e: e.activation(out=C.rstd, in_=C.rstd, func=AF.Sqrt), reads=["rstd"], writes=["rstd"])
    S.op("dve", lambda e: e.reciprocal(out=C.rstd, in_=C.rstd), reads=["rstd"], writes=["rstd"])
    S.op("dve", lambda e: e.scalar_tensor_tensor(out=C.xsb, in0=xt, scalar=C.rstd[:, 0:1], in1=C.gbc,
                                                 op0=ALU.mult, op1=ALU.mult),
         reads=[xn, "rstd", "gbc"], writes=["xsb"])


def _norm_post(C):
    S = C.S
    pT = C.pbank(0, BF16)
    for k in range(8):
        S.op("pe", lambda e, k=k: e.transpose(out=pT[:, k * 128:(k + 1) * 128],
                                              in_=C.xsb[:, k * 128:(k + 1) * 128], identity=C.ident),
             reads=["xsb", "ident"], writes=["pb0"])
    S.op("act", lambda e: e.copy(out=C.hT.rearrange("p k t -> p (k t)"), in_=pT), reads=["pb0"], writes=["hT"])


def _norm_T(C, slot, nxt=None):
    _norm_post(C)
    if nxt is not None:
        _norm_pre(C, nxt)


def _pstride(ap, step, count):
    pat = [list(x) for x in ap.ap]
    pat[0] = [pat[0][0] * step, count]
    return bass.AP(ap.tensor, ap.offset, pat)


def _load_w(C, dst, dname, src_cols, eng="pool"):
    for k in range(8):
        C.S.dma(eng, lambda e, k=k: e.dma_start(out=dst[:, k, :], in_=src_cols[k * 128:(k + 1) * 128, :]),
                writes=[dname])


def _load_wbf(C, dst, dname, srcname, c0, n):
    src = C.T[srcname]
    for k in range(8):
        C.S.dma("sp" if k % 2 == 0 else "act",
                lambda e, k=k: e.dma_start(out=dst[:, k, :], in_=src[k * 128:(k + 1) * 128, c0:c0 + n]),
                reads=[srcname], writes=[dname])


def _nextbank(C):
    pool = getattr(C, 'rotbanks', (1, 2, 3, 4))
    b = pool[C.rot % len(pool)]
    C.rot += 1
    return b


def _proj_tm(C, W, wname, c0, ncols, bank):
    pb = C.pbank(bank)
    for k in range(8):
        C.S.op("pe", lambda e, k=k: e.matmul(pb[:, 0:ncols], lhsT=C.hT[:, k, :], rhs=W[:, k, c0:c0 + ncols],
                                              start=(k == 0), stop=(k == 7)),
               reads=["hT", wname], writes=["pb%d" % bank])
    return pb


def phase1(C):
    C.rotbanks = (1, 2, 3, 4)
    S, alloc, I, O, T, nc = C.S, C.alloc, C.I, C.O, C.T, C.nc
    C.reset()
    _common_consts(C)
    w_in = I["w_in"]
    Wqk = alloc([128, 8, 1280], BF16)
    Wkv = alloc([128, 8, 512], BF16)
    Wga = alloc([128, 8, 1024], BF16)
    Woa = alloc([128, 8, 1024], BF16)
    for k in range(8):
        for pair in range(2):
            for half in range(2):
                c0 = pair * 512 + half * 256
                src = w_in[k * 128:(k + 1) * 128, c0:c0 + 256].rearrange("p (g d) -> p g d", g=4)
                dst = Wqk[:, k, pair * 512:(pair + 1) * 512].rearrange(
                    "p (g half d) -> p g half d", g=4, half=2)[:, :, half, :]
                S.dma("pool", lambda e, src=src, dst=dst: e.dma_start(out=dst, in_=src), writes=["Wqk"])
        S.dma("pool", lambda e, k=k: e.dma_start(out=Wqk[:, k, 1024:1280],
                                                 in_=w_in[k * 128:(k + 1) * 128, 1024:1280]), writes=["Wqk"])
    _load_w(C, Wkv, "Wkv", w_in[:, 1024:1536])
    _load_w(C, Wga, "Wga", w_in[:, 1536:2560])
    _load_w(C, Woa, "Woa", I["w_out_a"])
    if DBG.get("precast", True):
        for k in range(8):
            rows = slice(k * 128, (k + 1) * 128)
            S.dma("pool", lambda e, rows=rows: e.dma_start(out=T["wbf_in"][rows, :], in_=w_in[rows, 2560:8832]), writes=["wbf_in"])
        for k in range(8):
            rows = slice(k * 128, (k + 1) * 128)
            S.dma("pool", lambda e, rows=rows: e.dma_start(out=T["wbf_ob"][rows, :], in_=I["w_out_b"][rows, :]), writes=["wbf_ob"])
            S.dma("pool", lambda e, rows=rows: e.dma_start(out=T["wbf_o"][rows, :], in_=I["w_o"][rows, :]), writes=["wbf_o"])

    rb = alloc([32, 16])
    oh = alloc([32, 129])
    ext = alloc([16, 384])
    MTp = alloc([128, 16, 128])
    MTc = alloc([128, 16, 128])
    MTs = alloc([128, 16, 128])
    bsel = alloc([16, 128])
    bd = alloc([128, 128])
    esink = alloc([128, 16])
    S.dma("sp", lambda e: e.dma_start(out=rb, in_=I["rel_bias"]), writes=["rb"])
    S.dma("sp", lambda e: e.dma_start(out=oh, in_=I["onehot"]), writes=["oh"])
    S.dma("sp", lambda e: e.dma_start(out=esink, in_=I["sinks"].partition_broadcast(128)), writes=["esink"])
    S.op("act", lambda e: e.activation(out=esink, in_=esink, func=AF.Exp), reads=["esink"], writes=["esink"])
    pb = C.pbank(1)
    S.op("pe", lambda e: e.matmul(pb[0:16, 0:129], lhsT=rb, rhs=oh, start=True, stop=True),
         reads=["rb", "oh"], writes=["pb1"])
    S.op("pool", lambda e: e.memset(ext, NEG), writes=["ext"])
    S.op("dve", lambda e: e.tensor_copy(out=ext[:, 127:256], in_=pb[0:16, 0:129]), reads=["pb1", "ext"], writes=["ext"])
    S.dma("sp", lambda e: e.dma_start(out=T["extd"], in_=ext.unsqueeze(1).to_broadcast([16, 128, 384])),
          reads=["ext"], writes=["extd"])
    S.dma("sp", lambda e: e.dma_start(out=MTc, in_=bass.AP(T["extd"].tensor, 127, [[383, 128], [49152, 16], [1, 128]])),
          reads=["extd"], writes=["MTc"])
    S.dma("sp", lambda e: e.dma_start(out=MTp, in_=bass.AP(T["extd"].tensor, 255, [[383, 128], [49152, 16], [1, 128]])),
          reads=["extd"], writes=["MTp"])
    S.op("pool", lambda e: e.memset(bsel, 1.0), writes=["bsel"])
    S.op("pool", lambda e: e.affine_select(out=bsel, in_=bsel, pattern=[[1, 128]], compare_op=ALU.is_ge,
                                           fill=0.0, base=0, channel_multiplier=-8), reads=["bsel"], writes=["bsel"])
    S.op("pool", lambda e: e.affine_select(out=bsel, in_=bsel, pattern=[[-1, 128]], compare_op=ALU.is_ge,
                                           fill=0.0, base=7, channel_multiplier=8), reads=["bsel"], writes=["bsel"])
    pb2 = C.pbank(2)
    S.op("pe", lambda e: e.matmul(pb2[:, 0:128], lhsT=bsel, rhs=bsel, start=True, stop=True),
         reads=["bsel"], writes=["pb2"])
    S.op("dve", lambda e: e.tensor_scalar(out=bd, in0=pb2[:, 0:128], scalar1=-1.0, scalar2=-NEG,
                                          op0=ALU.add, op1=ALU.mult), reads=["pb2"], writes=["bd"])
    S.op("dve", lambda e: e.tensor_tensor(out=MTs, in0=MTc, in1=bd.unsqueeze(1).to_broadcast([128, 16, 128]),
                                          op=ALU.add), reads=["MTc", "bd"], writes=["MTs"])

    if C.debug:
        S.dma("sp", lambda e: e.dma_start(out=T["dMTp"], in_=MTp), reads=["MTp"], writes=["dMTp"])
        S.dma("sp", lambda e: e.dma_start(out=T["dMTc"], in_=MTc), reads=["MTc"], writes=["dMTc"])
        S.dma("sp", lambda e: e.dma_start(out=T["dMTs"], in_=MTs), reads=["MTs"], writes=["dMTs"])
    qT = alloc([128, 8, 128], BF16)
    kT = [alloc([128, 2, 128], BF16), alloc([128, 2, 128], BF16)]
    va = [alloc([128, 4, 65], BF16), alloc([128, 4, 65], BF16)]
    kvo = alloc([128, 512])
    sga = alloc([128, D])
    stb = [alloc([128, 512]) for _ in range(2)]
    pTb = [alloc([128, 4, 128], BF16) for _ in range(4)]
    den = alloc([128, 16])
    t1 = alloc([128, 16, 64])
    goa = alloc([128, D], BF16)
    goaT = alloc([128, 8, 128], BF16)
    yat = [alloc([128, D]), alloc([128, D])]
    ckb = [alloc([128, 256], BF16) for _ in range(2)]
    vac = [alloc([128, 4, 65], BF16) for _ in range(2)]
    kTc = [alloc([128, 2, 128], BF16) for _ in range(2)]
    Zb = [alloc([128, 16, 128], BF16) for _ in range(2)]
    stc = [alloc([128, 16, 8]) for _ in range(2)]
    qTs = alloc([128, 16, 8, 8], BF16)
    zl = alloc([128, 128], BF16)
    zr = alloc([128, 512], BF16)
    S.op("pool", lambda e: e.memset(zl, 0.0), writes=["zl"])
    S.op("pool", lambda e: e.memset(zr, 0.0), writes=["zr"])
    for i in range(2):
        S.op("pool", lambda e, i=i: e.memset(va[i][:, :, 64:65], 1.0), writes=["va%d" % i])
        S.op("pool", lambda e, i=i: e.memset(vac[i][:, :, 64:65], 1.0), writes=["vac%d" % i])

    def oslot(h):
        bank = 5 + h // 7
        off = (h % 7) * 65
        return C.pbank(bank)[:, off:off + 65], "pb%d" % bank
    tiles = ([NT] if DBG.get('sample', True) else []) + list(range(C.ntiles))
    if not tiles:
        return
    if DBG.get('fake_sample'):
        tiles = [0]
    _load_x(C, tiles[0], 0)
    if len(tiles) > 1:
        _load_x(C, tiles[1], 1)
    _norm_pre(C, 0)
    cnt = {"st": 0, "pT": 0}
    for ti, t in enumerate(tiles):
        slot = ti % 2
        if ti + 2 < len(tiles):
            _load_x(C, tiles[ti + 2], (ti + 2) % 3)
        _norm_T(C, ti % 3, ((ti + 1) % 3) if ti + 1 < len(tiles) else None)
        sample = (t == NT) or bool(DBG.get('fake_sample'))
        cur = (t % 2) if not sample else 0
        prev = 1 - cur
        for chunks in ([0, 1, 2, 3], [4, 5, 6, 7], [8, 9]):
            bank = _nextbank(C)
            pb = C.pbank(bank)
            for ci, c in enumerate(chunks):
                for k in range(8):
                    S.op("pe", lambda e, k=k, ci=ci, c=c, pb=pb: e.matmul(
                        pb[:, ci * 128:(ci + 1) * 128], lhsT=Wqk[:, k, c * 128:(c + 1) * 128], rhs=C.hT[:, k, :],
                        start=(k == 0), stop=(k == 7)), reads=["hT", "Wqk"], writes=["pb%d" % bank])
            n = len(chunks) * 128
            if chunks[0] < 8:
                c0 = chunks[0]
                S.op("act", lambda e, pb=pb, c0=c0, n=n: e.copy(
                    out=qT[:, c0:c0 + 4, :].rearrange("p c t -> p (c t)"), in_=pb[:, 0:n]),
                    reads=["pb%d" % bank], writes=["qT"])
            else:
                S.op("act", lambda e, pb=pb, n=n, cur=cur: e.copy(out=kT[cur].rearrange("p c t -> p (c t)"), in_=pb[:, 0:n]),
                     reads=["pb%d" % bank], writes=["kT%d" % cur])
        bank = _nextbank(C)
        pb = _proj_tm(C, Wkv, "Wkv", 0, 512, bank)
        S.op("dve", lambda e, pb=pb, cur=cur: e.tensor_copy(out=va[cur][:, :, 0:64],
                                                   in_=pb[:, 256:512].rearrange("p (h d) -> p h d", h=4)),
             reads=["pb%d" % bank], writes=["va%d" % cur])
        if sample or t == NT - 1:
            S.op("dve", lambda e, pb=pb: e.tensor_copy(out=kvo, in_=pb), reads=["pb%d" % bank], writes=["kvo"])
            if sample and not DBG.get('s_out', True):
                pass
            elif sample:
                for b in range(16):
                    S.dma("sp", lambda e, b=b: e.dma_start(out=O["sk"][b, 120:128, :], in_=kvo[8 * b:8 * b + 8, 0:256]),
                          reads=["kvo"], writes=["o_sk"])
                    S.dma("sp", lambda e, b=b: e.dma_start(out=O["sv"][b, 120:128, :], in_=kvo[8 * b:8 * b + 8, 256:512]),
                          reads=["kvo"], writes=["o_sv"])
                S.dma("sp", lambda e: e.dma_start(out=O["sk"][:, 0:120, :], in_=I["ck"][:, 8:128, :]), writes=["o_sk2"])
                S.dma("sp", lambda e: e.dma_start(out=O["sv"][:, 0:120, :], in_=I["cv"][:, 8:128, :]), writes=["o_sv2"])
            else:
                S.dma("sp", lambda e: e.dma_start(out=O["pk"], in_=kvo[:, 0:256]), reads=["kvo"], writes=["o_pk"])
                S.dma("sp", lambda e: e.dma_start(out=O["pv"], in_=kvo[:, 256:512]), reads=["kvo"], writes=["o_pv"])
        for hf in range(2):
            bank = _nextbank(C)
            pb = _proj_tm(C, Wga, "Wga", hf * 512, 512, bank)
            S.op("act", lambda e, pb=pb, hf=hf: e.activation(out=sga[:, hf * 512:(hf + 1) * 512], in_=pb, func=AF.Silu),
                 reads=["pb%d" % bank], writes=["sga"])
        for bk, nh in enumerate((7, 7, 2)):
            S.op("pe", lambda e, bk=bk, nh=nh: e.matmul(C.pbank(5 + bk)[:, 0:nh * 65], lhsT=zl, rhs=zr[:, 0:nh * 65],
                                                        start=True, stop=False, skip_group_check=True),
                 reads=["zl", "zr"], writes=["pb%d" % (5 + bk)])
        blocks = [("cur", cur)] if (sample or t == 0) else [("prev", prev), ("cur", cur)]
        nlast = 1 if not sample else 17
        done = [0] * 16
        its = [(kvh, kind, sl) for kvh in range(4) for (kind, sl) in blocks]
        nblk = len(blocks) if not sample else 1 + DBG.get('s_nb', 16)

        def stage_a(kvh, kind, sl):
            pair, half = kvh // 2, kvh % 2
            rows = slice(half * 64, half * 64 + 64)
            bank = _nextbank(C)
            pb = C.pbank(bank)
            S.op("pe", lambda e, pb=pb, sl=sl, rows=rows, pair=pair: e.matmul(
                pb, lhsT=kT[sl][rows, pair, :], rhs=qT[rows, pair * 4:pair * 4 + 4, :], start=True, stop=True),
                reads=["kT%d" % sl, "qT"], writes=["pb%d" % bank])
            MT = MTs if sample else (MTp if kind == "prev" else MTc)
            mtn = "MTs" if sample else ("MTp" if kind == "prev" else "MTc")
            si = cnt["st"] % 2
            cnt["st"] += 1
            pi = cnt["pT"] % 4
            cnt["pT"] += 1
            S.op("dve", lambda e, pb=pb, si=si, MT=MT, kvh=kvh: e.scalar_tensor_tensor(
                out=stb[si], in0=pb, scalar=0.125, in1=MT[:, kvh * 4:kvh * 4 + 4, :].rearrange("p h q -> p (h q)"),
                op0=ALU.mult, op1=ALU.add), reads=["pb%d" % bank, mtn], writes=["st%d" % si])
            S.op("act", lambda e, si=si, pi=pi: e.activation(out=pTb[pi].rearrange("p g q -> p (g q)"),
                                                             in_=stb[si], func=AF.Exp),
                 reads=["st%d" % si], writes=["pT%d" % pi])
            return pi

        def stage_b(kvh, kind, sl, pi):
            for g in range(4):
                h = kvh * 4 + g
                osl, on = oslot(h)
                done[h] += 1
                S.op("pe", lambda e, osl=osl, pi=pi, g=g, sl=sl, kvh=kvh, last=(done[h] == nblk):
                     e.matmul(osl, lhsT=pTb[pi][:, g, :], rhs=va[sl][:, kvh, :], start=False, stop=last,
                              skip_group_check=True),
                     reads=["pT%d" % pi, "va%d" % sl], writes=[on])
        pis = {}
        for ii, it in enumerate(its):
            pis[ii] = stage_a(*it)
            if ii >= 1:
                stage_b(*its[ii - 1], pis[ii - 1])
        stage_b(*its[-1], pis[len(its) - 1])
        if sample:
            for b in range(DBG.get('s_nb', 16)):
                j = b % 2
                S.dma("pool", lambda e, b=b, j=j: e.dma_start(out=ckb[j], in_=I["ck"][b]), writes=["ckb%d" % j])
                S.dma("pool", lambda e, b=b, j=j: e.dma_start(out=vac[j][:, :, 0:64],
                                                              in_=I["cv"][b].rearrange("p (h d) -> p h d", h=4)),
                      writes=["vac%d" % j])
                pT0 = C.pbank(0, BF16)
                for pr in range(2):
                    S.op("pe", lambda e, pr=pr, j=j: e.transpose(out=pT0[:, pr * 128:(pr + 1) * 128],
                                                                  in_=ckb[j][:, pr * 128:(pr + 1) * 128], identity=C.ident),
                         reads=["ckb%d" % j, "ident"], writes=["pb0"])
                S.op("act", lambda e, j=j: e.copy(out=kTc[j].rearrange("p c t -> p (c t)"), in_=pT0[:, 0:256]),
                     reads=["pb0"], writes=["kTc%d" % j])
                lvl = DBG.get('s_lvl', 4)
                if lvl < 2:
                    continue
                for kvh in range(4):
                    pair, half = kvh // 2, kvh % 2
                    rows = slice(half * 64, half * 64 + 64)
                    bank = _nextbank(C)
                    pb = C.pbank(bank)
                    S.op("pe", lambda e, pb=pb, rows=rows, pair=pair, j=j: e.matmul(
                        pb, lhsT=kTc[j][rows, pair, :], rhs=qT[rows, pair * 4:pair * 4 + 4, :],
                        start=True, stop=True), reads=["kTc%d" % j, "qT"], writes=["pb%d" % bank])
                    S.op("dve", lambda e, pb=pb, j=j, kvh=kvh, b=b: e.scalar_tensor_tensor(
                        out=stc[j][:, kvh * 4:kvh * 4 + 4, :],
                        in0=pb.rearrange("p (g q) -> p g q", g=4)[:, :, 8 * b:8 * b + 8], scalar=0.125,
                        in1=MTp[:, kvh * 4:kvh * 4 + 4, 0:8], op0=ALU.mult, op1=ALU.add),
                        reads=["pb%d" % bank, "MTp"], writes=["stc%d" % j])
                if lvl < 3:
                    continue
                S.op("pool", lambda e, j=j: e.memset(Zb[j], 0.0), writes=["Zb%d" % j])
                S.op("act", lambda e, j=j, b=b: e.activation(out=Zb[j][:, :, 8 * b:8 * b + 8], in_=stc[j], func=AF.Exp),
                     reads=["stc%d" % j, "Zb%d" % j], writes=["Zb%d" % j])
                if lvl < 4:
                    continue
                for h in range(16):
                    osl, on = oslot(h)
                    done[h] += 1
                    S.op("pe", lambda e, osl=osl, h=h, j=j, last=(done[h] == 1 + DBG.get('s_nb', 16)): e.matmul(
                        osl, lhsT=Zb[j][:, h, :], rhs=vac[j][:, h // 4, :], start=False, stop=last,
                        skip_group_check=True),
                        reads=["Zb%d" % j, "vac%d" % j], writes=[on])
        for bk, (h0, nh) in enumerate([(0, 7), (7, 7), (14, 2)]):
            ob = C.pbank(5 + bk)[:, 0:nh * 65].rearrange("p (h e) -> p h e", e=65)
            S.op("dve", lambda e, ob=ob, h0=h0, nh=nh: e.tensor_tensor(
                out=den[:, h0:h0 + nh].unsqueeze(2), in0=ob[:, :, 64:65], in1=esink[:, h0:h0 + nh].unsqueeze(2), op=ALU.add),
                reads=["pb%d" % (5 + bk), "esink"], writes=["den"])
        S.op("dve", lambda e: e.reciprocal(out=den, in_=den), reads=["den"], writes=["den"])
        for bk, (h0, nh) in enumerate([(0, 7), (7, 7), (14, 2)]):
            ob = C.pbank(5 + bk)[:, 0:nh * 65].rearrange("p (h e) -> p h e", e=65)
            S.op("dve", lambda e, ob=ob, h0=h0, nh=nh: e.tensor_tensor(
                out=t1[:, h0:h0 + nh, :], in0=ob[:, :, 0:64],
                in1=den[:, h0:h0 + nh].unsqueeze(2).to_broadcast([128, nh, 64]), op=ALU.mult),
                reads=["pb%d" % (5 + bk), "den"], writes=["t1"])
        if C.debug:
            S.dma("sp", lambda e, t=t: e.dma_start(out=T["dt1"][t], in_=t1.rearrange("p h d -> p (h d)")), reads=["t1"], writes=["d_dt1"], home="t1")
        S.op("dve", lambda e: e.tensor_tensor(out=goa, in0=t1.rearrange("p h d -> p (h d)"), in1=sga, op=ALU.mult),
             reads=["t1", "sga"], writes=["goa"])
        pT0 = C.pbank(0, BF16)
        for k in range(8):
            S.op("pe", lambda e, k=k: e.transpose(out=pT0[:, k * 128:(k + 1) * 128], in_=goa[:, k * 128:(k + 1) * 128],
                                                  identity=C.ident), reads=["goa", "ident"], writes=["pb0"])
        S.op("act", lambda e: e.copy(out=goaT.rearrange("p k t -> p (k t)"), in_=pT0), reads=["pb0"], writes=["goaT"])
        yslot = ti % 2
        for hf in range(2):
            bank = _nextbank(C)
            pb = C.pbank(bank)
            for k in range(8):
                S.op("pe", lambda e, k=k, pb=pb, hf=hf: e.matmul(pb, lhsT=goaT[:, k, :], rhs=Woa[:, k, hf * 512:(hf + 1) * 512],
                                                                  start=(k == 0), stop=(k == 7)),
                     reads=["goaT", "Woa"], writes=["pb%d" % bank])
            S.op("act", lambda e, pb=pb, hf=hf, yslot=yslot: e.copy(out=yat[yslot][:, hf * 512:(hf + 1) * 512], in_=pb),
                 reads=["pb%d" % bank], writes=["yat%d" % yslot])
        S.dma("sp", lambda e, t=t, yslot=yslot: e.dma_start(out=T["ya"][t], in_=yat[yslot]), reads=["yat%d" % yslot], writes=["d_ya"], home="yat%d" % yslot)


def phase2a(C):
    C.rotbanks = (1, 2, 3, 4)
    S, alloc, I, O, T = C.S, C.alloc, C.I, C.O, C.T
    C.reset()
    _common_consts(C)
    w_in = I["w_in"]
    Wps = alloc([128, 8, 3200], BF16)
    Wgb = alloc([128, 8, 1024], BF16)
    if DBG.get("precast", True) and "p1" in C.phases:
        _load_wbf(C, Wps, "Wps", "wbf_in", 0, 3200)
        _load_wbf(C, Wgb, "Wgb", "wbf_in", 3200, 1024)
    else:
        for k in range(8):
            for c0 in range(0, 3200, 640):
                S.dma("pool", lambda e, k=k, c0=c0: e.dma_start(out=Wps[:, k, c0:c0 + 640],
                                                                in_=w_in[k * 128:(k + 1) * 128, 2560 + c0:2560 + c0 + 640]),
                      writes=["Wps"])
        _load_w(C, Wgb, "Wgb", w_in[:, 5760:6784])
    mubc = alloc([128, 3200])
    S.dma("sp", lambda e: e.dma_start(out=mubc, in_=I["mu"].partition_broadcast(128)), writes=["mubc"])
    psb = [alloc([128, 3200]), alloc([128, 3200])]
    zb = [alloc([128, 3200]), alloc([128, 3200])]
    sgb = [alloc([128, D]), alloc([128, D])]
    sst = alloc([16, 3200])
    S.dma("sp", lambda e: e.dma_start(out=sst, in_=I["sshift"]), writes=["sst"])

    def sel(name, shape, pattern, cm, base):
        m = alloc(shape)
        S.op("pool", lambda e: e.memset(m, 0.0), writes=[name])
        S.op("pool", lambda e: e.affine_select(out=m, in_=m, pattern=pattern, compare_op=ALU.not_equal, fill=1.0,
                                               base=base, channel_multiplier=cm), reads=[name], writes=[name])
        return m
    ShI = sel("ShI", [128, 128], [[1, 128]], -1, -1)
    ShsI = sel("ShsI", [128, 128], [[1, 128]], -1, -1)
    S.op("pool", lambda e: e.memset(ShsI.rearrange("p (b i) -> p b i", i=8)[:, :, 0:1], 0.0), reads=["ShsI"], writes=["ShsI"])
    for m, n in ((ShI, "ShI"), (ShsI, "ShsI")):
        S.op("dve", lambda e, m=m: e.tensor_tensor(out=m, in0=m, in1=C.identf, op=ALU.subtract), reads=[n, "identf"], writes=[n])
    Ecar = alloc([128, 128])
    S.op("pool", lambda e: e.memset(Ecar, 0.0), writes=["Ecar"])
    S.op("pool", lambda e: e.memset(Ecar[:, 0:1], 1.0), reads=["Ecar"], writes=["Ecar"])
    S.op("pool", lambda e: e.affine_select(out=Ecar[:, 0:1], in_=Ecar[:, 0:1], pattern=[[0, 1]], compare_op=ALU.is_ge, fill=0.0,
                                           base=-127, channel_multiplier=1), reads=["Ecar"], writes=["Ecar"])
    Esel = sel("Esel", [16, 128], [[1, 128]], -8, 0)

    tiles = list(range(C.ntiles)) + ([NT] if DBG.get('sample', True) else [])
    if not tiles:
        return
    _load_x(C, tiles[0], 0)
    if len(tiles) > 1:
        _load_x(C, tiles[1], 1)
    _norm_pre(C, 0)
    groups = [(c0, 512) for c0 in range(0, 3072, 512)] + [(3072, 128)]
    for ti, t in enumerate(tiles):
        slot = ti % 2
        if ti + 2 < len(tiles):
            _load_x(C, tiles[ti + 2], (ti + 2) % 3)
        _norm_T(C, ti % 3, ((ti + 1) % 3) if ti + 1 < len(tiles) else None)
        sample = (t == NT)
        ps, psn = psb[ti % 2], "ps%d" % (ti % 2)
        pp, ppn = psb[(ti + 1) % 2], "ps%d" % ((ti + 1) % 2)
        zt, ztn = zb[ti % 2], "zb%d" % (ti % 2)
        for (c0, n) in groups:
            bank = _nextbank(C)
            pb = _proj_tm(C, Wps, "Wps", c0, n, bank)
            S.op("act", lambda e, pb=pb, c0=c0, n=n, ps=ps: e.copy(out=ps[:, c0:c0 + n], in_=pb[:, 0:n]),
                 reads=["pb%d" % bank], writes=[psn])
        for hf in range(2):
            bank = _nextbank(C)
            pb = _proj_tm(C, Wgb, "Wgb", hf * 512, 512, bank)
            S.op("act", lambda e, pb=pb, hf=hf, slot=slot: e.activation(out=sgb[slot][:, hf * 512:(hf + 1) * 512],
                                                                         in_=pb, func=AF.Silu),
                 reads=["pb%d" % bank], writes=["sgb%d" % slot])
        S.dma("sp", lambda e, t=t, slot=slot: e.dma_start(out=T["sgb"][t], in_=sgb[slot]),
              reads=["sgb%d" % slot], writes=["d_sgb"], home="sgb%d" % slot)
        if sample:
            S.dma("sp", lambda e, ps=ps: e.dma_start(out=O["ssh"], in_=_pstride(ps[7:8, :], 8, 16)),
                  reads=[psn], writes=["o_ssh"], home=psn)
        elif t == NT - 1:
            S.dma("sp", lambda e, ps=ps: e.dma_start(out=O["psh"].rearrange("(o c) -> o c", o=1), in_=ps[127:128, :]),
                  reads=[psn], writes=["o_psh"], home=psn)
        carry = (not sample) and ti > 0
        for (c0, n) in groups:
            bank = _nextbank(C)
            pb = C.pbank(bank)
            last1 = not (carry or sample)
            S.op("pe", lambda e, pb=pb, c0=c0, n=n, ps=ps, m=(ShsI if sample else ShI), last1=last1: e.matmul(
                pb[:, 0:n], lhsT=m, rhs=ps[:, c0:c0 + n], start=True, stop=last1),
                reads=["ShsI" if sample else "ShI", psn], writes=["pb%d" % bank])
            if carry:
                S.op("pe", lambda e, pb=pb, c0=c0, n=n, pp=pp: e.matmul(pb[:, 0:n], lhsT=Ecar, rhs=pp[:, c0:c0 + n],
                                                                         start=False, stop=True),
                     reads=["Ecar", ppn], writes=["pb%d" % bank])
            if sample:
                S.op("pe", lambda e, pb=pb, c0=c0, n=n: e.matmul(pb[:, 0:n], lhsT=Esel, rhs=sst[:, c0:c0 + n],
                                                                  start=False, stop=True),
                     reads=["Esel", "sst"], writes=["pb%d" % bank])
            S.op("dve", lambda e, pb=pb, c0=c0, n=n, zt=zt: e.tensor_tensor(out=zt[:, c0:c0 + n], in0=pb[:, 0:n],
                                                                            in1=mubc[:, c0:c0 + n], op=ALU.mult),
                 reads=["pb%d" % bank, "mubc"], writes=[ztn])
        S.op("dve", lambda e, zt=zt, ps=ps: e.tensor_tensor(out=zt, in0=zt, in1=ps, op=ALU.add), reads=[ztn, psn], writes=[ztn])
        S.dma("sp", lambda e, t=t, zt=zt: e.dma_start(out=T["z"][t], in_=zt), reads=[ztn], writes=["d_z"], home=ztn)


def phase3(C, tiles=None, scan=False, reuse=False):
    C.rotbanks = (1, 2, 3, 4)
    S, alloc, I, O, T = C.S, C.alloc, C.I, C.O, C.T
    C.reset()
    _common_consts(C, init=not reuse)
    w_in = I["w_in"]
    Wm = alloc([128, 8, 2048], BF16)
    Wo = alloc([128, 8, 1024], BF16)
    if reuse:
        pass
    elif DBG.get("precast", True) and "p1" in C.phases:
        _load_wbf(C, Wm, "Wm", "wbf_in", 4224, 2048)
        _load_wbf(C, Wo, "Wo", "wbf_o", 0, 1024)
    else:
        for k in range(8):
            for c0 in range(0, 2048, 512):
                S.dma("pool", lambda e, k=k, c0=c0: e.dma_start(out=Wm[:, k, c0:c0 + 512],
                                                                in_=w_in[k * 128:(k + 1) * 128, 6784 + c0:6784 + c0 + 512]),
                      writes=["Wm"])
        _load_w(C, Wo, "Wo", I["w_o"])
    fgbc = alloc([128, D])
    if not reuse:
        S.dma("sp", lambda e: e.dma_start(out=fgbc, in_=I["final_g"].partition_broadcast(128)), writes=["fgbc"])
    C.p3_persist = C.apos
    sm = alloc([128, 2048])
    yab = [alloc([128, D]), alloc([128, D])]
    ybb = [alloc([128, D]), alloc([128, D])]
    mg = alloc([128, D], BF16)
    mgT = alloc([128, 8, 128], BF16)
    res = alloc([128, D])
    yo = [alloc([128, D]), alloc([128, D])]
    ss2 = alloc([128, 1])
    rs2 = alloc([128, 1])

    if tiles is None:
        tiles = list(range(C.ntiles)) + ([NT] if DBG.get('sample', True) else [])
    scan_ops = _scan_setup(C) if scan else []
    per_tile = (len(scan_ops) + max(len(tiles), 1) - 1) // max(len(tiles), 1)
    if not tiles:
        for fn in scan_ops:
            fn()
        return

    def emit_scan(n):
        for _ in range(max(n, 0)):
            if scan_ops:
                scan_ops.pop(0)()

    def loads(ti):
        t = tiles[ti]
        sl = ti % 2
        S.dma("sp", lambda e: e.dma_start(out=yab[sl], in_=T["ya"][t]), reads=["d_ya"], writes=["yab%d" % sl])
        S.dma("sp", lambda e: e.dma_start(out=ybb[sl], in_=T["yb"][t]), reads=["d_yb"], writes=["ybb%d" % sl])
    _load_x(C, tiles[0], 0)
    if len(tiles) > 1:
        _load_x(C, tiles[1], 1)
    loads(0)
    _norm_pre(C, 0)
    for ti, t in enumerate(tiles):
        slot = ti % 2
        if ti + 2 < len(tiles):
            _load_x(C, tiles[ti + 2], (ti + 2) % 3)
        if ti + 1 < len(tiles):
            loads(ti + 1)
        _norm_T(C, ti % 3, ((ti + 1) % 3) if ti + 1 < len(tiles) else None)
        emit_scan(2)
        for gi in range(4):
            bank = _nextbank(C)
            pb = _proj_tm(C, Wm, "Wm", gi * 512, 512, bank)
            S.op("act", lambda e, pb=pb, gi=gi: e.activation(out=sm[:, gi * 512:(gi + 1) * 512], in_=pb, func=AF.Sigmoid),
                 reads=["pb%d" % bank], writes=["sm"])
        ya, yb = yab[slot], ybb[slot]
        S.op("dve", lambda e, ya=ya: e.tensor_tensor(out=ya, in0=ya, in1=sm[:, 0:1024], op=ALU.mult),
             reads=["yab%d" % slot, "sm"], writes=["yab%d" % slot])
        S.op("pool", lambda e, yb=yb: e.tensor_tensor(out=yb, in0=yb, in1=sm[:, 1024:2048], op=ALU.mult),
             reads=["ybb%d" % slot, "sm"], writes=["ybb%d" % slot])
        S.op("dve", lambda e, ya=ya, yb=yb: e.tensor_tensor(out=mg, in0=ya, in1=yb, op=ALU.add),
             reads=["yab%d" % slot, "ybb%d" % slot], writes=["mg"])
        emit_scan(1)
        pT0 = C.pbank(0, BF16)
        for k in range(8):
            S.op("pe", lambda e, k=k: e.transpose(out=pT0[:, k * 128:(k + 1) * 128], in_=mg[:, k * 128:(k + 1) * 128],
                                                  identity=C.ident), reads=["mg", "ident"], writes=["pb0"])
        S.op("act", lambda e: e.copy(out=mgT.rearrange("p k t -> p (k t)"), in_=pT0), reads=["pb0"], writes=["mgT"])
        xt = C.xt[ti % 3]
        for hf in range(2):
            bank = _nextbank(C)
            pb = C.pbank(bank)
            for k in range(8):
                S.op("pe", lambda e, k=k, pb=pb, hf=hf: e.matmul(pb, lhsT=mgT[:, k, :], rhs=Wo[:, k, hf * 512:(hf + 1) * 512],
                                                                  start=(k == 0), stop=(k == 7)),
                     reads=["mgT", "Wo"], writes=["pb%d" % bank])
            S.op("dve", lambda e, pb=pb, hf=hf, xt=xt: e.tensor_tensor(out=res[:, hf * 512:(hf + 1) * 512], in0=pb,
                                                                       in1=xt[:, hf * 512:(hf + 1) * 512], op=ALU.add),
                 reads=["pb%d" % bank, "xt%d" % (ti % 3)], writes=["res"])
        emit_scan(1)
        S.op("pool", lambda e: e.memset(ss2, 0.0), writes=["ss2"])
        S.op("act", lambda e: e.activation(out=C.junk, in_=res, func=AF.Square, accum_out=ss2),
             reads=["res", "ss2"], writes=["junk", "ss2"])
        S.op("dve", lambda e: e.tensor_scalar(out=rs2, in0=ss2, scalar1=1.0 / D, scalar2=1e-6, op0=ALU.mult, op1=ALU.add),
             reads=["ss2"], writes=["rs2"])
        S.op("act", lambda e: e.activation(out=rs2, in_=rs2, func=AF.Sqrt), reads=["rs2"], writes=["rs2"])
        S.op("dve", lambda e: e.reciprocal(out=rs2, in_=rs2), reads=["rs2"], writes=["rs2"])
        yt = yo[slot]
        S.op("dve", lambda e, yt=yt: e.scalar_tensor_tensor(out=yt, in0=res, scalar=rs2[:, 0:1], in1=fgbc,
                                                            op0=ALU.mult, op1=ALU.mult),
             reads=["res", "rs2", "fgbc"], writes=["yo%d" % slot])
        dst = O["ys"] if t == NT else O["yp"][t * 128:(t + 1) * 128, :]
        S.dma("sp", lambda e, yt=yt, dst=dst: e.dma_start(out=dst, in_=yt), reads=["yo%d" % slot], writes=["o_y"],
              home="yo%d" % slot)
        emit_scan(per_tile - 4)
    while scan_ops:
        scan_ops.pop(0)()


def _rwkv_post(C, B, y, yn_, v, vn_, sbon, sgbt, sgn_, t, mark_pe=None, sbon_n="sbon"):
    S, T = C.S, C.T
    tD, tE, s16 = B["tD"], B["tE"], B["s16"]
    mean, var = s16[:, 0:16], s16[:, 16:32]
    v3 = lambda ap: ap.rearrange("p (h d) -> p h d", h=16)
    bc = lambda ap: ap.unsqueeze(2).to_broadcast([128, 16, 64])
    S.op("dve", lambda e: e.tensor_reduce(out=mean, in_=v3(y), axis=AX.X, op=ALU.add), reads=[yn_], writes=["s16m"])
    S.op("dve", lambda e: e.tensor_scalar(out=mean, in0=mean, scalar1=1.0 / 64, scalar2=None, op0=ALU.mult),
         reads=["s16m"], writes=["s16m"])
    S.op("dve", lambda e: e.tensor_tensor(out=v3(tD), in0=v3(y), in1=bc(mean), op=ALU.subtract),
         reads=[yn_, "s16m"], writes=[B["tDn"]])
    S.op("dve", lambda e: e.tensor_tensor(out=tE, in0=tD, in1=tD, op=ALU.mult), reads=[B["tDn"]], writes=[B["tEn"]])
    S.op("dve", lambda e: e.tensor_reduce(out=var, in_=v3(tE), axis=AX.X, op=ALU.add), reads=[B["tEn"]], writes=["s16v"])
    S.op("dve", lambda e: e.tensor_scalar(out=var, in0=var, scalar1=1.0 / 64, scalar2=64e-5, op0=ALU.mult, op1=ALU.add),
         reads=["s16v"], writes=["s16v"])
    S.op("act", lambda e: e.activation(out=var, in_=var, func=AF.Sqrt), reads=["s16v"], writes=["s16v"])
    S.op("dve", lambda e: e.reciprocal(out=var, in_=var), reads=["s16v"], writes=["s16v"])
    S.op("dve", lambda e: e.tensor_tensor(out=v3(tD), in0=v3(tD), in1=bc(var), op=ALU.mult), reads=[B["tDn"], "s16v"], writes=[B["tDn"]])
    S.op("dve", lambda e: e.tensor_tensor(out=tD, in0=tD, in1=B["lgbc"], op=ALU.mult), reads=[B["tDn"], "lgbc"], writes=[B["tDn"]])
    S.op("dve", lambda e: e.tensor_tensor(out=tD, in0=tD, in1=B["lbbc"], op=ALU.add), reads=[B["tDn"], "lbbc"], writes=[B["tDn"]])
    S.op("dve", lambda e: e.tensor_tensor(out=v3(tE), in0=v3(v), in1=bc(sbon), op=ALU.mult), reads=[vn_, sbon_n], writes=[B["tEn"]])
    S.op("dve", lambda e: e.tensor_tensor(out=tD, in0=tD, in1=tE, op=ALU.add), reads=[B["tDn"], B["tEn"]], writes=[B["tDn"]])
    S.op("dve", lambda e: e.tensor_tensor(out=B["ybg"], in0=tD, in1=sgbt, op=ALU.mult), reads=[B["tDn"], sgn_], writes=[B["ybgn"]])
    if mark_pe is not None:
        S.mark(mark_pe)
    pT0 = C.pbank(0, BF16)
    for k in range(8):
        S.op("pe", lambda e, k=k: e.transpose(out=pT0[:, k * 128:(k + 1) * 128], in_=B["ybg"][:, k * 128:(k + 1) * 128],
                                              identity=B["ident"]), reads=[B["ybgn"], "ident"], writes=["pb0"])
    S.op("act", lambda e: e.copy(out=B["ybT"].rearrange("p k t -> p (k t)"), in_=pT0), reads=["pb0"], writes=[B["ybTn"]])
    for hf in range(2):
        bank = _nextbank(C)
        pb = C.pbank(bank)
        for k in range(8):
            S.op("pe", lambda e, k=k, pb=pb, hf=hf: e.matmul(pb, lhsT=B["ybT"][:, k, :], rhs=B["WoB"][:, k, hf * 512:(hf + 1) * 512],
                                                              start=(k == 0), stop=(k == 7)),
                 reads=[B["ybTn"], "WoB"], writes=["pb%d" % bank])
        S.op("act", lambda e, pb=pb, hf=hf: e.copy(out=B["ybo"][:, hf * 512:(hf + 1) * 512], in_=pb),
             reads=["pb%d" % bank], writes=[B["ybon"]])
    S.dma("sp", lambda e, t=t: e.dma_start(out=T["yb"][t], in_=B["ybo"]), reads=[B["ybon"]], writes=["d_yb"], home=B["ybon"])


def _post_bufs(C, B):
    S, alloc, I = C.S, C.alloc, C.I
    B["identf"] = alloc([128, 128])
    B["ident"] = alloc([128, 128], BF16)
    S.op("pool", lambda e: e.memset(B["identf"], 0.0), writes=["identf"])
    S.op("pool", lambda e: e.affine_select(out=B["identf"], in_=B["identf"], pattern=[[-1, 128]], compare_op=ALU.not_equal,
                                           fill=1.0, base=0, channel_multiplier=1), reads=["identf"], writes=["identf"])
    S.op("dve", lambda e: e.tensor_copy(out=B["ident"], in_=B["identf"]), reads=["identf"], writes=["ident"])
    B["WoB"] = alloc([128, 8, 1024], BF16)
    if DBG.get("precast", True) and "p1" in C.phases:
        _load_wbf(C, B["WoB"], "WoB", "wbf_ob", 0, 1024)
    else:
        _load_w(C, B["WoB"], "WoB", I["w_out_b"])
    for nm, src in (("lgbc", "lnx_g"), ("lbbc", "lnx_b")):
        B[nm] = alloc([128, D])
        S.dma("sp", lambda e, nm=nm, src=src: e.dma_start(out=B[nm], in_=I[src].partition_broadcast(128)), writes=[nm])
    B["tD"] = alloc([128, D])
    B["tDn"] = "tD"
    if B.get("alloc_tE", True):
        B["tE"] = alloc([128, D])
        B["tEn"] = "tE"
    B["s16"] = alloc([128, 96])
    B["ybgn"], B["ybTn"], B["ybon"] = "ybg", "ybT", "ybo"
    if B.get("alloc_yb", True):
        B["ybg"] = alloc([128, D], BF16)
        B["ybT"] = alloc([128, 8, 128], BF16)
        B["ybo"] = alloc([128, D])


def phase2b(C):
    S, alloc, I, O, T = C.S, C.alloc, C.I, C.O, C.T
    C.reset()
    B = {"alloc_tE": False, "alloc_yb": False}
    _post_bufs(C, B)
    C.rotbanks = (1, 2, 3)
    identf, ident = B["identf"], B["ident"]
    tD, s16 = B["tD"], B["s16"]
    W2A2 = alloc([128, D], BF16)
    S.dma("pool", lambda e: e.dma_start(out=W2A2[0:64, :], in_=I["w2"]), writes=["W2A2"])
    S.dma("pool", lambda e: e.dma_start(out=W2A2[64:128, :], in_=I["a2"]), writes=["W2A2"])
    vecs = alloc([128, D])
    S.dma("sp", lambda e: e.dma_start(out=vecs[0:1, :], in_=I["w0"].rearrange("(o c) -> o c", o=1)), writes=["vecs"])
    S.dma("sp", lambda e: e.dma_start(out=vecs[32:33, :], in_=I["a0"].rearrange("(o c) -> o c", o=1)), writes=["vecs"])
    ones = alloc([128, 128])
    S.op("pool", lambda e: e.memset(ones, 1.0), writes=["ones"])
    negcol = alloc([128, 1])
    S.op("pool", lambda e: e.memset(negcol, -CDEC), writes=["negcol"])
    zl = alloc([128, 128], BF16)
    zr = alloc([128, 512], BF16)
    S.op("pool", lambda e: e.memset(zl, 0.0), writes=["zl"])
    S.op("pool", lambda e: e.memset(zr, 0.0), writes=["zr"])

    def tri(name, val, pattern, cm, base):
        m = alloc([128, 128])
        S.op("pool", lambda e: e.memset(m, val), writes=[name])
        S.op("pool", lambda e: e.affine_select(out=m, in_=m, pattern=pattern, compare_op=ALU.is_ge, fill=0.0,
                                               base=base, channel_multiplier=cm), reads=[name], writes=[name])
        return m
    Lincl = tri("Lincl", -CDEC, [[1, 128]], -1, 0)
    Lstr = tri("Lstr", -CDEC, [[1, 128]], -1, -1)
    Ustr = tri("Ustr", -CDEC, [[-1, 128]], 1, -1)
    MlowS = alloc([128, 512])
    S.op("pool", lambda e: e.memset(MlowS, 1.0), writes=["MlowS"])
    for blk in range(4):
        S.op("pool", lambda e, blk=blk: e.affine_select(out=MlowS[:, blk * 128:(blk + 1) * 128], in_=MlowS[:, blk * 128:(blk + 1) * 128],
                                                        pattern=[[-1, 128]], compare_op=ALU.is_ge, fill=0.0, base=-1, channel_multiplier=1),
             reads=["MlowS"], writes=["MlowS"])
    Mask4 = alloc([128, 512])
    S.op("pool", lambda e: e.memset(Mask4, 1.0), writes=["Mask4"])
    for blk in range(4):
        S.op("pool", lambda e, blk=blk: e.affine_select(out=Mask4[:, blk * 128:(blk + 1) * 128], in_=Mask4[:, blk * 128:(blk + 1) * 128],
                                                        pattern=[[1, 128]], compare_op=ALU.is_ge, fill=0.0,
                                                        base=(-1 if blk % 2 == 0 else 0), channel_multiplier=-1),
             reads=["Mask4"], writes=["Mask4"])
    bcs = {}
    for nm, src in (("kkbc", "k_k"), ("kabc", "k_a"), ("rkbc", "r_k")):
        bcs[nm] = alloc([128, D])
        S.dma("sp", lambda e, nm=nm, src=src: e.dma_start(out=bcs[nm], in_=I[src].partition_broadcast(128)), writes=[nm])
    ztb = [alloc([128, 3200]), alloc([128, 3200])]
    sgbt = alloc([128, D])
    sg = alloc([128, D])
    av = alloc([128, D])
    tA = alloc([128, D])
    tB = alloc([128, D])
    tC = alloc([128, D])
    Ea = alloc([128, D])
    B["ybo"], B["ybon"] = Ea, "Ea"
    Eb = alloc([128, D])
    B["tE"], B["tEn"] = Eb, "Eb"
    lT = alloc([128, 128], BF16)
    At, Rt, Bt, Kt, Bh, Kh, Vb = [alloc([128, D], BF16) for _ in range(7)]
    B["ybg"], B["ybgn"] = At, "At"
    ARt = alloc([128, 8, 2, 128], BF16)
    BtT = alloc([128, 8, 128], BF16)
    B["ybT"], B["ybTn"] = BtT, "BtT"
    KtT = alloc([128, 8, 128], BF16)
    WT = KtT
    Am = [alloc([128, 512], BF16) for _ in range(16)]
    Nb = [[alloc([128, 4, 128], BF16) for _ in range(2)] for _ in range(4)]
    Lb = [[alloc([128, 4, 128], BF16) for _ in range(2)] for _ in range(4)]
    L0 = [Lb[hg][1] for hg in range(4)]
    Gbf = [alloc([128, 4, 128], BF16) for _ in range(4)]
    Wall = Rt.rearrange("p (h d) -> p h d", h=16)
    Z32 = tA.rearrange("p (h d) -> p h d", h=16)
    Ub = Bt
    H32 = alloc([128, 8, 64])
    Hbf = [alloc([128, 8, 2, 64], BF16) for _ in range(2)]
    PC = alloc([128, 8])
    yv = Ea
    S.op("pool", lambda e: e.memset(H32, 0.0), writes=["H32"])
    S.op("pool", lambda e: e.memset(Hbf[0], 0.0), writes=["Hbf0"])
    S.op("pool", lambda e: e.memset(Hbf[1], 0.0), writes=["Hbf1"])
    ss16, rn16 = s16[:, 32:48], s16[:, 32:48]
    v3 = lambda ap: ap.rearrange("p (h d) -> p h d", h=16)
    bc = lambda ap: ap.unsqueeze(2).to_broadcast([128, 16, 64])

    tiles = ([NT] if DBG.get('sample', True) else []) + list(range(C.ntiles))
    def ldz(ti):
        S.dma("sp", lambda e, ti=ti: e.dma_start(out=ztb[ti % 2], in_=T["z"][tiles[ti]]), reads=["d_z"], writes=["zt%d" % (ti % 2)])
    if tiles:
        ldz(0)
    for ti, t in enumerate(tiles):
        sample = (t == NT)
        zt, ztn = ztb[ti % 2], "zt%d" % (ti % 2)
        sbon, sbn = s16[:, 48 + 16 * (ti % 2):64 + 16 * (ti % 2)], "sbon%d" % (ti % 2)
        S.mark("L%d" % ti)
        if ti + 1 < len(tiles):
            ldz(ti + 1)
        S.mark("P1_%d" % ti)
        r_, k_, v_, lor = zt[:, 0:1024], zt[:, 1024:2048], zt[:, 2048:3072], zt[:, 3072:3200]
        bank = _nextbank(C)
        pb = C.pbank(bank)
        S.op("pe", lambda e, r_=r_, k_=k_, v_=v_, lor=lor, pb=pb: e.transpose(out=pb[:, 0:128], in_=lor, identity=identf), reads=[ztn, "identf"], writes=["pb%d" % bank])
        S.op("act", lambda e, r_=r_, k_=k_, v_=v_, lor=lor, pb=pb: e.activation(out=lT[0:64, :], in_=pb[0:64, 0:128], func=AF.Tanh), reads=["pb%d" % bank], writes=["lT"])
        S.op("act", lambda e, r_=r_, k_=k_, v_=v_, lor=lor, pb=pb: e.copy(out=lT[64:128, :], in_=pb[64:128, 0:128]), reads=["pb%d" % bank], writes=["lT"])
        for (dst, dn, r0, v0) in ((sg, "sg", 0, 0), (av, "av", 64, 32)):
            for hf in range(2):
                bank = _nextbank(C)
                pb = C.pbank(bank)
                S.op("pe", lambda e, r_=r_, k_=k_, v_=v_, lor=lor, pb=pb, r0=r0, hf=hf: e.matmul(pb, lhsT=lT[r0:r0 + 64, :], rhs=W2A2[r0:r0 + 64, hf * 512:(hf + 1) * 512],
                                                                    start=True, stop=False), reads=["lT", "W2A2"], writes=["pb%d" % bank])
                S.op("pe", lambda e, r_=r_, k_=k_, v_=v_, lor=lor, pb=pb, v0=v0, hf=hf: e.matmul(pb, lhsT=ones[v0:v0 + 1, :], rhs=vecs[v0:v0 + 1, hf * 512:(hf + 1) * 512],
                                                                    start=False, stop=True), reads=["ones", "vecs"], writes=["pb%d" % bank])
                S.op("act", lambda e, r_=r_, k_=k_, v_=v_, lor=lor, pb=pb, dst=dst, hf=hf: e.activation(out=dst[:, hf * 512:(hf + 1) * 512], in_=pb, func=AF.Sigmoid),
                     reads=["pb%d" % bank], writes=[dn])
        if DBG.get('p2b_stop', 99) <= 1:
            continue
        S.mark("P2_%d" % ti)
        S.op("dve", lambda e, r_=r_, k_=k_, v_=v_, lor=lor: e.tensor_tensor(out=tA, in0=k_, in1=bcs["kkbc"], op=ALU.mult), reads=[ztn, "kkbc"], writes=["tA"])
        S.op("dve", lambda e, r_=r_, k_=k_, v_=v_, lor=lor: e.tensor_tensor(out=tB, in0=tA, in1=tA, op=ALU.mult), reads=["tA"], writes=["tB"])
        S.op("dve", lambda e, r_=r_, k_=k_, v_=v_, lor=lor: e.tensor_reduce(out=ss16, in_=v3(tB), axis=AX.X, op=ALU.add), reads=["tB"], writes=["s16n"])
        S.op("act", lambda e, r_=r_, k_=k_, v_=v_, lor=lor: e.activation(out=ss16, in_=ss16, func=AF.Sqrt), reads=["s16n"], writes=["s16n"])
        S.op("dve", lambda e, r_=r_, k_=k_, v_=v_, lor=lor: e.tensor_scalar(out=ss16, in0=ss16, scalar1=1e-12, scalar2=None, op0=ALU.max), reads=["s16n"], writes=["s16n"])
        S.op("dve", lambda e, r_=r_, k_=k_, v_=v_, lor=lor: e.reciprocal(out=ss16, in_=ss16), reads=["s16n"], writes=["s16n"])
        S.op("dve", lambda e, r_=r_, k_=k_, v_=v_, lor=lor: e.tensor_tensor(out=v3(tA), in0=v3(tA), in1=bc(rn16), op=ALU.mult), reads=["tA", "s16n"], writes=["tA"])
        S.op("dve", lambda e, r_=r_, k_=k_, v_=v_, lor=lor: e.scalar_tensor_tensor(out=tB, in0=av, scalar=-1.0, in1=bcs["kabc"], op0=ALU.add, op1=ALU.mult),
             reads=["av", "kabc"], writes=["tB"])
        S.op("dve", lambda e, r_=r_, k_=k_, v_=v_, lor=lor: e.scalar_tensor_tensor(out=tB, in0=tB, scalar=1.0, in1=k_, op0=ALU.add, op1=ALU.mult),
             reads=["tB", ztn], writes=["tB"])
        S.op("dve", lambda e, r_=r_, k_=k_, v_=v_, lor=lor: e.tensor_tensor(out=tC, in0=tA, in1=av, op=ALU.mult), reads=["tA", "av"], writes=["tC"])
        S.mark("P3_%d" % ti)
        S.op("dve", lambda e, r_=r_, k_=k_, v_=v_, lor=lor: e.tensor_tensor(out=tD, in0=r_, in1=tB, op=ALU.mult), reads=[ztn, "tB"], writes=["tD"])
        S.op("dve", lambda e, r_=r_, k_=k_, v_=v_, lor=lor: e.tensor_tensor(out=tD, in0=tD, in1=bcs["rkbc"], op=ALU.mult), reads=["tD", "rkbc"], writes=["tD"])
        S.op("dve", lambda e, r_=r_, k_=k_, v_=v_, lor=lor, sbon=sbon: e.tensor_reduce(out=sbon, in_=v3(tD), axis=AX.X, op=ALU.add), reads=["tD"], writes=[sbn])
        if DBG.get('p2b_stop', 99) <= 2:
            continue
        if sample:
            S.op("act", lambda e, r_=r_, k_=k_, v_=v_, lor=lor: e.activation(out=Ea, in_=sg, func=AF.Exp, scale=-CDEC), reads=["sg"], writes=["Ea"])
            S.op("dve", lambda e, r_=r_, k_=k_, v_=v_, lor=lor: e.tensor_scalar(out=tA, in0=tA, scalar1=-1.0, scalar2=None, op0=ALU.mult), reads=["tA"], writes=["tA"])
            for qi, (src, sn) in enumerate(((r_, ztn), (Ea, "Ea"), (tB, "tB"), (v_, ztn), (tA, "tA"), (tC, "tC"))):
                S.dma("sp", lambda e, r_=r_, k_=k_, v_=v_, lor=lor, qi=qi, src=src: e.dma_start(out=T["s6"][qi], in_=src), reads=[sn], writes=["d_s6"], home=sn)
            S.dma("sp", lambda e, r_=r_, k_=k_, v_=v_, lor=lor, sbon=sbon: e.dma_start(out=T["sextra"][:, 0:16], in_=sbon), reads=[sbn], writes=["d_sx"], home=sbn)
            continue
        def cums(Lm, ln, outs):
            for hf in range(2):
                bank = _nextbank(C)
                pb = C.pbank(bank)
                S.op("pe", lambda e, r_=r_, k_=k_, v_=v_, lor=lor, pb=pb, hf=hf: e.matmul(pb, lhsT=Lm, rhs=sg[:, hf * 512:(hf + 1) * 512], start=True, stop=True),
                     reads=[ln, "sg"], writes=["pb%d" % bank])
                for (dst, dn, sc) in outs:
                    S.op("act", lambda e, r_=r_, k_=k_, v_=v_, lor=lor, pb=pb, dst=dst, sc=sc, hf=hf: e.activation(out=dst[:, hf * 512:(hf + 1) * 512], in_=pb,
                                                                                       func=AF.Exp, scale=sc),
                         reads=["pb%d" % bank], writes=[dn])
        cums(Lincl, "Lincl", ((Ea, "Ea", 1.0), (Eb, "Eb", -1.0)))
        S.op("dve", lambda e, r_=r_, k_=k_, v_=v_, lor=lor: e.tensor_tensor(out=Rt, in0=r_, in1=Ea, op=ALU.mult), reads=[ztn, "Ea"], writes=["Rt"])
        S.op("pool", lambda e, r_=r_, k_=k_, v_=v_, lor=lor: e.tensor_tensor(out=Bt, in0=tC, in1=Eb, op=ALU.mult), reads=["tC", "Eb"], writes=["Bt"])
        S.op("dve", lambda e, r_=r_, k_=k_, v_=v_, lor=lor: e.tensor_tensor(out=Kt, in0=tB, in1=Eb, op=ALU.mult), reads=["tB", "Eb"], writes=["Kt"])
        cums(Lstr, "Lstr", ((Ea, "Ea", 1.0),))
        S.op("dve", lambda e, r_=r_, k_=k_, v_=v_, lor=lor: e.scalar_tensor_tensor(out=At, in0=tA, scalar=-1.0, in1=Ea, op0=ALU.mult, op1=ALU.mult),
             reads=["tA", "Ea"], writes=["At"])
        cums(Ustr, "Ustr", ((Eb, "Eb", 1.0),))
        S.op("pool", lambda e, r_=r_, k_=k_, v_=v_, lor=lor: e.tensor_tensor(out=Bh, in0=tC, in1=Eb, op=ALU.mult), reads=["tC", "Eb"], writes=["Bh"])
        S.op("dve", lambda e, r_=r_, k_=k_, v_=v_, lor=lor: e.tensor_tensor(out=Kh, in0=tB, in1=Eb, op=ALU.mult), reads=["tB", "Eb"], writes=["Kh"])
        S.op("act", lambda e, r_=r_, k_=k_, v_=v_, lor=lor: e.copy(out=Vb, in_=v_), reads=[ztn], writes=["Vb"])
        bank = _nextbank(C)
        pb = C.pbank(bank)
        for p in range(8):
            S.op("pe", lambda e, r_=r_, k_=k_, v_=v_, lor=lor, pb=pb, p=p: e.matmul(pb[:, p:p + 1], lhsT=sg[:, p * 128:(p + 1) * 128], rhs=negcol, start=True, stop=True),
                 reads=["sg", "negcol"], writes=["pb%d" % bank])
        S.op("act", lambda e, r_=r_, k_=k_, v_=v_, lor=lor, pb=pb: e.activation(out=PC, in_=pb[:, 0:8], func=AF.Exp), reads=["pb%d" % bank], writes=["PC"])
        if DBG.get('p2b_stop', 99) <= 3:
            continue
        S.mark("A_%d" % ti)
        pT0 = C.pbank(0, BF16)
        for qi, (src, sn, dst, dn) in enumerate(((At, "At", ARt[:, :, 0, :], "ARt"), (Rt, "Rt", ARt[:, :, 1, :], "ARt"),
                                                 (Bt, "Bt", BtT, "BtT"), (Kt, "Kt", KtT, "KtT"))):
            pTq = C.pbank(4 + qi, BF16)
            for k in range(8):
                S.op("pe", lambda e, r_=r_, k_=k_, v_=v_, lor=lor, k=k, src=src, pTq=pTq: e.transpose(
                    out=pTq[:, k * 128:(k + 1) * 128], in_=src[:, k * 128:(k + 1) * 128], identity=ident),
                    reads=[sn, "ident"], writes=["pb%d" % (4 + qi)])
            S.op("act", lambda e, r_=r_, k_=k_, v_=v_, lor=lor, dst=dst, pTq=pTq: e.copy(out=dst, in_=pTq.rearrange("p (k t) -> p k t", k=8)),
                 reads=["pb%d" % (4 + qi)], writes=[dn])
        if DBG.get('p2b_stop', 99) <= 4:
            continue
        C.rotbanks = (1, 2, 3, 4, 5, 6, 7)
        for h in range(16):
            p, rows = h // 2, slice((h % 2) * 64, (h % 2) * 64 + 64)
            bank = _nextbank(C)
            pb = C.pbank(bank)
            S.op("pe", lambda e, r_=r_, k_=k_, v_=v_, lor=lor, pb=pb, p=p, rows=rows: e.matmul(pb[:, 0:256], lhsT=BtT[rows, p, :],
                                                                  rhs=ARt[rows, p, :, :].rearrange("q a t -> q (a t)"), start=True, stop=True),
                 reads=["BtT", "ARt"], writes=["pb%d" % bank])
            S.op("pe", lambda e, r_=r_, k_=k_, v_=v_, lor=lor, pb=pb, p=p, rows=rows: e.matmul(pb[:, 256:512], lhsT=KtT[rows, p, :],
                                                                  rhs=ARt[rows, p, :, :].rearrange("q a t -> q (a t)"), start=True, stop=True),
                 reads=["KtT", "ARt"], writes=["pb%d" % bank])
            S.op("dve", lambda e, r_=r_, k_=k_, v_=v_, lor=lor, pb=pb, h=h: e.tensor_tensor(out=Am[h], in0=pb, in1=Mask4, op=ALU.mult),
                 reads=["pb%d" % bank, "Mask4"], writes=["Am%d" % h])
        C.rotbanks = (1, 2, 3)
        if DBG.get('p2b_stop', 99) == 45:
            continue
        for half in range(2):
            for i in range(8):
                h = half * 8 + i
                S.op("pe", lambda e, r_=r_, k_=k_, v_=v_, lor=lor, i=i, h=h: e.transpose(out=pT0[:, i * 128:(i + 1) * 128], in_=Am[h][:, 0:128], identity=ident),
                     reads=["Am%d" % h, "ident"], writes=["pb0"])
            for q in range(2):
                hg = half * 2 + q
                S.op("act", lambda e, r_=r_, k_=k_, v_=v_, lor=lor, hg=hg, q=q: e.copy(out=L0[hg].rearrange("p h s -> p (h s)"), in_=pT0[:, q * 512:(q + 1) * 512]),
                     reads=["pb0"], writes=["Lb%d_1" % hg])
        if DBG.get('p2b_stop', 99) <= 5:
            continue
        st_ = []
        for hg in range(4):
            pgb = 4 + hg
            PG = C.pbank(pgb)
            pgn = "pb%d" % pgb
            S.op("pe", lambda e, r_=r_, k_=k_, v_=v_, lor=lor, PG=PG: e.matmul(PG, lhsT=zl, rhs=zr, start=True, stop=False, skip_group_check=True),
                 reads=["zl", "zr"], writes=[pgn])
            for hh in range(4):
                h = hg * 4 + hh
                S.op("pe", lambda e, r_=r_, k_=k_, v_=v_, lor=lor, PG=PG, hh=hh, h=h: e.matmul(PG[:, hh * 128:hh * 128 + 64], lhsT=ident, rhs=At[:, h * 64:(h + 1) * 64],
                                                                  start=False, stop=False, skip_group_check=True),
                     reads=["ident", "At"], writes=[pgn])
                S.op("pe", lambda e, r_=r_, k_=k_, v_=v_, lor=lor, PG=PG, hh=hh, h=h: e.matmul(PG[:, hh * 128 + 64:(hh + 1) * 128], lhsT=Am[h][:, 256:384],
                                                                  rhs=Vb[:, h * 64:(h + 1) * 64], start=False, stop=False, skip_group_check=True),
                     reads=["Am%d" % h, "Vb"], writes=[pgn])
            S.op("dve", lambda e, r_=r_, k_=k_, v_=v_, lor=lor, PG=PG, hg=hg: e.tensor_copy(out=Gbf[hg].rearrange("p h s -> p (h s)"), in_=PG),
                 reads=[pgn], writes=["Gbf%d" % hg])
            st_.append(dict(PG=PG, pgn=pgn,
                            Ncur=[Am[hg * 4 + hh][:, 0:128] for hh in range(4)], Nn=["Am%d" % (hg * 4 + hh) for hh in range(4)],
                            Lcur=[L0[hg][:, hh, :] for hh in range(4)], Ln=["Lb%d_1" % hg] * 4))
        for j in range(7):
            for hg in range(4):
                q = st_[hg]
                PG, pgn, Ncur, Nn_, Lcur, Ln_ = q["PG"], q["pgn"], q["Ncur"], q["Nn"], q["Lcur"], q["Ln"]
                for hh in range(4):
                    S.op("pe", lambda e, r_=r_, k_=k_, v_=v_, lor=lor, PG=PG, hh=hh, nl=Ncur[hh], hg=hg, j=j: e.matmul(
                        PG[:, hh * 128:(hh + 1) * 128], lhsT=nl, rhs=Gbf[hg][:, hh, :], start=False, stop=(j == 6),
                        skip_group_check=True), reads=[Nn_[hh], "Gbf%d" % hg], writes=[pgn])
                if j < 6:
                    bank = _nextbank(C)
                    pb = C.pbank(bank)
                    for hh in range(4):
                        S.op("pe", lambda e, r_=r_, k_=k_, v_=v_, lor=lor, pb=pb, hh=hh, ll=Lcur[hh], nl=Ncur[hh]: e.matmul(
                            pb[:, hh * 128:(hh + 1) * 128], lhsT=ll, rhs=nl, start=True, stop=True),
                            reads=[Ln_[hh], Nn_[hh]], writes=["pb%d" % bank])
                    nb = Nb[hg][j % 2]
                    nbn = "Nb%d_%d" % (hg, j % 2)
                    S.op("act", lambda e, r_=r_, k_=k_, v_=v_, lor=lor, pb=pb, nb=nb: e.copy(out=nb.rearrange("p h s -> p (h s)"), in_=pb),
                         reads=["pb%d" % bank], writes=[nbn])
                    if j < 5:
                        bank = _nextbank(C)
                        pb = C.pbank(bank)
                        for hh in range(4):
                            S.op("pe", lambda e, r_=r_, k_=k_, v_=v_, lor=lor, pb=pb, hh=hh, ll=Lcur[hh], nl=Ncur[hh]: e.matmul(
                                pb[:, hh * 128:(hh + 1) * 128], lhsT=nl, rhs=ll, start=True, stop=True),
                                reads=[Ln_[hh], Nn_[hh]], writes=["pb%d" % bank])
                        lb = Lb[hg][j % 2]
                        lbn = "Lb%d_%d" % (hg, j % 2)
                        S.op("act", lambda e, r_=r_, k_=k_, v_=v_, lor=lor, pb=pb, lb=lb: e.copy(out=lb.rearrange("p h s -> p (h s)"), in_=pb),
                             reads=["pb%d" % bank], writes=[lbn])
                        q["Lcur"] = [lb[:, hh, :] for hh in range(4)]
                        q["Ln"] = [lbn] * 4
                    S.op("dve", lambda e, r_=r_, k_=k_, v_=v_, lor=lor, PG=PG, hg=hg: e.tensor_copy(out=Gbf[hg].rearrange("p h s -> p (h s)"), in_=PG),
                         reads=[pgn], writes=["Gbf%d" % hg])
                    q["Ncur"] = [nb[:, hh, :] for hh in range(4)]
                    q["Nn"] = [nbn] * 4
        for hg in range(4):
            PG, pgn = st_[hg]["PG"], st_[hg]["pgn"]
            PGv = PG.rearrange("p (h s) -> p h s", h=4)
            S.op("act", lambda e, r_=r_, k_=k_, v_=v_, lor=lor, PGv=PGv, hg=hg: e.copy(out=Wall[:, hg * 4:(hg + 1) * 4, :], in_=PGv[:, :, 0:64]),
                 reads=[pgn], writes=["Rt"])
            S.op("dve", lambda e, r_=r_, k_=k_, v_=v_, lor=lor, PGv=PGv, hg=hg: e.tensor_copy(out=Z32[:, hg * 4:(hg + 1) * 4, :], in_=PGv[:, :, 64:128]),
                 reads=[pgn], writes=["tA"])
        if DBG.get('p2b_stop', 99) <= 6:
            continue
        WallF = Wall.rearrange("p h d -> p (h d)")
        for k in range(8):
            S.op("pe", lambda e, r_=r_, k_=k_, v_=v_, lor=lor, k=k: e.transpose(out=pT0[:, k * 128:(k + 1) * 128], in_=WallF[:, k * 128:(k + 1) * 128], identity=ident),
                 reads=["Rt", "ident"], writes=["pb0"])
        S.op("act", lambda e, r_=r_, k_=k_, v_=v_, lor=lor: e.copy(out=WT.rearrange("p k t -> p (k t)"), in_=pT0), reads=["pb0"], writes=["KtT"])
        ho, hn = Hbf[ti % 2], Hbf[(ti + 1) % 2]
        hon, hnn = "Hbf%d" % (ti % 2), "Hbf%d" % ((ti + 1) % 2)
        Z32F = Z32.rearrange("p h d -> p (h d)")
        for hb in range(2):
            bank = _nextbank(C)
            pb = C.pbank(bank)
            for i in range(4):
                p = hb * 4 + i
                S.op("pe", lambda e, r_=r_, k_=k_, v_=v_, lor=lor, pb=pb, i=i, p=p, ho=ho: e.matmul(pb[:, i * 128:(i + 1) * 128], lhsT=WT[:, p, :],
                                                                       rhs=ho[:, p, :, :].rearrange("q a d -> q (a d)"), start=True, stop=True),
                     reads=["KtT", hon], writes=["pb%d" % bank])
            S.op("dve", lambda e, r_=r_, k_=k_, v_=v_, lor=lor, pb=pb, hb=hb: e.tensor_tensor(out=Ub[:, hb * 512:(hb + 1) * 512], in0=pb,
                                                                in1=Z32F[:, hb * 512:(hb + 1) * 512], op=ALU.add),
                 reads=["pb%d" % bank, "tA"], writes=["Bt"])
        if DBG.get('p2b_stop', 99) == 71:
            continue
        S.op("dve", lambda e, r_=r_, k_=k_, v_=v_, lor=lor: e.tensor_tensor(out=H32, in0=H32, in1=PC.unsqueeze(2).to_broadcast([128, 8, 64]), op=ALU.mult),
             reads=["H32", "PC"], writes=["H32"])
        for hb in range(2):
            bank = _nextbank(C)
            pb = C.pbank(bank)
            for i in range(8):
                h = hb * 8 + i
                p = h // 2
                S.op("pe", lambda e, r_=r_, k_=k_, v_=v_, lor=lor, pb=pb, i=i, p=p, h=h: e.matmul(pb[:, i * 64:(i + 1) * 64], lhsT=Bh[:, p * 128:(p + 1) * 128],
                                                                     rhs=Ub[:, h * 64:(h + 1) * 64], start=True, stop=False),
                     reads=["Bh", "Bt"], writes=["pb%d" % bank])
                S.op("pe", lambda e, r_=r_, k_=k_, v_=v_, lor=lor, pb=pb, i=i, p=p, h=h: e.matmul(pb[:, i * 64:(i + 1) * 64], lhsT=Kh[:, p * 128:(p + 1) * 128],
                                                                     rhs=Vb[:, h * 64:(h + 1) * 64], start=False, stop=True),
                     reads=["Kh", "Vb"], writes=["pb%d" % bank])
            pbv = pb.rearrange("q (p a d) -> q p a d", p=4, a=2)
            for hf in range(2):
                S.op("dve", lambda e, r_=r_, k_=k_, v_=v_, lor=lor, pbv=pbv, hf=hf, hb=hb: e.tensor_tensor(
                    out=H32[hf * 64:(hf + 1) * 64, hb * 4:(hb + 1) * 4, :], in0=H32[hf * 64:(hf + 1) * 64, hb * 4:(hb + 1) * 4, :],
                    in1=pbv[hf * 64:(hf + 1) * 64, :, hf, :], op=ALU.add), reads=["pb%d" % bank, "H32"], writes=["H32"])
        if DBG.get('p2b_stop', 99) == 72:
            continue
        for hf in range(2):
            S.op("act", lambda e, r_=r_, k_=k_, v_=v_, lor=lor, hn=hn, hf=hf: e.copy(out=hn[hf * 64:(hf + 1) * 64, :, hf, :], in_=H32[hf * 64:(hf + 1) * 64, :, :]),
                 reads=["H32"], writes=[hnn])
        for hb in range(2):
            bank = _nextbank(C)
            pb = C.pbank(bank)
            for i in range(4):
                p = hb * 4 + i
                S.op("pe", lambda e, r_=r_, k_=k_, v_=v_, lor=lor, pb=pb, i=i, p=p, ho=ho: e.matmul(pb[:, i * 128:(i + 1) * 128], lhsT=ARt[:, p, 1, :],
                                                                       rhs=ho[:, p, :, :].rearrange("q a d -> q (a d)"), start=True, stop=False),
                     reads=["ARt", hon], writes=["pb%d" % bank])
                for hf in range(2):
                    h = 2 * p + hf
                    c0 = i * 128 + hf * 64
                    S.op("pe", lambda e, r_=r_, k_=k_, v_=v_, lor=lor, pb=pb, c0=c0, h=h: e.matmul(pb[:, c0:c0 + 64], lhsT=Am[h][:, 128:256],
                                                                     rhs=Ub[:, h * 64:(h + 1) * 64], start=False, stop=False),
                         reads=["Am%d" % h, "Bt"], writes=["pb%d" % bank])
                    S.op("pe", lambda e, r_=r_, k_=k_, v_=v_, lor=lor, pb=pb, c0=c0, h=h, hf=hf: e.matmul(pb[:, c0:c0 + 64], lhsT=Am[h][:, 384:512],
                                                                            rhs=Vb[:, h * 64:(h + 1) * 64], start=False, stop=(hf == 1)),
                         reads=["Am%d" % h, "Vb"], writes=["pb%d" % bank])
            S.op("act", lambda e, r_=r_, k_=k_, v_=v_, lor=lor, pb=pb, hb=hb: e.copy(out=yv[:, hb * 512:(hb + 1) * 512], in_=pb), reads=["pb%d" % bank], writes=["Ea"])
        if DBG.get('p2b_stop', 99) <= 7:
            continue
        S.mark("PD_%d" % ti)
        S.dma("sp", lambda e, t=t: e.dma_start(out=sgbt, in_=T["sgb"][t]), reads=["d_sgb"], writes=["sgbt"])
        _rwkv_post(C, B, yv, "Ea", v_, ztn, sbon, sgbt, "sgbt", t, mark_pe="PP_%d" % ti, sbon_n=sbn)
        if C.debug:
            S.dma("sp", lambda e, t=t: e.dma_start(out=T["dy"][t], in_=yv), reads=["Ea"], writes=["d_dy"], home="Ea")
    S.mark(None)
    n_ = len(tiles)
    if n_:
        S.replay("L0", "P1_0", "P2_0", "P3_0")
    for ti in range(n_):
        S.replay("A_%d" % ti)
        if ti + 1 < n_:
            S.replay("P1_%d" % (ti + 1))
            S.interleave("PD_%d" % ti, "P2_%d" % (ti + 1))
            S.replay("PP_%d" % ti, "L%d" % (ti + 1), "P3_%d" % (ti + 1))
        else:
            S.replay("PD_%d" % ti, "PP_%d" % ti)
    assert not any(S.caps.values()), [k for k, v in S.caps.items() if v]
    if C.ntiles and DBG.get('p2b_stop', 99) > 8:
        Hout = Eb[0:64, :].rearrange("v (p x) -> v p x", p=8)
        nlast = len(tiles)
        for hb in range(2):
            bank = _nextbank(C)
            pb = C.pbank(bank)
            for i in range(4):
                p = hb * 4 + i
                S.op("pe", lambda e, pb=pb, i=i, p=p: e.transpose(out=pb[0:64, i * 128:(i + 1) * 128], in_=H32[:, p, :], identity=identf),
                     reads=["H32", "identf"], writes=["pb%d" % bank])
            S.op("dve", lambda e, pb=pb, hb=hb: e.tensor_copy(out=Hout[:, hb * 4:(hb + 1) * 4, :].rearrange("v p x -> v (p x)"),
                                                              in_=pb[0:64, :]), reads=["pb%d" % bank], writes=["Eb"])
        S.dma("sp", lambda e: e.dma_start(out=O["pw"].rearrange("h v k -> v h k"),
                                          in_=Hout.rearrange("v p (a k) -> v (p a) k", a=2)), reads=["Eb"], writes=["o_pw"], home="Eb")


def phase2c(C):
    C.rotbanks = (1, 2, 3, 4)
    S, alloc, I, O, T = C.S, C.alloc, C.I, C.O, C.T
    C.reset()
    if not DBG.get('sample', True):
        return
    B = {}
    _post_bufs(C, B)
    Sst = alloc([128, 2, 64, 64])
    tmp = alloc([128, 2, 64, 64])
    vec6 = alloc([128, 6, 8, 128])
    ysc = alloc([128, 8, 128])
    sa = alloc([128, 2, 64])
    yv = alloc([128, D])
    vbuf = alloc([128, D])
    sgbt = alloc([128, D])
    S.dma("sp", lambda e: e.dma_start(out=Sst.rearrange("p a v k -> p (a v k)"),
                                      in_=I["swkv"].rearrange("b (hh a) v k -> (b hh) (a v k)", a=2)), writes=["Sst0", "Sst1"])
    for q in range(6):
        for i in range(8):
            src = T["s6"][q].rearrange("(b i) (hh x) -> i b hh x", i=8, x=128)[i]
            S.dma("sp", lambda e, q=q, i=i, src=src: e.dma_start(out=vec6[:, q, i, :], in_=src),
                  reads=["d_s6"], writes=["vec6"])
    bk = lambda vec: vec.unsqueeze(1).to_broadcast([128, 64, 64])
    bv = lambda vec: vec.unsqueeze(2).to_broadcast([128, 64, 64])
    for i in range(8):
        ops = {0: [], 1: []}
        for a in range(2):
            eng = "dve" if a == 0 else "pool"
            Sv, Tv = Sst[:, a], tmp[:, a]
            sn, tn, san, yn = "Sst%d" % a, "tmp%d" % a, "sa%d" % a, "ysc%d" % a
            sl = slice(a * 64, a * 64 + 64)
            r_, w_, k_, v_, a_, b_ = [vec6[:, q, i, sl] for q in range(6)]
            sav = sa[:, a, :]
            L = ops[a]
            L.append((eng, lambda e, Sv=Sv, Tv=Tv, a_=a_: e.tensor_tensor(out=Tv, in0=Sv, in1=bk(a_), op=ALU.mult), [sn, "vec6"], [tn]))
            L.append(("dve", lambda e, Tv=Tv, sav=sav: e.tensor_reduce(out=sav, in_=Tv, axis=AX.X, op=ALU.add), [tn], [san]))
            L.append((eng, lambda e, Sv=Sv, w_=w_: e.tensor_tensor(out=Sv, in0=Sv, in1=bk(w_), op=ALU.mult), [sn, "vec6"], [sn]))
            L.append((eng, lambda e, Tv=Tv, sav=sav, b_=b_: e.tensor_tensor(out=Tv, in0=bv(sav), in1=bk(b_), op=ALU.mult), [san, "vec6"], [tn]))
            L.append((eng, lambda e, Sv=Sv, Tv=Tv: e.tensor_tensor(out=Sv, in0=Sv, in1=Tv, op=ALU.add), [sn, tn], [sn]))
            L.append((eng, lambda e, Tv=Tv, v_=v_, k_=k_: e.tensor_tensor(out=Tv, in0=bv(v_), in1=bk(k_), op=ALU.mult), ["vec6"], [tn]))
            L.append((eng, lambda e, Sv=Sv, Tv=Tv: e.tensor_tensor(out=Sv, in0=Sv, in1=Tv, op=ALU.add), [sn, tn], [sn]))
            L.append((eng, lambda e, Sv=Sv, Tv=Tv, r_=r_: e.tensor_tensor(out=Tv, in0=Sv, in1=bk(r_), op=ALU.mult), [sn, "vec6"], [tn]))
            L.append(("dve", lambda e, Tv=Tv, i=i, sl=sl: e.tensor_reduce(out=ysc[:, i, sl], in_=Tv, axis=AX.X, op=ALU.add), [tn], [yn]))
        for kk_ in range(9):
            for a in (1, 0):
                eng, fn, rd, wr = ops[a][kk_]
                S.op(eng, fn, reads=rd, writes=wr)
    S.dma("sp", lambda e: e.dma_start(out=O["sw"].rearrange("b (hh a) v k -> (b hh) (a v k)", a=2),
                                      in_=Sst.rearrange("p a v k -> p (a v k)")), reads=["Sst0", "Sst1"], writes=["o_sw"], home="Sst0")
    for i in range(8):
        dst = T["sy"].rearrange("(b i) (hh x) -> i b hh x", i=8, x=128)[i]
        S.dma("sp", lambda e, i=i, dst=dst: e.dma_start(out=dst, in_=ysc[:, i, :]), reads=["ysc0", "ysc1"], writes=["d_sy"], home="ysc0")
    S.dma("sp", lambda e: e.dma_start(out=yv, in_=T["sy"]), reads=["d_sy"], writes=["yv"])
    S.dma("sp", lambda e: e.dma_start(out=vbuf, in_=T["z"][NT][:, 2048:3072]), reads=["d_z"], writes=["vbuf"])
    S.dma("sp", lambda e: e.dma_start(out=sgbt, in_=T["sgb"][NT]), reads=["d_sgb"], writes=["sgbt"])
    sbon = B["s16"][:, 48:64]
    S.dma("sp", lambda e: e.dma_start(out=sbon, in_=T["sextra"][:, 0:16]), reads=["d_sx"], writes=["sbon"])
    _rwkv_post(C, B, yv, "yv", vbuf, "vbuf", sbon, sgbt, "sgbt", NT)


def _scan_setup(C):
    S, alloc, I, O, T = C.S, C.alloc, C.I, C.O, C.T
    Sst = alloc([128, 2, 64, 64])
    tmp = alloc([128, 64, 64])
    vecs = [alloc([128, 6, 128]), alloc([128, 6, 128])]
    ysc = alloc([128, 8, 128])
    sa = alloc([128, 64])
    ops = []
    ops.append(lambda: S.dma("sp", lambda e: e.dma_start(out=Sst.rearrange("p a v k -> p (a v k)"),
                                                         in_=I["swkv"].rearrange("b (hh a) v k -> (b hh) (a v k)", a=2)),
                             writes=["Sst"]))
    bk = lambda vec: vec.unsqueeze(1).to_broadcast([128, 64, 64])
    bv = lambda vec: vec.unsqueeze(2).to_broadcast([128, 64, 64])

    def ldv(i):
        def f():
            for q in range(6):
                src = T["s6"][q].rearrange("(b i) (hh x) -> i b hh x", i=8, x=128)[i]
                S.dma("sp", lambda e, q=q, src=src: e.dma_start(out=vecs[i % 2][:, q, :], in_=src),
                      reads=["d_s6"], writes=["vec%d" % (i % 2)])
        return f
    ops.append(ldv(0))
    for i in range(8):
        if i + 1 < 8:
            ops.append(ldv(i + 1))
        vn = "vec%d" % (i % 2)
        for a in range(2):
            Sv = Sst[:, a]
            sl = slice(a * 64, a * 64 + 64)
            r_, w_, k_, v_, a_, b_ = [vecs[i % 2][:, q, sl] for q in range(6)]
            L = [
                (lambda e, Sv=Sv, a_=a_: e.tensor_tensor(out=tmp, in0=Sv, in1=bk(a_), op=ALU.mult), ["Sst", vn], ["stmp"]),
                (lambda e: e.tensor_reduce(out=sa, in_=tmp, axis=AX.X, op=ALU.add), ["stmp"], ["ssa"]),
                (lambda e, Sv=Sv, w_=w_: e.tensor_tensor(out=Sv, in0=Sv, in1=bk(w_), op=ALU.mult), ["Sst", vn], ["Sst"]),
                (lambda e, b_=b_: e.tensor_tensor(out=tmp, in0=bv(sa), in1=bk(b_), op=ALU.mult), ["ssa", vn], ["stmp"]),
                (lambda e, Sv=Sv: e.tensor_tensor(out=Sv, in0=Sv, in1=tmp, op=ALU.add), ["Sst", "stmp"], ["Sst"]),
                (lambda e, v_=v_, k_=k_: e.tensor_tensor(out=tmp, in0=bv(v_), in1=bk(k_), op=ALU.mult), [vn], ["stmp"]),
                (lambda e, Sv=Sv: e.tensor_tensor(out=Sv, in0=Sv, in1=tmp, op=ALU.add), ["Sst", "stmp"], ["Sst"]),
                (lambda e, Sv=Sv, r_=r_: e.tensor_tensor(out=tmp, in0=Sv, in1=bk(r_), op=ALU.mult), ["Sst", vn], ["stmp"]),
                (lambda e, i=i, sl=sl: e.tensor_reduce(out=ysc[:, i, sl], in_=tmp, axis=AX.X, op=ALU.add), ["stmp"], ["ysc"]),
            ]
            for fn, rd, wr in L:
                ops.append(lambda fn=fn, rd=rd, wr=wr: S.op("dve", fn, reads=rd, writes=wr))

    def fin():
        S.dma("sp", lambda e: e.dma_start(out=O["sw"].rearrange("b (hh a) v k -> (b hh) (a v k)", a=2),
                                          in_=Sst.rearrange("p a v k -> p (a v k)")), reads=["Sst"], writes=["o_sw"], home="Sst")
        for i in range(8):
            dst = T["sy"].rearrange("(b i) (hh x) -> i b hh x", i=8, x=128)[i]
            S.dma("sp", lambda e, i=i, dst=dst: e.dma_start(out=dst, in_=ysc[:, i, :]), reads=["ysc"], writes=["d_sy"], home="ysc")
    ops.append(fin)
    return ops


def phase3a(C):
    phase3(C, tiles=list(range(C.ntiles)), scan=DBG.get('sample', True))


def phase3b(C):
    if DBG.get('sample', True):
        phase3(C, tiles=[NT], scan=False, reuse=("p3a" in C.phases and DBG.get('reuse', True)))


def phase2d(C):
    S, alloc, I, O, T = C.S, C.alloc, C.I, C.O, C.T
    C.rotbanks = (1, 2, 3, 4)
    C.reset()
    if "p3a" in C.phases and DBG.get('reuse', True):
        C.apos = (getattr(C, "p3_persist", 19152) + 63) // 64 * 64
    if not DBG.get('sample', True):
        return
    B = {}
    _post_bufs(C, B)
    yv = alloc([128, D])
    vbuf = alloc([128, D])
    sgbt = alloc([128, D])
    S.dma("sp", lambda e: e.dma_start(out=yv, in_=T["sy"]), reads=["d_sy"], writes=["yv"])
    S.dma("sp", lambda e: e.dma_start(out=vbuf, in_=T["z"][NT][:, 2048:3072]), reads=["d_z"], writes=["vbuf"])
    S.dma("sp", lambda e: e.dma_start(out=sgbt, in_=T["sgb"][NT]), reads=["d_sgb"], writes=["sgbt"])
    sbon = B["s16"][:, 48:64]
    S.dma("sp", lambda e: e.dma_start(out=sbon, in_=T["sextra"][:, 0:16]), reads=["d_sx"], writes=["sbon"])
    _rwkv_post(C, B, yv, "yv", vbuf, "vbuf", sbon, sgbt, "sgbt", NT)


PHASES = {"p1": phase1, "p2a": phase2a, "p2b": phase2b, "p2c": phase2c, "p3": phase3,
          "p3a": phase3a, "p2d": phase2d, "p3b": phase3b}


def _shard_inputs(inp):
    g = lambda k: np.ascontiguousarray(np.asarray(inp[k], dtype=np.float32))
    oh = _t5_bucket_onehot()
    shared = {
        "rel_bias": g("rel_bias"), "onehot": oh, "norm_g": g("norm_g")[0], "w_in": g("w_in")[0],
        "sinks": g("attn_sinks")[0], "mu": g("shift_mu")[0], "w0": g("rwkv_w0")[0], "w2": g("rwkv_w2")[0],
        "a0": g("rwkv_a0")[0], "a2": g("rwkv_a2")[0], "k_k": g("rwkv_k_k")[0], "k_a": g("rwkv_k_a")[0],
        "r_k": g("rwkv_r_k")[0].reshape(-1), "lnx_g": g("lnx_g")[0], "lnx_b": g("lnx_b")[0],
        "w_out_a": g("w_out_a")[0], "w_out_b": g("w_out_b")[0], "w_o": g("w_o")[0], "final_g": g("final_g"),
    }
    xp, xs = g("x_prompt"), g("x_sample")
    ck, cv = g("cache_k_win")[0], g("cache_v_win")[0]
    sw, ssh = g("state_wkv")[0], g("state_shift")[0]
    maps = []
    for c in range(NCORES):
        m = dict(shared)
        b0 = 16 * c
        m["xp"] = xp[c]
        m["xs"] = xs[b0:b0 + 16].reshape(128, D)
        m["ck"] = ck[b0:b0 + 16].reshape(16, 128, 256)
        m["cv"] = cv[b0:b0 + 16].reshape(16, 128, 256)
        m["swkv"] = sw[b0:b0 + 16]
        m["sshift"] = ssh[b0:b0 + 16]
        maps.append(m)
    return maps


_NC_CACHE = {}


def kernel(**inputs):
    if "nc" not in _NC_CACHE:
        _NC_CACHE["nc"] = build()
    nc = _NC_CACHE["nc"]
    maps = _shard_inputs(inputs)
    res = run_bass_kernel_spmd(nc, maps, core_ids=list(range(NCORES)))
    R = res.results
    cat = lambda k: np.stack([np.asarray(r[k]) for r in R])
    y_prompt = cat("yp").reshape(8, 4096, D)
    y_sample = cat("ys").reshape(128, 8, D)
    pk = cat("pk").reshape(1, 8, 128, 4, 64)
    pv = cat("pv").reshape(1, 8, 128, 4, 64)
    pw = cat("pw").reshape(1, 8, 16, 64, 64)
    psh = cat("psh").reshape(1, 8, 3200)
    sk = cat("sk").reshape(1, 128, 128, 4, 64)
    sv = cat("sv").reshape(1, 128, 128, 4, 64)
    sw = cat("sw").reshape(1, 128, 16, 64, 64)
    ssh = cat("ssh").reshape(1, 128, 3200)
    return (y_prompt, y_sample, pk, pv, pw, psh, sk, sv, sw, ssh)
```

```python
import math
from contextlib import ExitStack

import numpy as np
import concourse.bass as bass
import concourse.mybir as mybir
from concourse.bass_utils import run_bass_kernel_spmd

F32 = mybir.dt.float32
BF16 = mybir.dt.bfloat16
ALU = mybir.AluOpType
AF = mybir.ActivationFunctionType
AX = mybir.AxisListType

NCORES = 8
D = 1024
NT = 32
NEG = -30000.0
CDEC = math.exp(-0.5)
ARENA = 45056


class _Buf:
    __slots__ = ("last_write", "readers", "dsem")

    def __init__(self):
        self.last_write = None
        self.readers = {}
        self.dsem = None


class Sched:
    ENGS = ("pe", "act", "dve", "pool", "sp")

    def __init__(self):
        self.q = {e: [] for e in self.ENGS}
        self.cnt = {e: 0 for e in self.ENGS}
        self.seen = {e: {} for e in self.ENGS}
        self.dma_sems = {}
        self.bufs = {}
        self.cap = None
        self.caps = {}

    def buf(self, name):
        b = self.bufs.get(name)
        if b is None:
            b = self.bufs[name] = _Buf()
        return b

    def _deps(self, reads, writes):
        deps = {}

        def add(tok):
            if tok is not None and deps.get(tok[0], 0) < tok[1]:
                deps[tok[0]] = tok[1]
        for r in reads:
            add(self.buf(r).last_write)
        for w in writes:
            b = self.buf(w)
            add(b.last_write)
            for k, v in b.readers.items():
                add((k, v))
        return deps

    def _waits(self, e, deps):
        for k, v in deps.items():
            if k[0] == "dma":
                v = self.dma_sems[k]
            if k == ("eng", "pe") and e == "pe":
                continue
            if DBG.get("nosame") and k == ("eng", e):
                continue
            if self.seen[e].get(k, 0) >= v:
                continue
            self.seen[e][k] = v
            self.q[e].append(("wait", k, v))

    def _post(self, tok, reads, writes):
        for w in writes:
            b = self.buf(w)
            b.last_write = tok
            b.readers = {}
        for r in reads:
            if r not in writes:
                self.buf(r).readers[tok[0]] = tok[1]

    @staticmethod
    def _excl(reads, writes):
        pr = [r for r in reads if r.startswith("pb")]
        if pr:
            writes = list(writes) + [r for r in pr if r not in writes]
        return reads, writes

    def mark(self, name):
        if name is None:
            self.cap = None
        else:
            self.cap = self.caps.setdefault(name, [])

    def _emit_item(self, it):
        if it[0] == "op":
            self.op(*it[1:])
        else:
            self.dma(*it[1:-1], home=it[-1])

    def replay(self, *names):
        cap, self.cap = self.cap, None
        for n in names:
            for it in self.caps.pop(n, []):
                self._emit_item(it)
        self.cap = cap

    def interleave(self, na, nb):
        cap, self.cap = self.cap, None
        a, b = self.caps.pop(na, []), self.caps.pop(nb, [])
        for i in range(max(len(a), len(b))):
            if i < len(a):
                self._emit_item(a[i])
            if i < len(b):
                self._emit_item(b[i])
        self.cap = cap

    def op(self, e, fn, reads=(), writes=()):
        if self.cap is not None:
            self.cap.append(("op", e, fn, tuple(reads), tuple(writes)))
            return
        reads, writes = self._excl(reads, writes)
        self._waits(e, self._deps(reads, writes))
        self.cnt[e] += 1
        tok = (("eng", e), self.cnt[e])
        self.q[e].append(("ins", fn, tok))
        self._post(tok, reads, writes)

    def dma(self, e, fn, reads=(), writes=(), home=None):
        if self.cap is not None:
            self.cap.append(("dma", e, fn, tuple(reads), tuple(writes), home))
            return
        self._waits(e, self._deps(reads, writes))
        hb = self.buf(home if home is not None else (list(writes) + list(reads))[0])
        if hb.dsem is None:
            hb.dsem = ("dma", len(self.dma_sems))
            self.dma_sems[hb.dsem] = 0
        self.dma_sems[hb.dsem] += 16
        tok = (hb.dsem, self.dma_sems[hb.dsem])
        self.q[e].append(("dma", fn, tok))
        self._post(tok, reads, writes)

    def barrier(self):
        deps = {("eng", en): self.cnt[en] for en in self.ENGS if self.cnt[en]}
        for k, v in self.dma_sems.items():
            deps[k] = v
        for e in self.ENGS:
            for k, v in deps.items():
                if k == ("eng", e) or self.seen[e].get(k, 0) >= v:
                    continue
                self.seen[e][k] = v
                self.q[e].append(("wait", k, v))
        for e in self.ENGS:
            if e == "sp":
                continue
            self.cnt[e] += 1
            self.q[e].append(("ins", lambda eng: eng.nop(), (("eng", e), self.cnt[e])))
        deps = {("eng", en): self.cnt[en] for en in self.ENGS if self.cnt[en]}
        for e in self.ENGS:
            for k, v in deps.items():
                if k == ("eng", e) or self.seen[e].get(k, 0) >= v:
                    continue
                self.seen[e][k] = v
                self.q[e].append(("wait", k, v))

    def emit(self, nc, stack):
        semh = {}
        for e in self.ENGS:
            semh[("eng", e)] = stack.enter_context(nc.semaphore("s_" + e))
        for k in self.dma_sems:
            semh[k] = stack.enter_context(nc.semaphore("d_%d" % k[1]))
        block = stack.enter_context(nc.Block())
        q = self.q

        def run(eng, items):
            for it in items:
                if it[0] == "wait":
                    eng.wait_ge(semh[it[1]], it[2])
                elif it[0] == "ins":
                    it[1](eng).then_inc(semh[it[2][0]], 1)
                else:
                    it[1](eng).then_inc(semh[it[2][0]], 16)

        @block.tensor
        def _(eng):
            run(eng, q["pe"])

        @block.scalar
        def _(eng):
            run(eng, q["act"])

        @block.vector
        def _(eng):
            run(eng, q["dve"])

        @block.gpsimd
        def _(eng):
            run(eng, q["pool"])

        @block.sync
        def _(eng):
            run(eng, q["sp"])


def _t5_bucket_onehot():
    d = np.arange(129)
    dd = np.maximum(d, 1).astype(np.float32)
    large = 16 + (np.log(dd / np.float32(16)) / np.float32(math.log(128 / 16)) * np.float32(16)).astype(np.int32)
    large = np.minimum(large, 31)
    bkt = np.where(d < 16, d, large)
    e = np.zeros((32, 129), np.float32)
    e[bkt, d] = 1.0
    return e


class _NullSched:
    def op(self, *a, **k):
        pass

    def dma(self, *a, **k):
        pass


class Ctx:
    pass


DBG = {}


def build(phases=("p1", "p2a", "p2b", "p3a", "p2d", "p3b"), debug=False, ntiles=NT):
    nc = bass.Bass("TRN2", target_bir_lowering=False)
    S = Sched()
    C = Ctx()
    C.nc, C.S, C.debug, C.ntiles = nc, S, debug, ntiles
    C.phases = tuple(phases)

    def din(name, shape):
        return nc.dram_tensor(name, list(shape), F32, kind="ExternalInput").ap()

    def dout(name, shape):
        return nc.dram_tensor(name, list(shape), F32, kind="ExternalOutput").ap()

    def dscr(name, shape, dt=F32):
        if debug and dt == F32:
            return nc.dram_tensor(name, list(shape), dt, kind="ExternalOutput").ap()
        return nc.dram_tensor(name, list(shape), dt).ap()

    I = C.I = {}
    for name, shape in [
        ("xp", (4096, D)), ("xs", (128, D)), ("ck", (16, 128, 256)), ("cv", (16, 128, 256)),
        ("swkv", (16, 16, 64, 64)), ("sshift", (16, 3200)), ("rel_bias", (32, 16)),
        ("onehot", (32, 129)), ("norm_g", (D,)), ("w_in", (D, 8832)), ("sinks", (16,)),
        ("mu", (3200,)), ("w0", (D,)), ("w2", (64, D)), ("a0", (D,)), ("a2", (64, D)),
        ("k_k", (D,)), ("k_a", (D,)), ("r_k", (D,)), ("lnx_g", (D,)), ("lnx_b", (D,)),
        ("w_out_a", (D, D)), ("w_out_b", (D, D)), ("w_o", (D, D)), ("final_g", (D,)),
    ]:
        I[name] = din(name, shape)
    O = C.O = {}
    for name, shape in [
        ("yp", (4096, D)), ("ys", (128, D)), ("pk", (128, 256)), ("pv", (128, 256)),
        ("pw", (16, 64, 64)), ("psh", (3200,)), ("sk", (16, 128, 256)), ("sv", (16, 128, 256)),
        ("sw", (16, 16, 64, 64)), ("ssh", (16, 3200)),
    ]:
        O[name] = dout(name, shape)
    T = C.T = {}
    T["extd"] = dscr("extd", (16, 128, 384))
    T["ya"] = dscr("ya", (NT + 1, 128, D))
    T["yb"] = dscr("yb", (NT + 1, 128, D))
    T["z"] = dscr("z", (NT + 1, 128, 3200))
    T["sgb"] = dscr("sgb", (NT + 1, 128, D))
    T["carry"] = dscr("carry", (NT + 1, 3200))
    T["s6"] = dscr("s6", (6, 128, D))
    T["sy"] = dscr("sy", (128, D))
    T["sextra"] = dscr("sextra", (128, 2 * D + 16))
    T["wbf_in"] = dscr("wbf_in", (D, 6272), BF16)
    T["wbf_ob"] = dscr("wbf_ob", (D, D), BF16)
    T["wbf_o"] = dscr("wbf_o", (D, D), BF16)
    if debug:
        T["dMTp"] = dscr("dMTp", (128, 16, 128))
        T["dMTc"] = dscr("dMTc", (128, 16, 128))
        T["dMTs"] = dscr("dMTs", (128, 16, 128))
        T["dt1"] = dscr("dt1", (NT + 1, 128, D))
        T["dy"] = dscr("dy", (NT + 1, 128, D))

    with ExitStack() as st:
        arena = st.enter_context(nc.sbuf_tensor("arena", [128, ARENA], F32))
        psum = st.enter_context(nc.psum_tensor("psum", [128, 8, 512], F32))
        C.arena, C.psum = arena, psum
        C.apos = 0

        def alloc(shape, dt=F32):
            n = 1
            for s_ in shape[1:]:
                n *= s_
            words = n if dt == F32 else (n + 1) // 2
            words = (words + 7) // 8 * 8
            off = C.apos
            C.apos += words
            assert C.apos <= ARENA, ("SBUF arena overflow", C.apos)
            v = arena[0:shape[0], off:off + words]
            if dt != F32:
                v = v.bitcast(dt)
            v = v[:, 0:n]
            if len(shape) == 2:
                return v
            names = " ".join("a%d" % i for i in range(len(shape) - 1))
            kw = {"a%d" % i: shape[i + 1] for i in range(len(shape) - 1)}
            return v.rearrange("p (%s) -> p %s" % (names, names), **kw)
        C.alloc = alloc

        def reset():
            C.apos = 0
        C.reset = reset

        def pbank(i, dt=F32):
            v = psum[:, i, :]
            if dt != F32:
                v = v.bitcast(dt)
            return v
        C.pbank = pbank
        C.rot = 0

        for ph in phases:
            PHASES[ph](C)
            S.barrier()
        S.emit(nc, st)
    return nc


def _common_consts(C, init=True):
    S, alloc, I = C.S, C.alloc, C.I
    C.identf = alloc([128, 128])
    C.ident = alloc([128, 128], BF16)
    C.gbc = alloc([128, D])
    if not init:
        S = _NullSched()
    S.op("pool", lambda e: e.memset(C.identf, 0.0), writes=["identf"])
    S.op("pool", lambda e: e.affine_select(out=C.identf, in_=C.identf, pattern=[[-1, 128]],
                                           compare_op=ALU.not_equal, fill=1.0, base=0,
                                           channel_multiplier=1),
         reads=["identf"], writes=["identf"])
    S.op("dve", lambda e: e.tensor_copy(out=C.ident, in_=C.identf), reads=["identf"], writes=["ident"])
    S.dma("sp", lambda e: e.dma_start(out=C.gbc, in_=I["norm_g"].partition_broadcast(128)), writes=["gbc"])
    C.xt = [alloc([128, D]), alloc([128, D]), alloc([128, D])]
    C.junk = alloc([128, D], BF16)
    C.ss = alloc([128, 1])
    C.rstd = alloc([128, 1])
    C.xsb = alloc([128, D], BF16)
    C.hT = alloc([128, 8, 128], BF16)


def _x_src(C, t):
    return C.I["xs"] if t == NT else C.I["xp"][t * 128:(t + 1) * 128, :]


def _load_x(C, t, slot):
    C.S.dma("sp", lambda e: e.dma_start(out=C.xt[slot], in_=_x_src(C, t)), writes=["xt%d" % slot])


def _norm_pre(C, slot):
    S = C.S
    xt = C.xt[slot]
    xn = "xt%d" % slot
    S.op("pool", lambda e: e.memset(C.ss, 0.0), writes=["ss"])
    S.op("act", lambda e: e.activation(out=C.junk, in_=xt, func=AF.Square, accum_out=C.ss),
         reads=[xn, "ss"], writes=["junk", "ss"])
    S.op("dve", lambda e: e.tensor_scalar(out=C.rstd, in0=C.ss, scalar1=1.0 / D, scalar2=1e-6,
                                          op0=ALU.mult, op1=ALU.add), reads=["ss"], writes=["rstd"])
    S.op("act", lambda e: e.activation(out=C.rstd, in_=C.rstd, func=AF.Sqrt), reads=["rstd"], writes=["rstd"])
    S.op("dve", lambda e: e.reciprocal(out=C.rstd, in_=C.rstd), reads=["rstd"], writes=["rstd"])
    S.op("dve", lambda e: e.scalar_tensor_tensor(out=C.xsb, in0=xt, scalar=C.rstd[:, 0:1], in1=C.gbc,
                                                 op0=ALU.mult, op1=ALU.mult),
         reads=[xn, "rstd", "gbc"], writes=["xsb"])


def _norm_post(C):
    S = C.S
    pT = C.pbank(0, BF16)
    for k in range(8):
        S.op("pe", lambda e, k=k: e.transpose(out=pT[:, k * 128:(k + 1) * 128],
                                              in_=C.xsb[:, k * 128:(k + 1) * 128], identity=C.ident),
             reads=["xsb", "ident"], writes=["pb0"])
    S.op("act", lambda e: e.copy(out=C.hT.rearrange("p k t -> p (k t)"), in_=pT), reads=["pb0"], writes=["hT"])


def _norm_T(C, slot, nxt=None):
    _norm_post(C)
    if nxt is not None:
        _norm_pre(C, nxt)


def _pstride(ap, step, count):
    pat = [list(x) for x in ap.ap]
    pat[0] = [pat[0][0] * step, count]
    return bass.AP(ap.tensor, ap.offset, pat)


def _load_w(C, dst, dname, src_cols, eng="pool"):
    for k in range(8):
        C.S.dma(eng, lambda e, k=k: e.dma_start(out=dst[:, k, :], in_=src_cols[k * 128:(k + 1) * 128, :]),
                writes=[dname])


def _load_wbf(C, dst, dname, srcname, c0, n):
    src = C.T[srcname]
    for k in range(8):
        C.S.dma("sp" if k % 2 == 0 else "act",
                lambda e, k=k: e.dma_start(out=dst[:, k, :], in_=src[k * 128:(k + 1) * 128, c0:c0 + n]),
                reads=[srcname], writes=[dname])


def _nextbank(C):
    pool = getattr(C, 'rotbanks', (1, 2, 3, 4))
    b = pool[C.rot % len(pool)]
    C.rot += 1
    return b


def _proj_tm(C, W, wname, c0, ncols, bank):
    pb = C.pbank(bank)
    for k in range(8):
        C.S.op("pe", lambda e, k=k: e.matmul(pb[:, 0:ncols], lhsT=C.hT[:, k, :], rhs=W[:, k, c0:c0 + ncols],
                                              start=(k == 0), stop=(k == 7)),
               reads=["hT", wname], writes=["pb%d" % bank])
    return pb


def phase1(C):
    C.rotbanks = (1, 2, 3, 4)
    S, alloc, I, O, T, nc = C.S, C.alloc, C.I, C.O, C.T, C.nc
    C.reset()
    _common_consts(C)
    w_in = I["w_in"]
    Wqk = alloc([128, 8, 1280], BF16)
    Wkv = alloc([128, 8, 512], BF16)
    Wga = alloc([128, 8, 1024], BF16)
    Woa = alloc([128, 8, 1024], BF16)
    for k in range(8):
        for pair in range(2):
            for half in range(2):
                c0 = pair * 512 + half * 256
                src = w_in[k * 128:(k + 1) * 128, c0:c0 + 256].rearrange("p (g d) -> p g d", g=4)
                dst = Wqk[:, k, pair * 512:(pair + 1) * 512].rearrange(
                    "p (g half d) -> p g half d", g=4, half=2)[:, :, half, :]
                S.dma("pool", lambda e, src=src, dst=dst: e.dma_start(out=dst, in_=src), writes=["Wqk"])
        S.dma("pool", lambda e, k=k: e.dma_start(out=Wqk[:, k, 1024:1280],
                                                 in_=w_in[k * 128:(k + 1) * 128, 1024:1280]), writes=["Wqk"])
    _load_w(C, Wkv, "Wkv", w_in[:, 1024:1536])
    _load_w(C, Wga, "Wga", w_in[:, 1536:2560])
    _load_w(C, Woa, "Woa", I["w_out_a"])
    pre_list = []
    if DBG.get("precast", True):
        for k in range(8):
            rows = slice(k * 128, (k + 1) * 128)
            pre_list.append(lambda rows=rows: S.dma("pool", lambda e: e.dma_start(out=T["wbf_in"][rows, :], in_=w_in[rows, 2560:8832]), writes=["wbf_in"]))
        for k in range(8):
            rows = slice(k * 128, (k + 1) * 128)
            pre_list.append(lambda rows=rows: S.dma("pool", lambda e: e.dma_start(out=T["wbf_ob"][rows, :], in_=I["w_out_b"][rows, :]), writes=["wbf_ob"]))
            pre_list.append(lambda rows=rows: S.dma("pool", lambda e: e.dma_start(out=T["wbf_o"][rows, :], in_=I["w_o"][rows, :]), writes=["wbf_o"]))

    def precast(n):
        for _ in range(n):
            if pre_list:
                pre_list.pop(0)()

    rb = alloc([32, 16])
    oh = alloc([32, 129])
    ext = alloc([16, 384])
    MTp = alloc([128, 16, 128])
    MTc = alloc([128, 16, 128])
    MTs = alloc([128, 16, 128])
    bsel = alloc([16, 128])
    bd = alloc([128, 128])
    esink = alloc([128, 16])
    S.dma("sp", lambda e: e.dma_start(out=rb, in_=I["rel_bias"]), writes=["rb"])
    S.dma("sp", lambda e: e.dma_start(out=oh, in_=I["onehot"]), writes=["oh"])
    S.dma("sp", lambda e: e.dma_start(out=esink, in_=I["sinks"].partition_broadcast(128)), writes=["esink"])
    S.op("act", lambda e: e.activation(out=esink, in_=esink, func=AF.Exp), reads=["esink"], writes=["esink"])
    pb = C.pbank(1)
    S.op("pe", lambda e: e.matmul(pb[0:16, 0:129], lhsT=rb, rhs=oh, start=True, stop=True),
         reads=["rb", "oh"], writes=["pb1"])
    S.op("pool", lambda e: e.memset(ext, NEG), writes=["ext"])
    S.op("dve", lambda e: e.tensor_copy(out=ext[:, 127:256], in_=pb[0:16, 0:129]), reads=["pb1", "ext"], writes=["ext"])
    S.dma("sp", lambda e: e.dma_start(out=T["extd"], in_=ext.unsqueeze(1).to_broadcast([16, 128, 384])),
          reads=["ext"], writes=["extd"])
    S.dma("sp", lambda e: e.dma_start(out=MTc, in_=bass.AP(T["extd"].tensor, 127, [[383, 128], [49152, 16], [1, 128]])),
          reads=["extd"], writes=["MTc"])
    S.dma("sp", lambda e: e.dma_start(out=MTp, in_=bass.AP(T["extd"].tensor, 255, [[383, 128], [49152, 16], [1, 128]])),
          reads=["extd"], writes=["MTp"])
    S.op("pool", lambda e: e.memset(bsel, 1.0), writes=["bsel"])
    S.op("pool", lambda e: e.affine_select(out=bsel, in_=bsel, pattern=[[1, 128]], compare_op=ALU.is_ge,
                                           fill=0.0, base=0, channel_multiplier=-8), reads=["bsel"], writes=["bsel"])
    S.op("pool", lambda e: e.affine_select(out=bsel, in_=bsel, pattern=[[-1, 128]], compare_op=ALU.is_ge,
                                           fill=0.0, base=7, channel_multiplier=8), reads=["bsel"], writes=["bsel"])
    pb2 = C.pbank(2)
    S.op("pe", lambda e: e.matmul(pb2[:, 0:128], lhsT=bsel, rhs=bsel, start=True, stop=True),
         reads=["bsel"], writes=["pb2"])
    S.op("dve", lambda e: e.tensor_scalar(out=bd, in0=pb2[:, 0:128], scalar1=-1.0, scalar2=-NEG,
                                          op0=ALU.add, op1=ALU.mult), reads=["pb2"], writes=["bd"])
    S.op("dve", lambda e: e.tensor_tensor(out=MTs, in0=MTc, in1=bd.unsqueeze(1).to_broadcast([128, 16, 128]),
                                          op=ALU.add), reads=["MTc", "bd"], writes=["MTs"])

    if C.debug:
        S.dma("sp", lambda e: e.dma_start(out=T["dMTp"], in_=MTp), reads=["MTp"], writes=["dMTp"])
        S.dma("sp", lambda e: e.dma_start(out=T["dMTc"], in_=MTc), reads=["MTc"], writes=["dMTc"])
        S.dma("sp", lambda e: e.dma_start(out=T["dMTs"], in_=MTs), reads=["MTs"], writes=["dMTs"])
    qT = alloc([128, 8, 128], BF16)
    kT = [alloc([128, 2, 128], BF16), alloc([128, 2, 128], BF16)]
    va = [alloc([128, 4, 65], BF16), alloc([128, 4, 65], BF16)]
    kvo = alloc([128, 512])
    sga = alloc([128, D])
    stb = [alloc([128, 512]) for _ in range(2)]
    pTb = [alloc([128, 4, 128], BF16) for _ in range(4)]
    den = alloc([128, 16])
    t1 = alloc([128, 16, 64])
    goa = alloc([128, D], BF16)
    goaT = alloc([128, 8, 128], BF16)
    yat = [alloc([128, D]), alloc([128, D])]
    ckb = [alloc([128, 256], BF16) for _ in range(2)]
    vac = [alloc([128, 4, 65], BF16) for _ in range(2)]
    kTc = [alloc([128, 2, 128], BF16) for _ in range(2)]
    Zb = [alloc([128, 16, 128], BF16) for _ in range(2)]
    stc = [alloc([128, 16, 8]) for _ in range(2)]
    qTs = alloc([128, 16, 8, 8], BF16)
    zl = alloc([128, 128], BF16)
    zr = alloc([128, 512], BF16)
    S.op("pool", lambda e: e.memset(zl, 0.0), writes=["zl"])
    S.op("pool", lambda e: e.memset(zr, 0.0), writes=["zr"])
    for i in range(2):
        S.op("pool", lambda e, i=i: e.memset(va[i][:, :, 64:65], 1.0), writes=["va%d" % i])
        S.op("pool", lambda e, i=i: e.memset(vac[i][:, :, 64:65], 1.0), writes=["vac%d" % i])

    def oslot(h):
        bank = 5 + h // 7
        off = (h % 7) * 65
        return C.pbank(bank)[:, off:off + 65], "pb%d" % bank
    tiles = ([NT] if DBG.get('sample', True) else []) + list(range(C.ntiles))
    if not tiles:
        return
    if DBG.get('fake_sample'):
        tiles = [0]
    _load_x(C, tiles[0], 0)
    if len(tiles) > 1:
        _load_x(C, tiles[1], 1)
    _norm_pre(C, 0)
    cnt = {"st": 0, "pT": 0}
    for ti, t in enumerate(tiles):
        slot = ti % 2
        if ti + 2 < len(tiles):
            _load_x(C, tiles[ti + 2], (ti + 2) % 3)
        _norm_T(C, ti % 3, ((ti + 1) % 3) if ti + 1 < len(tiles) else None)
        sample = (t == NT) or bool(DBG.get('fake_sample'))
        cur = (t % 2) if not sample else 0
        prev = 1 - cur
        for chunks in ([0, 1, 2, 3], [4, 5, 6, 7], [8, 9]):
            bank = _nextbank(C)
            pb = C.pbank(bank)
            for ci, c in enumerate(chunks):
                for k in range(8):
                    S.op("pe", lambda e, k=k, ci=ci, c=c, pb=pb: e.matmul(
                        pb[:, ci * 128:(ci + 1) * 128], lhsT=Wqk[:, k, c * 128:(c + 1) * 128], rhs=C.hT[:, k, :],
                        start=(k == 0), stop=(k == 7)), reads=["hT", "Wqk"], writes=["pb%d" % bank])
            n = len(chunks) * 128
            if chunks[0] < 8:
                c0 = chunks[0]
                S.op("act", lambda e, pb=pb, c0=c0, n=n: e.copy(
                    out=qT[:, c0:c0 + 4, :].rearrange("p c t -> p (c t)"), in_=pb[:, 0:n]),
                    reads=["pb%d" % bank], writes=["qT"])
            else:
                S.op("act", lambda e, pb=pb, n=n, cur=cur: e.copy(out=kT[cur].rearrange("p c t -> p (c t)"), in_=pb[:, 0:n]),
                     reads=["pb%d" % bank], writes=["kT%d" % cur])
        bank = _nextbank(C)
        pb = _proj_tm(C, Wkv, "Wkv", 0, 512, bank)
        S.op("dve", lambda e, pb=pb, cur=cur: e.tensor_copy(out=va[cur][:, :, 0:64],
                                                   in_=pb[:, 256:512].rearrange("p (h d) -> p h d", h=4)),
             reads=["pb%d" % bank], writes=["va%d" % cur])
        if sample or t == NT - 1:
            S.op("dve", lambda e, pb=pb: e.tensor_copy(out=kvo, in_=pb), reads=["pb%d" % bank], writes=["kvo"])
            if sample and not DBG.get('s_out', True):
                pass
            elif sample:
                for b in range(16):
                    S.dma("sp", lambda e, b=b: e.dma_start(out=O["sk"][b, 120:128, :], in_=kvo[8 * b:8 * b + 8, 0:256]),
                          reads=["kvo"], writes=["o_sk"])
                    S.dma("sp", lambda e, b=b: e.dma_start(out=O["sv"][b, 120:128, :], in_=kvo[8 * b:8 * b + 8, 256:512]),
                          reads=["kvo"], writes=["o_sv"])
                S.dma("sp", lambda e: e.dma_start(out=O["sk"][:, 0:120, :], in_=I["ck"][:, 8:128, :]), writes=["o_sk2"])
                S.dma("sp", lambda e: e.dma_start(out=O["sv"][:, 0:120, :], in_=I["cv"][:, 8:128, :]), writes=["o_sv2"])
            else:
                S.dma("sp", lambda e: e.dma_start(out=O["pk"], in_=kvo[:, 0:256]), reads=["kvo"], writes=["o_pk"])
                S.dma("sp", lambda e: e.dma_start(out=O["pv"], in_=kvo[:, 256:512]), reads=["kvo"], writes=["o_pv"])
        for hf in range(2):
            bank = _nextbank(C)
            pb = _proj_tm(C, Wga, "Wga", hf * 512, 512, bank)
            S.op("act", lambda e, pb=pb, hf=hf: e.activation(out=sga[:, hf * 512:(hf + 1) * 512], in_=pb, func=AF.Silu),
                 reads=["pb%d" % bank], writes=["sga"])
        for bk, nh in enumerate((7, 7, 2)):
            S.op("pe", lambda e, bk=bk, nh=nh: e.matmul(C.pbank(5 + bk)[:, 0:nh * 65], lhsT=zl, rhs=zr[:, 0:nh * 65],
                                                        start=True, stop=False, skip_group_check=True),
                 reads=["zl", "zr"], writes=["pb%d" % (5 + bk)])
        blocks = [("cur", cur)] if (sample or t == 0) else [("prev", prev), ("cur", cur)]
        nlast = 1 if not sample else 17
        done = [0] * 16
        its = [(kvh, kind, sl) for kvh in range(4) for (kind, sl) in blocks]
        nblk = len(blocks) if not sample else 1 + DBG.get('s_nb', 16)

        def stage_a(kvh, kind, sl):
            pair, half = kvh // 2, kvh % 2
            rows = slice(half * 64, half * 64 + 64)
            bank = _nextbank(C)
            pb = C.pbank(bank)
            S.op("pe", lambda e, pb=pb, sl=sl, rows=rows, pair=pair: e.matmul(
                pb, lhsT=kT[sl][rows, pair, :], rhs=qT[rows, pair * 4:pair * 4 + 4, :], start=True, stop=True),
                reads=["kT%d" % sl, "qT"], writes=["pb%d" % bank])
            MT = MTs if sample else (MTp if kind == "prev" else MTc)
            mtn = "MTs" if sample else ("MTp" if kind == "prev" else "MTc")
            si = cnt["st"] % 2
            cnt["st"] += 1
            pi = cnt["pT"] % 4
            cnt["pT"] += 1
            S.op("dve", lambda e, pb=pb, si=si, MT=MT, kvh=kvh: e.scalar_tensor_tensor(
                out=stb[si], in0=pb, scalar=0.125, in1=MT[:, kvh * 4:kvh * 4 + 4, :].rearrange("p h q -> p (h q)"),
                op0=ALU.mult, op1=ALU.add), reads=["pb%d" % bank, mtn], writes=["st%d" % si])
            S.op("act", lambda e, si=si, pi=pi: e.activation(out=pTb[pi].rearrange("p g q -> p (g q)"),
                                                             in_=stb[si], func=AF.Exp),
                 reads=["st%d" % si], writes=["pT%d" % pi])
            return pi

        def stage_b(kvh, kind, sl, pi):
            for g in range(4):
                h = kvh * 4 + g
                osl, on = oslot(h)
                done[h] += 1
                S.op("pe", lambda e, osl=osl, pi=pi, g=g, sl=sl, kvh=kvh, last=(done[h] == nblk):
                     e.matmul(osl, lhsT=pTb[pi][:, g, :], rhs=va[sl][:, kvh, :], start=False, stop=last,
                              skip_group_check=True),
                     reads=["pT%d" % pi, "va%d" % sl], writes=[on])
        pis = {}
        for ii, it in enumerate(its):
            pis[ii] = stage_a(*it)
            if ii >= 1:
                stage_b(*its[ii - 1], pis[ii - 1])
        stage_b(*its[-1], pis[len(its) - 1])
        if sample:
            for b in range(DBG.get('s_nb', 16)):
                j = b % 2
                S.mark("SF_%d" % b)
                S.dma("pool", lambda e, b=b, j=j: e.dma_start(out=ckb[j], in_=I["ck"][b]), writes=["ckb%d" % j])
                S.dma("pool", lambda e, b=b, j=j: e.dma_start(out=vac[j][:, :, 0:64],
                                                              in_=I["cv"][b].rearrange("p (h d) -> p h d", h=4)),
                      writes=["vac%d" % j])
                pT0 = C.pbank(0, BF16)
                for pr in range(2):
                    S.op("pe", lambda e, pr=pr, j=j: e.transpose(out=pT0[:, pr * 128:(pr + 1) * 128],
                                                                  in_=ckb[j][:, pr * 128:(pr + 1) * 128], identity=C.ident),
                         reads=["ckb%d" % j, "ident"], writes=["pb0"])
                S.op("act", lambda e, j=j: e.copy(out=kTc[j].rearrange("p c t -> p (c t)"), in_=pT0[:, 0:256]),
                     reads=["pb0"], writes=["kTc%d" % j])
                lvl = DBG.get('s_lvl', 4)
                if lvl < 2:
                    continue
                for kvh in range(4):
                    pair, half = kvh // 2, kvh % 2
                    rows = slice(half * 64, half * 64 + 64)
                    bank = _nextbank(C)
                    pb = C.pbank(bank)
                    S.op("pe", lambda e, pb=pb, rows=rows, pair=pair, j=j: e.matmul(
                        pb, lhsT=kTc[j][rows, pair, :], rhs=qT[rows, pair * 4:pair * 4 + 4, :],
                        start=True, stop=True), reads=["kTc%d" % j, "qT"], writes=["pb%d" % bank])
                    S.op("dve", lambda e, pb=pb, j=j, kvh=kvh, b=b: e.scalar_tensor_tensor(
                        out=stc[j][:, kvh * 4:kvh * 4 + 4, :],
                        in0=pb.rearrange("p (g q) -> p g q", g=4)[:, :, 8 * b:8 * b + 8], scalar=0.125,
                        in1=MTp[:, kvh * 4:kvh * 4 + 4, 0:8], op0=ALU.mult, op1=ALU.add),
                        reads=["pb%d" % bank, "MTp"], writes=["stc%d" % j])
                if lvl < 3:
                    continue
                S.op("pool", lambda e, j=j: e.memset(Zb[j], 0.0), writes=["Zb%d" % j])
                S.op("act", lambda e, j=j, b=b: e.activation(out=Zb[j][:, :, 8 * b:8 * b + 8], in_=stc[j], func=AF.Exp),
                     reads=["stc%d" % j, "Zb%d" % j], writes=["Zb%d" % j])
                if lvl < 4:
                    continue
                S.mark("SB_%d" % b)
                for h in range(16):
                    osl, on = oslot(h)
                    done[h] += 1
                    S.op("pe", lambda e, osl=osl, h=h, j=j, last=(done[h] == 1 + DBG.get('s_nb', 16)): e.matmul(
                        osl, lhsT=Zb[j][:, h, :], rhs=vac[j][:, h // 4, :], start=False, stop=last,
                        skip_group_check=True),
                        reads=["Zb%d" % j, "vac%d" % j], writes=[on])
        if sample:
            S.mark(None)
            nb_ = DBG.get('s_nb', 16)
            for b in range(nb_):
                S.replay("SF_%d" % b)
                if b >= 1:
                    S.replay("SB_%d" % (b - 1))
            if nb_:
                S.replay("SB_%d" % (nb_ - 1))
        for bk, (h0, nh) in enumerate([(0, 7), (7, 7), (14, 2)]):
            ob = C.pbank(5 + bk)[:, 0:nh * 65].rearrange("p (h e) -> p h e", e=65)
            S.op("dve", lambda e, ob=ob, h0=h0, nh=nh: e.tensor_tensor(
                out=den[:, h0:h0 + nh].unsqueeze(2), in0=ob[:, :, 64:65], in1=esink[:, h0:h0 + nh].unsqueeze(2), op=ALU.add),
                reads=["pb%d" % (5 + bk), "esink"], writes=["den"])
        S.op("dve", lambda e: e.reciprocal(out=den, in_=den), reads=["den"], writes=["den"])
        for bk, (h0, nh) in enumerate([(0, 7), (7, 7), (14, 2)]):
            ob = C.pbank(5 + bk)[:, 0:nh * 65].rearrange("p (h e) -> p h e", e=65)
            S.op("dve", lambda e, ob=ob, h0=h0, nh=nh: e.tensor_tensor(
                out=t1[:, h0:h0 + nh, :], in0=ob[:, :, 0:64],
                in1=den[:, h0:h0 + nh].unsqueeze(2).to_broadcast([128, nh, 64]), op=ALU.mult),
                reads=["pb%d" % (5 + bk), "den"], writes=["t1"])
        if C.debug:
            S.dma("sp", lambda e, t=t: e.dma_start(out=T["dt1"][t], in_=t1.rearrange("p h d -> p (h d)")), reads=["t1"], writes=["d_dt1"], home="t1")
        S.op("dve", lambda e: e.tensor_tensor(out=goa, in0=t1.rearrange("p h d -> p (h d)"), in1=sga, op=ALU.mult),
             reads=["t1", "sga"], writes=["goa"])
        pT0 = C.pbank(0, BF16)
        for k in range(8):
            S.op("pe", lambda e, k=k: e.transpose(out=pT0[:, k * 128:(k + 1) * 128], in_=goa[:, k * 128:(k + 1) * 128],
                                                  identity=C.ident), reads=["goa", "ident"], writes=["pb0"])
        S.op("act", lambda e: e.copy(out=goaT.rearrange("p k t -> p (k t)"), in_=pT0), reads=["pb0"], writes=["goaT"])
        yslot = ti % 2
        for hf in range(2):
            bank = _nextbank(C)
            pb = C.pbank(bank)
            for k in range(8):
                S.op("pe", lambda e, k=k, pb=pb, hf=hf: e.matmul(pb, lhsT=goaT[:, k, :], rhs=Woa[:, k, hf * 512:(hf + 1) * 512],
                                                                  start=(k == 0), stop=(k == 7)),
                     reads=["goaT", "Woa"], writes=["pb%d" % bank])
            S.op("act", lambda e, pb=pb, hf=hf, yslot=yslot: e.copy(out=yat[yslot][:, hf * 512:(hf + 1) * 512], in_=pb),
                 reads=["pb%d" % bank], writes=["yat%d" % yslot])
        S.dma("sp", lambda e, t=t, yslot=yslot: e.dma_start(out=T["ya"][t], in_=yat[yslot]), reads=["yat%d" % yslot], writes=["d_ya"], home="yat%d" % yslot)
        precast(1 if len(tiles) > 26 else 24)


    precast(99)


def phase2a(C):
    C.rotbanks = (1, 2, 3, 4)
    S, alloc, I, O, T = C.S, C.alloc, C.I, C.O, C.T
    C.reset()
    _common_consts(C)
    w_in = I["w_in"]
    Wps = alloc([128, 8, 3200], BF16)
    Wgb = alloc([128, 8, 1024], BF16)
    if DBG.get("precast", True) and "p1" in C.phases:
        _load_wbf(C, Wps, "Wps", "wbf_in", 0, 3200)
        _load_wbf(C, Wgb, "Wgb", "wbf_in", 3200, 1024)
    else:
        for k in range(8):
            for c0 in range(0, 3200, 640):
                S.dma("pool", lambda e, k=k, c0=c0: e.dma_start(out=Wps[:, k, c0:c0 + 640],
                                                                in_=w_in[k * 128:(k + 1) * 128, 2560 + c0:2560 + c0 + 640]),
                      writes=["Wps"])
        _load_w(C, Wgb, "Wgb", w_in[:, 5760:6784])
    mubc = alloc([128, 3200])
    S.dma("sp", lambda e: e.dma_start(out=mubc, in_=I["mu"].partition_broadcast(128)), writes=["mubc"])
    psb = [alloc([128, 3200]), alloc([128, 3200])]
    zb = [alloc([128, 3200]), alloc([128, 3200])]
    sgb = [alloc([128, D]), alloc([128, D])]
    sst = alloc([16, 3200])
    S.dma("sp", lambda e: e.dma_start(out=sst, in_=I["sshift"]), writes=["sst"])

    def sel(name, shape, pattern, cm, base):
        m = alloc(shape)
        S.op("pool", lambda e: e.memset(m, 0.0), writes=[name])
        S.op("pool", lambda e: e.affine_select(out=m, in_=m, pattern=pattern, compare_op=ALU.not_equal, fill=1.0,
                                               base=base, channel_multiplier=cm), reads=[name], writes=[name])
        return m
    ShI = sel("ShI", [128, 128], [[1, 128]], -1, -1)
    ShsI = sel("ShsI", [128, 128], [[1, 128]], -1, -1)
    S.op("pool", lambda e: e.memset(ShsI.rearrange("p (b i) -> p b i", i=8)[:, :, 0:1], 0.0), reads=["ShsI"], writes=["ShsI"])
    for m, n in ((ShI, "ShI"), (ShsI, "ShsI")):
        S.op("dve", lambda e, m=m: e.tensor_tensor(out=m, in0=m, in1=C.identf, op=ALU.subtract), reads=[n, "identf"], writes=[n])
    Ecar = alloc([128, 128])
    S.op("pool", lambda e: e.memset(Ecar, 0.0), writes=["Ecar"])
    S.op("pool", lambda e: e.memset(Ecar[:, 0:1], 1.0), reads=["Ecar"], writes=["Ecar"])
    S.op("pool", lambda e: e.affine_select(out=Ecar[:, 0:1], in_=Ecar[:, 0:1], pattern=[[0, 1]], compare_op=ALU.is_ge, fill=0.0,
                                           base=-127, channel_multiplier=1), reads=["Ecar"], writes=["Ecar"])
    Esel = sel("Esel", [16, 128], [[1, 128]], -8, 0)

    tiles = list(range(C.ntiles)) + ([NT] if DBG.get('sample', True) else [])
    if not tiles:
        return
    _load_x(C, tiles[0], 0)
    if len(tiles) > 1:
        _load_x(C, tiles[1], 1)
    _norm_pre(C, 0)
    groups = [(c0, 512) for c0 in range(0, 3072, 512)] + [(3072, 128)]
    for ti, t in enumerate(tiles):
        slot = ti % 2
        if ti + 2 < len(tiles):
            _load_x(C, tiles[ti + 2], (ti + 2) % 3)
        _norm_T(C, ti % 3, ((ti + 1) % 3) if ti + 1 < len(tiles) else None)
        sample = (t == NT)
        ps, psn = psb[ti % 2], "ps%d" % (ti % 2)
        pp, ppn = psb[(ti + 1) % 2], "ps%d" % ((ti + 1) % 2)
        zt, ztn = zb[ti % 2], "zb%d" % (ti % 2)
        for (c0, n) in groups:
            bank = _nextbank(C)
            pb = _proj_tm(C, Wps, "Wps", c0, n, bank)
            S.op("act", lambda e, pb=pb, c0=c0, n=n, ps=ps: e.copy(out=ps[:, c0:c0 + n], in_=pb[:, 0:n]),
                 reads=["pb%d" % bank], writes=[psn])
        for hf in range(2):
            bank = _nextbank(C)
            pb = _proj_tm(C, Wgb, "Wgb", hf * 512, 512, bank)
            S.op("act", lambda e, pb=pb, hf=hf, slot=slot: e.activation(out=sgb[slot][:, hf * 512:(hf + 1) * 512],
                                                                         in_=pb, func=AF.Silu),
                 reads=["pb%d" % bank], writes=["sgb%d" % slot])
        S.dma("sp", lambda e, t=t, slot=slot: e.dma_start(out=T["sgb"][t], in_=sgb[slot]),
              reads=["sgb%d" % slot], writes=["d_sgb"], home="sgb%d" % slot)
        if sample:
            S.dma("sp", lambda e, ps=ps: e.dma_start(out=O["ssh"], in_=_pstride(ps[7:8, :], 8, 16)),
                  reads=[psn], writes=["o_ssh"], home=psn)
        elif t == NT - 1:
            S.dma("sp", lambda e, ps=ps: e.dma_start(out=O["psh"].rearrange("(o c) -> o c", o=1), in_=ps[127:128, :]),
                  reads=[psn], writes=["o_psh"], home=psn)
        carry = (not sample) and ti > 0
        for (c0, n) in groups:
            bank = _nextbank(C)
            pb = C.pbank(bank)
            last1 = not (carry or sample)
            S.op("pe", lambda e, pb=pb, c0=c0, n=n, ps=ps, m=(ShsI if sample else ShI), last1=last1: e.matmul(
                pb[:, 0:n], lhsT=m, rhs=ps[:, c0:c0 + n], start=True, stop=last1),
                reads=["ShsI" if sample else "ShI", psn], writes=["pb%d" % bank])
            if carry:
                S.op("pe", lambda e, pb=pb, c0=c0, n=n, pp=pp: e.matmul(pb[:, 0:n], lhsT=Ecar, rhs=pp[:, c0:c0 + n],
                                                                         start=False, stop=True),
                     reads=["Ecar", ppn], writes=["pb%d" % bank])
            if sample:
                S.op("pe", lambda e, pb=pb, c0=c0, n=n: e.matmul(pb[:, 0:n], lhsT=Esel, rhs=sst[:, c0:c0 + n],
                                                                  start=False, stop=True),
                     reads=["Esel", "sst"], writes=["pb%d" % bank])
            S.op("dve", lambda e, pb=pb, c0=c0, n=n, zt=zt: e.tensor_tensor(out=zt[:, c0:c0 + n], in0=pb[:, 0:n],
                                                                            in1=mubc[:, c0:c0 + n], op=ALU.mult),
                 reads=["pb%d" % bank, "mubc"], writes=[ztn])
        S.op("dve", lambda e, zt=zt, ps=ps: e.tensor_tensor(out=zt, in0=zt, in1=ps, op=ALU.add), reads=[ztn, psn], writes=[ztn])
        S.dma("sp", lambda e, t=t, zt=zt: e.dma_start(out=T["z"][t], in_=zt), reads=[ztn], writes=["d_z"], home=ztn)


def phase3(C, tiles=None, scan=False, reuse=False):
    C.rotbanks = (1, 2, 3, 4)
    S, alloc, I, O, T = C.S, C.alloc, C.I, C.O, C.T
    C.reset()
    _common_consts(C, init=not reuse)
    w_in = I["w_in"]
    Wm = alloc([128, 8, 2048], BF16)
    Wo = alloc([128, 8, 1024], BF16)
    if reuse:
        pass
    elif DBG.get("precast", True) and "p1" in C.phases:
        _load_wbf(C, Wm, "Wm", "wbf_in", 4224, 2048)
        _load_wbf(C, Wo, "Wo", "wbf_o", 0, 1024)
    else:
        for k in range(8):
            for c0 in range(0, 2048, 512):
                S.dma("pool", lambda e, k=k, c0=c0: e.dma_start(out=Wm[:, k, c0:c0 + 512],
                                                                in_=w_in[k * 128:(k + 1) * 128, 6784 + c0:6784 + c0 + 512]),
                      writes=["Wm"])
        _load_w(C, Wo, "Wo", I["w_o"])
    fgbc = alloc([128, D])
    if not reuse:
        S.dma("sp", lambda e: e.dma_start(out=fgbc, in_=I["final_g"].partition_broadcast(128)), writes=["fgbc"])
    C.p3_persist = C.apos
    sm = alloc([128, 2048])
    yab = [alloc([128, D]), alloc([128, D])]
    ybb = [alloc([128, D]), alloc([128, D])]
    mg = alloc([128, D], BF16)
    mgT = alloc([128, 8, 128], BF16)
    res = alloc([128, D])
    yo = [alloc([128, D]), alloc([128, D])]
    ss2 = alloc([128, 1])
    rs2 = alloc([128, 1])

    if tiles is None:
        tiles = list(range(C.ntiles)) + ([NT] if DBG.get('sample', True) else [])
    scan_ops = _scan_setup(C) if scan else []
    per_tile = (len(scan_ops) + max(len(tiles), 1) - 1) // max(len(tiles), 1)
    if not tiles:
        for fn in scan_ops:
            fn()
        return

    def emit_scan(n):
        for _ in range(max(n, 0)):
            if scan_ops:
                scan_ops.pop(0)()

    def loads(ti):
        t = tiles[ti]
        sl = ti % 2
        S.dma("sp", lambda e: e.dma_start(out=yab[sl], in_=T["ya"][t]), reads=["d_ya"], writes=["yab%d" % sl])
        S.dma("sp", lambda e: e.dma_start(out=ybb[sl], in_=T["yb"][t]), reads=["d_yb"], writes=["ybb%d" % sl])
    _load_x(C, tiles[0], 0)
    if len(tiles) > 1:
        _load_x(C, tiles[1], 1)
    loads(0)
    _norm_pre(C, 0)
    for ti, t in enumerate(tiles):
        slot = ti % 2
        if ti + 2 < len(tiles):
            _load_x(C, tiles[ti + 2], (ti + 2) % 3)
        if ti + 1 < len(tiles):
            loads(ti + 1)
        _norm_T(C, ti % 3, ((ti + 1) % 3) if ti + 1 < len(tiles) else None)
        emit_scan(2)
        for gi in range(4):
            bank = _nextbank(C)
            pb = _proj_tm(C, Wm, "Wm", gi * 512, 512, bank)
            S.op("act", lambda e, pb=pb, gi=gi: e.activation(out=sm[:, gi * 512:(gi + 1) * 512], in_=pb, func=AF.Sigmoid),
                 reads=["pb%d" % bank], writes=["sm"])
        ya, yb = yab[slot], ybb[slot]
        S.op("dve", lambda e, ya=ya: e.tensor_tensor(out=ya, in0=ya, in1=sm[:, 0:1024], op=ALU.mult),
             reads=["yab%d" % slot, "sm"], writes=["yab%d" % slot])
        S.op("pool", lambda e, yb=yb: e.tensor_tensor(out=yb, in0=yb, in1=sm[:, 1024:2048], op=ALU.mult),
             reads=["ybb%d" % slot, "sm"], writes=["ybb%d" % slot])
        S.op("dve", lambda e, ya=ya, yb=yb: e.tensor_tensor(out=mg, in0=ya, in1=yb, op=ALU.add),
             reads=["yab%d" % slot, "ybb%d" % slot], writes=["mg"])
        emit_scan(1)
        pT0 = C.pbank(0, BF16)
        for k in range(8):
            S.op("pe", lambda e, k=k: e.transpose(out=pT0[:, k * 128:(k + 1) * 128], in_=mg[:, k * 128:(k + 1) * 128],
                                                  identity=C.ident), reads=["mg", "ident"], writes=["pb0"])
        S.op("act", lambda e: e.copy(out=mgT.rearrange("p k t -> p (k t)"), in_=pT0), reads=["pb0"], writes=["mgT"])
        xt = C.xt[ti % 3]
        for hf in range(2):
            bank = _nextbank(C)
            pb = C.pbank(bank)
            for k in range(8):
                S.op("pe", lambda e, k=k, pb=pb, hf=hf: e.matmul(pb, lhsT=mgT[:, k, :], rhs=Wo[:, k, hf * 512:(hf + 1) * 512],
                                                                  start=(k == 0), stop=(k == 7)),
                     reads=["mgT", "Wo"], writes=["pb%d" % bank])
            S.op("dve", lambda e, pb=pb, hf=hf, xt=xt: e.tensor_tensor(out=res[:, hf * 512:(hf + 1) * 512], in0=pb,
                                                                       in1=xt[:, hf * 512:(hf + 1) * 512], op=ALU.add),
                 reads=["pb%d" % bank, "xt%d" % (ti % 3)], writes=["res"])
        emit_scan(1)
        S.op("pool", lambda e: e.memset(ss2, 0.0), writes=["ss2"])
        S.op("act", lambda e: e.activation(out=C.junk, in_=res, func=AF.Square, accum_out=ss2),
             reads=["res", "ss2"], writes=["junk", "ss2"])
        S.op("dve", lambda e: e.tensor_scalar(out=rs2, in0=ss2, scalar1=1.0 / D, scalar2=1e-6, op0=ALU.mult, op1=ALU.add),
             reads=["ss2"], writes=["rs2"])
        S.op("act", lambda e: e.activation(out=rs2, in_=rs2, func=AF.Sqrt), reads=["rs2"], writes=["rs2"])
        S.op("dve", lambda e: e.reciprocal(out=rs2, in_=rs2), reads=["rs2"], writes=["rs2"])
        yt = yo[slot]
        S.op("dve", lambda e, yt=yt: e.scalar_tensor_tensor(out=yt, in0=res, scalar=rs2[:, 0:1], in1=fgbc,
                                                            op0=ALU.mult, op1=ALU.mult),
             reads=["res", "rs2", "fgbc"], writes=["yo%d" % slot])
        dst = O["ys"] if t == NT else O["yp"][t * 128:(t + 1) * 128, :]
        S.dma("sp", lambda e, yt=yt, dst=dst: e.dma_start(out=dst, in_=yt), reads=["yo%d" % slot], writes=["o_y"],
              home="yo%d" % slot)
        emit_scan(per_tile - 4)
    while scan_ops:
        scan_ops.pop(0)()


def _rwkv_post(C, B, y, yn_, v, vn_, sbon, sgbt, sgn_, t, mark_pe=None, sbon_n="sbon"):
    S, T = C.S, C.T
    tD, tE, s16 = B["tD"], B["tE"], B["s16"]
    mean, var = s16[:, 0:16], s16[:, 16:32]
    v3 = lambda ap: ap.rearrange("p (h d) -> p h d", h=16)
    bc = lambda ap: ap.unsqueeze(2).to_broadcast([128, 16, 64])
    S.op("dve", lambda e: e.tensor_reduce(out=mean, in_=v3(y), axis=AX.X, op=ALU.add), reads=[yn_], writes=["s16m"])
    S.op("dve", lambda e: e.tensor_scalar(out=mean, in0=mean, scalar1=1.0 / 64, scalar2=None, op0=ALU.mult),
         reads=["s16m"], writes=["s16m"])
    S.op("dve", lambda e: e.tensor_tensor(out=v3(tD), in0=v3(y), in1=bc(mean), op=ALU.subtract),
         reads=[yn_, "s16m"], writes=[B["tDn"]])
    S.op("dve", lambda e: e.tensor_tensor(out=tE, in0=tD, in1=tD, op=ALU.mult), reads=[B["tDn"]], writes=[B["tEn"]])
    S.op("dve", lambda e: e.tensor_reduce(out=var, in_=v3(tE), axis=AX.X, op=ALU.add), reads=[B["tEn"]], writes=["s16v"])
    S.op("dve", lambda e: e.tensor_scalar(out=var, in0=var, scalar1=1.0 / 64, scalar2=64e-5, op0=ALU.mult, op1=ALU.add),
         reads=["s16v"], writes=["s16v"])
    S.op("act", lambda e: e.activation(out=var, in_=var, func=AF.Sqrt), reads=["s16v"], writes=["s16v"])
    S.op("dve", lambda e: e.reciprocal(out=var, in_=var), reads=["s16v"], writes=["s16v"])
    S.op("dve", lambda e: e.tensor_tensor(out=v3(tD), in0=v3(tD), in1=bc(var), op=ALU.mult), reads=[B["tDn"], "s16v"], writes=[B["tDn"]])
    S.op("dve", lambda e: e.tensor_tensor(out=tD, in0=tD, in1=B["lgbc"], op=ALU.mult), reads=[B["tDn"], "lgbc"], writes=[B["tDn"]])
    S.op("dve", lambda e: e.tensor_tensor(out=tD, in0=tD, in1=B["lbbc"], op=ALU.add), reads=[B["tDn"], "lbbc"], writes=[B["tDn"]])
    S.op("dve", lambda e: e.tensor_tensor(out=v3(tE), in0=v3(v), in1=bc(sbon), op=ALU.mult), reads=[vn_, sbon_n], writes=[B["tEn"]])
    S.op("dve", lambda e: e.tensor_tensor(out=tD, in0=tD, in1=tE, op=ALU.add), reads=[B["tDn"], B["tEn"]], writes=[B["tDn"]])
    S.op("dve", lambda e: e.tensor_tensor(out=B["ybg"], in0=tD, in1=sgbt, op=ALU.mult), reads=[B["tDn"], sgn_], writes=[B["ybgn"]])
    if mark_pe is not None:
        S.mark(mark_pe)
    pT0 = C.pbank(0, BF16)
    for k in range(8):
        S.op("pe", lambda e, k=k: e.transpose(out=pT0[:, k * 128:(k + 1) * 128], in_=B["ybg"][:, k * 128:(k + 1) * 128],
                                              identity=B["ident"]), reads=[B["ybgn"], "ident"], writes=["pb0"])
    S.op("act", lambda e: e.copy(out=B["ybT"].rearrange("p k t -> p (k t)"), in_=pT0), reads=["pb0"], writes=[B["ybTn"]])
    for hf in range(2):
        bank = _nextbank(C)
        pb = C.pbank(bank)
        for k in range(8):
            S.op("pe", lambda e, k=k, pb=pb, hf=hf: e.matmul(pb, lhsT=B["ybT"][:, k, :], rhs=B["WoB"][:, k, hf * 512:(hf + 1) * 512],
                                                              start=(k == 0), stop=(k == 7)),
                 reads=[B["ybTn"], "WoB"], writes=["pb%d" % bank])
        S.op("act", lambda e, pb=pb, hf=hf: e.copy(out=B["ybo"][:, hf * 512:(hf + 1) * 512], in_=pb),
             reads=["pb%d" % bank], writes=[B["ybon"]])
    S.dma("sp", lambda e, t=t: e.dma_start(out=T["yb"][t], in_=B["ybo"]), reads=[B["ybon"]], writes=["d_yb"], home=B["ybon"])


def _post_bufs(C, B):
    S, alloc, I = C.S, C.alloc, C.I
    B["identf"] = alloc([128, 128])
    B["ident"] = alloc([128, 128], BF16)
    S.op("pool", lambda e: e.memset(B["identf"], 0.0), writes=["identf"])
    S.op("pool", lambda e: e.affine_select(out=B["identf"], in_=B["identf"], pattern=[[-1, 128]], compare_op=ALU.not_equal,
                                           fill=1.0, base=0, channel_multiplier=1), reads=["identf"], writes=["identf"])
    S.op("dve", lambda e: e.tensor_copy(out=B["ident"], in_=B["identf"]), reads=["identf"], writes=["ident"])
    B["WoB"] = alloc([128, 8, 1024], BF16)
    if DBG.get("precast", True) and "p1" in C.phases:
        _load_wbf(C, B["WoB"], "WoB", "wbf_ob", 0, 1024)
    else:
        _load_w(C, B["WoB"], "WoB", I["w_out_b"])
    for nm, src in (("lgbc", "lnx_g"), ("lbbc", "lnx_b")):
        B[nm] = alloc([128, D])
        S.dma("sp", lambda e, nm=nm, src=src: e.dma_start(out=B[nm], in_=I[src].partition_broadcast(128)), writes=[nm])
    B["tD"] = alloc([128, D])
    B["tDn"] = "tD"
    if B.get("alloc_tE", True):
        B["tE"] = alloc([128, D])
        B["tEn"] = "tE"
    B["s16"] = alloc([128, 96])
    B["ybgn"], B["ybTn"], B["ybon"] = "ybg", "ybT", "ybo"
    if B.get("alloc_yb", True):
        B["ybg"] = alloc([128, D], BF16)
        B["ybT"] = alloc([128, 8, 128], BF16)
        B["ybo"] = alloc([128, D])


def phase2b(C):
    S, alloc, I, O, T = C.S, C.alloc, C.I, C.O, C.T
    C.reset()
    B = {"alloc_tE": False, "alloc_yb": False}
    _post_bufs(C, B)
    C.rotbanks = (1, 2, 3)
    identf, ident = B["identf"], B["ident"]
    tD, s16 = B["tD"], B["s16"]
    W2A2 = alloc([128, D], BF16)
    S.dma("pool", lambda e: e.dma_start(out=W2A2[0:64, :], in_=I["w2"]), writes=["W2A2"])
    S.dma("pool", lambda e: e.dma_start(out=W2A2[64:128, :], in_=I["a2"]), writes=["W2A2"])
    vecs = alloc([128, D])
    S.dma("sp", lambda e: e.dma_start(out=vecs[0:1, :], in_=I["w0"].rearrange("(o c) -> o c", o=1)), writes=["vecs"])
    S.dma("sp", lambda e: e.dma_start(out=vecs[32:33, :], in_=I["a0"].rearrange("(o c) -> o c", o=1)), writes=["vecs"])
    ones = alloc([128, 128])
    S.op("pool", lambda e: e.memset(ones, 1.0), writes=["ones"])
    negcol = alloc([128, 1])
    S.op("pool", lambda e: e.memset(negcol, -CDEC), writes=["negcol"])
    zl = alloc([128, 128], BF16)
    zr = alloc([128, 512], BF16)
    S.op("pool", lambda e: e.memset(zl, 0.0), writes=["zl"])
    S.op("pool", lambda e: e.memset(zr, 0.0), writes=["zr"])

    def tri(name, val, pattern, cm, base):
        m = alloc([128, 128])
        S.op("pool", lambda e: e.memset(m, val), writes=[name])
        S.op("pool", lambda e: e.affine_select(out=m, in_=m, pattern=pattern, compare_op=ALU.is_ge, fill=0.0,
                                               base=base, channel_multiplier=cm), reads=[name], writes=[name])
        return m
    Lincl = tri("Lincl", -CDEC, [[1, 128]], -1, 0)
    Lstr = tri("Lstr", -CDEC, [[1, 128]], -1, -1)
    Ustr = tri("Ustr", -CDEC, [[-1, 128]], 1, -1)
    MlowS = alloc([128, 512])
    S.op("pool", lambda e: e.memset(MlowS, 1.0), writes=["MlowS"])
    for blk in range(4):
        S.op("pool", lambda e, blk=blk: e.affine_select(out=MlowS[:, blk * 128:(blk + 1) * 128], in_=MlowS[:, blk * 128:(blk + 1) * 128],
                                                        pattern=[[-1, 128]], compare_op=ALU.is_ge, fill=0.0, base=-1, channel_multiplier=1),
             reads=["MlowS"], writes=["MlowS"])
    Mask4 = alloc([128, 512])
    S.op("pool", lambda e: e.memset(Mask4, 1.0), writes=["Mask4"])
    for blk in range(4):
        S.op("pool", lambda e, blk=blk: e.affine_select(out=Mask4[:, blk * 128:(blk + 1) * 128], in_=Mask4[:, blk * 128:(blk + 1) * 128],
                                                        pattern=[[1, 128]], compare_op=ALU.is_ge, fill=0.0,
                                                        base=(-1 if blk % 2 == 0 else 0), channel_multiplier=-1),
             reads=["Mask4"], writes=["Mask4"])
    bcs = {}
    for nm, src in (("kkbc", "k_k"), ("kabc", "k_a"), ("rkbc", "r_k")):
        bcs[nm] = alloc([128, D])
        S.dma("sp", lambda e, nm=nm, src=src: e.dma_start(out=bcs[nm], in_=I[src].partition_broadcast(128)), writes=[nm])
    ztb = [alloc([128, 3200]), alloc([128, 3200])]
    sgbt = alloc([128, D])
    sg = alloc([128, D])
    av = alloc([128, D])
    tA = alloc([128, D])
    tB = alloc([128, D])
    tC = alloc([128, D])
    Ea = alloc([128, D])
    B["ybo"], B["ybon"] = Ea, "Ea"
    Eb = alloc([128, D])
    B["tE"], B["tEn"] = Eb, "Eb"
    lT = alloc([128, 128], BF16)
    At, Rt, Bt, Kt, Bh, Kh, Vb = [alloc([128, D], BF16) for _ in range(7)]
    B["ybg"], B["ybgn"] = At, "At"
    ARt = alloc([128, 8, 2, 128], BF16)
    BtT = alloc([128, 8, 128], BF16)
    B["ybT"], B["ybTn"] = BtT, "BtT"
    KtT = alloc([128, 8, 128], BF16)
    WT = KtT
    Am = [alloc([128, 512], BF16) for _ in range(16)]
    Nb = [[alloc([128, 4, 128], BF16) for _ in range(2)] for _ in range(4)]
    Lb = [[alloc([128, 4, 128], BF16) for _ in range(2)] for _ in range(4)]
    L0 = [Lb[hg][1] for hg in range(4)]
    Gbf = [alloc([128, 4, 128], BF16) for _ in range(4)]
    Wall = Rt.rearrange("p (h d) -> p h d", h=16)
    Z32 = tA.rearrange("p (h d) -> p h d", h=16)
    Ub = Bt
    H32 = alloc([128, 8, 64])
    Hbf = [alloc([128, 8, 2, 64], BF16) for _ in range(2)]
    PC = alloc([128, 8])
    yv = Ea
    S.op("pool", lambda e: e.memset(H32, 0.0), writes=["H32"])
    S.op("pool", lambda e: e.memset(Hbf[0], 0.0), writes=["Hbf0"])
    S.op("pool", lambda e: e.memset(Hbf[1], 0.0), writes=["Hbf1"])
    ss16, rn16 = s16[:, 32:48], s16[:, 32:48]
    v3 = lambda ap: ap.rearrange("p (h d) -> p h d", h=16)
    bc = lambda ap: ap.unsqueeze(2).to_broadcast([128, 16, 64])

    tiles = ([NT] if DBG.get('sample', True) else []) + list(range(C.ntiles))
    def ldz(ti):
        S.dma("sp", lambda e, ti=ti: e.dma_start(out=ztb[ti % 2], in_=T["z"][tiles[ti]]), reads=["d_z"], writes=["zt%d" % (ti % 2)])
    if tiles:
        ldz(0)
    for ti, t in enumerate(tiles):
        sample = (t == NT)
        zt, ztn = ztb[ti % 2], "zt%d" % (ti % 2)
        sbon, sbn = s16[:, 48 + 16 * (ti % 2):64 + 16 * (ti % 2)], "sbon%d" % (ti % 2)
        S.mark("L%d" % ti)
        if ti + 1 < len(tiles):
            ldz(ti + 1)
        S.mark("P1_%d" % ti)
        r_, k_, v_, lor = zt[:, 0:1024], zt[:, 1024:2048], zt[:, 2048:3072], zt[:, 3072:3200]
        bank = _nextbank(C)
        pb = C.pbank(bank)
        S.op("pe", lambda e, r_=r_, k_=k_, v_=v_, lor=lor, pb=pb: e.transpose(out=pb[:, 0:128], in_=lor, identity=identf), reads=[ztn, "identf"], writes=["pb%d" % bank])
        S.op("act", lambda e, r_=r_, k_=k_, v_=v_, lor=lor, pb=pb: e.activation(out=lT[0:64, :], in_=pb[0:64, 0:128], func=AF.Tanh), reads=["pb%d" % bank], writes=["lT"])
        S.op("act", lambda e, r_=r_, k_=k_, v_=v_, lor=lor, pb=pb: e.copy(out=lT[64:128, :], in_=pb[64:128, 0:128]), reads=["pb%d" % bank], writes=["lT"])
        for (dst, dn, r0, v0) in ((sg, "sg", 0, 0), (av, "av", 64, 32)):
            for hf in range(2):
                bank = _nextbank(C)
                pb = C.pbank(bank)
                S.op("pe", lambda e, r_=r_, k_=k_, v_=v_, lor=lor, pb=pb, r0=r0, hf=hf: e.matmul(pb, lhsT=lT[r0:r0 + 64, :], rhs=W2A2[r0:r0 + 64, hf * 512:(hf + 1) * 512],
                                                                    start=True, stop=False), reads=["lT", "W2A2"], writes=["pb%d" % bank])
                S.op("pe", lambda e, r_=r_, k_=k_, v_=v_, lor=lor, pb=pb, v0=v0, hf=hf: e.matmul(pb, lhsT=ones[v0:v0 + 1, :], rhs=vecs[v0:v0 + 1, hf * 512:(hf + 1) * 512],
                                                                    start=False, stop=True), reads=["ones", "vecs"], writes=["pb%d" % bank])
                S.op("act", lambda e, r_=r_, k_=k_, v_=v_, lor=lor, pb=pb, dst=dst, hf=hf: e.activation(out=dst[:, hf * 512:(hf + 1) * 512], in_=pb, func=AF.Sigmoid),
                     reads=["pb%d" % bank], writes=[dn])
        if DBG.get('p2b_stop', 99) <= 1:
            continue
        S.mark("P2_%d" % ti)
        S.op("dve", lambda e, r_=r_, k_=k_, v_=v_, lor=lor: e.tensor_tensor(out=tA, in0=k_, in1=bcs["kkbc"], op=ALU.mult), reads=[ztn, "kkbc"], writes=["tA"])
        S.op("dve", lambda e, r_=r_, k_=k_, v_=v_, lor=lor: e.tensor_tensor(out=tB, in0=tA, in1=tA, op=ALU.mult), reads=["tA"], writes=["tB"])
        S.op("dve", lambda e, r_=r_, k_=k_, v_=v_, lor=lor: e.tensor_reduce(out=ss16, in_=v3(tB), axis=AX.X, op=ALU.add), reads=["tB"], writes=["s16n"])
        S.op("act", lambda e, r_=r_, k_=k_, v_=v_, lor=lor: e.activation(out=ss16, in_=ss16, func=AF.Sqrt), reads=["s16n"], writes=["s16n"])
        S.op("dve", lambda e, r_=r_, k_=k_, v_=v_, lor=lor: e.tensor_scalar(out=ss16, in0=ss16, scalar1=1e-12, scalar2=None, op0=ALU.max), reads=["s16n"], writes=["s16n"])
        S.op("dve", lambda e, r_=r_, k_=k_, v_=v_, lor=lor: e.reciprocal(out=ss16, in_=ss16), reads=["s16n"], writes=["s16n"])
        S.op("dve", lambda e, r_=r_, k_=k_, v_=v_, lor=lor: e.tensor_tensor(out=v3(tA), in0=v3(tA), in1=bc(rn16), op=ALU.mult), reads=["tA", "s16n"], writes=["tA"])
        S.op("dve", lambda e, r_=r_, k_=k_, v_=v_, lor=lor: e.scalar_tensor_tensor(out=tB, in0=av, scalar=-1.0, in1=bcs["kabc"], op0=ALU.add, op1=ALU.mult),
             reads=["av", "kabc"], writes=["tB"])
        S.op("dve", lambda e, r_=r_, k_=k_, v_=v_, lor=lor: e.scalar_tensor_tensor(out=tB, in0=tB, scalar=1.0, in1=k_, op0=ALU.add, op1=ALU.mult),
             reads=["tB", ztn], writes=["tB"])
        S.op("dve", lambda e, r_=r_, k_=k_, v_=v_, lor=lor: e.tensor_tensor(out=tC, in0=tA, in1=av, op=ALU.mult), reads=["tA", "av"], writes=["tC"])
        S.mark("P3_%d" % ti)
        S.op("dve", lambda e, r_=r_, k_=k_, v_=v_, lor=lor: e.tensor_tensor(out=tD, in0=r_, in1=tB, op=ALU.mult), reads=[ztn, "tB"], writes=["tD"])
        S.op("dve", lambda e, r_=r_, k_=k_, v_=v_, lor=lor: e.tensor_tensor(out=tD, in0=tD, in1=bcs["rkbc"], op=ALU.mult), reads=["tD", "rkbc"], writes=["tD"])
        S.op("dve", lambda e, r_=r_, k_=k_, v_=v_, lor=lor, sbon=sbon: e.tensor_reduce(out=sbon, in_=v3(tD), axis=AX.X, op=ALU.add), reads=["tD"], writes=[sbn])
        if DBG.get('p2b_stop', 99) <= 2:
            continue
        if sample:
            S.op("act", lambda e, r_=r_, k_=k_, v_=v_, lor=lor: e.activation(out=Ea, in_=sg, func=AF.Exp, scale=-CDEC), reads=["sg"], writes=["Ea"])
            S.op("dve", lambda e, r_=r_, k_=k_, v_=v_, lor=lor: e.tensor_scalar(out=tA, in0=tA, scalar1=-1.0, scalar2=None, op0=ALU.mult), reads=["tA"], writes=["tA"])
            for qi, (src, sn) in enumerate(((r_, ztn), (Ea, "Ea"), (tB, "tB"), (v_, ztn), (tA, "tA"), (tC, "tC"))):
                S.dma("sp", lambda e, r_=r_, k_=k_, v_=v_, lor=lor, qi=qi, src=src: e.dma_start(out=T["s6"][qi], in_=src), reads=[sn], writes=["d_s6"], home=sn)
            S.dma("sp", lambda e, r_=r_, k_=k_, v_=v_, lor=lor, sbon=sbon: e.dma_start(out=T["sextra"][:, 0:16], in_=sbon), reads=[sbn], writes=["d_sx"], home=sbn)
            continue
        def cums(Lm, ln, outs):
            for hf in range(2):
                bank = _nextbank(C)
                pb = C.pbank(bank)
                S.op("pe", lambda e, r_=r_, k_=k_, v_=v_, lor=lor, pb=pb, hf=hf: e.matmul(pb, lhsT=Lm, rhs=sg[:, hf * 512:(hf + 1) * 512], start=True, stop=True),
                     reads=[ln, "sg"], writes=["pb%d" % bank])
                for (dst, dn, sc) in outs:
                    S.op("act", lambda e, r_=r_, k_=k_, v_=v_, lor=lor, pb=pb, dst=dst, sc=sc, hf=hf: e.activation(out=dst[:, hf * 512:(hf + 1) * 512], in_=pb,
                                                                                       func=AF.Exp, scale=sc),
                         reads=["pb%d" % bank], writes=[dn])
        cums(Lincl, "Lincl", ((Ea, "Ea", 1.0), (Eb, "Eb", -1.0)))
        S.op("dve", lambda e, r_=r_, k_=k_, v_=v_, lor=lor: e.tensor_tensor(out=Rt, in0=r_, in1=Ea, op=ALU.mult), reads=[ztn, "Ea"], writes=["Rt"])
        S.op("pool", lambda e, r_=r_, k_=k_, v_=v_, lor=lor: e.tensor_tensor(out=Bt, in0=tC, in1=Eb, op=ALU.mult), reads=["tC", "Eb"], writes=["Bt"])
        S.op("dve", lambda e, r_=r_, k_=k_, v_=v_, lor=lor: e.tensor_tensor(out=Kt, in0=tB, in1=Eb, op=ALU.mult), reads=["tB", "Eb"], writes=["Kt"])
        cums(Lstr, "Lstr", ((Ea, "Ea", 1.0),))
        S.op("dve", lambda e, r_=r_, k_=k_, v_=v_, lor=lor: e.scalar_tensor_tensor(out=At, in0=tA, scalar=-1.0, in1=Ea, op0=ALU.mult, op1=ALU.mult),
             reads=["tA", "Ea"], writes=["At"])
        cums(Ustr, "Ustr", ((Eb, "Eb", 1.0),))
        S.op("pool", lambda e, r_=r_, k_=k_, v_=v_, lor=lor: e.tensor_tensor(out=Bh, in0=tC, in1=Eb, op=ALU.mult), reads=["tC", "Eb"], writes=["Bh"])
        S.op("dve", lambda e, r_=r_, k_=k_, v_=v_, lor=lor: e.tensor_tensor(out=Kh, in0=tB, in1=Eb, op=ALU.mult), reads=["tB", "Eb"], writes=["Kh"])
        S.op("act", lambda e, r_=r_, k_=k_, v_=v_, lor=lor: e.copy(out=Vb, in_=v_), reads=[ztn], writes=["Vb"])
        bank = _nextbank(C)
        pb = C.pbank(bank)
        for p in range(8):
            S.op("pe", lambda e, r_=r_, k_=k_, v_=v_, lor=lor, pb=pb, p=p: e.matmul(pb[:, p:p + 1], lhsT=sg[:, p * 128:(p + 1) * 128], rhs=negcol, start=True, stop=True),
                 reads=["sg", "negcol"], writes=["pb%d" % bank])
        S.op("act", lambda e, r_=r_, k_=k_, v_=v_, lor=lor, pb=pb: e.activation(out=PC, in_=pb[:, 0:8], func=AF.Exp), reads=["pb%d" % bank], writes=["PC"])
        if DBG.get('p2b_stop', 99) <= 3:
            continue
        S.mark("A_%d" % ti)
        pT0 = C.pbank(0, BF16)
        for qi, (src, sn, dst, dn) in enumerate(((At, "At", ARt[:, :, 0, :], "ARt"), (Rt, "Rt", ARt[:, :, 1, :], "ARt"),
                                                 (Bt, "Bt", BtT, "BtT"), (Kt, "Kt", KtT, "KtT"))):
            pTq = C.pbank(4 + qi, BF16)
            for k in range(8):
                S.op("pe", lambda e, r_=r_, k_=k_, v_=v_, lor=lor, k=k, src=src, pTq=pTq: e.transpose(
                    out=pTq[:, k * 128:(k + 1) * 128], in_=src[:, k * 128:(k + 1) * 128], identity=ident),
                    reads=[sn, "ident"], writes=["pb%d" % (4 + qi)])
            S.op("act", lambda e, r_=r_, k_=k_, v_=v_, lor=lor, dst=dst, pTq=pTq: e.copy(out=dst, in_=pTq.rearrange("p (k t) -> p k t", k=8)),
                 reads=["pb%d" % (4 + qi)], writes=[dn])
        if DBG.get('p2b_stop', 99) <= 4:
            continue
        C.rotbanks = (1, 2, 3, 4, 5, 6, 7)
        for h in range(16):
            p, rows = h // 2, slice((h % 2) * 64, (h % 2) * 64 + 64)
            bank = _nextbank(C)
            pb = C.pbank(bank)
            S.op("pe", lambda e, r_=r_, k_=k_, v_=v_, lor=lor, pb=pb, p=p, rows=rows: e.matmul(pb[:, 0:256], lhsT=BtT[rows, p, :],
                                                                  rhs=ARt[rows, p, :, :].rearrange("q a t -> q (a t)"), start=True, stop=True),
                 reads=["BtT", "ARt"], writes=["pb%d" % bank])
            S.op("pe", lambda e, r_=r_, k_=k_, v_=v_, lor=lor, pb=pb, p=p, rows=rows: e.matmul(pb[:, 256:512], lhsT=KtT[rows, p, :],
                                                                  rhs=ARt[rows, p, :, :].rearrange("q a t -> q (a t)"), start=True, stop=True),
                 reads=["KtT", "ARt"], writes=["pb%d" % bank])
            S.op("dve", lambda e, r_=r_, k_=k_, v_=v_, lor=lor, pb=pb, h=h: e.tensor_tensor(out=Am[h], in0=pb, in1=Mask4, op=ALU.mult),
                 reads=["pb%d" % bank, "Mask4"], writes=["Am%d" % h])
        C.rotbanks = (1, 2, 3)
        if DBG.get('p2b_stop', 99) == 45:
            continue
        for half in range(2):
            for i in range(8):
                h = half * 8 + i
                S.op("pe", lambda e, r_=r_, k_=k_, v_=v_, lor=lor, i=i, h=h: e.transpose(out=pT0[:, i * 128:(i + 1) * 128], in_=Am[h][:, 0:128], identity=ident),
                     reads=["Am%d" % h, "ident"], writes=["pb0"])
            for q in range(2):
                hg = half * 2 + q
                S.op("act", lambda e, r_=r_, k_=k_, v_=v_, lor=lor, hg=hg, q=q: e.copy(out=L0[hg].rearrange("p h s -> p (h s)"), in_=pT0[:, q * 512:(q + 1) * 512]),
                     reads=["pb0"], writes=["Lb%d_1" % hg])
        if DBG.get('p2b_stop', 99) <= 5:
            continue
        st_ = []
        for hg in range(4):
            pgb = 4 + hg
            PG = C.pbank(pgb)
            pgn = "pb%d" % pgb
            S.op("pe", lambda e, r_=r_, k_=k_, v_=v_, lor=lor, PG=PG: e.matmul(PG, lhsT=zl, rhs=zr, start=True, stop=False, skip_group_check=True),
                 reads=["zl", "zr"], writes=[pgn])
            for hh in range(4):
                h = hg * 4 + hh
                S.op("pe", lambda e, r_=r_, k_=k_, v_=v_, lor=lor, PG=PG, hh=hh, h=h: e.matmul(PG[:, hh * 128:hh * 128 + 64], lhsT=ident, rhs=At[:, h * 64:(h + 1) * 64],
                                                                  start=False, stop=False, skip_group_check=True),
                     reads=["ident", "At"], writes=[pgn])
                S.op("pe", lambda e, r_=r_, k_=k_, v_=v_, lor=lor, PG=PG, hh=hh, h=h: e.matmul(PG[:, hh * 128 + 64:(hh + 1) * 128], lhsT=Am[h][:, 256:384],
                                                                  rhs=Vb[:, h * 64:(h + 1) * 64], start=False, stop=False, skip_group_check=True),
                     reads=["Am%d" % h, "Vb"], writes=[pgn])
            S.op("dve", lambda e, r_=r_, k_=k_, v_=v_, lor=lor, PG=PG, hg=hg: e.tensor_copy(out=Gbf[hg].rearrange("p h s -> p (h s)"), in_=PG),
                 reads=[pgn], writes=["Gbf%d" % hg])
            st_.append(dict(PG=PG, pgn=pgn,
                            Ncur=[Am[hg * 4 + hh][:, 0:128] for hh in range(4)], Nn=["Am%d" % (hg * 4 + hh) for hh in range(4)],
                            Lcur=[L0[hg][:, hh, :] for hh in range(4)], Ln=["Lb%d_1" % hg] * 4))
        for j in range(7):
            for hg in range(4):
                q = st_[hg]
                PG, pgn, Ncur, Nn_, Lcur, Ln_ = q["PG"], q["pgn"], q["Ncur"], q["Nn"], q["Lcur"], q["Ln"]
                for hh in range(4):
                    S.op("pe", lambda e, r_=r_, k_=k_, v_=v_, lor=lor, PG=PG, hh=hh, nl=Ncur[hh], hg=hg, j=j: e.matmul(
                        PG[:, hh * 128:(hh + 1) * 128], lhsT=nl, rhs=Gbf[hg][:, hh, :], start=False, stop=(j == 6),
                        skip_group_check=True), reads=[Nn_[hh], "Gbf%d" % hg], writes=[pgn])
                if j < 6:
                    bank = _nextbank(C)
                    pb = C.pbank(bank)
                    for hh in range(4):
                        S.op("pe", lambda e, r_=r_, k_=k_, v_=v_, lor=lor, pb=pb, hh=hh, ll=Lcur[hh], nl=Ncur[hh]: e.matmul(
                            pb[:, hh * 128:(hh + 1) * 128], lhsT=ll, rhs=nl, start=True, stop=True),
                            reads=[Ln_[hh], Nn_[hh]], writes=["pb%d" % bank])
                    nb = Nb[hg][j % 2]
                    nbn = "Nb%d_%d" % (hg, j % 2)
                    S.op("act", lambda e, r_=r_, k_=k_, v_=v_, lor=lor, pb=pb, nb=nb: e.copy(out=nb.rearrange("p h s -> p (h s)"), in_=pb),
                         reads=["pb%d" % bank], writes=[nbn])
                    if j < 5:
                        bank = _nextbank(C)
                        pb = C.pbank(bank)
                        for hh in range(4):
                            S.op("pe", lambda e, r_=r_, k_=k_, v_=v_, lor=lor, pb=pb, hh=hh, ll=Lcur[hh], nl=Ncur[hh]: e.matmul(
                                pb[:, hh * 128:(hh + 1) * 128], lhsT=nl, rhs=ll, start=True, stop=True),
                                reads=[Ln_[hh], Nn_[hh]], writes=["pb%d" % bank])
                        lb = Lb[hg][j % 2]
                        lbn = "Lb%d_%d" % (hg, j % 2)
                        S.op("act", lambda e, r_=r_, k_=k_, v_=v_, lor=lor, pb=pb, lb=lb: e.copy(out=lb.rearrange("p h s -> p (h s)"), in_=pb),
                             reads=["pb%d" % bank], writes=[lbn])
                        q["Lcur"] = [lb[:, hh, :] for hh in range(4)]
                        q["Ln"] = [lbn] * 4
                    S.op("dve", lambda e, r_=r_, k_=k_, v_=v_, lor=lor, PG=PG, hg=hg: e.tensor_copy(out=Gbf[hg].rearrange("p h s -> p (h s)"), in_=PG),
                         reads=[pgn], writes=["Gbf%d" % hg])
                    q["Ncur"] = [nb[:, hh, :] for hh in range(4)]
                    q["Nn"] = [nbn] * 4
        for hg in range(4):
            PG, pgn = st_[hg]["PG"], st_[hg]["pgn"]
            PGv = PG.rearrange("p (h s) -> p h s", h=4)
            S.op("act", lambda e, r_=r_, k_=k_, v_=v_, lor=lor, PGv=PGv, hg=hg: e.copy(out=Wall[:, hg * 4:(hg + 1) * 4, :], in_=PGv[:, :, 0:64]),
                 reads=[pgn], writes=["Rt"])
            S.op("dve", lambda e, r_=r_, k_=k_, v_=v_, lor=lor, PGv=PGv, hg=hg: e.tensor_copy(out=Z32[:, hg * 4:(hg + 1) * 4, :], in_=PGv[:, :, 64:128]),
                 reads=[pgn], writes=["tA"])
        if DBG.get('p2b_stop', 99) <= 6:
            continue
        WallF = Wall.rearrange("p h d -> p (h d)")
        for k in range(8):
            S.op("pe", lambda e, r_=r_, k_=k_, v_=v_, lor=lor, k=k: e.transpose(out=pT0[:, k * 128:(k + 1) * 128], in_=WallF[:, k * 128:(k + 1) * 128], identity=ident),
                 reads=["Rt", "ident"], writes=["pb0"])
        S.op("act", lambda e, r_=r_, k_=k_, v_=v_, lor=lor: e.copy(out=WT.rearrange("p k t -> p (k t)"), in_=pT0), reads=["pb0"], writes=["KtT"])
        ho, hn = Hbf[ti % 2], Hbf[(ti + 1) % 2]
        hon, hnn = "Hbf%d" % (ti % 2), "Hbf%d" % ((ti + 1) % 2)
        Z32F = Z32.rearrange("p h d -> p (h d)")
        for hb in range(2):
            bank = _nextbank(C)
            pb = C.pbank(bank)
            for i in range(4):
                p = hb * 4 + i
                S.op("pe", lambda e, r_=r_, k_=k_, v_=v_, lor=lor, pb=pb, i=i, p=p, ho=ho: e.matmul(pb[:, i * 128:(i + 1) * 128], lhsT=WT[:, p, :],
                                                                       rhs=ho[:, p, :, :].rearrange("q a d -> q (a d)"), start=True, stop=True),
                     reads=["KtT", hon], writes=["pb%d" % bank])
            S.op("dve", lambda e, r_=r_, k_=k_, v_=v_, lor=lor, pb=pb, hb=hb: e.tensor_tensor(out=Ub[:, hb * 512:(hb + 1) * 512], in0=pb,
                                                                in1=Z32F[:, hb * 512:(hb + 1) * 512], op=ALU.add),
                 reads=["pb%d" % bank, "tA"], writes=["Bt"])
        if DBG.get('p2b_stop', 99) == 71:
            continue
        S.op("dve", lambda e, r_=r_, k_=k_, v_=v_, lor=lor: e.tensor_tensor(out=H32, in0=H32, in1=PC.unsqueeze(2).to_broadcast([128, 8, 64]), op=ALU.mult),
             reads=["H32", "PC"], writes=["H32"])
        for hb in range(2):
            bank = _nextbank(C)
            pb = C.pbank(bank)
            for i in range(8):
                h = hb * 8 + i
                p = h // 2
                S.op("pe", lambda e, r_=r_, k_=k_, v_=v_, lor=lor, pb=pb, i=i, p=p, h=h: e.matmul(pb[:, i * 64:(i + 1) * 64], lhsT=Bh[:, p * 128:(p + 1) * 128],
                                                                     rhs=Ub[:, h * 64:(h + 1) * 64], start=True, stop=False),
                     reads=["Bh", "Bt"], writes=["pb%d" % bank])
                S.op("pe", lambda e, r_=r_, k_=k_, v_=v_, lor=lor, pb=pb, i=i, p=p, h=h: e.matmul(pb[:, i * 64:(i + 1) * 64], lhsT=Kh[:, p * 128:(p + 1) * 128],
                                                                     rhs=Vb[:, h * 64:(h + 1) * 64], start=False, stop=True),
                     reads=["Kh", "Vb"], writes=["pb%d" % bank])
            pbv = pb.rearrange("q (p a d) -> q p a d", p=4, a=2)
            for hf in range(2):
                S.op("dve", lambda e, r_=r_, k_=k_, v_=v_, lor=lor, pbv=pbv, hf=hf, hb=hb: e.tensor_tensor(
                    out=H32[hf * 64:(hf + 1) * 64, hb * 4:(hb + 1) * 4, :], in0=H32[hf * 64:(hf + 1) * 64, hb * 4:(hb + 1) * 4, :],
                    in1=pbv[hf * 64:(hf + 1) * 64, :, hf, :], op=ALU.add), reads=["pb%d" % bank, "H32"], writes=["H32"])
        if DBG.get('p2b_stop', 99) == 72:
            continue
        for hf in range(2):
            S.op("act", lambda e, r_=r_, k_=k_, v_=v_, lor=lor, hn=hn, hf=hf: e.copy(out=hn[hf * 64:(hf + 1) * 64, :, hf, :], in_=H32[hf * 64:(hf + 1) * 64, :, :]),
                 reads=["H32"], writes=[hnn])
        for hb in range(2):
            bank = _nextbank(C)
            pb = C.pbank(bank)
            for i in range(4):
                p = hb * 4 + i
                S.op("pe", lambda e, r_=r_, k_=k_, v_=v_, lor=lor, pb=pb, i=i, p=p, ho=ho: e.matmul(pb[:, i * 128:(i + 1) * 128], lhsT=ARt[:, p, 1, :],
                                                                       rhs=ho[:, p, :, :].rearrange("q a d -> q (a d)"), start=True, stop=False),
                     reads=["ARt", hon], writes=["pb%d" % bank])
                for hf in range(2):
                    h = 2 * p + hf
                    c0 = i * 128 + hf * 64
                    S.op("pe", lambda e, r_=r_, k_=k_, v_=v_, lor=lor, pb=pb, c0=c0, h=h: e.matmul(pb[:, c0:c0 + 64], lhsT=Am[h][:, 128:256],
                                                                     rhs=Ub[:, h * 64:(h + 1) * 64], start=False, stop=False),
                         reads=["Am%d" % h, "Bt"], writes=["pb%d" % bank])
                    S.op("pe", lambda e, r_=r_, k_=k_, v_=v_, lor=lor, pb=pb, c0=c0, h=h, hf=hf: e.matmul(pb[:, c0:c0 + 64], lhsT=Am[h][:, 384:512],
                                                                            rhs=Vb[:, h * 64:(h + 1) * 64], start=False, stop=(hf == 1)),
                         reads=["Am%d" % h, "Vb"], writes=["pb%d" % bank])
            S.op("act", lambda e, r_=r_, k_=k_, v_=v_, lor=lor, pb=pb, hb=hb: e.copy(out=yv[:, hb * 512:(hb + 1) * 512], in_=pb), reads=["pb%d" % bank], writes=["Ea"])
        if DBG.get('p2b_stop', 99) <= 7:
            continue
        S.mark("PD_%d" % ti)
        S.dma("sp", lambda e, t=t: e.dma_start(out=sgbt, in_=T["sgb"][t]), reads=["d_sgb"], writes=["sgbt"])
        _rwkv_post(C, B, yv, "Ea", v_, ztn, sbon, sgbt, "sgbt", t, mark_pe="PP_%d" % ti, sbon_n=sbn)
        if C.debug:
            S.dma("sp", lambda e, t=t: e.dma_start(out=T["dy"][t], in_=yv), reads=["Ea"], writes=["d_dy"], home="Ea")
    S.mark(None)
    n_ = len(tiles)
    if n_:
        S.replay("L0", "P1_0", "P2_0", "P3_0")
    for ti in range(n_):
        S.replay("A_%d" % ti)
        if ti + 1 < n_:
            S.replay("P1_%d" % (ti + 1))
            S.interleave("PD_%d" % ti, "P2_%d" % (ti + 1))
            S.replay("PP_%d" % ti, "L%d" % (ti + 1), "P3_%d" % (ti + 1))
        else:
            S.replay("PD_%d" % ti, "PP_%d" % ti)
    assert not any(S.caps.values()), [k for k, v in S.caps.items() if v]
    if C.ntiles and DBG.get('p2b_stop', 99) > 8:
        Hout = Eb[0:64, :].rearrange("v (p x) -> v p x", p=8)
        nlast = len(tiles)
        for hb in range(2):
            bank = _nextbank(C)
            pb = C.pbank(bank)
            for i in range(4):
                p = hb * 4 + i
                S.op("pe", lambda e, pb=pb, i=i, p=p: e.transpose(out=pb[0:64, i * 128:(i + 1) * 128], in_=H32[:, p, :], identity=identf),
                     reads=["H32", "identf"], writes=["pb%d" % bank])
            S.op("dve", lambda e, pb=pb, hb=hb: e.tensor_copy(out=Hout[:, hb * 4:(hb + 1) * 4, :].rearrange("v p x -> v (p x)"),
                                                              in_=pb[0:64, :]), reads=["pb%d" % bank], writes=["Eb"])
        S.dma("sp", lambda e: e.dma_start(out=O["pw"].rearrange("h v k -> v h k"),
                                          in_=Hout.rearrange("v p (a k) -> v (p a) k", a=2)), reads=["Eb"], writes=["o_pw"], home="Eb")


def phase2c(C):
    C.rotbanks = (1, 2, 3, 4)
    S, alloc, I, O, T = C.S, C.alloc, C.I, C.O, C.T
    C.reset()
    if not DBG.get('sample', True):
        return
    B = {}
    _post_bufs(C, B)
    Sst = alloc([128, 2, 64, 64])
    tmp = alloc([128, 2, 64, 64])
    vec6 = alloc([128, 6, 8, 128])
    ysc = alloc([128, 8, 128])
    sa = alloc([128, 2, 64])
    yv = alloc([128, D])
    vbuf = alloc([128, D])
    sgbt = alloc([128, D])
    S.dma("sp", lambda e: e.dma_start(out=Sst.rearrange("p a v k -> p (a v k)"),
                                      in_=I["swkv"].rearrange("b (hh a) v k -> (b hh) (a v k)", a=2)), writes=["Sst0", "Sst1"])
    for q in range(6):
        for i in range(8):
            src = T["s6"][q].rearrange("(b i) (hh x) -> i b hh x", i=8, x=128)[i]
            S.dma("sp", lambda e, q=q, i=i, src=src: e.dma_start(out=vec6[:, q, i, :], in_=src),
                  reads=["d_s6"], writes=["vec6"])
    bk = lambda vec: vec.unsqueeze(1).to_broadcast([128, 64, 64])
    bv = lambda vec: vec.unsqueeze(2).to_broadcast([128, 64, 64])
    for i in range(8):
        ops = {0: [], 1: []}
        for a in range(2):
            eng = "dve" if a == 0 else "pool"
            Sv, Tv = Sst[:, a], tmp[:, a]
            sn, tn, san, yn = "Sst%d" % a, "tmp%d" % a, "sa%d" % a, "ysc%d" % a
            sl = slice(a * 64, a * 64 + 64)
            r_, w_, k_, v_, a_, b_ = [vec6[:, q, i, sl] for q in range(6)]
            sav = sa[:, a, :]
            L = ops[a]
            L.append((eng, lambda e, Sv=Sv, Tv=Tv, a_=a_: e.tensor_tensor(out=Tv, in0=Sv, in1=bk(a_), op=ALU.mult), [sn, "vec6"], [tn]))
            L.append(("dve", lambda e, Tv=Tv, sav=sav: e.tensor_reduce(out=sav, in_=Tv, axis=AX.X, op=ALU.add), [tn], [san]))
            L.append((eng, lambda e, Sv=Sv, w_=w_: e.tensor_tensor(out=Sv, in0=Sv, in1=bk(w_), op=ALU.mult), [sn, "vec6"], [sn]))
            L.append((eng, lambda e, Tv=Tv, sav=sav, b_=b_: e.tensor_tensor(out=Tv, in0=bv(sav), in1=bk(b_), op=ALU.mult), [san, "vec6"], [tn]))
            L.append((eng, lambda e, Sv=Sv, Tv=Tv: e.tensor_tensor(out=Sv, in0=Sv, in1=Tv, op=ALU.add), [sn, tn], [sn]))
            L.append((eng, lambda e, Tv=Tv, v_=v_, k_=k_: e.tensor_tensor(out=Tv, in0=bv(v_), in1=bk(k_), op=ALU.mult), ["vec6"], [tn]))
            L.append((eng, lambda e, Sv=Sv, Tv=Tv: e.tensor_tensor(out=Sv, in0=Sv, in1=Tv, op=ALU.add), [sn, tn], [sn]))
            L.append((eng, lambda e, Sv=Sv, Tv=Tv, r_=r_: e.tensor_tensor(out=Tv, in0=Sv, in1=bk(r_), op=ALU.mult), [sn, "vec6"], [tn]))
            L.append(("dve", lambda e, Tv=Tv, i=i, sl=sl: e.tensor_reduce(out=ysc[:, i, sl], in_=Tv, axis=AX.X, op=ALU.add), [tn], [yn]))
        for kk_ in range(9):
            for a in (1, 0):
                eng, fn, rd, wr = ops[a][kk_]
                S.op(eng, fn, reads=rd, writes=wr)
    S.dma("sp", lambda e: e.dma_start(out=O["sw"].rearrange("b (hh a) v k -> (b hh) (a v k)", a=2),
                                      in_=Sst.rearrange("p a v k -> p (a v k)")), reads=["Sst0", "Sst1"], writes=["o_sw"], home="Sst0")
    for i in range(8):
        dst = T["sy"].rearrange("(b i) (hh x) -> i b hh x", i=8, x=128)[i]
        S.dma("sp", lambda e, i=i, dst=dst: e.dma_start(out=dst, in_=ysc[:, i, :]), reads=["ysc0", "ysc1"], writes=["d_sy"], home="ysc0")
    S.dma("sp", lambda e: e.dma_start(out=yv, in_=T["sy"]), reads=["d_sy"], writes=["yv"])
    S.dma("sp", lambda e: e.dma_start(out=vbuf, in_=T["z"][NT][:, 2048:3072]), reads=["d_z"], writes=["vbuf"])
    S.dma("sp", lambda e: e.dma_start(out=sgbt, in_=T["sgb"][NT]), reads=["d_sgb"], writes=["sgbt"])
    sbon = B["s16"][:, 48:64]
    S.dma("sp", lambda e: e.dma_start(out=sbon, in_=T["sextra"][:, 0:16]), reads=["d_sx"], writes=["sbon"])
    _rwkv_post(C, B, yv, "yv", vbuf, "vbuf", sbon, sgbt, "sgbt", NT)


def _scan_setup(C):
    S, alloc, I, O, T = C.S, C.alloc, C.I, C.O, C.T
    Sst = alloc([128, 2, 64, 64])
    tmp = alloc([128, 64, 64])
    vecs = [alloc([128, 6, 128]), alloc([128, 6, 128])]
    ysc = alloc([128, 8, 128])
    sa = alloc([128, 64])
    ops = []
    ops.append(lambda: S.dma("sp", lambda e: e.dma_start(out=Sst.rearrange("p a v k -> p (a v k)"),
                                                         in_=I["swkv"].rearrange("b (hh a) v k -> (b hh) (a v k)", a=2)),
                             writes=["Sst"]))
    bk = lambda vec: vec.unsqueeze(1).to_broadcast([128, 64, 64])
    bv = lambda vec: vec.unsqueeze(2).to_broadcast([128, 64, 64])

    def ldv(i):
        def f():
            for q in range(6):
                src = T["s6"][q].rearrange("(b i) (hh x) -> i b hh x", i=8, x=128)[i]
                S.dma("sp", lambda e, q=q, src=src: e.dma_start(out=vecs[i % 2][:, q, :], in_=src),
                      reads=["d_s6"], writes=["vec%d" % (i % 2)])
        return f
    ops.append(ldv(0))
    for i in range(8):
        if i + 1 < 8:
            ops.append(ldv(i + 1))
        vn = "vec%d" % (i % 2)
        for a in range(2):
            Sv = Sst[:, a]
            sl = slice(a * 64, a * 64 + 64)
            r_, w_, k_, v_, a_, b_ = [vecs[i % 2][:, q, sl] for q in range(6)]
            L = [
                (lambda e, Sv=Sv, a_=a_: e.tensor_tensor(out=tmp, in0=Sv, in1=bk(a_), op=ALU.mult), ["Sst", vn], ["stmp"]),
                (lambda e: e.tensor_reduce(out=sa, in_=tmp, axis=AX.X, op=ALU.add), ["stmp"], ["ssa"]),
                (lambda e, Sv=Sv, w_=w_: e.tensor_tensor(out=Sv, in0=Sv, in1=bk(w_), op=ALU.mult), ["Sst", vn], ["Sst"]),
                (lambda e, b_=b_: e.tensor_tensor(out=tmp, in0=bv(sa), in1=bk(b_), op=ALU.mult), ["ssa", vn], ["stmp"]),
                (lambda e, Sv=Sv: e.tensor_tensor(out=Sv, in0=Sv, in1=tmp, op=ALU.add), ["Sst", "stmp"], ["Sst"]),
                (lambda e, v_=v_, k_=k_: e.tensor_tensor(out=tmp, in0=bv(v_), in1=bk(k_), op=ALU.mult), [vn], ["stmp"]),
                (lambda e, Sv=Sv: e.tensor_tensor(out=Sv, in0=Sv, in1=tmp, op=ALU.add), ["Sst", "stmp"], ["Sst"]),
                (lambda e, Sv=Sv, r_=r_: e.tensor_tensor(out=tmp, in0=Sv, in1=bk(r_), op=ALU.mult), ["Sst", vn], ["stmp"]),
                (lambda e, i=i, sl=sl: e.tensor_reduce(out=ysc[:, i, sl], in_=tmp, axis=AX.X, op=ALU.add), ["stmp"], ["ysc"]),
            ]
            for fn, rd, wr in L:
                ops.append(lambda fn=fn, rd=rd, wr=wr: S.op("dve", fn, reads=rd, writes=wr))

    def fin():
        S.dma("sp", lambda e: e.dma_start(out=O["sw"].rearrange("b (hh a) v k -> (b hh) (a v k)", a=2),
                                          in_=Sst.rearrange("p a v k -> p (a v k)")), reads=["Sst"], writes=["o_sw"], home="Sst")
        for i in range(8):
            dst = T["sy"].rearrange("(b i) (hh x) -> i b hh x", i=8, x=128)[i]
            S.dma("sp", lambda e, i=i, dst=dst: e.dma_start(out=dst, in_=ysc[:, i, :]), reads=["ysc"], writes=["d_sy"], home="ysc")
    ops.append(fin)
    return ops


def phase3a(C):
    phase3(C, tiles=list(range(C.ntiles)), scan=DBG.get('sample', True))


def phase3b(C):
    if DBG.get('sample', True):
        phase3(C, tiles=[NT], scan=False, reuse=("p3a" in C.phases and DBG.get('reuse', True)))


def phase2d(C):
    S, alloc, I, O, T = C.S, C.alloc, C.I, C.O, C.T
    C.rotbanks = (1, 2, 3, 4)
    C.reset()
    if "p3a" in C.phases and DBG.get('reuse', True):
        C.apos = (getattr(C, "p3_persist", 19152) + 63) // 64 * 64
    if not DBG.get('sample', True):
        return
    B = {}
    _post_bufs(C, B)
    yv = alloc([128, D])
    vbuf = alloc([128, D])
    sgbt = alloc([128, D])
    S.dma("sp", lambda e: e.dma_start(out=yv, in_=T["sy"]), reads=["d_sy"], writes=["yv"])
    S.dma("sp", lambda e: e.dma_start(out=vbuf, in_=T["z"][NT][:, 2048:3072]), reads=["d_z"], writes=["vbuf"])
    S.dma("sp", lambda e: e.dma_start(out=sgbt, in_=T["sgb"][NT]), reads=["d_sgb"], writes=["sgbt"])
    sbon = B["s16"][:, 48:64]
    S.dma("sp", lambda e: e.dma_start(out=sbon, in_=T["sextra"][:, 0:16]), reads=["d_sx"], writes=["sbon"])
    _rwkv_post(C, B, yv, "yv", vbuf, "vbuf", sbon, sgbt, "sgbt", NT)


PHASES = {"p1": phase1, "p2a": phase2a, "p2b": phase2b, "p2c": phase2c, "p3": phase3,
          "p3a": phase3a, "p2d": phase2d, "p3b": phase3b}


def _shard_inputs(inp):
    g = lambda k: np.ascontiguousarray(np.asarray(inp[k], dtype=np.float32))
    oh = _t5_bucket_onehot()
    shared = {
        "rel_bias": g("rel_bias"), "onehot": oh, "norm_g": g("norm_g")[0], "w_in": g("w_in")[0],
        "sinks": g("attn_sinks")[0], "mu": g("shift_mu")[0], "w0": g("rwkv_w0")[0], "w2": g("rwkv_w2")[0],
        "a0": g("rwkv_a0")[0], "a2": g("rwkv_a2")[0], "k_k": g("rwkv_k_k")[0], "k_a": g("rwkv_k_a")[0],
        "r_k": g("rwkv_r_k")[0].reshape(-1), "lnx_g": g("lnx_g")[0], "lnx_b": g("lnx_b")[0],
        "w_out_a": g("w_out_a")[0], "w_out_b": g("w_out_b")[0], "w_o": g("w_o")[0], "final_g": g("final_g"),
    }
    xp, xs = g("x_prompt"), g("x_sample")
    ck, cv = g("cache_k_win")[0], g("cache_v_win")[0]
    sw, ssh = g("state_wkv")[0], g("state_shift")[0]
    maps = []
    for c in range(NCORES):
        m = dict(shared)
        b0 = 16 * c
        m["xp"] = xp[c]
        m["xs"] = xs[b0:b0 + 16].reshape(128, D)
        m["ck"] = ck[b0:b0 + 16].reshape(16, 128, 256)
        m["cv"] = cv[b0:b0 + 16].reshape(16, 128, 256)
        m["swkv"] = sw[b0:b0 + 16]
        m["sshift"] = ssh[b0:b0 + 16]
        maps.append(m)
    return maps


_NC_CACHE = {}


def kernel(**inputs):
    if "nc" not in _NC_CACHE:
        _NC_CACHE["nc"] = build()
    nc = _NC_CACHE["nc"]
    maps = _shard_inputs(inputs)
    res = run_bass_kernel_spmd(nc, maps, core_ids=list(range(NCORES)))
    R = res.results
    cat = lambda k: np.stack([np.asarray(r[k]) for r in R])
    y_prompt = cat("yp").reshape(8, 4096, D)
    y_sample = cat("ys").reshape(128, 8, D)
    pk = cat("pk").reshape(1, 8, 128, 4, 64)
    pv = cat("pv").reshape(1, 8, 128, 4, 64)
    pw = cat("pw").reshape(1, 8, 16, 64, 64)
    psh = cat("psh").reshape(1, 8, 3200)
    sk = cat("sk").reshape(1, 128, 128, 4, 64)
    sv = cat("sv").reshape(1, 128, 128, 4, 64)
    sw = cat("sw").reshape(1, 128, 16, 64, 64)
    ssh = cat("ssh").reshape(1, 128, 3200)
    return (y_prompt, y_sample, pk, pv, pw, psh, sk, sv, sw, ssh)
```

```python
import math
from contextlib import ExitStack

import numpy as np
import concourse.bass as bass
import concourse.mybir as mybir
from concourse.bass_utils import run_bass_kernel_spmd

F32 = mybir.dt.float32
BF16 = mybir.dt.bfloat16
ALU = mybir.AluOpType
AF = mybir.ActivationFunctionType
AX = mybir.AxisListType

NCORES = 8
D = 1024
NT = 32
NEG = -30000.0
CDEC = math.exp(-0.5)
ARENA = 45056


class _Buf:
    __slots__ = ("last_write", "readers", "dsem")

    def __init__(self):
        self.last_write = None
        self.readers = {}
        self.dsem = None


class Sched:
    ENGS = ("pe", "act", "dve", "pool", "sp")

    def __init__(self):
        self.q = {e: [] for e in self.ENGS}
        self.cnt = {e: 0 for e in self.ENGS}
        self.seen = {e: {} for e in self.ENGS}
        self.dma_sems = {}
        self.bufs = {}
        self.cap = None
        self.caps = {}

    def buf(self, name):
        b = self.bufs.get(name)
        if b is None:
            b = self.bufs[name] = _Buf()
        return b

    def _deps(self, reads, writes):
        deps = {}

        def add(tok):
            if tok is not None and deps.get(tok[0], 0) < tok[1]:
                deps[tok[0]] = tok[1]
        for r in reads:
            add(self.buf(r).last_write)
        for w in writes:
            b = self.buf(w)
            add(b.last_write)
            for k, v in b.readers.items():
                add((k, v))
        return deps

    def _waits(self, e, deps):
        for k, v in deps.items():
            if k[0] == "dma":
                v = self.dma_sems[k]
            if k == ("eng", "pe") and e == "pe":
                continue
            if DBG.get("nosame") and k == ("eng", e):
                continue
            if self.seen[e].get(k, 0) >= v:
                continue
            self.seen[e][k] = v
            self.q[e].append(("wait", k, v))

    def _post(self, tok, reads, writes):
        for w in writes:
            b = self.buf(w)
            b.last_write = tok
            b.readers = {}
        for r in reads:
            if r not in writes:
                self.buf(r).readers[tok[0]] = tok[1]

    @staticmethod
    def _excl(reads, writes):
        pr = [r for r in reads if r.startswith("pb")]
        if pr:
            writes = list(writes) + [r for r in pr if r not in writes]
        return reads, writes

    def mark(self, name):
        if name is None:
            self.cap = None
        else:
            self.cap = self.caps.setdefault(name, [])

    def _emit_item(self, it):
        if it[0] == "op":
            self.op(*it[1:])
        else:
            self.dma(*it[1:-1], home=it[-1])

    def replay(self, *names):
        cap, self.cap = self.cap, None
        for n in names:
            for it in self.caps.pop(n, []):
                self._emit_item(it)
        self.cap = cap

    def interleave(self, na, nb):
        cap, self.cap = self.cap, None
        a, b = self.caps.pop(na, []), self.caps.pop(nb, [])
        for i in range(max(len(a), len(b))):
            if i < len(a):
                self._emit_item(a[i])
            if i < len(b):
                self._emit_item(b[i])
        self.cap = cap

    def op(self, e, fn, reads=(), writes=()):
        if self.cap is not None:
            self.cap.append(("op", e, fn, tuple(reads), tuple(writes)))
            return
        reads, writes = self._excl(reads, writes)
        self._waits(e, self._deps(reads, writes))
        self.cnt[e] += 1
        tok = (("eng", e), self.cnt[e])
        self.q[e].append(("ins", fn, tok))
        self._post(tok, reads, writes)

    def dma(self, e, fn, reads=(), writes=(), home=None):
        if self.cap is not None:
            self.cap.append(("dma", e, fn, tuple(reads), tuple(writes), home))
            return
        self._waits(e, self._deps(reads, writes))
        hb = self.buf(home if home is not None else (list(writes) + list(reads))[0])
        if hb.dsem is None:
            hb.dsem = ("dma", len(self.dma_sems))
            self.dma_sems[hb.dsem] = 0
        self.dma_sems[hb.dsem] += 16
        tok = (hb.dsem, self.dma_sems[hb.dsem])
        self.q[e].append(("dma", fn, tok))
        self._post(tok, reads, writes)

    def barrier(self):
        deps = {("eng", en): self.cnt[en] for en in self.ENGS if self.cnt[en]}
        for k, v in self.dma_sems.items():
            deps[k] = v
        for e in self.ENGS:
            for k, v in deps.items():
                if k == ("eng", e) or self.seen[e].get(k, 0) >= v:
                    continue
                self.seen[e][k] = v
                self.q[e].append(("wait", k, v))
        for e in self.ENGS:
            if e == "sp":
                continue
            self.cnt[e] += 1
            self.q[e].append(("ins", lambda eng: eng.nop(), (("eng", e), self.cnt[e])))
        deps = {("eng", en): self.cnt[en] for en in self.ENGS if self.cnt[en]}
        for e in self.ENGS:
            for k, v in deps.items():
                if k == ("eng", e) or self.seen[e].get(k, 0) >= v:
                    continue
                self.seen[e][k] = v
                self.q[e].append(("wait", k, v))

    def emit(self, nc, stack):
        semh = {}
        for e in self.ENGS:
            semh[("eng", e)] = stack.enter_context(nc.semaphore("s_" + e))
        for k in self.dma_sems:
            semh[k] = stack.enter_context(nc.semaphore("d_%d" % k[1]))
        block = stack.enter_context(nc.Block())
        q = self.q

        def run(eng, items):
            for it in items:
                if it[0] == "wait":
                    eng.wait_ge(semh[it[1]], it[2])
                elif it[0] == "ins":
                    it[1](eng).then_inc(semh[it[2][0]], 1)
                else:
                    it[1](eng).then_inc(semh[it[2][0]], 16)

        @block.tensor
        def _(eng):
            run(eng, q["pe"])

        @block.scalar
        def _(eng):
            run(eng, q["act"])

        @block.vector
        def _(eng):
            run(eng, q["dve"])

        @block.gpsimd
        def _(eng):
            run(eng, q["pool"])

        @block.sync
        def _(eng):
            run(eng, q["sp"])


def _t5_bucket_onehot():
    d = np.arange(129)
    dd = np.maximum(d, 1).astype(np.float32)
    large = 16 + (np.log(dd / np.float32(16)) / np.float32(math.log(128 / 16)) * np.float32(16)).astype(np.int32)
    large = np.minimum(large, 31)
    bkt = np.where(d < 16, d, large)
    e = np.zeros((32, 129), np.float32)
    e[bkt, d] = 1.0
    return e


class _NullSched:
    def op(self, *a, **k):
        pass

    def dma(self, *a, **k):
        pass


class Ctx:
    pass


DBG = {}


def build(phases=("p1", "p2a", "p2b", "p3a", "p2d", "p3b"), debug=False, ntiles=NT):
    nc = bass.Bass("TRN2", target_bir_lowering=False)
    S = Sched()
    C = Ctx()
    C.nc, C.S, C.debug, C.ntiles = nc, S, debug, ntiles
    C.phases = tuple(phases)

    def din(name, shape):
        return nc.dram_tensor(name, list(shape), F32, kind="ExternalInput").ap()

    def dout(name, shape):
        return nc.dram_tensor(name, list(shape), F32, kind="ExternalOutput").ap()

    def dscr(name, shape, dt=F32):
        if debug and dt == F32:
            return nc.dram_tensor(name, list(shape), dt, kind="ExternalOutput").ap()
        return nc.dram_tensor(name, list(shape), dt).ap()

    I = C.I = {}
    for name, shape in [
        ("xp", (4096, D)), ("xs", (128, D)), ("ck", (16, 128, 256)), ("cv", (16, 128, 256)),
        ("swkv", (16, 16, 64, 64)), ("sshift", (16, 3200)), ("rel_bias", (32, 16)),
        ("onehot", (32, 129)), ("norm_g", (D,)), ("w_in", (D, 8832)), ("sinks", (16,)),
        ("mu", (3200,)), ("w0", (D,)), ("w2", (64, D)), ("a0", (D,)), ("a2", (64, D)),
        ("k_k", (D,)), ("k_a", (D,)), ("r_k", (D,)), ("lnx_g", (D,)), ("lnx_b", (D,)),
        ("w_out_a", (D, D)), ("w_out_b", (D, D)), ("w_o", (D, D)), ("final_g", (D,)),
    ]:
        I[name] = din(name, shape)
    O = C.O = {}
    for name, shape in [
        ("yp", (4096, D)), ("ys", (128, D)), ("pk", (128, 256)), ("pv", (128, 256)),
        ("pw", (16, 64, 64)), ("psh", (3200,)), ("sk", (16, 128, 256)), ("sv", (16, 128, 256)),
        ("sw", (16, 16, 64, 64)), ("ssh", (16, 3200)),
    ]:
        O[name] = dout(name, shape)
    T = C.T = {}
    T["extd"] = dscr("extd", (16, 128, 384))
    T["ya"] = dscr("ya", (NT + 1, 128, D))
    T["yb"] = dscr("yb", (NT + 1, 128, D))
    T["z"] = dscr("z", (NT + 1, 128, 3200))
    T["sgb"] = dscr("sgb", (NT + 1, 128, D))
    T["carry"] = dscr("carry", (NT + 1, 3200))
    T["s6"] = dscr("s6", (6, 128, D))
    T["sy"] = dscr("sy", (128, D))
    T["sextra"] = dscr("sextra", (128, 2 * D + 16))
    T["wbf_in"] = dscr("wbf_in", (D, 6272), BF16)
    T["wbf_ob"] = dscr("wbf_ob", (D, D), BF16)
    T["wbf_o"] = dscr("wbf_o", (D, D), BF16)
    if debug:
        T["dMTp"] = dscr("dMTp", (128, 16, 128))
        T["dMTc"] = dscr("dMTc", (128, 16, 128))
        T["dMTs"] = dscr("dMTs", (128, 16, 128))
        T["dt1"] = dscr("dt1", (NT + 1, 128, D))
        T["dy"] = dscr("dy", (NT + 1, 128, D))

    with ExitStack() as st:
        arena = st.enter_context(nc.sbuf_tensor("arena", [128, ARENA], F32))
        psum = st.enter_context(nc.psum_tensor("psum", [128, 8, 512], F32))
        C.arena, C.psum = arena, psum
        C.apos = 0

        def alloc(shape, dt=F32):
            n = 1
            for s_ in shape[1:]:
                n *= s_
            words = n if dt == F32 else (n + 1) // 2
            words = (words + 7) // 8 * 8
            off = C.apos
            C.apos += words
            assert C.apos <= ARENA, ("SBUF arena overflow", C.apos)
            v = arena[0:shape[0], off:off + words]
            if dt != F32:
                v = v.bitcast(dt)
            v = v[:, 0:n]
            if len(shape) == 2:
                return v
            names = " ".join("a%d" % i for i in range(len(shape) - 1))
            kw = {"a%d" % i: shape[i + 1] for i in range(len(shape) - 1)}
            return v.rearrange("p (%s) -> p %s" % (names, names), **kw)
        C.alloc = alloc

        def reset():
            C.apos = 0
        C.reset = reset

        def pbank(i, dt=F32):
            v = psum[:, i, :]
            if dt != F32:
                v = v.bitcast(dt)
            return v
        C.pbank = pbank
        C.rot = 0

        for ph in phases:
            PHASES[ph](C)
            S.barrier()
        S.emit(nc, st)
    return nc


def _common_consts(C, init=True):
    S, alloc, I = C.S, C.alloc, C.I
    C.identf = alloc([128, 128])
    C.ident = alloc([128, 128], BF16)
    C.gbc = alloc([128, D])
    if not init:
        S = _NullSched()
    S.op("pool", lambda e: e.memset(C.identf, 0.0), writes=["identf"])
    S.op("pool", lambda e: e.affine_select(out=C.identf, in_=C.identf, pattern=[[-1, 128]],
                                           compare_op=ALU.not_equal, fill=1.0, base=0,
                                           channel_multiplier=1),
         reads=["identf"], writes=["identf"])
    S.op("dve", lambda e: e.tensor_copy(out=C.ident, in_=C.identf), reads=["identf"], writes=["ident"])
    S.dma("sp", lambda e: e.dma_start(out=C.gbc, in_=I["norm_g"].partition_broadcast(128)), writes=["gbc"])
    C.xt = [alloc([128, D]), alloc([128, D]), alloc([128, D])]
    C.junk = alloc([128, D], BF16)
    C.ss = alloc([128, 1])
    C.rstd = alloc([128, 1])
    C.xsb = alloc([128, D], BF16)
    C.hT = alloc([128, 8, 128], BF16)


def _x_src(C, t):
    return C.I["xs"] if t == NT else C.I["xp"][t * 128:(t + 1) * 128, :]


def _load_x(C, t, slot):
    C.S.dma("sp", lambda e: e.dma_start(out=C.xt[slot], in_=_x_src(C, t)), writes=["xt%d" % slot])


def _norm_pre(C, slot):
    S = C.S
    xt = C.xt[slot]
    xn = "xt%d" % slot
    S.op("pool", lambda e: e.memset(C.ss, 0.0), writes=["ss"])
    S.op("act", lambda e: e.activation(out=C.junk, in_=xt, func=AF.Square, accum_out=C.ss),
         reads=[xn, "ss"], writes=["junk", "ss"])
    S.op("dve", lambda e: e.tensor_scalar(out=C.rstd, in0=C.ss, scalar1=1.0 / D, scalar2=1e-6,
                                          op0=ALU.mult, op1=ALU.add), reads=["ss"], writes=["rstd"])
    S.op("act", lambda e: e.activation(out=C.rstd, in_=C.rstd, func=AF.Sqrt), reads=["rstd"], writes=["rstd"])
    S.op("dve", lambda e: e.reciprocal(out=C.rstd, in_=C.rstd), reads=["rstd"], writes=["rstd"])
    S.op("dve", lambda e: e.scalar_tensor_tensor(out=C.xsb, in0=xt, scalar=C.rstd[:, 0:1], in1=C.gbc,
                                                 op0=ALU.mult, op1=ALU.mult),
         reads=[xn, "rstd", "gbc"], writes=["xsb"])


def _norm_post(C):
    S = C.S
    pT = C.pbank(0, BF16)
    for k in range(8):
        S.op("pe", lambda e, k=k: e.transpose(out=pT[:, k * 128:(k + 1) * 128],
                                              in_=C.xsb[:, k * 128:(k + 1) * 128], identity=C.ident),
             reads=["xsb", "ident"], writes=["pb0"])
    S.op("act", lambda e: e.copy(out=C.hT.rearrange("p k t -> p (k t)"), in_=pT), reads=["pb0"], writes=["hT"])


def _norm_T(C, slot, nxt=None):
    _norm_post(C)
    if nxt is not None:
        _norm_pre(C, nxt)


def _pstride(ap, step, count):
    pat = [list(x) for x in ap.ap]
    pat[0] = [pat[0][0] * step, count]
    return bass.AP(ap.tensor, ap.offset, pat)


def _load_w(C, dst, dname, src_cols, eng="pool"):
    for k in range(8):
        C.S.dma(eng, lambda e, k=k: e.dma_start(out=dst[:, k, :], in_=src_cols[k * 128:(k + 1) * 128, :]),
                writes=[dname])


def _load_wbf(C, dst, dname, srcname, c0, n):
    src = C.T[srcname]
    for k in range(8):
        C.S.dma("sp" if k % 2 == 0 else "act",
                lambda e, k=k: e.dma_start(out=dst[:, k, :], in_=src[k * 128:(k + 1) * 128, c0:c0 + n]),
                reads=[srcname], writes=[dname])


def _nextbank(C):
    pool = getattr(C, 'rotbanks', (1, 2, 3, 4))
    b = pool[C.rot % len(pool)]
    C.rot += 1
    return b


def _proj_tm(C, W, wname, c0, ncols, bank):
    pb = C.pbank(bank)
    for k in range(8):
        C.S.op("pe", lambda e, k=k: e.matmul(pb[:, 0:ncols], lhsT=C.hT[:, k, :], rhs=W[:, k, c0:c0 + ncols],
                                              start=(k == 0), stop=(k == 7)),
               reads=["hT", wname], writes=["pb%d" % bank])
    return pb


def phase1(C):
    C.rotbanks = (1, 2, 3, 4)
    S, alloc, I, O, T, nc = C.S, C.alloc, C.I, C.O, C.T, C.nc
    C.reset()
    _common_consts(C)
    w_in = I["w_in"]
    Wqk = alloc([128, 8, 1280], BF16)
    Wkv = alloc([128, 8, 512], BF16)
    Wga = alloc([128, 8, 1024], BF16)
    Woa = alloc([128, 8, 1024], BF16)
    rb = alloc([32, 16])
    oh = alloc([32, 129])
    ext = alloc([16, 384])
    MTp = alloc([128, 16, 128])
    MTc = alloc([128, 16, 128])
    MTs = alloc([128, 16, 128])
    bsel = alloc([16, 128])
    bd = alloc([128, 128])
    esink = alloc([128, 16])
    S.dma("sp", lambda e: e.dma_start(out=rb, in_=I["rel_bias"]), writes=["rb"])
    S.dma("sp", lambda e: e.dma_start(out=oh, in_=I["onehot"]), writes=["oh"])
    S.dma("sp", lambda e: e.dma_start(out=esink, in_=I["sinks"].partition_broadcast(128)), writes=["esink"])
    S.op("act", lambda e: e.activation(out=esink, in_=esink, func=AF.Exp), reads=["esink"], writes=["esink"])
    pb = C.pbank(1)
    S.op("pe", lambda e: e.matmul(pb[0:16, 0:129], lhsT=rb, rhs=oh, start=True, stop=True),
         reads=["rb", "oh"], writes=["pb1"])
    S.op("pool", lambda e: e.memset(ext, NEG), writes=["ext"])
    S.op("dve", lambda e: e.tensor_copy(out=ext[:, 127:256], in_=pb[0:16, 0:129]), reads=["pb1", "ext"], writes=["ext"])
    S.dma("sp", lambda e: e.dma_start(out=T["extd"], in_=ext.unsqueeze(1).to_broadcast([16, 128, 384])),
          reads=["ext"], writes=["extd"])
    S.dma("sp", lambda e: e.dma_start(out=MTc, in_=bass.AP(T["extd"].tensor, 127, [[383, 128], [49152, 16], [1, 128]])),
          reads=["extd"], writes=["MTc"])
    S.dma("sp", lambda e: e.dma_start(out=MTp, in_=bass.AP(T["extd"].tensor, 255, [[383, 128], [49152, 16], [1, 128]])),
          reads=["extd"], writes=["MTp"])
    S.op("pool", lambda e: e.memset(bsel, 1.0), writes=["bsel"])
    S.op("pool", lambda e: e.affine_select(out=bsel, in_=bsel, pattern=[[1, 128]], compare_op=ALU.is_ge,
                                           fill=0.0, base=0, channel_multiplier=-8), reads=["bsel"], writes=["bsel"])
    S.op("pool", lambda e: e.affine_select(out=bsel, in_=bsel, pattern=[[-1, 128]], compare_op=ALU.is_ge,
                                           fill=0.0, base=7, channel_multiplier=8), reads=["bsel"], writes=["bsel"])
    pb2 = C.pbank(2)
    S.op("pe", lambda e: e.matmul(pb2[:, 0:128], lhsT=bsel, rhs=bsel, start=True, stop=True),
         reads=["bsel"], writes=["pb2"])
    S.op("dve", lambda e: e.tensor_scalar(out=bd, in0=pb2[:, 0:128], scalar1=-1.0, scalar2=-NEG,
                                          op0=ALU.add, op1=ALU.mult), reads=["pb2"], writes=["bd"])
    S.op("dve", lambda e: e.tensor_tensor(out=MTs, in0=MTc, in1=bd.unsqueeze(1).to_broadcast([128, 16, 128]),
                                          op=ALU.add), reads=["MTc", "bd"], writes=["MTs"])

    if C.debug:
        S.dma("sp", lambda e: e.dma_start(out=T["dMTp"], in_=MTp), reads=["MTp"], writes=["dMTp"])
        S.dma("sp", lambda e: e.dma_start(out=T["dMTc"], in_=MTc), reads=["MTc"], writes=["dMTc"])
        S.dma("sp", lambda e: e.dma_start(out=T["dMTs"], in_=MTs), reads=["MTs"], writes=["dMTs"])
    for k in range(8):
        for pair in range(2):
            for half in range(2):
                c0 = pair * 512 + half * 256
                src = w_in[k * 128:(k + 1) * 128, c0:c0 + 256].rearrange("p (g d) -> p g d", g=4)
                dst = Wqk[:, k, pair * 512:(pair + 1) * 512].rearrange(
                    "p (g half d) -> p g half d", g=4, half=2)[:, :, half, :]
                S.dma("pool", lambda e, src=src, dst=dst: e.dma_start(out=dst, in_=src), writes=["Wqk"])
        S.dma("pool", lambda e, k=k: e.dma_start(out=Wqk[:, k, 1024:1280],
                                                 in_=w_in[k * 128:(k + 1) * 128, 1024:1280]), writes=["Wqk"])
    _load_w(C, Wkv, "Wkv", w_in[:, 1024:1536])
    _load_w(C, Wga, "Wga", w_in[:, 1536:2560])
    _load_w(C, Woa, "Woa", I["w_out_a"])
    pre_list = []
    if DBG.get("precast", True):
        for k in range(8):
            rows = slice(k * 128, (k + 1) * 128)
            pre_list.append(lambda rows=rows: S.dma("pool", lambda e: e.dma_start(out=T["wbf_in"][rows, :], in_=w_in[rows, 2560:8832]), writes=["wbf_in"]))
        for k in range(8):
            rows = slice(k * 128, (k + 1) * 128)
            pre_list.append(lambda rows=rows: S.dma("pool", lambda e: e.dma_start(out=T["wbf_ob"][rows, :], in_=I["w_out_b"][rows, :]), writes=["wbf_ob"]))
            pre_list.append(lambda rows=rows: S.dma("pool", lambda e: e.dma_start(out=T["wbf_o"][rows, :], in_=I["w_o"][rows, :]), writes=["wbf_o"]))

    def precast(n):
        for _ in range(n):
            if pre_list:
                pre_list.pop(0)()

    qT = alloc([128, 8, 128], BF16)
    kT = [alloc([128, 2, 128], BF16), alloc([128, 2, 128], BF16)]
    va = [alloc([128, 4, 65], BF16), alloc([128, 4, 65], BF16)]
    kvo = alloc([128, 512])
    sga = alloc([128, D])
    stb = [alloc([128, 512]) for _ in range(2)]
    pTb = [alloc([128, 4, 128], BF16) for _ in range(4)]
    den = alloc([128, 16])
    t1 = alloc([128, 16, 64])
    goa = alloc([128, D], BF16)
    goaT = alloc([128, 8, 128], BF16)
    yat = [alloc([128, D]), alloc([128, D])]
    ckb = [alloc([128, 256], BF16) for _ in range(2)]
    vac = [alloc([128, 4, 65], BF16) for _ in range(2)]
    kTc = [alloc([128, 2, 128], BF16) for _ in range(2)]
    Zb = [alloc([128, 16, 128], BF16) for _ in range(2)]
    stc = [alloc([128, 16, 8]) for _ in range(2)]
    qTs = alloc([128, 16, 8, 8], BF16)
    zl = alloc([128, 128], BF16)
    zr = alloc([128, 512], BF16)
    S.op("pool", lambda e: e.memset(zl, 0.0), writes=["zl"])
    S.op("pool", lambda e: e.memset(zr, 0.0), writes=["zr"])
    for i in range(2):
        S.op("pool", lambda e, i=i: e.memset(va[i][:, :, 64:65], 1.0), writes=["va%d" % i])
        S.op("pool", lambda e, i=i: e.memset(vac[i][:, :, 64:65], 1.0), writes=["vac%d" % i])

    def oslot(h):
        bank = 5 + h // 7
        off = (h % 7) * 65
        return C.pbank(bank)[:, off:off + 65], "pb%d" % bank
    tiles = ([NT] if DBG.get('sample', True) else []) + list(range(C.ntiles))
    if not tiles:
        return
    if DBG.get('fake_sample'):
        tiles = [0]
    _load_x(C, tiles[0], 0)
    if len(tiles) > 1:
        _load_x(C, tiles[1], 1)
    _norm_pre(C, 0)
    cnt = {"st": 0, "pT": 0}
    for ti, t in enumerate(tiles):
        slot = ti % 2
        if ti + 2 < len(tiles):
            _load_x(C, tiles[ti + 2], (ti + 2) % 3)
        _norm_T(C, ti % 3, ((ti + 1) % 3) if ti + 1 < len(tiles) else None)
        sample = (t == NT) or bool(DBG.get('fake_sample'))
        cur = (t % 2) if not sample else 0
        prev = 1 - cur
        for chunks in ([0, 1, 2, 3], [4, 5, 6, 7], [8, 9]):
            bank = _nextbank(C)
            pb = C.pbank(bank)
            for ci, c in enumerate(chunks):
                for k in range(8):
                    S.op("pe", lambda e, k=k, ci=ci, c=c, pb=pb: e.matmul(
                        pb[:, ci * 128:(ci + 1) * 128], lhsT=Wqk[:, k, c * 128:(c + 1) * 128], rhs=C.hT[:, k, :],
                        start=(k == 0), stop=(k == 7)), reads=["hT", "Wqk"], writes=["pb%d" % bank])
            n = len(chunks) * 128
            if chunks[0] < 8:
                c0 = chunks[0]
                S.op("act", lambda e, pb=pb, c0=c0, n=n: e.copy(
                    out=qT[:, c0:c0 + 4, :].rearrange("p c t -> p (c t)"), in_=pb[:, 0:n]),
                    reads=["pb%d" % bank], writes=["qT"])
            else:
                S.op("act", lambda e, pb=pb, n=n, cur=cur: e.copy(out=kT[cur].rearrange("p c t -> p (c t)"), in_=pb[:, 0:n]),
                     reads=["pb%d" % bank], writes=["kT%d" % cur])
        bank = _nextbank(C)
        pb = _proj_tm(C, Wkv, "Wkv", 0, 512, bank)
        S.op("dve", lambda e, pb=pb, cur=cur: e.tensor_copy(out=va[cur][:, :, 0:64],
                                                   in_=pb[:, 256:512].rearrange("p (h d) -> p h d", h=4)),
             reads=["pb%d" % bank], writes=["va%d" % cur])
        if sample or t == NT - 1:
            S.op("dve", lambda e, pb=pb: e.tensor_copy(out=kvo, in_=pb), reads=["pb%d" % bank], writes=["kvo"])
            if sample and not DBG.get('s_out', True):
                pass
            elif sample:
                for b in range(16):
                    S.dma("sp", lambda e, b=b: e.dma_start(out=O["sk"][b, 120:128, :], in_=kvo[8 * b:8 * b + 8, 0:256]),
                          reads=["kvo"], writes=["o_sk"])
                    S.dma("sp", lambda e, b=b: e.dma_start(out=O["sv"][b, 120:128, :], in_=kvo[8 * b:8 * b + 8, 256:512]),
                          reads=["kvo"], writes=["o_sv"])
                S.dma("sp", lambda e: e.dma_start(out=O["sk"][:, 0:120, :], in_=I["ck"][:, 8:128, :]), writes=["o_sk2"])
                S.dma("sp", lambda e: e.dma_start(out=O["sv"][:, 0:120, :], in_=I["cv"][:, 8:128, :]), writes=["o_sv2"])
            else:
                S.dma("sp", lambda e: e.dma_start(out=O["pk"], in_=kvo[:, 0:256]), reads=["kvo"], writes=["o_pk"])
                S.dma("sp", lambda e: e.dma_start(out=O["pv"], in_=kvo[:, 256:512]), reads=["kvo"], writes=["o_pv"])
        for hf in range(2):
            bank = _nextbank(C)
            pb = _proj_tm(C, Wga, "Wga", hf * 512, 512, bank)
            S.op("act", lambda e, pb=pb, hf=hf: e.activation(out=sga[:, hf * 512:(hf + 1) * 512], in_=pb, func=AF.Silu),
                 reads=["pb%d" % bank], writes=["sga"])
        for bk, nh in enumerate((7, 7, 2)):
            S.op("pe", lambda e, bk=bk, nh=nh: e.matmul(C.pbank(5 + bk)[:, 0:nh * 65], lhsT=zl, rhs=zr[:, 0:nh * 65],
                                                        start=True, stop=False, skip_group_check=True),
                 reads=["zl", "zr"], writes=["pb%d" % (5 + bk)])
        blocks = [("cur", cur)] if (sample or t == 0) else [("prev", prev), ("cur", cur)]
        nlast = 1 if not sample else 17
        done = [0] * 16
        its = [(kvh, kind, sl) for kvh in range(4) for (kind, sl) in blocks]
        nblk = len(blocks) if not sample else 1 + DBG.get('s_nb', 16)

        def stage_a(kvh, kind, sl):
            pair, half = kvh // 2, kvh % 2
            rows = slice(half * 64, half * 64 + 64)
            bank = _nextbank(C)
            pb = C.pbank(bank)
            S.op("pe", lambda e, pb=pb, sl=sl, rows=rows, pair=pair: e.matmul(
                pb, lhsT=kT[sl][rows, pair, :], rhs=qT[rows, pair * 4:pair * 4 + 4, :], start=True, stop=True),
                reads=["kT%d" % sl, "qT"], writes=["pb%d" % bank])
            MT = MTs if sample else (MTp if kind == "prev" else MTc)
            mtn = "MTs" if sample else ("MTp" if kind == "prev" else "MTc")
            si = cnt["st"] % 2
            cnt["st"] += 1
            pi = cnt["pT"] % 4
            cnt["pT"] += 1
            S.op("dve", lambda e, pb=pb, si=si, MT=MT, kvh=kvh: e.scalar_tensor_tensor(
                out=stb[si], in0=pb, scalar=0.125, in1=MT[:, kvh * 4:kvh * 4 + 4, :].rearrange("p h q -> p (h q)"),
                op0=ALU.mult, op1=ALU.add), reads=["pb%d" % bank, mtn], writes=["st%d" % si])
            S.op("act", lambda e, si=si, pi=pi: e.activation(out=pTb[pi].rearrange("p g q -> p (g q)"),
                                                             in_=stb[si], func=AF.Exp),
                 reads=["st%d" % si], writes=["pT%d" % pi])
            return pi

        def stage_b(kvh, kind, sl, pi):
            for g in range(4):
                h = kvh * 4 + g
                osl, on = oslot(h)
                done[h] += 1
                S.op("pe", lambda e, osl=osl, pi=pi, g=g, sl=sl, kvh=kvh, last=(done[h] == nblk):
                     e.matmul(osl, lhsT=pTb[pi][:, g, :], rhs=va[sl][:, kvh, :], start=False, stop=last,
                              skip_group_check=True),
                     reads=["pT%d" % pi, "va%d" % sl], writes=[on])
        pis = {}
        for ii, it in enumerate(its):
            pis[ii] = stage_a(*it)
            if ii >= 1:
                stage_b(*its[ii - 1], pis[ii - 1])
        stage_b(*its[-1], pis[len(its) - 1])
        if sample:
            for b in range(DBG.get('s_nb', 16)):
                j = b % 2
                S.mark("SF_%d" % b)
                S.dma("pool", lambda e, b=b, j=j: e.dma_start(out=ckb[j], in_=I["ck"][b]), writes=["ckb%d" % j])
                S.dma("pool", lambda e, b=b, j=j: e.dma_start(out=vac[j][:, :, 0:64],
                                                              in_=I["cv"][b].rearrange("p (h d) -> p h d", h=4)),
                      writes=["vac%d" % j])
                pT0 = C.pbank(0, BF16)
                for pr in range(2):
                    S.op("pe", lambda e, pr=pr, j=j: e.transpose(out=pT0[:, pr * 128:(pr + 1) * 128],
                                                                  in_=ckb[j][:, pr * 128:(pr + 1) * 128], identity=C.ident),
                         reads=["ckb%d" % j, "ident"], writes=["pb0"])
                S.op("act", lambda e, j=j: e.copy(out=kTc[j].rearrange("p c t -> p (c t)"), in_=pT0[:, 0:256]),
                     reads=["pb0"], writes=["kTc%d" % j])
                lvl = DBG.get('s_lvl', 4)
                if lvl < 2:
                    continue
                for kvh in range(4):
                    pair, half = kvh // 2, kvh % 2
                    rows = slice(half * 64, half * 64 + 64)
                    bank = _nextbank(C)
                    pb = C.pbank(bank)
                    S.op("pe", lambda e, pb=pb, rows=rows, pair=pair, j=j: e.matmul(
                        pb, lhsT=kTc[j][rows, pair, :], rhs=qT[rows, pair * 4:pair * 4 + 4, :],
                        start=True, stop=True), reads=["kTc%d" % j, "qT"], writes=["pb%d" % bank])
                    S.op("dve", lambda e, pb=pb, j=j, kvh=kvh, b=b: e.scalar_tensor_tensor(
                        out=stc[j][:, kvh * 4:kvh * 4 + 4, :],
                        in0=pb.rearrange("p (g q) -> p g q", g=4)[:, :, 8 * b:8 * b + 8], scalar=0.125,
                        in1=MTp[:, kvh * 4:kvh * 4 + 4, 0:8], op0=ALU.mult, op1=ALU.add),
                        reads=["pb%d" % bank, "MTp"], writes=["stc%d" % j])
                if lvl < 3:
                    continue
                S.op("pool", lambda e, j=j: e.memset(Zb[j], 0.0), writes=["Zb%d" % j])
                S.op("act", lambda e, j=j, b=b: e.activation(out=Zb[j][:, :, 8 * b:8 * b + 8], in_=stc[j], func=AF.Exp),
                     reads=["stc%d" % j, "Zb%d" % j], writes=["Zb%d" % j])
                if lvl < 4:
                    continue
                S.mark("SB_%d" % b)
                for h in range(16):
                    osl, on = oslot(h)
                    done[h] += 1
                    S.op("pe", lambda e, osl=osl, h=h, j=j, last=(done[h] == 1 + DBG.get('s_nb', 16)): e.matmul(
                        osl, lhsT=Zb[j][:, h, :], rhs=vac[j][:, h // 4, :], start=False, stop=last,
                        skip_group_check=True),
                        reads=["Zb%d" % j, "vac%d" % j], writes=[on])
        if sample:
            S.mark(None)
            nb_ = DBG.get('s_nb', 16)
            for b in range(nb_):
                S.replay("SF_%d" % b)
                if b >= 1:
                    S.replay("SB_%d" % (b - 1))
            if nb_:
                S.replay("SB_%d" % (nb_ - 1))
        for bk, (h0, nh) in enumerate([(0, 7), (7, 7), (14, 2)]):
            ob = C.pbank(5 + bk)[:, 0:nh * 65].rearrange("p (h e) -> p h e", e=65)
            S.op("dve", lambda e, ob=ob, h0=h0, nh=nh: e.tensor_tensor(
                out=den[:, h0:h0 + nh].unsqueeze(2), in0=ob[:, :, 64:65], in1=esink[:, h0:h0 + nh].unsqueeze(2), op=ALU.add),
                reads=["pb%d" % (5 + bk), "esink"], writes=["den"])
        S.op("dve", lambda e: e.reciprocal(out=den, in_=den), reads=["den"], writes=["den"])
        for bk, (h0, nh) in enumerate([(0, 7), (7, 7), (14, 2)]):
            ob = C.pbank(5 + bk)[:, 0:nh * 65].rearrange("p (h e) -> p h e", e=65)
            S.op("dve", lambda e, ob=ob, h0=h0, nh=nh: e.tensor_tensor(
                out=t1[:, h0:h0 + nh, :], in0=ob[:, :, 0:64],
                in1=den[:, h0:h0 + nh].unsqueeze(2).to_broadcast([128, nh, 64]), op=ALU.mult),
                reads=["pb%d" % (5 + bk), "den"], writes=["t1"])
        if C.debug:
            S.dma("sp", lambda e, t=t: e.dma_start(out=T["dt1"][t], in_=t1.rearrange("p h d -> p (h d)")), reads=["t1"], writes=["d_dt1"], home="t1")
        S.op("dve", lambda e: e.tensor_tensor(out=goa, in0=t1.rearrange("p h d -> p (h d)"), in1=sga, op=ALU.mult),
             reads=["t1", "sga"], writes=["goa"])
        pT0 = C.pbank(0, BF16)
        for k in range(8):
            S.op("pe", lambda e, k=k: e.transpose(out=pT0[:, k * 128:(k + 1) * 128], in_=goa[:, k * 128:(k + 1) * 128],
                                                  identity=C.ident), reads=["goa", "ident"], writes=["pb0"])
        S.op("act", lambda e: e.copy(out=goaT.rearrange("p k t -> p (k t)"), in_=pT0), reads=["pb0"], writes=["goaT"])
        yslot = ti % 2
        for hf in range(2):
            bank = _nextbank(C)
            pb = C.pbank(bank)
            for k in range(8):
                S.op("pe", lambda e, k=k, pb=pb, hf=hf: e.matmul(pb, lhsT=goaT[:, k, :], rhs=Woa[:, k, hf * 512:(hf + 1) * 512],
                                                                  start=(k == 0), stop=(k == 7)),
                     reads=["goaT", "Woa"], writes=["pb%d" % bank])
            S.op("act", lambda e, pb=pb, hf=hf, yslot=yslot: e.copy(out=yat[yslot][:, hf * 512:(hf + 1) * 512], in_=pb),
                 reads=["pb%d" % bank], writes=["yat%d" % yslot])
        S.dma("sp", lambda e, t=t, yslot=yslot: e.dma_start(out=T["ya"][t], in_=yat[yslot]), reads=["yat%d" % yslot], writes=["d_ya"], home="yat%d" % yslot)
        precast(1 if len(tiles) > 26 else 24)


    precast(99)


def phase2a(C):
    C.rotbanks = (1, 2, 3, 4)
    S, alloc, I, O, T = C.S, C.alloc, C.I, C.O, C.T
    C.reset()
    _common_consts(C)
    w_in = I["w_in"]
    Wps = alloc([128, 8, 3200], BF16)
    Wgb = alloc([128, 8, 1024], BF16)
    if DBG.get("precast", True) and "p1" in C.phases:
        _load_wbf(C, Wps, "Wps", "wbf_in", 0, 3200)
        _load_wbf(C, Wgb, "Wgb", "wbf_in", 3200, 1024)
    else:
        for k in range(8):
            for c0 in range(0, 3200, 640):
                S.dma("pool", lambda e, k=k, c0=c0: e.dma_start(out=Wps[:, k, c0:c0 + 640],
                                                                in_=w_in[k * 128:(k + 1) * 128, 2560 + c0:2560 + c0 + 640]),
                      writes=["Wps"])
        _load_w(C, Wgb, "Wgb", w_in[:, 5760:6784])
    mubc = alloc([128, 3200])
    S.dma("sp", lambda e: e.dma_start(out=mubc, in_=I["mu"].partition_broadcast(128)), writes=["mubc"])
    psb = [alloc([128, 3200]), alloc([128, 3200])]
    zb = [alloc([128, 3200]), alloc([128, 3200])]
    sgb = [alloc([128, D]), alloc([128, D])]
    sst = alloc([16, 3200])
    S.dma("sp", lambda e: e.dma_start(out=sst, in_=I["sshift"]), writes=["sst"])

    def sel(name, shape, pattern, cm, base):
        m = alloc(shape)
        S.op("pool", lambda e: e.memset(m, 0.0), writes=[name])
        S.op("pool", lambda e: e.affine_select(out=m, in_=m, pattern=pattern, compare_op=ALU.not_equal, fill=1.0,
                                               base=base, channel_multiplier=cm), reads=[name], writes=[name])
        return m
    ShI = sel("ShI", [128, 128], [[1, 128]], -1, -1)
    ShsI = sel("ShsI", [128, 128], [[1, 128]], -1, -1)
    S.op("pool", lambda e: e.memset(ShsI.rearrange("p (b i) -> p b i", i=8)[:, :, 0:1], 0.0), reads=["ShsI"], writes=["ShsI"])
    for m, n in ((ShI, "ShI"), (ShsI, "ShsI")):
        S.op("dve", lambda e, m=m: e.tensor_tensor(out=m, in0=m, in1=C.identf, op=ALU.subtract), reads=[n, "identf"], writes=[n])
    Ecar = alloc([128, 128])
    S.op("pool", lambda e: e.memset(Ecar, 0.0), writes=["Ecar"])
    S.op("pool", lambda e: e.memset(Ecar[:, 0:1], 1.0), reads=["Ecar"], writes=["Ecar"])
    S.op("pool", lambda e: e.affine_select(out=Ecar[:, 0:1], in_=Ecar[:, 0:1], pattern=[[0, 1]], compare_op=ALU.is_ge, fill=0.0,
                                           base=-127, channel_multiplier=1), reads=["Ecar"], writes=["Ecar"])
    Esel = sel("Esel", [16, 128], [[1, 128]], -8, 0)

    tiles = list(range(C.ntiles)) + ([NT] if DBG.get('sample', True) else [])
    if not tiles:
        return
    _load_x(C, tiles[0], 0)
    if len(tiles) > 1:
        _load_x(C, tiles[1], 1)
    _norm_pre(C, 0)
    groups = [(c0, 512) for c0 in range(0, 3072, 512)] + [(3072, 128)]
    for ti, t in enumerate(tiles):
        slot = ti % 2
        if ti + 2 < len(tiles):
            _load_x(C, tiles[ti + 2], (ti + 2) % 3)
        _norm_T(C, ti % 3, ((ti + 1) % 3) if ti + 1 < len(tiles) else None)
        sample = (t == NT)
        ps, psn = psb[ti % 2], "ps%d" % (ti % 2)
        pp, ppn = psb[(ti + 1) % 2], "ps%d" % ((ti + 1) % 2)
        zt, ztn = zb[ti % 2], "zb%d" % (ti % 2)
        for (c0, n) in groups:
            bank = _nextbank(C)
            pb = _proj_tm(C, Wps, "Wps", c0, n, bank)
            S.op("act", lambda e, pb=pb, c0=c0, n=n, ps=ps: e.copy(out=ps[:, c0:c0 + n], in_=pb[:, 0:n]),
                 reads=["pb%d" % bank], writes=[psn])
        for hf in range(2):
            bank = _nextbank(C)
            pb = _proj_tm(C, Wgb, "Wgb", hf * 512, 512, bank)
            S.op("act", lambda e, pb=pb, hf=hf, slot=slot: e.activation(out=sgb[slot][:, hf * 512:(hf + 1) * 512],
                                                                         in_=pb, func=AF.Silu),
                 reads=["pb%d" % bank], writes=["sgb%d" % slot])
        S.dma("sp", lambda e, t=t, slot=slot: e.dma_start(out=T["sgb"][t], in_=sgb[slot]),
              reads=["sgb%d" % slot], writes=["d_sgb"], home="sgb%d" % slot)
        if sample:
            S.dma("sp", lambda e, ps=ps: e.dma_start(out=O["ssh"], in_=_pstride(ps[7:8, :], 8, 16)),
                  reads=[psn], writes=["o_ssh"], home=psn)
        elif t == NT - 1:
            S.dma("sp", lambda e, ps=ps: e.dma_start(out=O["psh"].rearrange("(o c) -> o c", o=1), in_=ps[127:128, :]),
                  reads=[psn], writes=["o_psh"], home=psn)
        carry = (not sample) and ti > 0
        for (c0, n) in groups:
            bank = _nextbank(C)
            pb = C.pbank(bank)
            last1 = not (carry or sample)
            S.op("pe", lambda e, pb=pb, c0=c0, n=n, ps=ps, m=(ShsI if sample else ShI), last1=last1: e.matmul(
                pb[:, 0:n], lhsT=m, rhs=ps[:, c0:c0 + n], start=True, stop=last1),
                reads=["ShsI" if sample else "ShI", psn], writes=["pb%d" % bank])
            if carry:
                S.op("pe", lambda e, pb=pb, c0=c0, n=n, pp=pp: e.matmul(pb[:, 0:n], lhsT=Ecar, rhs=pp[:, c0:c0 + n],
                                                                         start=False, stop=True),
                     reads=["Ecar", ppn], writes=["pb%d" % bank])
            if sample:
                S.op("pe", lambda e, pb=pb, c0=c0, n=n: e.matmul(pb[:, 0:n], lhsT=Esel, rhs=sst[:, c0:c0 + n],
                                                                  start=False, stop=True),
                     reads=["Esel", "sst"], writes=["pb%d" % bank])
            S.op("dve", lambda e, pb=pb, c0=c0, n=n, zt=zt: e.tensor_tensor(out=zt[:, c0:c0 + n], in0=pb[:, 0:n],
                                                                            in1=mubc[:, c0:c0 + n], op=ALU.mult),
                 reads=["pb%d" % bank, "mubc"], writes=[ztn])
        S.op("dve", lambda e, zt=zt, ps=ps: e.tensor_tensor(out=zt, in0=zt, in1=ps, op=ALU.add), reads=[ztn, psn], writes=[ztn])
        S.dma("sp", lambda e, t=t, zt=zt: e.dma_start(out=T["z"][t], in_=zt), reads=[ztn], writes=["d_z"], home=ztn)


def phase3(C, tiles=None, scan=False, reuse=False):
    C.rotbanks = (1, 2, 3, 4)
    S, alloc, I, O, T = C.S, C.alloc, C.I, C.O, C.T
    C.reset()
    _common_consts(C, init=not reuse)
    w_in = I["w_in"]
    Wm = alloc([128, 8, 2048], BF16)
    Wo = alloc([128, 8, 1024], BF16)
    if reuse:
        pass
    elif DBG.get("precast", True) and "p1" in C.phases:
        _load_wbf(C, Wm, "Wm", "wbf_in", 4224, 2048)
        _load_wbf(C, Wo, "Wo", "wbf_o", 0, 1024)
    else:
        for k in range(8):
            for c0 in range(0, 2048, 512):
                S.dma("pool", lambda e, k=k, c0=c0: e.dma_start(out=Wm[:, k, c0:c0 + 512],
                                                                in_=w_in[k * 128:(k + 1) * 128, 6784 + c0:6784 + c0 + 512]),
                      writes=["Wm"])
        _load_w(C, Wo, "Wo", I["w_o"])
    fgbc = alloc([128, D])
    if not reuse:
        S.dma("sp", lambda e: e.dma_start(out=fgbc, in_=I["final_g"].partition_broadcast(128)), writes=["fgbc"])
    C.p3_persist = C.apos
    sm = alloc([128, 2048])
    yab = [alloc([128, D]), alloc([128, D])]
    ybb = [alloc([128, D]), alloc([128, D])]
    mg = alloc([128, D], BF16)
    mgT = alloc([128, 8, 128], BF16)
    res = alloc([128, D])
    yo = [alloc([128, D]), alloc([128, D])]
    ss2 = alloc([128, 1])
    rs2 = alloc([128, 1])

    if tiles is None:
        tiles = list(range(C.ntiles)) + ([NT] if DBG.get('sample', True) else [])
    scan_ops = _scan_setup(C) if scan else []
    per_tile = (len(scan_ops) + max(len(tiles), 1) - 1) // max(len(tiles), 1)
    if not tiles:
        for fn in scan_ops:
            fn()
        return

    def emit_scan(n):
        for _ in range(max(n, 0)):
            if scan_ops:
                scan_ops.pop(0)()

    def loads(ti):
        t = tiles[ti]
        sl = ti % 2
        S.dma("sp", lambda e: e.dma_start(out=yab[sl], in_=T["ya"][t]), reads=["d_ya"], writes=["yab%d" % sl])
        S.dma("sp", lambda e: e.dma_start(out=ybb[sl], in_=T["yb"][t]), reads=["d_yb"], writes=["ybb%d" % sl])
    _load_x(C, tiles[0], 0)
    if len(tiles) > 1:
        _load_x(C, tiles[1], 1)
    loads(0)
    _norm_pre(C, 0)
    for ti, t in enumerate(tiles):
        slot = ti % 2
        if ti + 2 < len(tiles):
            _load_x(C, tiles[ti + 2], (ti + 2) % 3)
        if ti + 1 < len(tiles):
            loads(ti + 1)
        _norm_T(C, ti % 3, ((ti + 1) % 3) if ti + 1 < len(tiles) else None)
        emit_scan(2)
        for gi in range(4):
            bank = _nextbank(C)
            pb = _proj_tm(C, Wm, "Wm", gi * 512, 512, bank)
            S.op("act", lambda e, pb=pb, gi=gi: e.activation(out=sm[:, gi * 512:(gi + 1) * 512], in_=pb, func=AF.Sigmoid),
                 reads=["pb%d" % bank], writes=["sm"])
        ya, yb = yab[slot], ybb[slot]
        S.op("dve", lambda e, ya=ya: e.tensor_tensor(out=ya, in0=ya, in1=sm[:, 0:1024], op=ALU.mult),
             reads=["yab%d" % slot, "sm"], writes=["yab%d" % slot])
        S.op("pool", lambda e, yb=yb: e.tensor_tensor(out=yb, in0=yb, in1=sm[:, 1024:2048], op=ALU.mult),
             reads=["ybb%d" % slot, "sm"], writes=["ybb%d" % slot])
        S.op("dve", lambda e, ya=ya, yb=yb: e.tensor_tensor(out=mg, in0=ya, in1=yb, op=ALU.add),
             reads=["yab%d" % slot, "ybb%d" % slot], writes=["mg"])
        emit_scan(1)
        pT0 = C.pbank(0, BF16)
        for k in range(8):
            S.op("pe", lambda e, k=k: e.transpose(out=pT0[:, k * 128:(k + 1) * 128], in_=mg[:, k * 128:(k + 1) * 128],
                                                  identity=C.ident), reads=["mg", "ident"], writes=["pb0"])
        S.op("act", lambda e: e.copy(out=mgT.rearrange("p k t -> p (k t)"), in_=pT0), reads=["pb0"], writes=["mgT"])
        xt = C.xt[ti % 3]
        for hf in range(2):
            bank = _nextbank(C)
            pb = C.pbank(bank)
            for k in range(8):
                S.op("pe", lambda e, k=k, pb=pb, hf=hf: e.matmul(pb, lhsT=mgT[:, k, :], rhs=Wo[:, k, hf * 512:(hf + 1) * 512],
                                                                  start=(k == 0), stop=(k == 7)),
                     reads=["mgT", "Wo"], writes=["pb%d" % bank])
            S.op("dve", lambda e, pb=pb, hf=hf, xt=xt: e.tensor_tensor(out=res[:, hf * 512:(hf + 1) * 512], in0=pb,
                                                                       in1=xt[:, hf * 512:(hf + 1) * 512], op=ALU.add),
                 reads=["pb%d" % bank, "xt%d" % (ti % 3)], writes=["res"])
        emit_scan(1)
        S.op("pool", lambda e: e.memset(ss2, 0.0), writes=["ss2"])
        S.op("act", lambda e: e.activation(out=C.junk, in_=res, func=AF.Square, accum_out=ss2),
             reads=["res", "ss2"], writes=["junk", "ss2"])
        S.op("dve", lambda e: e.tensor_scalar(out=rs2, in0=ss2, scalar1=1.0 / D, scalar2=1e-6, op0=ALU.mult, op1=ALU.add),
             reads=["ss2"], writes=["rs2"])
        S.op("act", lambda e: e.activation(out=rs2, in_=rs2, func=AF.Sqrt), reads=["rs2"], writes=["rs2"])
        S.op("dve", lambda e: e.reciprocal(out=rs2, in_=rs2), reads=["rs2"], writes=["rs2"])
        yt = yo[slot]
        S.op("dve", lambda e, yt=yt: e.scalar_tensor_tensor(out=yt, in0=res, scalar=rs2[:, 0:1], in1=fgbc,
                                                            op0=ALU.mult, op1=ALU.mult),
             reads=["res", "rs2", "fgbc"], writes=["yo%d" % slot])
        dst = O["ys"] if t == NT else O["yp"][t * 128:(t + 1) * 128, :]
        S.dma("sp", lambda e, yt=yt, dst=dst: e.dma_start(out=dst, in_=yt), reads=["yo%d" % slot], writes=["o_y"],
              home="yo%d" % slot)
        emit_scan(per_tile - 4)
    while scan_ops:
        scan_ops.pop(0)()


def _rwkv_post(C, B, y, yn_, v, vn_, sbon, sgbt, sgn_, t, mark_pe=None, sbon_n="sbon"):
    S, T = C.S, C.T
    tD, tE, s16 = B["tD"], B["tE"], B["s16"]
    mean, var = s16[:, 0:16], s16[:, 16:32]
    v3 = lambda ap: ap.rearrange("p (h d) -> p h d", h=16)
    bc = lambda ap: ap.unsqueeze(2).to_broadcast([128, 16, 64])
    S.op("dve", lambda e: e.tensor_reduce(out=mean, in_=v3(y), axis=AX.X, op=ALU.add), reads=[yn_], writes=["s16m"])
    S.op("dve", lambda e: e.tensor_scalar(out=mean, in0=mean, scalar1=1.0 / 64, scalar2=None, op0=ALU.mult),
         reads=["s16m"], writes=["s16m"])
    S.op("dve", lambda e: e.tensor_tensor(out=v3(tD), in0=v3(y), in1=bc(mean), op=ALU.subtract),
         reads=[yn_, "s16m"], writes=[B["tDn"]])
    S.op("dve", lambda e: e.tensor_tensor(out=tE, in0=tD, in1=tD, op=ALU.mult), reads=[B["tDn"]], writes=[B["tEn"]])
    S.op("dve", lambda e: e.tensor_reduce(out=var, in_=v3(tE), axis=AX.X, op=ALU.add), reads=[B["tEn"]], writes=["s16v"])
    S.op("dve", lambda e: e.tensor_scalar(out=var, in0=var, scalar1=1.0 / 64, scalar2=64e-5, op0=ALU.mult, op1=ALU.add),
         reads=["s16v"], writes=["s16v"])
    S.op("act", lambda e: e.activation(out=var, in_=var, func=AF.Sqrt), reads=["s16v"], writes=["s16v"])
    S.op("dve", lambda e: e.reciprocal(out=var, in_=var), reads=["s16v"], writes=["s16v"])
    S.op("dve", lambda e: e.tensor_tensor(out=v3(tD), in0=v3(tD), in1=bc(var), op=ALU.mult), reads=[B["tDn"], "s16v"], writes=[B["tDn"]])
    S.op("dve", lambda e: e.tensor_tensor(out=tD, in0=tD, in1=B["lgbc"], op=ALU.mult), reads=[B["tDn"], "lgbc"], writes=[B["tDn"]])
    S.op("dve", lambda e: e.tensor_tensor(out=tD, in0=tD, in1=B["lbbc"], op=ALU.add), reads=[B["tDn"], "lbbc"], writes=[B["tDn"]])
    S.op("dve", lambda e: e.tensor_tensor(out=v3(tE), in0=v3(v), in1=bc(sbon), op=ALU.mult), reads=[vn_, sbon_n], writes=[B["tEn"]])
    S.op("dve", lambda e: e.tensor_tensor(out=tD, in0=tD, in1=tE, op=ALU.add), reads=[B["tDn"], B["tEn"]], writes=[B["tDn"]])
    S.op("dve", lambda e: e.tensor_tensor(out=B["ybg"], in0=tD, in1=sgbt, op=ALU.mult), reads=[B["tDn"], sgn_], writes=[B["ybgn"]])
    if mark_pe is not None:
        S.mark(mark_pe)
    pT0 = C.pbank(0, BF16)
    for k in range(8):
        S.op("pe", lambda e, k=k: e.transpose(out=pT0[:, k * 128:(k + 1) * 128], in_=B["ybg"][:, k * 128:(k + 1) * 128],
                                              identity=B["ident"]), reads=[B["ybgn"], "ident"], writes=["pb0"])
    S.op("act", lambda e: e.copy(out=B["ybT"].rearrange("p k t -> p (k t)"), in_=pT0), reads=["pb0"], writes=[B["ybTn"]])
    for hf in range(2):
        bank = _nextbank(C)
        pb = C.pbank(bank)
        for k in range(8):
            S.op("pe", lambda e, k=k, pb=pb, hf=hf: e.matmul(pb, lhsT=B["ybT"][:, k, :], rhs=B["WoB"][:, k, hf * 512:(hf + 1) * 512],
                                                              start=(k == 0), stop=(k == 7)),
                 reads=[B["ybTn"], "WoB"], writes=["pb%d" % bank])
        S.op("act", lambda e, pb=pb, hf=hf: e.copy(out=B["ybo"][:, hf * 512:(hf + 1) * 512], in_=pb),
             reads=["pb%d" % bank], writes=[B["ybon"]])
    S.dma("sp", lambda e, t=t: e.dma_start(out=T["yb"][t], in_=B["ybo"]), reads=[B["ybon"]], writes=["d_yb"], home=B["ybon"])


def _post_bufs(C, B):
    S, alloc, I = C.S, C.alloc, C.I
    B["identf"] = alloc([128, 128])
    B["ident"] = alloc([128, 128], BF16)
    S.op("pool", lambda e: e.memset(B["identf"], 0.0), writes=["identf"])
    S.op("pool", lambda e: e.affine_select(out=B["identf"], in_=B["identf"], pattern=[[-1, 128]], compare_op=ALU.not_equal,
                                           fill=1.0, base=0, channel_multiplier=1), reads=["identf"], writes=["identf"])
    S.op("dve", lambda e: e.tensor_copy(out=B["ident"], in_=B["identf"]), reads=["identf"], writes=["ident"])
    B["WoB"] = alloc([128, 8, 1024], BF16)
    if DBG.get("precast", True) and "p1" in C.phases:
        _load_wbf(C, B["WoB"], "WoB", "wbf_ob", 0, 1024)
    else:
        _load_w(C, B["WoB"], "WoB", I["w_out_b"])
    for nm, src in (("lgbc", "lnx_g"), ("lbbc", "lnx_b")):
        B[nm] = alloc([128, D])
        S.dma("sp", lambda e, nm=nm, src=src: e.dma_start(out=B[nm], in_=I[src].partition_broadcast(128)), writes=[nm])
    B["tD"] = alloc([128, D])
    B["tDn"] = "tD"
    if B.get("alloc_tE", True):
        B["tE"] = alloc([128, D])
        B["tEn"] = "tE"
    B["s16"] = alloc([128, 96])
    B["ybgn"], B["ybTn"], B["ybon"] = "ybg", "ybT", "ybo"
    if B.get("alloc_yb", True):
        B["ybg"] = alloc([128, D], BF16)
        B["ybT"] = alloc([128, 8, 128], BF16)
        B["ybo"] = alloc([128, D])


def phase2b(C):
    S, alloc, I, O, T = C.S, C.alloc, C.I, C.O, C.T
    C.reset()
    B = {"alloc_tE": False, "alloc_yb": False}
    _post_bufs(C, B)
    C.rotbanks = (1, 2, 3)
    identf, ident = B["identf"], B["ident"]
    tD, s16 = B["tD"], B["s16"]
    W2A2 = alloc([128, D], BF16)
    S.dma("pool", lambda e: e.dma_start(out=W2A2[0:64, :], in_=I["w2"]), writes=["W2A2"])
    S.dma("pool", lambda e: e.dma_start(out=W2A2[64:128, :], in_=I["a2"]), writes=["W2A2"])
    vecs = alloc([128, D])
    S.dma("sp", lambda e: e.dma_start(out=vecs[0:1, :], in_=I["w0"].rearrange("(o c) -> o c", o=1)), writes=["vecs"])
    S.dma("sp", lambda e: e.dma_start(out=vecs[32:33, :], in_=I["a0"].rearrange("(o c) -> o c", o=1)), writes=["vecs"])
    ones = alloc([128, 128])
    S.op("pool", lambda e: e.memset(ones, 1.0), writes=["ones"])
    negcol = alloc([128, 1])
    S.op("pool", lambda e: e.memset(negcol, -CDEC), writes=["negcol"])
    zl = alloc([128, 128], BF16)
    zr = alloc([128, 512], BF16)
    S.op("pool", lambda e: e.memset(zl, 0.0), writes=["zl"])
    S.op("pool", lambda e: e.memset(zr, 0.0), writes=["zr"])

    def tri(name, val, pattern, cm, base):
        m = alloc([128, 128])
        S.op("pool", lambda e: e.memset(m, val), writes=[name])
        S.op("pool", lambda e: e.affine_select(out=m, in_=m, pattern=pattern, compare_op=ALU.is_ge, fill=0.0,
                                               base=base, channel_multiplier=cm), reads=[name], writes=[name])
        return m
    Lincl = tri("Lincl", -CDEC, [[1, 128]], -1, 0)
    Lstr = tri("Lstr", -CDEC, [[1, 128]], -1, -1)
    Ustr = tri("Ustr", -CDEC, [[-1, 128]], 1, -1)
    MlowS = alloc([128, 512])
    S.op("pool", lambda e: e.memset(MlowS, 1.0), writes=["MlowS"])
    for blk in range(4):
        S.op("pool", lambda e, blk=blk: e.affine_select(out=MlowS[:, blk * 128:(blk + 1) * 128], in_=MlowS[:, blk * 128:(blk + 1) * 128],
                                                        pattern=[[-1, 128]], compare_op=ALU.is_ge, fill=0.0, base=-1, channel_multiplier=1),
             reads=["MlowS"], writes=["MlowS"])
    Mask4 = alloc([128, 512])
    S.op("pool", lambda e: e.memset(Mask4, 1.0), writes=["Mask4"])
    for blk in range(4):
        S.op("pool", lambda e, blk=blk: e.affine_select(out=Mask4[:, blk * 128:(blk + 1) * 128], in_=Mask4[:, blk * 128:(blk + 1) * 128],
                                                        pattern=[[1, 128]], compare_op=ALU.is_ge, fill=0.0,
                                                        base=(-1 if blk % 2 == 0 else 0), channel_multiplier=-1),
             reads=["Mask4"], writes=["Mask4"])
    bcs = {}
    for nm, src in (("kkbc", "k_k"), ("kabc", "k_a"), ("rkbc", "r_k")):
        bcs[nm] = alloc([128, D])
        S.dma("sp", lambda e, nm=nm, src=src: e.dma_start(out=bcs[nm], in_=I[src].partition_broadcast(128)), writes=[nm])
    ztb = [alloc([128, 3200]), alloc([128, 3200])]
    sgbt = alloc([128, D])
    sg = alloc([128, D])
    av = alloc([128, D])
    tA = alloc([128, D])
    tB = alloc([128, D])
    tC = alloc([128, D])
    Ea = alloc([128, D])
    B["ybo"], B["ybon"] = Ea, "Ea"
    Eb = alloc([128, D])
    B["tE"], B["tEn"] = Eb, "Eb"
    lT = alloc([128, 128], BF16)
    At, Rt, Bt, Kt, Bh, Kh, Vb = [alloc([128, D], BF16) for _ in range(7)]
    B["ybg"], B["ybgn"] = At, "At"
    ARt = alloc([128, 8, 2, 128], BF16)
    BtT = alloc([128, 8, 128], BF16)
    B["ybT"], B["ybTn"] = BtT, "BtT"
    KtT = alloc([128, 8, 128], BF16)
    WT = KtT
    Am = [alloc([128, 512], BF16) for _ in range(16)]
    Nb = [[alloc([128, 4, 128], BF16) for _ in range(2)] for _ in range(4)]
    Lb = [[alloc([128, 4, 128], BF16) for _ in range(2)] for _ in range(4)]
    L0 = [Lb[hg][1] for hg in range(4)]
    Gbf = [alloc([128, 4, 128], BF16) for _ in range(4)]
    Wall = Rt.rearrange("p (h d) -> p h d", h=16)
    Z32 = tA.rearrange("p (h d) -> p h d", h=16)
    Ub = Bt
    H32 = alloc([128, 8, 64])
    Hbf = [alloc([128, 8, 2, 64], BF16) for _ in range(2)]
    PC = alloc([128, 8])
    yv = Ea
    S.op("pool", lambda e: e.memset(H32, 0.0), writes=["H32"])
    S.op("pool", lambda e: e.memset(Hbf[0], 0.0), writes=["Hbf0"])
    S.op("pool", lambda e: e.memset(Hbf[1], 0.0), writes=["Hbf1"])
    ss16, rn16 = s16[:, 32:48], s16[:, 32:48]
    v3 = lambda ap: ap.rearrange("p (h d) -> p h d", h=16)
    bc = lambda ap: ap.unsqueeze(2).to_broadcast([128, 16, 64])

    tiles = ([NT] if DBG.get('sample', True) else []) + list(range(C.ntiles))
    def ldz(ti):
        S.dma("sp", lambda e, ti=ti: e.dma_start(out=ztb[ti % 2], in_=T["z"][tiles[ti]]), reads=["d_z"], writes=["zt%d" % (ti % 2)])
    if tiles:
        ldz(0)
    for ti, t in enumerate(tiles):
        sample = (t == NT)
        zt, ztn = ztb[ti % 2], "zt%d" % (ti % 2)
        sbon, sbn = s16[:, 48 + 16 * (ti % 2):64 + 16 * (ti % 2)], "sbon%d" % (ti % 2)
        S.mark("L%d" % ti)
        if ti + 1 < len(tiles):
            ldz(ti + 1)
        S.mark("P1_%d" % ti)
        r_, k_, v_, lor = zt[:, 0:1024], zt[:, 1024:2048], zt[:, 2048:3072], zt[:, 3072:3200]
        bank = _nextbank(C)
        pb = C.pbank(bank)
        S.op("pe", lambda e, r_=r_, k_=k_, v_=v_, lor=lor, pb=pb: e.transpose(out=pb[:, 0:128], in_=lor, identity=identf), reads=[ztn, "identf"], writes=["pb%d" % bank])
        S.op("act", lambda e, r_=r_, k_=k_, v_=v_, lor=lor, pb=pb: e.activation(out=lT[0:64, :], in_=pb[0:64, 0:128], func=AF.Tanh), reads=["pb%d" % bank], writes=["lT"])
        S.op("act", lambda e, r_=r_, k_=k_, v_=v_, lor=lor, pb=pb: e.copy(out=lT[64:128, :], in_=pb[64:128, 0:128]), reads=["pb%d" % bank], writes=["lT"])
        for (dst, dn, r0, v0) in ((sg, "sg", 0, 0), (av, "av", 64, 32)):
            for hf in range(2):
                bank = _nextbank(C)
                pb = C.pbank(bank)
                S.op("pe", lambda e, r_=r_, k_=k_, v_=v_, lor=lor, pb=pb, r0=r0, hf=hf: e.matmul(pb, lhsT=lT[r0:r0 + 64, :], rhs=W2A2[r0:r0 + 64, hf * 512:(hf + 1) * 512],
                                                                    start=True, stop=False), reads=["lT", "W2A2"], writes=["pb%d" % bank])
                S.op("pe", lambda e, r_=r_, k_=k_, v_=v_, lor=lor, pb=pb, v0=v0, hf=hf: e.matmul(pb, lhsT=ones[v0:v0 + 1, :], rhs=vecs[v0:v0 + 1, hf * 512:(hf + 1) * 512],
                                                                    start=False, stop=True), reads=["ones", "vecs"], writes=["pb%d" % bank])
                S.op("act", lambda e, r_=r_, k_=k_, v_=v_, lor=lor, pb=pb, dst=dst, hf=hf: e.activation(out=dst[:, hf * 512:(hf + 1) * 512], in_=pb, func=AF.Sigmoid),
                     reads=["pb%d" % bank], writes=[dn])
        if DBG.get('p2b_stop', 99) <= 1:
            continue
        S.mark("P2_%d" % ti)
        S.op("dve", lambda e, r_=r_, k_=k_, v_=v_, lor=lor: e.tensor_tensor(out=tA, in0=k_, in1=bcs["kkbc"], op=ALU.mult), reads=[ztn, "kkbc"], writes=["tA"])
        S.op("dve", lambda e, r_=r_, k_=k_, v_=v_, lor=lor: e.tensor_tensor(out=tB, in0=tA, in1=tA, op=ALU.mult), reads=["tA"], writes=["tB"])
        S.op("dve", lambda e, r_=r_, k_=k_, v_=v_, lor=lor: e.tensor_reduce(out=ss16, in_=v3(tB), axis=AX.X, op=ALU.add), reads=["tB"], writes=["s16n"])
        S.op("act", lambda e, r_=r_, k_=k_, v_=v_, lor=lor: e.activation(out=ss16, in_=ss16, func=AF.Sqrt), reads=["s16n"], writes=["s16n"])
        S.op("dve", lambda e, r_=r_, k_=k_, v_=v_, lor=lor: e.tensor_scalar(out=ss16, in0=ss16, scalar1=1e-12, scalar2=None, op0=ALU.max), reads=["s16n"], writes=["s16n"])
        S.op("dve", lambda e, r_=r_, k_=k_, v_=v_, lor=lor: e.reciprocal(out=ss16, in_=ss16), reads=["s16n"], writes=["s16n"])
        S.op("dve", lambda e, r_=r_, k_=k_, v_=v_, lor=lor: e.tensor_tensor(out=v3(tA), in0=v3(tA), in1=bc(rn16), op=ALU.mult), reads=["tA", "s16n"], writes=["tA"])
        S.op("dve", lambda e, r_=r_, k_=k_, v_=v_, lor=lor: e.scalar_tensor_tensor(out=tB, in0=av, scalar=-1.0, in1=bcs["kabc"], op0=ALU.add, op1=ALU.mult),
             reads=["av", "kabc"], writes=["tB"])
        S.op("dve", lambda e, r_=r_, k_=k_, v_=v_, lor=lor: e.scalar_tensor_tensor(out=tB, in0=tB, scalar=1.0, in1=k_, op0=ALU.add, op1=ALU.mult),
             reads=["tB", ztn], writes=["tB"])
        S.op("dve", lambda e, r_=r_, k_=k_, v_=v_, lor=lor: e.tensor_tensor(out=tC, in0=tA, in1=av, op=ALU.mult), reads=["tA", "av"], writes=["tC"])
        S.mark("P3_%d" % ti)
        S.op("dve", lambda e, r_=r_, k_=k_, v_=v_, lor=lor: e.tensor_tensor(out=tD, in0=r_, in1=tB, op=ALU.mult), reads=[ztn, "tB"], writes=["tD"])
        S.op("dve", lambda e, r_=r_, k_=k_, v_=v_, lor=lor: e.tensor_tensor(out=tD, in0=tD, in1=bcs["rkbc"], op=ALU.mult), reads=["tD", "rkbc"], writes=["tD"])
        S.op("dve", lambda e, r_=r_, k_=k_, v_=v_, lor=lor, sbon=sbon: e.tensor_reduce(out=sbon, in_=v3(tD), axis=AX.X, op=ALU.add), reads=["tD"], writes=[sbn])
        if DBG.get('p2b_stop', 99) <= 2:
            continue
        if sample:
            S.op("act", lambda e, r_=r_, k_=k_, v_=v_, lor=lor: e.activation(out=Ea, in_=sg, func=AF.Exp, scale=-CDEC), reads=["sg"], writes=["Ea"])
            S.op("dve", lambda e, r_=r_, k_=k_, v_=v_, lor=lor: e.tensor_scalar(out=tA, in0=tA, scalar1=-1.0, scalar2=None, op0=ALU.mult), reads=["tA"], writes=["tA"])
            for qi, (src, sn) in enumerate(((r_, ztn), (Ea, "Ea"), (tB, "tB"), (v_, ztn), (tA, "tA"), (tC, "tC"))):
                S.dma("sp", lambda e, r_=r_, k_=k_, v_=v_, lor=lor, qi=qi, src=src: e.dma_start(out=T["s6"][qi], in_=src), reads=[sn], writes=["d_s6"], home=sn)
            S.dma("sp", lambda e, r_=r_, k_=k_, v_=v_, lor=lor, sbon=sbon: e.dma_start(out=T["sextra"][:, 0:16], in_=sbon), reads=[sbn], writes=["d_sx"], home=sbn)
            continue
        def cums(Lm, ln, outs):
            for hf in range(2):
                bank = _nextbank(C)
                pb = C.pbank(bank)
                S.op("pe", lambda e, r_=r_, k_=k_, v_=v_, lor=lor, pb=pb, hf=hf: e.matmul(pb, lhsT=Lm, rhs=sg[:, hf * 512:(hf + 1) * 512], start=True, stop=True),
                     reads=[ln, "sg"], writes=["pb%d" % bank])
                for (dst, dn, sc) in outs:
                    S.op("act", lambda e, r_=r_, k_=k_, v_=v_, lor=lor, pb=pb, dst=dst, sc=sc, hf=hf: e.activation(out=dst[:, hf * 512:(hf + 1) * 512], in_=pb,
                                                                                       func=AF.Exp, scale=sc),
                         reads=["pb%d" % bank], writes=[dn])
        cums(Lincl, "Lincl", ((Ea, "Ea", 1.0), (Eb, "Eb", -1.0)))
        S.op("dve", lambda e, r_=r_, k_=k_, v_=v_, lor=lor: e.tensor_tensor(out=Rt, in0=r_, in1=Ea, op=ALU.mult), reads=[ztn, "Ea"], writes=["Rt"])
        S.op("pool", lambda e, r_=r_, k_=k_, v_=v_, lor=lor: e.tensor_tensor(out=Bt, in0=tC, in1=Eb, op=ALU.mult), reads=["tC", "Eb"], writes=["Bt"])
        S.op("dve", lambda e, r_=r_, k_=k_, v_=v_, lor=lor: e.tensor_tensor(out=Kt, in0=tB, in1=Eb, op=ALU.mult), reads=["tB", "Eb"], writes=["Kt"])
        cums(Lstr, "Lstr", ((Ea, "Ea", 1.0),))
        S.op("dve", lambda e, r_=r_, k_=k_, v_=v_, lor=lor: e.scalar_tensor_tensor(out=At, in0=tA, scalar=-1.0, in1=Ea, op0=ALU.mult, op1=ALU.mult),
             reads=["tA", "Ea"], writes=["At"])
        cums(Ustr, "Ustr", ((Eb, "Eb", 1.0),))
        S.op("pool", lambda e, r_=r_, k_=k_, v_=v_, lor=lor: e.tensor_tensor(out=Bh, in0=tC, in1=Eb, op=ALU.mult), reads=["tC", "Eb"], writes=["Bh"])
        S.op("dve", lambda e, r_=r_, k_=k_, v_=v_, lor=lor: e.tensor_tensor(out=Kh, in0=tB, in1=Eb, op=ALU.mult), reads=["tB", "Eb"], writes=["Kh"])
        S.op("act", lambda e, r_=r_, k_=k_, v_=v_, lor=lor: e.copy(out=Vb, in_=v_), reads=[ztn], writes=["Vb"])
        bank = _nextbank(C)
        pb = C.pbank(bank)
        for p in range(8):
            S.op("pe", lambda e, r_=r_, k_=k_, v_=v_, lor=lor, pb=pb, p=p: e.matmul(pb[:, p:p + 1], lhsT=sg[:, p * 128:(p + 1) * 128], rhs=negcol, start=True, stop=True),
                 reads=["sg", "negcol"], writes=["pb%d" % bank])
        S.op("act", lambda e, r_=r_, k_=k_, v_=v_, lor=lor, pb=pb: e.activation(out=PC, in_=pb[:, 0:8], func=AF.Exp), reads=["pb%d" % bank], writes=["PC"])
        if DBG.get('p2b_stop', 99) <= 3:
            continue
        S.mark("A_%d" % ti)
        pT0 = C.pbank(0, BF16)
        for qi, (src, sn, dst, dn) in enumerate(((At, "At", ARt[:, :, 0, :], "ARt"), (Rt, "Rt", ARt[:, :, 1, :], "ARt"),
                                                 (Bt, "Bt", BtT, "BtT"), (Kt, "Kt", KtT, "KtT"))):
            pTq = C.pbank(4 + qi, BF16)
            for k in range(8):
                S.op("pe", lambda e, r_=r_, k_=k_, v_=v_, lor=lor, k=k, src=src, pTq=pTq: e.transpose(
                    out=pTq[:, k * 128:(k + 1) * 128], in_=src[:, k * 128:(k + 1) * 128], identity=ident),
                    reads=[sn, "ident"], writes=["pb%d" % (4 + qi)])
            S.op("act", lambda e, r_=r_, k_=k_, v_=v_, lor=lor, dst=dst, pTq=pTq: e.copy(out=dst, in_=pTq.rearrange("p (k t) -> p k t", k=8)),
                 reads=["pb%d" % (4 + qi)], writes=[dn])
        if DBG.get('p2b_stop', 99) <= 4:
            continue
        C.rotbanks = (1, 2, 3, 4, 5, 6, 7)
        for h in range(16):
            p, rows = h // 2, slice((h % 2) * 64, (h % 2) * 64 + 64)
            bank = _nextbank(C)
            pb = C.pbank(bank)
            S.op("pe", lambda e, r_=r_, k_=k_, v_=v_, lor=lor, pb=pb, p=p, rows=rows: e.matmul(pb[:, 0:256], lhsT=BtT[rows, p, :],
                                                                  rhs=ARt[rows, p, :, :].rearrange("q a t -> q (a t)"), start=True, stop=True),
                 reads=["BtT", "ARt"], writes=["pb%d" % bank])
            S.op("pe", lambda e, r_=r_, k_=k_, v_=v_, lor=lor, pb=pb, p=p, rows=rows: e.matmul(pb[:, 256:512], lhsT=KtT[rows, p, :],
                                                                  rhs=ARt[rows, p, :, :].rearrange("q a t -> q (a t)"), start=True, stop=True),
                 reads=["KtT", "ARt"], writes=["pb%d" % bank])
            S.op("dve", lambda e, r_=r_, k_=k_, v_=v_, lor=lor, pb=pb, h=h: e.tensor_tensor(out=Am[h], in0=pb, in1=Mask4, op=ALU.mult),
                 reads=["pb%d" % bank, "Mask4"], writes=["Am%d" % h])
        C.rotbanks = (1, 2, 3)
        if DBG.get('p2b_stop', 99) == 45:
            continue
        for half in range(2):
            for i in range(8):
                h = half * 8 + i
                S.op("pe", lambda e, r_=r_, k_=k_, v_=v_, lor=lor, i=i, h=h: e.transpose(out=pT0[:, i * 128:(i + 1) * 128], in_=Am[h][:, 0:128], identity=ident),
                     reads=["Am%d" % h, "ident"], writes=["pb0"])
            for q in range(2):
                hg = half * 2 + q
                S.op("act", lambda e, r_=r_, k_=k_, v_=v_, lor=lor, hg=hg, q=q: e.copy(out=L0[hg].rearrange("p h s -> p (h s)"), in_=pT0[:, q * 512:(q + 1) * 512]),
                     reads=["pb0"], writes=["Lb%d_1" % hg])
        if DBG.get('p2b_stop', 99) <= 5:
            continue
        st_ = []
        for hg in range(4):
            pgb = 4 + hg
            PG = C.pbank(pgb)
            pgn = "pb%d" % pgb
            S.op("pe", lambda e, r_=r_, k_=k_, v_=v_, lor=lor, PG=PG: e.matmul(PG, lhsT=zl, rhs=zr, start=True, stop=False, skip_group_check=True),
                 reads=["zl", "zr"], writes=[pgn])
            for hh in range(4):
                h = hg * 4 + hh
                S.op("pe", lambda e, r_=r_, k_=k_, v_=v_, lor=lor, PG=PG, hh=hh, h=h: e.matmul(PG[:, hh * 128:hh * 128 + 64], lhsT=ident, rhs=At[:, h * 64:(h + 1) * 64],
                                                                  start=False, stop=False, skip_group_check=True),
                     reads=["ident", "At"], writes=[pgn])
                S.op("pe", lambda e, r_=r_, k_=k_, v_=v_, lor=lor, PG=PG, hh=hh, h=h: e.matmul(PG[:, hh * 128 + 64:(hh + 1) * 128], lhsT=Am[h][:, 256:384],
                                                                  rhs=Vb[:, h * 64:(h + 1) * 64], start=False, stop=False, skip_group_check=True),
                     reads=["Am%d" % h, "Vb"], writes=[pgn])
            S.op("dve", lambda e, r_=r_, k_=k_, v_=v_, lor=lor, PG=PG, hg=hg: e.tensor_copy(out=Gbf[hg].rearrange("p h s -> p (h s)"), in_=PG),
                 reads=[pgn], writes=["Gbf%d" % hg])
            st_.append(dict(PG=PG, pgn=pgn,
                            Ncur=[Am[hg * 4 + hh][:, 0:128] for hh in range(4)], Nn=["Am%d" % (hg * 4 + hh) for hh in range(4)],
                            Lcur=[L0[hg][:, hh, :] for hh in range(4)], Ln=["Lb%d_1" % hg] * 4))
        for j in range(7):
            for hg in range(4):
                q = st_[hg]
                PG, pgn, Ncur, Nn_, Lcur, Ln_ = q["PG"], q["pgn"], q["Ncur"], q["Nn"], q["Lcur"], q["Ln"]
                for hh in range(4):
                    S.op("pe", lambda e, r_=r_, k_=k_, v_=v_, lor=lor, PG=PG, hh=hh, nl=Ncur[hh], hg=hg, j=j: e.matmul(
                        PG[:, hh * 128:(hh + 1) * 128], lhsT=nl, rhs=Gbf[hg][:, hh, :], start=False, stop=(j == 6),
                        skip_group_check=True), reads=[Nn_[hh], "Gbf%d" % hg], writes=[pgn])
                if j < 6:
                    bank = _nextbank(C)
                    pb = C.pbank(bank)
                    for hh in range(4):
                        S.op("pe", lambda e, r_=r_, k_=k_, v_=v_, lor=lor, pb=pb, hh=hh, ll=Lcur[hh], nl=Ncur[hh]: e.matmul(
                            pb[:, hh * 128:(hh + 1) * 128], lhsT=ll, rhs=nl, start=True, stop=True),
                            reads=[Ln_[hh], Nn_[hh]], writes=["pb%d" % bank])
                    nb = Nb[hg][j % 2]
                    nbn = "Nb%d_%d" % (hg, j % 2)
                    S.op("act", lambda e, r_=r_, k_=k_, v_=v_, lor=lor, pb=pb, nb=nb: e.copy(out=nb.rearrange("p h s -> p (h s)"), in_=pb),
                         reads=["pb%d" % bank], writes=[nbn])
                    if j < 5:
                        bank = _nextbank(C)
                        pb = C.pbank(bank)
                        for hh in range(4):
                            S.op("pe", lambda e, r_=r_, k_=k_, v_=v_, lor=lor, pb=pb, hh=hh, ll=Lcur[hh], nl=Ncur[hh]: e.matmul(
                                pb[:, hh * 128:(hh + 1) * 128], lhsT=nl, rhs=ll, start=True, stop=True),
                                reads=[Ln_[hh], Nn_[hh]], writes=["pb%d" % bank])
                        lb = Lb[hg][j % 2]
                        lbn = "Lb%d_%d" % (hg, j % 2)
                        S.op("act", lambda e, r_=r_, k_=k_, v_=v_, lor=lor, pb=pb, lb=lb: e.copy(out=lb.rearrange("p h s -> p (h s)"), in_=pb),
                             reads=["pb%d" % bank], writes=[lbn])
                        q["Lcur"] = [lb[:, hh, :] for hh in range(4)]
                        q["Ln"] = [lbn] * 4
                    S.op("dve", lambda e, r_=r_, k_=k_, v_=v_, lor=lor, PG=PG, hg=hg: e.tensor_copy(out=Gbf[hg].rearrange("p h s -> p (h s)"), in_=PG),
                         reads=[pgn], writes=["Gbf%d" % hg])
                    q["Ncur"] = [nb[:, hh, :] for hh in range(4)]
                    q["Nn"] = [nbn] * 4
        for hg in range(4):
            PG, pgn = st_[hg]["PG"], st_[hg]["pgn"]
            PGv = PG.rearrange("p (h s) -> p h s", h=4)
            S.op("act", lambda e, r_=r_, k_=k_, v_=v_, lor=lor, PGv=PGv, hg=hg: e.copy(out=Wall[:, hg * 4:(hg + 1) * 4, :], in_=PGv[:, :, 0:64]),
                 reads=[pgn], writes=["Rt"])
            S.op("dve", lambda e, r_=r_, k_=k_, v_=v_, lor=lor, PGv=PGv, hg=hg: e.tensor_copy(out=Z32[:, hg * 4:(hg + 1) * 4, :], in_=PGv[:, :, 64:128]),
                 reads=[pgn], writes=["tA"])
        if DBG.get('p2b_stop', 99) <= 6:
            continue
        WallF = Wall.rearrange("p h d -> p (h d)")
        for k in range(8):
            S.op("pe", lambda e, r_=r_, k_=k_, v_=v_, lor=lor, k=k: e.transpose(out=pT0[:, k * 128:(k + 1) * 128], in_=WallF[:, k * 128:(k + 1) * 128], identity=ident),
                 reads=["Rt", "ident"], writes=["pb0"])
        S.op("act", lambda e, r_=r_, k_=k_, v_=v_, lor=lor: e.copy(out=WT.rearrange("p k t -> p (k t)"), in_=pT0), reads=["pb0"], writes=["KtT"])
        ho, hn = Hbf[ti % 2], Hbf[(ti + 1) % 2]
        hon, hnn = "Hbf%d" % (ti % 2), "Hbf%d" % ((ti + 1) % 2)
        Z32F = Z32.rearrange("p h d -> p (h d)")
        for hb in range(2):
            bank = _nextbank(C)
            pb = C.pbank(bank)
            for i in range(4):
                p = hb * 4 + i
                S.op("pe", lambda e, r_=r_, k_=k_, v_=v_, lor=lor, pb=pb, i=i, p=p, ho=ho: e.matmul(pb[:, i * 128:(i + 1) * 128], lhsT=WT[:, p, :],
                                                                       rhs=ho[:, p, :, :].rearrange("q a d -> q (a d)"), start=True, stop=True),
                     reads=["KtT", hon], writes=["pb%d" % bank])
            S.op("dve", lambda e, r_=r_, k_=k_, v_=v_, lor=lor, pb=pb, hb=hb: e.tensor_tensor(out=Ub[:, hb * 512:(hb + 1) * 512], in0=pb,
                                                                in1=Z32F[:, hb * 512:(hb + 1) * 512], op=ALU.add),
                 reads=["pb%d" % bank, "tA"], writes=["Bt"])
        if DBG.get('p2b_stop', 99) == 71:
            continue
        S.op("dve", lambda e, r_=r_, k_=k_, v_=v_, lor=lor: e.tensor_tensor(out=H32, in0=H32, in1=PC.unsqueeze(2).to_broadcast([128, 8, 64]), op=ALU.mult),
             reads=["H32", "PC"], writes=["H32"])
        for hb in range(2):
            bank = _nextbank(C)
            pb = C.pbank(bank)
            for i in range(8):
                h = hb * 8 + i
                p = h // 2
                S.op("pe", lambda e, r_=r_, k_=k_, v_=v_, lor=lor, pb=pb, i=i, p=p, h=h: e.matmul(pb[:, i * 64:(i + 1) * 64], lhsT=Bh[:, p * 128:(p + 1) * 128],
                                                                     rhs=Ub[:, h * 64:(h + 1) * 64], start=True, stop=False),
                     reads=["Bh", "Bt"], writes=["pb%d" % bank])
                S.op("pe", lambda e, r_=r_, k_=k_, v_=v_, lor=lor, pb=pb, i=i, p=p, h=h: e.matmul(pb[:, i * 64:(i + 1) * 64], lhsT=Kh[:, p * 128:(p + 1) * 128],
                                                                     rhs=Vb[:, h * 64:(h + 1) * 64], start=False, stop=True),
                     reads=["Kh", "Vb"], writes=["pb%d" % bank])
            pbv = pb.rearrange("q (p a d) -> q p a d", p=4, a=2)
            for hf in range(2):
                S.op("dve", lambda e, r_=r_, k_=k_, v_=v_, lor=lor, pbv=pbv, hf=hf, hb=hb: e.tensor_tensor(
                    out=H32[hf * 64:(hf + 1) * 64, hb * 4:(hb + 1) * 4, :], in0=H32[hf * 64:(hf + 1) * 64, hb * 4:(hb + 1) * 4, :],
                    in1=pbv[hf * 64:(hf + 1) * 64, :, hf, :], op=ALU.add), reads=["pb%d" % bank, "H32"], writes=["H32"])
        if DBG.get('p2b_stop', 99) == 72:
            continue
        for hf in range(2):
            S.op("act", lambda e, r_=r_, k_=k_, v_=v_, lor=lor, hn=hn, hf=hf: e.copy(out=hn[hf * 64:(hf + 1) * 64, :, hf, :], in_=H32[hf * 64:(hf + 1) * 64, :, :]),
                 reads=["H32"], writes=[hnn])
        for hb in range(2):
            bank = _nextbank(C)
            pb = C.pbank(bank)
            for i in range(4):
                p = hb * 4 + i
                S.op("pe", lambda e, r_=r_, k_=k_, v_=v_, lor=lor, pb=pb, i=i, p=p, ho=ho: e.matmul(pb[:, i * 128:(i + 1) * 128], lhsT=ARt[:, p, 1, :],
                                                                       rhs=ho[:, p, :, :].rearrange("q a d -> q (a d)"), start=True, stop=False),
                     reads=["ARt", hon], writes=["pb%d" % bank])
                for hf in range(2):
                    h = 2 * p + hf
                    c0 = i * 128 + hf * 64
                    S.op("pe", lambda e, r_=r_, k_=k_, v_=v_, lor=lor, pb=pb, c0=c0, h=h: e.matmul(pb[:, c0:c0 + 64], lhsT=Am[h][:, 128:256],
                                                                     rhs=Ub[:, h * 64:(h + 1) * 64], start=False, stop=False),
                         reads=["Am%d" % h, "Bt"], writes=["pb%d" % bank])
                    S.op("pe", lambda e, r_=r_, k_=k_, v_=v_, lor=lor, pb=pb, c0=c0, h=h, hf=hf: e.matmul(pb[:, c0:c0 + 64], lhsT=Am[h][:, 384:512],
                                                                            rhs=Vb[:, h * 64:(h + 1) * 64], start=False, stop=(hf == 1)),
                         reads=["Am%d" % h, "Vb"], writes=["pb%d" % bank])
            S.op("act", lambda e, r_=r_, k_=k_, v_=v_, lor=lor, pb=pb, hb=hb: e.copy(out=yv[:, hb * 512:(hb + 1) * 512], in_=pb), reads=["pb%d" % bank], writes=["Ea"])
        if DBG.get('p2b_stop', 99) <= 7:
            continue
        S.mark("PD_%d" % ti)
        S.dma("sp", lambda e, t=t: e.dma_start(out=sgbt, in_=T["sgb"][t]), reads=["d_sgb"], writes=["sgbt"])
        _rwkv_post(C, B, yv, "Ea", v_, ztn, sbon, sgbt, "sgbt", t, mark_pe="PP_%d" % ti, sbon_n=sbn)
        if C.debug:
            S.dma("sp", lambda e, t=t: e.dma_start(out=T["dy"][t], in_=yv), reads=["Ea"], writes=["d_dy"], home="Ea")
    S.mark(None)
    n_ = len(tiles)
    if n_:
        S.replay("L0", "P1_0", "P2_0", "P3_0")
    for ti in range(n_):
        S.replay("A_%d" % ti)
        if ti + 1 < n_:
            S.replay("P1_%d" % (ti + 1))
            S.interleave("PD_%d" % ti, "P2_%d" % (ti + 1))
            S.replay("PP_%d" % ti, "L%d" % (ti + 1), "P3_%d" % (ti + 1))
        else:
            S.replay("PD_%d" % ti, "PP_%d" % ti)
    assert not any(S.caps.values()), [k for k, v in S.caps.items() if v]
    if C.ntiles and DBG.get('p2b_stop', 99) > 8:
        Hout = Eb[0:64, :].rearrange("v (p x) -> v p x", p=8)
        nlast = len(tiles)
        for hb in range(2):
            bank = _nextbank(C)
            pb = C.pbank(bank)
            for i in range(4):
                p = hb * 4 + i
                S.op("pe", lambda e, pb=pb, i=i, p=p: e.transpose(out=pb[0:64, i * 128:(i + 1) * 128], in_=H32[:, p, :], identity=identf),
                     reads=["H32", "identf"], writes=["pb%d" % bank])
            S.op("dve", lambda e, pb=pb, hb=hb: e.tensor_copy(out=Hout[:, hb * 4:(hb + 1) * 4, :].rearrange("v p x -> v (p x)"),
                                                              in_=pb[0:64, :]), reads=["pb%d" % bank], writes=["Eb"])
        S.dma("sp", lambda e: e.dma_start(out=O["pw"].rearrange("h v k -> v h k"),
                                          in_=Hout.rearrange("v p (a k) -> v (p a) k", a=2)), reads=["Eb"], writes=["o_pw"], home="Eb")


def phase2c(C):
    C.rotbanks = (1, 2, 3, 4)
    S, alloc, I, O, T = C.S, C.alloc, C.I, C.O, C.T
    C.reset()
    if not DBG.get('sample', True):
        return
    B = {}
    _post_bufs(C, B)
    Sst = alloc([128, 2, 64, 64])
    tmp = alloc([128, 2, 64, 64])
    vec6 = alloc([128, 6, 8, 128])
    ysc = alloc([128, 8, 128])
    sa = alloc([128, 2, 64])
    yv = alloc([128, D])
    vbuf = alloc([128, D])
    sgbt = alloc([128, D])
    S.dma("sp", lambda e: e.dma_start(out=Sst.rearrange("p a v k -> p (a v k)"),
                                      in_=I["swkv"].rearrange("b (hh a) v k -> (b hh) (a v k)", a=2)), writes=["Sst0", "Sst1"])
    for q in range(6):
        for i in range(8):
            src = T["s6"][q].rearrange("(b i) (hh x) -> i b hh x", i=8, x=128)[i]
            S.dma("sp", lambda e, q=q, i=i, src=src: e.dma_start(out=vec6[:, q, i, :], in_=src),
                  reads=["d_s6"], writes=["vec6"])
    bk = lambda vec: vec.unsqueeze(1).to_broadcast([128, 64, 64])
    bv = lambda vec: vec.unsqueeze(2).to_broadcast([128, 64, 64])
    for i in range(8):
        ops = {0: [], 1: []}
        for a in range(2):
            eng = "dve" if a == 0 else "pool"
            Sv, Tv = Sst[:, a], tmp[:, a]
            sn, tn, san, yn = "Sst%d" % a, "tmp%d" % a, "sa%d" % a, "ysc%d" % a
            sl = slice(a * 64, a * 64 + 64)
            r_, w_, k_, v_, a_, b_ = [vec6[:, q, i, sl] for q in range(6)]
            sav = sa[:, a, :]
            L = ops[a]
            L.append((eng, lambda e, Sv=Sv, Tv=Tv, a_=a_: e.tensor_tensor(out=Tv, in0=Sv, in1=bk(a_), op=ALU.mult), [sn, "vec6"], [tn]))
            L.append(("dve", lambda e, Tv=Tv, sav=sav: e.tensor_reduce(out=sav, in_=Tv, axis=AX.X, op=ALU.add), [tn], [san]))
            L.append((eng, lambda e, Sv=Sv, w_=w_: e.tensor_tensor(out=Sv, in0=Sv, in1=bk(w_), op=ALU.mult), [sn, "vec6"], [sn]))
            L.append((eng, lambda e, Tv=Tv, sav=sav, b_=b_: e.tensor_tensor(out=Tv, in0=bv(sav), in1=bk(b_), op=ALU.mult), [san, "vec6"], [tn]))
            L.append((eng, lambda e, Sv=Sv, Tv=Tv: e.tensor_tensor(out=Sv, in0=Sv, in1=Tv, op=ALU.add), [sn, tn], [sn]))
            L.append((eng, lambda e, Tv=Tv, v_=v_, k_=k_: e.tensor_tensor(out=Tv, in0=bv(v_), in1=bk(k_), op=ALU.mult), ["vec6"], [tn]))
            L.append((eng, lambda e, Sv=Sv, Tv=Tv: e.tensor_tensor(out=Sv, in0=Sv, in1=Tv, op=ALU.add), [sn, tn], [sn]))
            L.append((eng, lambda e, Sv=Sv, Tv=Tv, r_=r_: e.tensor_tensor(out=Tv, in0=Sv, in1=bk(r_), op=ALU.mult), [sn, "vec6"], [tn]))
            L.append(("dve", lambda e, Tv=Tv, i=i, sl=sl: e.tensor_reduce(out=ysc[:, i, sl], in_=Tv, axis=AX.X, op=ALU.add), [tn], [yn]))
        for kk_ in range(9):
            for a in (1, 0):
                eng, fn, rd, wr = ops[a][kk_]
                S.op(eng, fn, reads=rd, writes=wr)
    S.dma("sp", lambda e: e.dma_start(out=O["sw"].rearrange("b (hh a) v k -> (b hh) (a v k)", a=2),
                                      in_=Sst.rearrange("p a v k -> p (a v k)")), reads=["Sst0", "Sst1"], writes=["o_sw"], home="Sst0")
    for i in range(8):
        dst = T["sy"].rearrange("(b i) (hh x) -> i b hh x", i=8, x=128)[i]
        S.dma("sp", lambda e, i=i, dst=dst: e.dma_start(out=dst, in_=ysc[:, i, :]), reads=["ysc0", "ysc1"], writes=["d_sy"], home="ysc0")
    S.dma("sp", lambda e: e.dma_start(out=yv, in_=T["sy"]), reads=["d_sy"], writes=["yv"])
    S.dma("sp", lambda e: e.dma_start(out=vbuf, in_=T["z"][NT][:, 2048:3072]), reads=["d_z"], writes=["vbuf"])
    S.dma("sp", lambda e: e.dma_start(out=sgbt, in_=T["sgb"][NT]), reads=["d_sgb"], writes=["sgbt"])
    sbon = B["s16"][:, 48:64]
    S.dma("sp", lambda e: e.dma_start(out=sbon, in_=T["sextra"][:, 0:16]), reads=["d_sx"], writes=["sbon"])
    _rwkv_post(C, B, yv, "yv", vbuf, "vbuf", sbon, sgbt, "sgbt", NT)


def _scan_setup(C):
    S, alloc, I, O, T = C.S, C.alloc, C.I, C.O, C.T
    Sst = alloc([128, 2, 64, 64])
    tmp = alloc([128, 64, 64])
    vecs = [alloc([128, 6, 128]), alloc([128, 6, 128])]
    ysc = alloc([128, 8, 128])
    sa = alloc([128, 64])
    ops = []
    ops.append(lambda: S.dma("sp", lambda e: e.dma_start(out=Sst.rearrange("p a v k -> p (a v k)"),
                                                         in_=I["swkv"].rearrange("b (hh a) v k -> (b hh) (a v k)", a=2)),
                             writes=["Sst"]))
    bk = lambda vec: vec.unsqueeze(1).to_broadcast([128, 64, 64])
    bv = lambda vec: vec.unsqueeze(2).to_broadcast([128, 64, 64])

    def ldv(i):
        def f():
            for q in range(6):
                src = T["s6"][q].rearrange("(b i) (hh x) -> i b hh x", i=8, x=128)[i]
                S.dma("sp", lambda e, q=q, src=src: e.dma_start(out=vecs[i % 2][:, q, :], in_=src),
                      reads=["d_s6"], writes=["vec%d" % (i % 2)])
        return f
    ops.append(ldv(0))
    for i in range(8):
        if i + 1 < 8:
            ops.append(ldv(i + 1))
        vn = "vec%d" % (i % 2)
        for a in range(2):
            Sv = Sst[:, a]
            sl = slice(a * 64, a * 64 + 64)
            r_, w_, k_, v_, a_, b_ = [vecs[i % 2][:, q, sl] for q in range(6)]
            L = [
                (lambda e, Sv=Sv, a_=a_: e.tensor_tensor(out=tmp, in0=Sv, in1=bk(a_), op=ALU.mult), ["Sst", vn], ["stmp"]),
                (lambda e: e.tensor_reduce(out=sa, in_=tmp, axis=AX.X, op=ALU.add), ["stmp"], ["ssa"]),
                (lambda e, Sv=Sv, w_=w_: e.tensor_tensor(out=Sv, in0=Sv, in1=bk(w_), op=ALU.mult), ["Sst", vn], ["Sst"]),
                (lambda e, b_=b_: e.tensor_tensor(out=tmp, in0=bv(sa), in1=bk(b_), op=ALU.mult), ["ssa", vn], ["stmp"]),
                (lambda e, Sv=Sv: e.tensor_tensor(out=Sv, in0=Sv, in1=tmp, op=ALU.add), ["Sst", "stmp"], ["Sst"]),
                (lambda e, v_=v_, k_=k_: e.tensor_tensor(out=tmp, in0=bv(v_), in1=bk(k_), op=ALU.mult), [vn], ["stmp"]),
                (lambda e, Sv=Sv: e.tensor_tensor(out=Sv, in0=Sv, in1=tmp, op=ALU.add), ["Sst", "stmp"], ["Sst"]),
                (lambda e, Sv=Sv, r_=r_: e.tensor_tensor(out=tmp, in0=Sv, in1=bk(r_), op=ALU.mult), ["Sst", vn], ["stmp"]),
                (lambda e, i=i, sl=sl: e.tensor_reduce(out=ysc[:, i, sl], in_=tmp, axis=AX.X, op=ALU.add), ["stmp"], ["ysc"]),
            ]
            for fn, rd, wr in L:
                ops.append(lambda fn=fn, rd=rd, wr=wr: S.op("dve", fn, reads=rd, writes=wr))

    def fin():
        S.dma("sp", lambda e: e.dma_start(out=O["sw"].rearrange("b (hh a) v k -> (b hh) (a v k)", a=2),
                                          in_=Sst.rearrange("p a v k -> p (a v k)")), reads=["Sst"], writes=["o_sw"], home="Sst")
        for i in range(8):
            dst = T["sy"].rearrange("(b i) (hh x) -> i b hh x", i=8, x=128)[i]
            S.dma("sp", lambda e, i=i, dst=dst: e.dma_start(out=dst, in_=ysc[:, i, :]), reads=["ysc"], writes=["d_sy"], home="ysc")
    ops.append(fin)
    return ops


def phase3a(C):
    phase3(C, tiles=list(range(C.ntiles)), scan=DBG.get('sample', True))


def phase3b(C):
    if DBG.get('sample', True):
        phase3(C, tiles=[NT], scan=False, reuse=("p3a" in C.phases and DBG.get('reuse', True)))


def phase2d(C):
    S, alloc, I, O, T = C.S, C.alloc, C.I, C.O, C.T
    C.rotbanks = (1, 2, 3, 4)
    C.reset()
    if "p3a" in C.phases and DBG.get('reuse', True):
        C.apos = (getattr(C, "p3_persist", 19152) + 63) // 64 * 64
    if not DBG.get('sample', True):
        return
    B = {}
    _post_bufs(C, B)
    yv = alloc([128, D])
    vbuf = alloc([128, D])
    sgbt = alloc([128, D])
    S.dma("sp", lambda e: e.dma_start(out=yv, in_=T["sy"]), reads=["d_sy"], writes=["yv"])
    S.dma("sp", lambda e: e.dma_start(out=vbuf, in_=T["z"][NT][:, 2048:3072]), reads=["d_z"], writes=["vbuf"])
    S.dma("sp", lambda e: e.dma_start(out=sgbt, in_=T["sgb"][NT]), reads=["d_sgb"], writes=["sgbt"])
    sbon = B["s16"][:, 48:64]
    S.dma("sp", lambda e: e.dma_start(out=sbon, in_=T["sextra"][:, 0:16]), reads=["d_sx"], writes=["sbon"])
    _rwkv_post(C, B, yv, "yv", vbuf, "vbuf", sbon, sgbt, "sgbt", NT)


PHASES = {"p1": phase1, "p2a": phase2a, "p2b": phase2b, "p2c": phase2c, "p3": phase3,
          "p3a": phase3a, "p2d": phase2d, "p3b": phase3b}


def _shard_inputs(inp):
    g = lambda k: np.ascontiguousarray(np.asarray(inp[k], dtype=np.float32))
    oh = _t5_bucket_onehot()
    shared = {
        "rel_bias": g("rel_bias"), "onehot": oh, "norm_g": g("norm_g")[0], "w_in": g("w_in")[0],
        "sinks": g("attn_sinks")[0], "mu": g("shift_mu")[0], "w0": g("rwkv_w0")[0], "w2": g("rwkv_w2")[0],
        "a0": g("rwkv_a0")[0], "a2": g("rwkv_a2")[0], "k_k": g("rwkv_k_k")[0], "k_a": g("rwkv_k_a")[0],
        "r_k": g("rwkv_r_k")[0].reshape(-1), "lnx_g": g("lnx_g")[0], "lnx_b": g("lnx_b")[0],
        "w_out_a": g("w_out_a")[0], "w_out_b": g("w_out_b")[0], "w_o": g("w_o")[0], "final_g": g("final_g"),
    }
    xp, xs = g("x_prompt"), g("x_sample")
    ck, cv = g("cache_k_win")[0], g("cache_v_win")[0]
    sw, ssh = g("state_wkv")[0], g("state_shift")[0]
    maps = []
    for c in range(NCORES):
        m = dict(shared)
        b0 = 16 * c
        m["xp"] = xp[c]
        m["xs"] = xs[b0:b0 + 16].reshape(128, D)
        m["ck"] = ck[b0:b0 + 16].reshape(16, 128, 256)
        m["cv"] = cv[b0:b0 + 16].reshape(16, 128, 256)
        m["swkv"] = sw[b0:b0 + 16]
        m["sshift"] = ssh[b0:b0 + 16]
        maps.append(m)
    return maps


_NC_CACHE = {}


def kernel(**inputs):
    if "nc" not in _NC_CACHE:
        _NC_CACHE["nc"] = build()
    nc = _NC_CACHE["nc"]
    maps = _shard_inputs(inputs)
    res = run_bass_kernel_spmd(nc, maps, core_ids=list(range(NCORES)))
    R = res.results
    cat = lambda k: np.stack([np.asarray(r[k]) for r in R])
    y_prompt = cat("yp").reshape(8, 4096, D)
    y_sample = cat("ys").reshape(128, 8, D)
    pk = cat("pk").reshape(1, 8, 128, 4, 64)
    pv = cat("pv").reshape(1, 8, 128, 4, 64)
    pw = cat("pw").reshape(1, 8, 16, 64, 64)
    psh = cat("psh").reshape(1, 8, 3200)
    sk = cat("sk").reshape(1, 128, 128, 4, 64)
    sv = cat("sv").reshape(1, 128, 128, 4, 64)
    sw = cat("sw").reshape(1, 128, 16, 64, 64)
    ssh = cat("ssh").reshape(1, 128, 3200)
    return (y_prompt, y_sample, pk, pv, pw, psh, sk, sv, sw, ssh)
```

```python
import math
from contextlib import ExitStack

import numpy as np
import concourse.bass as bass
import concourse.mybir as mybir
from concourse.bass_utils import run_bass_kernel_spmd

F32 = mybir.dt.float32
BF16 = mybir.dt.bfloat16
ALU = mybir.AluOpType
AF = mybir.ActivationFunctionType
AX = mybir.AxisListType

NCORES = 8
D = 1024
NT = 32
NEG = -30000.0
CDEC = math.exp(-0.5)
ARENA = 45056


class _Buf:
    __slots__ = ("last_write", "readers", "dsem")

    def __init__(self):
        self.last_write = None
        self.readers = {}
        self.dsem = None


class Sched:
    ENGS = ("pe", "act", "dve", "pool", "sp")

    def __init__(self):
        self.q = {e: [] for e in self.ENGS}
        self.cnt = {e: 0 for e in self.ENGS}
        self.seen = {e: {} for e in self.ENGS}
        self.dma_sems = {}
        self.bufs = {}
        self.cap = None
        self.caps = {}

    def buf(self, name):
        b = self.bufs.get(name)
        if b is None:
            b = self.bufs[name] = _Buf()
        return b

    def _deps(self, reads, writes):
        deps = {}

        def add(tok):
            if tok is not None and deps.get(tok[0], 0) < tok[1]:
                deps[tok[0]] = tok[1]
        for r in reads:
            add(self.buf(r).last_write)
        for w in writes:
            b = self.buf(w)
            add(b.last_write)
            for k, v in b.readers.items():
                add((k, v))
        return deps

    def _waits(self, e, deps):
        for k, v in deps.items():
            if k[0] == "dma":
                v = self.dma_sems[k]
            if k == ("eng", "pe") and e == "pe":
                continue
            if DBG.get("nosame") and k == ("eng", e):
                continue
            if self.seen[e].get(k, 0) >= v:
                continue
            self.seen[e][k] = v
            self.q[e].append(("wait", k, v))

    def _post(self, tok, reads, writes):
        for w in writes:
            b = self.buf(w)
            b.last_write = tok
            b.readers = {}
        for r in reads:
            if r not in writes:
                self.buf(r).readers[tok[0]] = tok[1]

    @staticmethod
    def _excl(reads, writes):
        pr = [r for r in reads if r.startswith("pb")]
        if pr:
            writes = list(writes) + [r for r in pr if r not in writes]
        return reads, writes

    def mark(self, name):
        if name is None:
            self.cap = None
        else:
            self.cap = self.caps.setdefault(name, [])

    def _emit_item(self, it):
        if it[0] == "op":
            self.op(*it[1:])
        else:
            self.dma(*it[1:-1], home=it[-1])

    def replay(self, *names):
        cap, self.cap = self.cap, None
        for n in names:
            for it in self.caps.pop(n, []):
                self._emit_item(it)
        self.cap = cap

    def interleave(self, na, nb):
        cap, self.cap = self.cap, None
        a, b = self.caps.pop(na, []), self.caps.pop(nb, [])
        for i in range(max(len(a), len(b))):
            if i < len(a):
                self._emit_item(a[i])
            if i < len(b):
                self._emit_item(b[i])
        self.cap = cap

    def op(self, e, fn, reads=(), writes=()):
        if self.cap is not None:
            self.cap.append(("op", e, fn, tuple(reads), tuple(writes)))
            return
        reads, writes = self._excl(reads, writes)
        self._waits(e, self._deps(reads, writes))
        self.cnt[e] += 1
        tok = (("eng", e), self.cnt[e])
        self.q[e].append(("ins", fn, tok))
        self._post(tok, reads, writes)

    def dma(self, e, fn, reads=(), writes=(), home=None):
        if self.cap is not None:
            self.cap.append(("dma", e, fn, tuple(reads), tuple(writes), home))
            return
        self._waits(e, self._deps(reads, writes))
        hb = self.buf(home if home is not None else (list(writes) + list(reads))[0])
        if hb.dsem is None:
            hb.dsem = ("dma", len(self.dma_sems))
            self.dma_sems[hb.dsem] = 0
        self.dma_sems[hb.dsem] += 16
        tok = (hb.dsem, self.dma_sems[hb.dsem])
        self.q[e].append(("dma", fn, tok))
        self._post(tok, reads, writes)

    def barrier(self):
        deps = {("eng", en): self.cnt[en] for en in self.ENGS if self.cnt[en]}
        for k, v in self.dma_sems.items():
            deps[k] = v
        for e in self.ENGS:
            for k, v in deps.items():
                if k == ("eng", e) or self.seen[e].get(k, 0) >= v:
                    continue
                self.seen[e][k] = v
                self.q[e].append(("wait", k, v))
        for e in self.ENGS:
            if e == "sp":
                continue
            self.cnt[e] += 1
            self.q[e].append(("ins", lambda eng: eng.nop(), (("eng", e), self.cnt[e])))
        deps = {("eng", en): self.cnt[en] for en in self.ENGS if self.cnt[en]}
        for e in self.ENGS:
            for k, v in deps.items():
                if k == ("eng", e) or self.seen[e].get(k, 0) >= v:
                    continue
                self.seen[e][k] = v
                self.q[e].append(("wait", k, v))

    def emit(self, nc, stack):
        semh = {}
        for e in self.ENGS:
            semh[("eng", e)] = stack.enter_context(nc.semaphore("s_" + e))
        for k in self.dma_sems:
            semh[k] = stack.enter_context(nc.semaphore("d_%d" % k[1]))
        block = stack.enter_context(nc.Block())
        q = self.q

        def run(eng, items):
            for it in items:
                if it[0] == "wait":
                    eng.wait_ge(semh[it[1]], it[2])
                elif it[0] == "ins":
                    it[1](eng).then_inc(semh[it[2][0]], 1)
                else:
                    it[1](eng).then_inc(semh[it[2][0]], 16)

        @block.tensor
        def _(eng):
            run(eng, q["pe"])

        @block.scalar
        def _(eng):
            run(eng, q["act"])

        @block.vector
        def _(eng):
            run(eng, q["dve"])

        @block.gpsimd
        def _(eng):
            run(eng, q["pool"])

        @block.sync
        def _(eng):
            run(eng, q["sp"])


def _t5_bucket_onehot():
    d = np.arange(129)
    dd = np.maximum(d, 1).astype(np.float32)
    large = 16 + (np.log(dd / np.float32(16)) / np.float32(math.log(128 / 16)) * np.float32(16)).astype(np.int32)
    large = np.minimum(large, 31)
    bkt = np.where(d < 16, d, large)
    e = np.zeros((32, 129), np.float32)
    e[bkt, d] = 1.0
    return e


class _NullSched:
    def op(self, *a, **k):
        pass

    def dma(self, *a, **k):
        pass


class Ctx:
    pass


DBG = {}


def build(phases=("p1", "p2a", "p2b", "p3a", "p2d", "p3b"), debug=False, ntiles=NT):
    nc = bass.Bass("TRN2", target_bir_lowering=False)
    S = Sched()
    C = Ctx()
    C.nc, C.S, C.debug, C.ntiles = nc, S, debug, ntiles
    C.phases = tuple(phases)

    def din(name, shape):
        return nc.dram_tensor(name, list(shape), F32, kind="ExternalInput").ap()

    def dout(name, shape):
        return nc.dram_tensor(name, list(shape), F32, kind="ExternalOutput").ap()

    def dscr(name, shape, dt=F32):
        if debug and dt == F32:
            return nc.dram_tensor(name, list(shape), dt, kind="ExternalOutput").ap()
        return nc.dram_tensor(name, list(shape), dt).ap()

    I = C.I = {}
    for name, shape in [
        ("xp", (4096, D)), ("xs", (128, D)), ("ck", (16, 128, 256)), ("cv", (16, 128, 256)),
        ("swkv", (16, 16, 64, 64)), ("sshift", (16, 3200)), ("rel_bias", (32, 16)),
        ("onehot", (32, 129)), ("norm_g", (D,)), ("w_in", (D, 8832)), ("sinks", (16,)),
        ("mu", (3200,)), ("w0", (D,)), ("w2", (64, D)), ("a0", (D,)), ("a2", (64, D)),
        ("k_k", (D,)), ("k_a", (D,)), ("r_k", (D,)), ("lnx_g", (D,)), ("lnx_b", (D,)),
        ("w_out_a", (D, D)), ("w_out_b", (D, D)), ("w_o", (D, D)), ("final_g", (D,)),
    ]:
        I[name] = din(name, shape)
    O = C.O = {}
    for name, shape in [
        ("yp", (4096, D)), ("ys", (128, D)), ("pk", (128, 256)), ("pv", (128, 256)),
        ("pw", (16, 64, 64)), ("psh", (3200,)), ("sk", (16, 128, 256)), ("sv", (16, 128, 256)),
        ("sw", (16, 16, 64, 64)), ("ssh", (16, 3200)),
    ]:
        O[name] = dout(name, shape)
    T = C.T = {}
    T["extd"] = dscr("extd", (16, 128, 384))
    T["ya"] = dscr("ya", (NT + 1, 128, D))
    T["yb"] = dscr("yb", (NT + 1, 128, D))
    T["z"] = dscr("z", (NT + 1, 128, 3200))
    T["sgb"] = dscr("sgb", (NT + 1, 128, D))
    T["carry"] = dscr("carry", (NT + 1, 3200))
    T["s6"] = dscr("s6", (6, 128, D))
    T["sy"] = dscr("sy", (128, D))
    T["sextra"] = dscr("sextra", (128, 2 * D + 16))
    T["wbf_in"] = dscr("wbf_in", (D, 6272), BF16)
    T["wbf_ob"] = dscr("wbf_ob", (D, D), BF16)
    T["wbf_o"] = dscr("wbf_o", (D, D), BF16)
    if debug:
        T["dMTp"] = dscr("dMTp", (128, 16, 128))
        T["dMTc"] = dscr("dMTc", (128, 16, 128))
        T["dMTs"] = dscr("dMTs", (128, 16, 128))
        T["dt1"] = dscr("dt1", (NT + 1, 128, D))
        T["dy"] = dscr("dy", (NT + 1, 128, D))

    with ExitStack() as st:
        arena = st.enter_context(nc.sbuf_tensor("arena", [128, ARENA], F32))
        psum = st.enter_context(nc.psum_tensor("psum", [128, 8, 512], F32))
        C.arena, C.psum = arena, psum
        C.apos = 0

        def alloc(shape, dt=F32):
            n = 1
            for s_ in shape[1:]:
                n *= s_
            words = n if dt == F32 else (n + 1) // 2
            words = (words + 7) // 8 * 8
            off = C.apos
            C.apos += words
            assert C.apos <= ARENA, ("SBUF arena overflow", C.apos)
            v = arena[0:shape[0], off:off + words]
            if dt != F32:
                v = v.bitcast(dt)
            v = v[:, 0:n]
            if len(shape) == 2:
                return v
            names = " ".join("a%d" % i for i in range(len(shape) - 1))
            kw = {"a%d" % i: shape[i + 1] for i in range(len(shape) - 1)}
            return v.rearrange("p (%s) -> p %s" % (names, names), **kw)
        C.alloc = alloc

        def reset():
            C.apos = 0
        C.reset = reset

        def pbank(i, dt=F32):
            v = psum[:, i, :]
            if dt != F32:
                v = v.bitcast(dt)
            return v
        C.pbank = pbank
        C.rot = 0

        for ph in phases:
            PHASES[ph](C)
            S.barrier()
        S.emit(nc, st)
    return nc


def _common_consts(C, init=True):
    S, alloc, I = C.S, C.alloc, C.I
    C.identf = alloc([128, 128])
    C.ident = alloc([128, 128], BF16)
    C.gbc = alloc([128, D])
    if not init:
        S = _NullSched()
    S.op("pool", lambda e: e.memset(C.identf, 0.0), writes=["identf"])
    S.op("pool", lambda e: e.affine_select(out=C.identf, in_=C.identf, pattern=[[-1, 128]],
                                           compare_op=ALU.not_equal, fill=1.0, base=0,
                                           channel_multiplier=1),
         reads=["identf"], writes=["identf"])
    S.op("dve", lambda e: e.tensor_copy(out=C.ident, in_=C.identf), reads=["identf"], writes=["ident"])
    S.dma("sp", lambda e: e.dma_start(out=C.gbc, in_=I["norm_g"].partition_broadcast(128)), writes=["gbc"])
    C.xt = [alloc([128, D]), alloc([128, D]), alloc([128, D])]
    C.junk = alloc([128, D], BF16)
    C.ss = alloc([128, 1])
    C.rstd = alloc([128, 1])
    C.xsb = alloc([128, D], BF16)
    C.hT = alloc([128, 8, 128], BF16)


def _x_src(C, t):
    return C.I["xs"] if t == NT else C.I["xp"][t * 128:(t + 1) * 128, :]


def _load_x(C, t, slot):
    C.S.dma("sp", lambda e: e.dma_start(out=C.xt[slot], in_=_x_src(C, t)), writes=["xt%d" % slot])


def _norm_pre(C, slot):
    S = C.S
    xt = C.xt[slot]
    xn = "xt%d" % slot
    S.op("pool", lambda e: e.memset(C.ss, 0.0), writes=["ss"])
    S.op("act", lambda e: e.activation(out=C.junk, in_=xt, func=AF.Square, accum_out=C.ss),
         reads=[xn, "ss"], writes=["junk", "ss"])
    S.op("dve", lambda e: e.tensor_scalar(out=C.rstd, in0=C.ss, scalar1=1.0 / D, scalar2=1e-6,
                                          op0=ALU.mult, op1=ALU.add), reads=["ss"], writes=["rstd"])
    S.op("act", lambda e: e.activation(out=C.rstd, in_=C.rstd, func=AF.Sqrt), reads=["rstd"], writes=["rstd"])
    S.op("dve", lambda e: e.reciprocal(out=C.rstd, in_=C.rstd), reads=["rstd"], writes=["rstd"])
    S.op("dve", lambda e: e.scalar_tensor_tensor(out=C.xsb, in0=xt, scalar=C.rstd[:, 0:1], in1=C.gbc,
                                                 op0=ALU.mult, op1=ALU.mult),
         reads=[xn, "rstd", "gbc"], writes=["xsb"])


def _norm_post(C):
    S = C.S
    pT = C.pbank(0, BF16)
    for k in range(8):
        S.op("pe", lambda e, k=k: e.transpose(out=pT[:, k * 128:(k + 1) * 128],
                                              in_=C.xsb[:, k * 128:(k + 1) * 128], identity=C.ident),
             reads=["xsb", "ident"], writes=["pb0"])
    S.op("act", lambda e: e.copy(out=C.hT.rearrange("p k t -> p (k t)"), in_=pT), reads=["pb0"], writes=["hT"])


def _norm_T(C, slot, nxt=None):
    _norm_post(C)
    if nxt is not None:
        _norm_pre(C, nxt)


def _pstride(ap, step, count):
    pat = [list(x) for x in ap.ap]
    pat[0] = [pat[0][0] * step, count]
    return bass.AP(ap.tensor, ap.offset, pat)


def _load_w(C, dst, dname, src_cols, eng="pool"):
    for k in range(8):
        C.S.dma(eng, lambda e, k=k: e.dma_start(out=dst[:, k, :], in_=src_cols[k * 128:(k + 1) * 128, :]),
                writes=[dname])


def _load_wbf(C, dst, dname, srcname, c0, n):
    src = C.T[srcname]
    for k in range(8):
        C.S.dma("sp" if k % 2 == 0 else "act",
                lambda e, k=k: e.dma_start(out=dst[:, k, :], in_=src[k * 128:(k + 1) * 128, c0:c0 + n]),
                reads=[srcname], writes=[dname])


def _nextbank(C):
    pool = getattr(C, 'rotbanks', (1, 2, 3, 4))
    b = pool[C.rot % len(pool)]
    C.rot += 1
    return b


def _proj_tm(C, W, wname, c0, ncols, bank):
    pb = C.pbank(bank)
    for k in range(8):
        C.S.op("pe", lambda e, k=k: e.matmul(pb[:, 0:ncols], lhsT=C.hT[:, k, :], rhs=W[:, k, c0:c0 + ncols],
                                              start=(k == 0), stop=(k == 7)),
               reads=["hT", wname], writes=["pb%d" % bank])
    return pb


def phase1(C):
    C.rotbanks = (1, 2, 3, 4)
    S, alloc, I, O, T, nc = C.S, C.alloc, C.I, C.O, C.T, C.nc
    C.reset()
    _common_consts(C)
    w_in = I["w_in"]
    Wqk = alloc([128, 8, 1280], BF16)
    Wkv = alloc([128, 8, 512], BF16)
    Wga = alloc([128, 8, 1024], BF16)
    Woa = alloc([128, 8, 1024], BF16)
    rb = alloc([32, 16])
    oh = alloc([32, 129])
    ext = alloc([16, 384])
    MTp = alloc([128, 16, 128])
    MTc = alloc([128, 16, 128])
    MTs = alloc([128, 16, 128])
    bsel = alloc([16, 128])
    bd = alloc([128, 128])
    esink = alloc([128, 16])
    S.dma("sp", lambda e: e.dma_start(out=rb, in_=I["rel_bias"]), writes=["rb"])
    S.dma("sp", lambda e: e.dma_start(out=oh, in_=I["onehot"]), writes=["oh"])
    S.dma("sp", lambda e: e.dma_start(out=esink, in_=I["sinks"].partition_broadcast(128)), writes=["esink"])
    S.op("act", lambda e: e.activation(out=esink, in_=esink, func=AF.Exp), reads=["esink"], writes=["esink"])
    pb = C.pbank(1)
    S.op("pe", lambda e: e.matmul(pb[0:16, 0:129], lhsT=rb, rhs=oh, start=True, stop=True),
         reads=["rb", "oh"], writes=["pb1"])
    S.op("pool", lambda e: e.memset(ext, NEG), writes=["ext"])
    S.op("dve", lambda e: e.tensor_copy(out=ext[:, 127:256], in_=pb[0:16, 0:129]), reads=["pb1", "ext"], writes=["ext"])
    S.dma("sp", lambda e: e.dma_start(out=T["extd"], in_=ext.unsqueeze(1).to_broadcast([16, 128, 384])),
          reads=["ext"], writes=["extd"])
    S.dma("sp", lambda e: e.dma_start(out=MTc, in_=bass.AP(T["extd"].tensor, 127, [[383, 128], [49152, 16], [1, 128]])),
          reads=["extd"], writes=["MTc"])
    S.dma("sp", lambda e: e.dma_start(out=MTp, in_=bass.AP(T["extd"].tensor, 255, [[383, 128], [49152, 16], [1, 128]])),
          reads=["extd"], writes=["MTp"])
    S.op("pool", lambda e: e.memset(bsel, 1.0), writes=["bsel"])
    S.op("pool", lambda e: e.affine_select(out=bsel, in_=bsel, pattern=[[1, 128]], compare_op=ALU.is_ge,
                                           fill=0.0, base=0, channel_multiplier=-8), reads=["bsel"], writes=["bsel"])
    S.op("pool", lambda e: e.affine_select(out=bsel, in_=bsel, pattern=[[-1, 128]], compare_op=ALU.is_ge,
                                           fill=0.0, base=7, channel_multiplier=8), reads=["bsel"], writes=["bsel"])
    pb2 = C.pbank(2)
    S.op("pe", lambda e: e.matmul(pb2[:, 0:128], lhsT=bsel, rhs=bsel, start=True, stop=True),
         reads=["bsel"], writes=["pb2"])
    S.op("dve", lambda e: e.tensor_scalar(out=bd, in0=pb2[:, 0:128], scalar1=-1.0, scalar2=-NEG,
                                          op0=ALU.add, op1=ALU.mult), reads=["pb2"], writes=["bd"])
    S.op("dve", lambda e: e.tensor_tensor(out=MTs, in0=MTc, in1=bd.unsqueeze(1).to_broadcast([128, 16, 128]),
                                          op=ALU.add), reads=["MTc", "bd"], writes=["MTs"])

    if C.debug:
        S.dma("sp", lambda e: e.dma_start(out=T["dMTp"], in_=MTp), reads=["MTp"], writes=["dMTp"])
        S.dma("sp", lambda e: e.dma_start(out=T["dMTc"], in_=MTc), reads=["MTc"], writes=["dMTc"])
        S.dma("sp", lambda e: e.dma_start(out=T["dMTs"], in_=MTs), reads=["MTs"], writes=["dMTs"])
    for k in range(8):
        for pair in range(2):
            for half in range(2):
                c0 = pair * 512 + half * 256
                src = w_in[k * 128:(k + 1) * 128, c0:c0 + 256].rearrange("p (g d) -> p g d", g=4)
                dst = Wqk[:, k, pair * 512:(pair + 1) * 512].rearrange(
                    "p (g half d) -> p g half d", g=4, half=2)[:, :, half, :]
                S.dma("pool", lambda e, src=src, dst=dst: e.dma_start(out=dst, in_=src), writes=["Wqk"])
        S.dma("pool", lambda e, k=k: e.dma_start(out=Wqk[:, k, 1024:1280],
                                                 in_=w_in[k * 128:(k + 1) * 128, 1024:1280]), writes=["Wqk"])
    _load_w(C, Wkv, "Wkv", w_in[:, 1024:1536])
    _load_w(C, Wga, "Wga", w_in[:, 1536:2560])
    _load_w(C, Woa, "Woa", I["w_out_a"])
    pre_list = []
    if DBG.get("precast", True):
        for k in range(8):
            rows = slice(k * 128, (k + 1) * 128)
            pre_list.append(lambda rows=rows: S.dma("pool", lambda e: e.dma_start(out=T["wbf_in"][rows, :], in_=w_in[rows, 2560:8832]), writes=["wbf_in"]))
        for k in range(8):
            rows = slice(k * 128, (k + 1) * 128)
            pre_list.append(lambda rows=rows: S.dma("pool", lambda e: e.dma_start(out=T["wbf_ob"][rows, :], in_=I["w_out_b"][rows, :]), writes=["wbf_ob"]))
            pre_list.append(lambda rows=rows: S.dma("pool", lambda e: e.dma_start(out=T["wbf_o"][rows, :], in_=I["w_o"][rows, :]), writes=["wbf_o"]))

    def precast(n):
        for _ in range(n):
            if pre_list:
                pre_list.pop(0)()

    qT = alloc([128, 8, 128], BF16)
    kT = [alloc([128, 2, 128], BF16), alloc([128, 2, 128], BF16)]
    va = [alloc([128, 4, 65], BF16), alloc([128, 4, 65], BF16)]
    kvo = alloc([128, 512])
    sga = alloc([128, D])
    stb = [alloc([128, 512]) for _ in range(2)]
    pTb = [alloc([128, 4, 128], BF16) for _ in range(4)]
    den = alloc([128, 16])
    t1 = alloc([128, 16, 64])
    goa = alloc([128, D], BF16)
    goaT = alloc([128, 8, 128], BF16)
    yat = [alloc([128, D]), alloc([128, D])]
    ckb = [alloc([128, 256], BF16) for _ in range(2)]
    vac = [alloc([128, 4, 65], BF16) for _ in range(2)]
    kTc = [alloc([128, 2, 128], BF16) for _ in range(2)]
    Zb = [alloc([128, 16, 128], BF16) for _ in range(2)]
    stc = [alloc([128, 16, 8]) for _ in range(2)]
    qTs = alloc([128, 16, 8, 8], BF16)
    zl = alloc([128, 128], BF16)
    zr = alloc([128, 512], BF16)
    S.op("pool", lambda e: e.memset(zl, 0.0), writes=["zl"])
    S.op("pool", lambda e: e.memset(zr, 0.0), writes=["zr"])
    for i in range(2):
        S.op("pool", lambda e, i=i: e.memset(va[i][:, :, 64:65], 1.0), writes=["va%d" % i])
        S.op("pool", lambda e, i=i: e.memset(vac[i][:, :, 64:65], 1.0), writes=["vac%d" % i])

    def oslot(h):
        bank = 5 + h // 7
        off = (h % 7) * 65
        return C.pbank(bank)[:, off:off + 65], "pb%d" % bank
    tiles = ([NT] if DBG.get('sample', True) else []) + list(range(C.ntiles))
    if not tiles:
        return
    if DBG.get('fake_sample'):
        tiles = [0]
    _load_x(C, tiles[0], 0)
    if len(tiles) > 1:
        _load_x(C, tiles[1], 1)
    _norm_pre(C, 0)
    cnt = {"st": 0, "pT": 0}
    for ti, t in enumerate(tiles):
        slot = ti % 2
        if ti + 2 < len(tiles):
            _load_x(C, tiles[ti + 2], (ti + 2) % 3)
        _norm_T(C, ti % 3, ((ti + 1) % 3) if ti + 1 < len(tiles) else None)
        sample = (t == NT) or bool(DBG.get('fake_sample'))
        cur = (t % 2) if not sample else 0
        prev = 1 - cur
        for chunks in ([0, 1, 2, 3], [4, 5, 6, 7], [8, 9]):
            bank = _nextbank(C)
            pb = C.pbank(bank)
            for ci, c in enumerate(chunks):
                for k in range(8):
                    S.op("pe", lambda e, k=k, ci=ci, c=c, pb=pb: e.matmul(
                        pb[:, ci * 128:(ci + 1) * 128], lhsT=Wqk[:, k, c * 128:(c + 1) * 128], rhs=C.hT[:, k, :],
                        start=(k == 0), stop=(k == 7)), reads=["hT", "Wqk"], writes=["pb%d" % bank])
            n = len(chunks) * 128
            if chunks[0] < 8:
                c0 = chunks[0]
                S.op("act", lambda e, pb=pb, c0=c0, n=n: e.copy(
                    out=qT[:, c0:c0 + 4, :].rearrange("p c t -> p (c t)"), in_=pb[:, 0:n]),
                    reads=["pb%d" % bank], writes=["qT"])
            else:
                S.op("act", lambda e, pb=pb, n=n, cur=cur: e.copy(out=kT[cur].rearrange("p c t -> p (c t)"), in_=pb[:, 0:n]),
                     reads=["pb%d" % bank], writes=["kT%d" % cur])
        bank = _nextbank(C)
        pb = _proj_tm(C, Wkv, "Wkv", 0, 512, bank)
        S.op("dve", lambda e, pb=pb, cur=cur: e.tensor_copy(out=va[cur][:, :, 0:64],
                                                   in_=pb[:, 256:512].rearrange("p (h d) -> p h d", h=4)),
             reads=["pb%d" % bank], writes=["va%d" % cur])
        if sample or t == NT - 1:
            S.op("dve", lambda e, pb=pb: e.tensor_copy(out=kvo, in_=pb), reads=["pb%d" % bank], writes=["kvo"])
            if sample and not DBG.get('s_out', True):
                pass
            elif sample:
                for b in range(16):
                    S.dma("sp", lambda e, b=b: e.dma_start(out=O["sk"][b, 120:128, :], in_=kvo[8 * b:8 * b + 8, 0:256]),
                          reads=["kvo"], writes=["o_sk"])
                    S.dma("sp", lambda e, b=b: e.dma_start(out=O["sv"][b, 120:128, :], in_=kvo[8 * b:8 * b + 8, 256:512]),
                          reads=["kvo"], writes=["o_sv"])
                S.dma("sp", lambda e: e.dma_start(out=O["sk"][:, 0:120, :], in_=I["ck"][:, 8:128, :]), writes=["o_sk2"])
                S.dma("sp", lambda e: e.dma_start(out=O["sv"][:, 0:120, :], in_=I["cv"][:, 8:128, :]), writes=["o_sv2"])
            else:
                S.dma("sp", lambda e: e.dma_start(out=O["pk"], in_=kvo[:, 0:256]), reads=["kvo"], writes=["o_pk"])
                S.dma("sp", lambda e: e.dma_start(out=O["pv"], in_=kvo[:, 256:512]), reads=["kvo"], writes=["o_pv"])
        for hf in range(2):
            bank = _nextbank(C)
            pb = _proj_tm(C, Wga, "Wga", hf * 512, 512, bank)
            S.op("act", lambda e, pb=pb, hf=hf: e.activation(out=sga[:, hf * 512:(hf + 1) * 512], in_=pb, func=AF.Silu),
                 reads=["pb%d" % bank], writes=["sga"])
        for bk, nh in enumerate((7, 7, 2)):
            S.op("pe", lambda e, bk=bk, nh=nh: e.matmul(C.pbank(5 + bk)[:, 0:nh * 65], lhsT=zl, rhs=zr[:, 0:nh * 65],
                                                        start=True, stop=False, skip_group_check=True),
                 reads=["zl", "zr"], writes=["pb%d" % (5 + bk)])
        blocks = [("cur", cur)] if (sample or t == 0) else [("prev", prev), ("cur", cur)]
        nlast = 1 if not sample else 17
        done = [0] * 16
        its = [(kvh, kind, sl) for kvh in range(4) for (kind, sl) in blocks]
        nblk = len(blocks) if not sample else 1 + DBG.get('s_nb', 16)

        def stage_a(kvh, kind, sl):
            pair, half = kvh // 2, kvh % 2
            rows = slice(half * 64, half * 64 + 64)
            bank = _nextbank(C)
            pb = C.pbank(bank)
            S.op("pe", lambda e, pb=pb, sl=sl, rows=rows, pair=pair: e.matmul(
                pb, lhsT=kT[sl][rows, pair, :], rhs=qT[rows, pair * 4:pair * 4 + 4, :], start=True, stop=True),
                reads=["kT%d" % sl, "qT"], writes=["pb%d" % bank])
            MT = MTs if sample else (MTp if kind == "prev" else MTc)
            mtn = "MTs" if sample else ("MTp" if kind == "prev" else "MTc")
            si = cnt["st"] % 2
            cnt["st"] += 1
            pi = cnt["pT"] % 4
            cnt["pT"] += 1
            S.op("dve", lambda e, pb=pb, si=si, MT=MT, kvh=kvh: e.scalar_tensor_tensor(
                out=stb[si], in0=pb, scalar=0.125, in1=MT[:, kvh * 4:kvh * 4 + 4, :].rearrange("p h q -> p (h q)"),
                op0=ALU.mult, op1=ALU.add), reads=["pb%d" % bank, mtn], writes=["st%d" % si])
            S.op("act", lambda e, si=si, pi=pi: e.activation(out=pTb[pi].rearrange("p g q -> p (g q)"),
                                                             in_=stb[si], func=AF.Exp),
                 reads=["st%d" % si], writes=["pT%d" % pi])
            return pi

        def stage_b(kvh, kind, sl, pi):
            for g in range(4):
                h = kvh * 4 + g
                osl, on = oslot(h)
                done[h] += 1
                S.op("pe", lambda e, osl=osl, pi=pi, g=g, sl=sl, kvh=kvh, last=(done[h] == nblk):
                     e.matmul(osl, lhsT=pTb[pi][:, g, :], rhs=va[sl][:, kvh, :], start=False, stop=last,
                              skip_group_check=True),
                     reads=["pT%d" % pi, "va%d" % sl], writes=[on])
        pis = {}
        for ii, it in enumerate(its):
            pis[ii] = stage_a(*it)
            if ii >= 1:
                stage_b(*its[ii - 1], pis[ii - 1])
        stage_b(*its[-1], pis[len(its) - 1])
        if sample:
            for b in range(DBG.get('s_nb', 16)):
                j = b % 2
                S.mark("SF_%d" % b)
                S.dma("pool", lambda e, b=b, j=j: e.dma_start(out=ckb[j], in_=I["ck"][b]), writes=["ckb%d" % j])
                S.dma("pool", lambda e, b=b, j=j: e.dma_start(out=vac[j][:, :, 0:64],
                                                              in_=I["cv"][b].rearrange("p (h d) -> p h d", h=4)),
                      writes=["vac%d" % j])
                pT0 = C.pbank(0, BF16)
                for pr in range(2):
                    S.op("pe", lambda e, pr=pr, j=j: e.transpose(out=pT0[:, pr * 128:(pr + 1) * 128],
                                                                  in_=ckb[j][:, pr * 128:(pr + 1) * 128], identity=C.ident),
                         reads=["ckb%d" % j, "ident"], writes=["pb0"])
                S.op("act", lambda e, j=j: e.copy(out=kTc[j].rearrange("p c t -> p (c t)"), in_=pT0[:, 0:256]),
                     reads=["pb0"], writes=["kTc%d" % j])
                lvl = DBG.get('s_lvl', 4)
                if lvl < 2:
                    continue
                for kvh in range(4):
                    pair, half = kvh // 2, kvh % 2
                    rows = slice(half * 64, half * 64 + 64)
                    bank = _nextbank(C)
                    pb = C.pbank(bank)
                    S.op("pe", lambda e, pb=pb, rows=rows, pair=pair, j=j: e.matmul(
                        pb, lhsT=kTc[j][rows, pair, :], rhs=qT[rows, pair * 4:pair * 4 + 4, :],
                        start=True, stop=True), reads=["kTc%d" % j, "qT"], writes=["pb%d" % bank])
                    S.op("dve", lambda e, pb=pb, j=j, kvh=kvh, b=b: e.scalar_tensor_tensor(
                        out=stc[j][:, kvh * 4:kvh * 4 + 4, :],
                        in0=pb.rearrange("p (g q) -> p g q", g=4)[:, :, 8 * b:8 * b + 8], scalar=0.125,
                        in1=MTp[:, kvh * 4:kvh * 4 + 4, 0:8], op0=ALU.mult, op1=ALU.add),
                        reads=["pb%d" % bank, "MTp"], writes=["stc%d" % j])
                if lvl < 3:
                    continue
                S.op("pool", lambda e, j=j: e.memset(Zb[j], 0.0), writes=["Zb%d" % j])
                S.op("act", lambda e, j=j, b=b: e.activation(out=Zb[j][:, :, 8 * b:8 * b + 8], in_=stc[j], func=AF.Exp),
                     reads=["stc%d" % j, "Zb%d" % j], writes=["Zb%d" % j])
                if lvl < 4:
                    continue
                S.mark("SB_%d" % b)
                for h in range(16):
                    osl, on = oslot(h)
                    done[h] += 1
                    S.op("pe", lambda e, osl=osl, h=h, j=j, last=(done[h] == 1 + DBG.get('s_nb', 16)): e.matmul(
                        osl, lhsT=Zb[j][:, h, :], rhs=vac[j][:, h // 4, :], start=False, stop=last,
                        skip_group_check=True),
                        reads=["Zb%d" % j, "vac%d" % j], writes=[on])
        if sample:
            S.mark(None)
            nb_ = DBG.get('s_nb', 16)
            for b in range(nb_):
                S.replay("SF_%d" % b)
                if b >= 1:
                    S.replay("SB_%d" % (b - 1))
            if nb_:
                S.replay("SB_%d" % (nb_ - 1))
        for bk, (h0, nh) in enumerate([(0, 7), (7, 7), (14, 2)]):
            ob = C.pbank(5 + bk)[:, 0:nh * 65].rearrange("p (h e) -> p h e", e=65)
            S.op("dve", lambda e, ob=ob, h0=h0, nh=nh: e.tensor_tensor(
                out=den[:, h0:h0 + nh].unsqueeze(2), in0=ob[:, :, 64:65], in1=esink[:, h0:h0 + nh].unsqueeze(2), op=ALU.add),
                reads=["pb%d" % (5 + bk), "esink"], writes=["den"])
        S.op("dve", lambda e: e.reciprocal(out=den, in_=den), reads=["den"], writes=["den"])
        for bk, (h0, nh) in enumerate([(0, 7), (7, 7), (14, 2)]):
            ob = C.pbank(5 + bk)[:, 0:nh * 65].rearrange("p (h e) -> p h e", e=65)
            S.op("dve", lambda e, ob=ob, h0=h0, nh=nh: e.tensor_tensor(
                out=t1[:, h0:h0 + nh, :], in0=ob[:, :, 0:64],
                in1=den[:, h0:h0 + nh].unsqueeze(2).to_broadcast([128, nh, 64]), op=ALU.mult),
                reads=["pb%d" % (5 + bk), "den"], writes=["t1"])
        if C.debug:
            S.dma("sp", lambda e, t=t: e.dma_start(out=T["dt1"][t], in_=t1.rearrange("p h d -> p (h d)")), reads=["t1"], writes=["d_dt1"], home="t1")
        S.op("dve", lambda e: e.tensor_tensor(out=goa, in0=t1.rearrange("p h d -> p (h d)"), in1=sga, op=ALU.mult),
             reads=["t1", "sga"], writes=["goa"])
        pT0 = C.pbank(0, BF16)
        for k in range(8):
            S.op("pe", lambda e, k=k: e.transpose(out=pT0[:, k * 128:(k + 1) * 128], in_=goa[:, k * 128:(k + 1) * 128],
                                                  identity=C.ident), reads=["goa", "ident"], writes=["pb0"])
        S.op("act", lambda e: e.copy(out=goaT.rearrange("p k t -> p (k t)"), in_=pT0), reads=["pb0"], writes=["goaT"])
        yslot = ti % 2
        for hf in range(2):
            bank = _nextbank(C)
            pb = C.pbank(bank)
            for k in range(8):
                S.op("pe", lambda e, k=k, pb=pb, hf=hf: e.matmul(pb, lhsT=goaT[:, k, :], rhs=Woa[:, k, hf * 512:(hf + 1) * 512],
                                                                  start=(k == 0), stop=(k == 7)),
                     reads=["goaT", "Woa"], writes=["pb%d" % bank])
            S.op("act", lambda e, pb=pb, hf=hf, yslot=yslot: e.copy(out=yat[yslot][:, hf * 512:(hf + 1) * 512], in_=pb),
                 reads=["pb%d" % bank], writes=["yat%d" % yslot])
        S.dma("sp", lambda e, t=t, yslot=yslot: e.dma_start(out=T["ya"][t], in_=yat[yslot]), reads=["yat%d" % yslot], writes=["d_ya"], home="yat%d" % yslot)
        precast(1 if len(tiles) > 26 else 24)


    precast(99)


def phase2a(C):
    C.rotbanks = (1, 2, 3, 4)
    S, alloc, I, O, T = C.S, C.alloc, C.I, C.O, C.T
    C.reset()
    _common_consts(C)
    tiles0 = list(range(C.ntiles)) + ([NT] if DBG.get('sample', True) else [])
    if tiles0:
        _load_x(C, tiles0[0], 0)
    if len(tiles0) > 1:
        _load_x(C, tiles0[1], 1)
    w_in = I["w_in"]
    Wps = alloc([128, 8, 3200], BF16)
    Wgb = alloc([128, 8, 1024], BF16)
    if DBG.get("precast", True) and "p1" in C.phases:
        _load_wbf(C, Wps, "Wps", "wbf_in", 0, 3200)
        _load_wbf(C, Wgb, "Wgb", "wbf_in", 3200, 1024)
    else:
        for k in range(8):
            for c0 in range(0, 3200, 640):
                S.dma("pool", lambda e, k=k, c0=c0: e.dma_start(out=Wps[:, k, c0:c0 + 640],
                                                                in_=w_in[k * 128:(k + 1) * 128, 2560 + c0:2560 + c0 + 640]),
                      writes=["Wps"])
        _load_w(C, Wgb, "Wgb", w_in[:, 5760:6784])
    mubc = alloc([128, 3200])
    S.dma("sp", lambda e: e.dma_start(out=mubc, in_=I["mu"].partition_broadcast(128)), writes=["mubc"])
    psb = [alloc([128, 3200]), alloc([128, 3200])]
    zb = [alloc([128, 3200]), alloc([128, 3200])]
    sgb = [alloc([128, D]), alloc([128, D])]
    sst = alloc([16, 3200])
    S.dma("sp", lambda e: e.dma_start(out=sst, in_=I["sshift"]), writes=["sst"])

    def sel(name, shape, pattern, cm, base):
        m = alloc(shape)
        S.op("pool", lambda e: e.memset(m, 0.0), writes=[name])
        S.op("pool", lambda e: e.affine_select(out=m, in_=m, pattern=pattern, compare_op=ALU.not_equal, fill=1.0,
                                               base=base, channel_multiplier=cm), reads=[name], writes=[name])
        return m
    ShI = sel("ShI", [128, 128], [[1, 128]], -1, -1)
    ShsI = sel("ShsI", [128, 128], [[1, 128]], -1, -1)
    S.op("pool", lambda e: e.memset(ShsI.rearrange("p (b i) -> p b i", i=8)[:, :, 0:1], 0.0), reads=["ShsI"], writes=["ShsI"])
    for m, n in ((ShI, "ShI"), (ShsI, "ShsI")):
        S.op("dve", lambda e, m=m: e.tensor_tensor(out=m, in0=m, in1=C.identf, op=ALU.subtract), reads=[n, "identf"], writes=[n])
    Ecar = alloc([128, 128])
    S.op("pool", lambda e: e.memset(Ecar, 0.0), writes=["Ecar"])
    S.op("pool", lambda e: e.memset(Ecar[:, 0:1], 1.0), reads=["Ecar"], writes=["Ecar"])
    S.op("pool", lambda e: e.affine_select(out=Ecar[:, 0:1], in_=Ecar[:, 0:1], pattern=[[0, 1]], compare_op=ALU.is_ge, fill=0.0,
                                           base=-127, channel_multiplier=1), reads=["Ecar"], writes=["Ecar"])
    Esel = sel("Esel", [16, 128], [[1, 128]], -8, 0)

    tiles = list(range(C.ntiles)) + ([NT] if DBG.get('sample', True) else [])
    if not tiles:
        return
    _norm_pre(C, 0)
    groups = [(c0, 512) for c0 in range(0, 3072, 512)] + [(3072, 128)]
    for ti, t in enumerate(tiles):
        slot = ti % 2
        if ti + 2 < len(tiles):
            _load_x(C, tiles[ti + 2], (ti + 2) % 3)
        _norm_T(C, ti % 3, ((ti + 1) % 3) if ti + 1 < len(tiles) else None)
        sample = (t == NT)
        ps, psn = psb[ti % 2], "ps%d" % (ti % 2)
        pp, ppn = psb[(ti + 1) % 2], "ps%d" % ((ti + 1) % 2)
        zt, ztn = zb[ti % 2], "zb%d" % (ti % 2)
        for (c0, n) in groups:
            bank = _nextbank(C)
            pb = _proj_tm(C, Wps, "Wps", c0, n, bank)
            S.op("act", lambda e, pb=pb, c0=c0, n=n, ps=ps: e.copy(out=ps[:, c0:c0 + n], in_=pb[:, 0:n]),
                 reads=["pb%d" % bank], writes=[psn])
        for hf in range(2):
            bank = _nextbank(C)
            pb = _proj_tm(C, Wgb, "Wgb", hf * 512, 512, bank)
            S.op("act", lambda e, pb=pb, hf=hf, slot=slot: e.activation(out=sgb[slot][:, hf * 512:(hf + 1) * 512],
                                                                         in_=pb, func=AF.Silu),
                 reads=["pb%d" % bank], writes=["sgb%d" % slot])
        S.dma("sp", lambda e, t=t, slot=slot: e.dma_start(out=T["sgb"][t], in_=sgb[slot]),
              reads=["sgb%d" % slot], writes=["d_sgb"], home="sgb%d" % slot)
        if sample:
            S.dma("sp", lambda e, ps=ps: e.dma_start(out=O["ssh"], in_=_pstride(ps[7:8, :], 8, 16)),
                  reads=[psn], writes=["o_ssh"], home=psn)
        elif t == NT - 1:
            S.dma("sp", lambda e, ps=ps: e.dma_start(out=O["psh"].rearrange("(o c) -> o c", o=1), in_=ps[127:128, :]),
                  reads=[psn], writes=["o_psh"], home=psn)
        carry = (not sample) and ti > 0
        for (c0, n) in groups:
            bank = _nextbank(C)
            pb = C.pbank(bank)
            last1 = not (carry or sample)
            S.op("pe", lambda e, pb=pb, c0=c0, n=n, ps=ps, m=(ShsI if sample else ShI), last1=last1: e.matmul(
                pb[:, 0:n], lhsT=m, rhs=ps[:, c0:c0 + n], start=True, stop=last1),
                reads=["ShsI" if sample else "ShI", psn], writes=["pb%d" % bank])
            if carry:
                S.op("pe", lambda e, pb=pb, c0=c0, n=n, pp=pp: e.matmul(pb[:, 0:n], lhsT=Ecar, rhs=pp[:, c0:c0 + n],
                                                                         start=False, stop=True),
                     reads=["Ecar", ppn], writes=["pb%d" % bank])
            if sample:
                S.op("pe", lambda e, pb=pb, c0=c0, n=n: e.matmul(pb[:, 0:n], lhsT=Esel, rhs=sst[:, c0:c0 + n],
                                                                  start=False, stop=True),
                     reads=["Esel", "sst"], writes=["pb%d" % bank])
            S.op("dve", lambda e, pb=pb, c0=c0, n=n, zt=zt: e.tensor_tensor(out=zt[:, c0:c0 + n], in0=pb[:, 0:n],
                                                                            in1=mubc[:, c0:c0 + n], op=ALU.mult),
                 reads=["pb%d" % bank, "mubc"], writes=[ztn])
        S.op("dve", lambda e, zt=zt, ps=ps: e.tensor_tensor(out=zt, in0=zt, in1=ps, op=ALU.add), reads=[ztn, psn], writes=[ztn])
        S.dma("sp", lambda e, t=t, zt=zt: e.dma_start(out=T["z"][t], in_=zt), reads=[ztn], writes=["d_z"], home=ztn)


def phase3(C, tiles=None, scan=False, reuse=False):
    C.rotbanks = (1, 2, 3, 4)
    S, alloc, I, O, T = C.S, C.alloc, C.I, C.O, C.T
    C.reset()
    _common_consts(C, init=not reuse)
    w_in = I["w_in"]
    Wm = alloc([128, 8, 2048], BF16)
    Wo = alloc([128, 8, 1024], BF16)
    if reuse:
        pass
    elif DBG.get("precast", True) and "p1" in C.phases:
        _load_wbf(C, Wm, "Wm", "wbf_in", 4224, 2048)
        _load_wbf(C, Wo, "Wo", "wbf_o", 0, 1024)
    else:
        for k in range(8):
            for c0 in range(0, 2048, 512):
                S.dma("pool", lambda e, k=k, c0=c0: e.dma_start(out=Wm[:, k, c0:c0 + 512],
                                                                in_=w_in[k * 128:(k + 1) * 128, 6784 + c0:6784 + c0 + 512]),
                      writes=["Wm"])
        _load_w(C, Wo, "Wo", I["w_o"])
    fgbc = alloc([128, D])
    if not reuse:
        S.dma("sp", lambda e: e.dma_start(out=fgbc, in_=I["final_g"].partition_broadcast(128)), writes=["fgbc"])
    C.p3_persist = C.apos
    sm = alloc([128, 2048])
    yab = [alloc([128, D]), alloc([128, D])]
    ybb = [alloc([128, D]), alloc([128, D])]
    mg = alloc([128, D], BF16)
    mgT = alloc([128, 8, 128], BF16)
    res = alloc([128, D])
    yo = [alloc([128, D]), alloc([128, D])]
    ss2 = alloc([128, 1])
    rs2 = alloc([128, 1])

    if tiles is None:
        tiles = list(range(C.ntiles)) + ([NT] if DBG.get('sample', True) else [])
    scan_ops = _scan_setup(C) if scan else []
    per_tile = (len(scan_ops) + max(len(tiles), 1) - 1) // max(len(tiles), 1)
    if not tiles:
        for fn in scan_ops:
            fn()
        return

    def emit_scan(n):
        for _ in range(max(n, 0)):
            if scan_ops:
                scan_ops.pop(0)()

    def loads(ti):
        t = tiles[ti]
        sl = ti % 2
        S.dma("sp", lambda e: e.dma_start(out=yab[sl], in_=T["ya"][t]), reads=["d_ya"], writes=["yab%d" % sl])
        S.dma("sp", lambda e: e.dma_start(out=ybb[sl], in_=T["yb"][t]), reads=["d_yb"], writes=["ybb%d" % sl])
    _load_x(C, tiles[0], 0)
    if len(tiles) > 1:
        _load_x(C, tiles[1], 1)
    loads(0)
    _norm_pre(C, 0)
    for ti, t in enumerate(tiles):
        slot = ti % 2
        if ti + 2 < len(tiles):
            _load_x(C, tiles[ti + 2], (ti + 2) % 3)
        if ti + 1 < len(tiles):
            loads(ti + 1)
        _norm_T(C, ti % 3, ((ti + 1) % 3) if ti + 1 < len(tiles) else None)
        emit_scan(2)
        for gi in range(4):
            bank = _nextbank(C)
            pb = _proj_tm(C, Wm, "Wm", gi * 512, 512, bank)
            S.op("act", lambda e, pb=pb, gi=gi: e.activation(out=sm[:, gi * 512:(gi + 1) * 512], in_=pb, func=AF.Sigmoid),
                 reads=["pb%d" % bank], writes=["sm"])
        ya, yb = yab[slot], ybb[slot]
        S.op("dve", lambda e, ya=ya: e.tensor_tensor(out=ya, in0=ya, in1=sm[:, 0:1024], op=ALU.mult),
             reads=["yab%d" % slot, "sm"], writes=["yab%d" % slot])
        S.op("pool", lambda e, yb=yb: e.tensor_tensor(out=yb, in0=yb, in1=sm[:, 1024:2048], op=ALU.mult),
             reads=["ybb%d" % slot, "sm"], writes=["ybb%d" % slot])
        S.op("dve", lambda e, ya=ya, yb=yb: e.tensor_tensor(out=mg, in0=ya, in1=yb, op=ALU.add),
             reads=["yab%d" % slot, "ybb%d" % slot], writes=["mg"])
        emit_scan(1)
        pT0 = C.pbank(0, BF16)
        for k in range(8):
            S.op("pe", lambda e, k=k: e.transpose(out=pT0[:, k * 128:(k + 1) * 128], in_=mg[:, k * 128:(k + 1) * 128],
                                                  identity=C.ident), reads=["mg", "ident"], writes=["pb0"])
        S.op("act", lambda e: e.copy(out=mgT.rearrange("p k t -> p (k t)"), in_=pT0), reads=["pb0"], writes=["mgT"])
        xt = C.xt[ti % 3]
        for hf in range(2):
            bank = _nextbank(C)
            pb = C.pbank(bank)
            for k in range(8):
                S.op("pe", lambda e, k=k, pb=pb, hf=hf: e.matmul(pb, lhsT=mgT[:, k, :], rhs=Wo[:, k, hf * 512:(hf + 1) * 512],
                                                                  start=(k == 0), stop=(k == 7)),
                     reads=["mgT", "Wo"], writes=["pb%d" % bank])
            S.op("dve", lambda e, pb=pb, hf=hf, xt=xt: e.tensor_tensor(out=res[:, hf * 512:(hf + 1) * 512], in0=pb,
                                                                       in1=xt[:, hf * 512:(hf + 1) * 512], op=ALU.add),
                 reads=["pb%d" % bank, "xt%d" % (ti % 3)], writes=["res"])
        emit_scan(1)
        S.op("pool", lambda e: e.memset(ss2, 0.0), writes=["ss2"])
        S.op("act", lambda e: e.activation(out=C.junk, in_=res, func=AF.Square, accum_out=ss2),
             reads=["res", "ss2"], writes=["junk", "ss2"])
        S.op("dve", lambda e: e.tensor_scalar(out=rs2, in0=ss2, scalar1=1.0 / D, scalar2=1e-6, op0=ALU.mult, op1=ALU.add),
             reads=["ss2"], writes=["rs2"])
        S.op("act", lambda e: e.activation(out=rs2, in_=rs2, func=AF.Sqrt), reads=["rs2"], writes=["rs2"])
        S.op("dve", lambda e: e.reciprocal(out=rs2, in_=rs2), reads=["rs2"], writes=["rs2"])
        yt = yo[slot]
        S.op("dve", lambda e, yt=yt: e.scalar_tensor_tensor(out=yt, in0=res, scalar=rs2[:, 0:1], in1=fgbc,
                                                            op0=ALU.mult, op1=ALU.mult),
             reads=["res", "rs2", "fgbc"], writes=["yo%d" % slot])
        dst = O["ys"] if t == NT else O["yp"][t * 128:(t + 1) * 128, :]
        S.dma("sp", lambda e, yt=yt, dst=dst: e.dma_start(out=dst, in_=yt), reads=["yo%d" % slot], writes=["o_y"],
              home="yo%d" % slot)
        emit_scan(per_tile - 4)
    while scan_ops:
        scan_ops.pop(0)()


def _rwkv_post(C, B, y, yn_, v, vn_, sbon, sgbt, sgn_, t, mark_pe=None, sbon_n="sbon"):
    S, T = C.S, C.T
    tD, tE, s16 = B["tD"], B["tE"], B["s16"]
    mean, var = s16[:, 0:16], s16[:, 16:32]
    v3 = lambda ap: ap.rearrange("p (h d) -> p h d", h=16)
    bc = lambda ap: ap.unsqueeze(2).to_broadcast([128, 16, 64])
    S.op("dve", lambda e: e.tensor_reduce(out=mean, in_=v3(y), axis=AX.X, op=ALU.add), reads=[yn_], writes=["s16m"])
    S.op("dve", lambda e: e.tensor_scalar(out=mean, in0=mean, scalar1=1.0 / 64, scalar2=None, op0=ALU.mult),
         reads=["s16m"], writes=["s16m"])
    S.op("dve", lambda e: e.tensor_tensor(out=v3(tD), in0=v3(y), in1=bc(mean), op=ALU.subtract),
         reads=[yn_, "s16m"], writes=[B["tDn"]])
    S.op("dve", lambda e: e.tensor_tensor(out=tE, in0=tD, in1=tD, op=ALU.mult), reads=[B["tDn"]], writes=[B["tEn"]])
    S.op("dve", lambda e: e.tensor_reduce(out=var, in_=v3(tE), axis=AX.X, op=ALU.add), reads=[B["tEn"]], writes=["s16v"])
    S.op("dve", lambda e: e.tensor_scalar(out=var, in0=var, scalar1=1.0 / 64, scalar2=64e-5, op0=ALU.mult, op1=ALU.add),
         reads=["s16v"], writes=["s16v"])
    S.op("act", lambda e: e.activation(out=var, in_=var, func=AF.Sqrt), reads=["s16v"], writes=["s16v"])
    S.op("dve", lambda e: e.reciprocal(out=var, in_=var), reads=["s16v"], writes=["s16v"])
    S.op("dve", lambda e: e.tensor_tensor(out=v3(tD), in0=v3(tD), in1=bc(var), op=ALU.mult), reads=[B["tDn"], "s16v"], writes=[B["tDn"]])
    S.op("dve", lambda e: e.tensor_tensor(out=tD, in0=tD, in1=B["lgbc"], op=ALU.mult), reads=[B["tDn"], "lgbc"], writes=[B["tDn"]])
    S.op("dve", lambda e: e.tensor_tensor(out=tD, in0=tD, in1=B["lbbc"], op=ALU.add), reads=[B["tDn"], "lbbc"], writes=[B["tDn"]])
    S.op("dve", lambda e: e.tensor_tensor(out=v3(tE), in0=v3(v), in1=bc(sbon), op=ALU.mult), reads=[vn_, sbon_n], writes=[B["tEn"]])
    S.op("dve", lambda e: e.tensor_tensor(out=tD, in0=tD, in1=tE, op=ALU.add), reads=[B["tDn"], B["tEn"]], writes=[B["tDn"]])
    S.op("dve", lambda e: e.tensor_tensor(out=B["ybg"], in0=tD, in1=sgbt, op=ALU.mult), reads=[B["tDn"], sgn_], writes=[B["ybgn"]])
    if mark_pe is not None:
        S.mark(mark_pe)
    pT0 = C.pbank(0, BF16)
    for k in range(8):
        S.op("pe", lambda e, k=k: e.transpose(out=pT0[:, k * 128:(k + 1) * 128], in_=B["ybg"][:, k * 128:(k + 1) * 128],
                                              identity=B["ident"]), reads=[B["ybgn"], "ident"], writes=["pb0"])
    S.op("act", lambda e: e.copy(out=B["ybT"].rearrange("p k t -> p (k t)"), in_=pT0), reads=["pb0"], writes=[B["ybTn"]])
    for hf in range(2):
        bank = _nextbank(C)
        pb = C.pbank(bank)
        for k in range(8):
            S.op("pe", lambda e, k=k, pb=pb, hf=hf: e.matmul(pb, lhsT=B["ybT"][:, k, :], rhs=B["WoB"][:, k, hf * 512:(hf + 1) * 512],
                                                              start=(k == 0), stop=(k == 7)),
                 reads=[B["ybTn"], "WoB"], writes=["pb%d" % bank])
        S.op("act", lambda e, pb=pb, hf=hf: e.copy(out=B["ybo"][:, hf * 512:(hf + 1) * 512], in_=pb),
             reads=["pb%d" % bank], writes=[B["ybon"]])
    S.dma("sp", lambda e, t=t: e.dma_start(out=T["yb"][t], in_=B["ybo"]), reads=[B["ybon"]], writes=["d_yb"], home=B["ybon"])


def _post_bufs(C, B):
    S, alloc, I = C.S, C.alloc, C.I
    B["identf"] = alloc([128, 128])
    B["ident"] = alloc([128, 128], BF16)
    S.op("pool", lambda e: e.memset(B["identf"], 0.0), writes=["identf"])
    S.op("pool", lambda e: e.affine_select(out=B["identf"], in_=B["identf"], pattern=[[-1, 128]], compare_op=ALU.not_equal,
                                           fill=1.0, base=0, channel_multiplier=1), reads=["identf"], writes=["identf"])
    S.op("dve", lambda e: e.tensor_copy(out=B["ident"], in_=B["identf"]), reads=["identf"], writes=["ident"])
    B["WoB"] = alloc([128, 8, 1024], BF16)
    if DBG.get("precast", True) and "p1" in C.phases:
        _load_wbf(C, B["WoB"], "WoB", "wbf_ob", 0, 1024)
    else:
        _load_w(C, B["WoB"], "WoB", I["w_out_b"])
    for nm, src in (("lgbc", "lnx_g"), ("lbbc", "lnx_b")):
        B[nm] = alloc([128, D])
        S.dma("sp", lambda e, nm=nm, src=src: e.dma_start(out=B[nm], in_=I[src].partition_broadcast(128)), writes=[nm])
    B["tD"] = alloc([128, D])
    B["tDn"] = "tD"
    if B.get("alloc_tE", True):
        B["tE"] = alloc([128, D])
        B["tEn"] = "tE"
    B["s16"] = alloc([128, 96])
    B["ybgn"], B["ybTn"], B["ybon"] = "ybg", "ybT", "ybo"
    if B.get("alloc_yb", True):
        B["ybg"] = alloc([128, D], BF16)
        B["ybT"] = alloc([128, 8, 128], BF16)
        B["ybo"] = alloc([128, D])


def phase2b(C):
    S, alloc, I, O, T = C.S, C.alloc, C.I, C.O, C.T
    C.reset()
    B = {"alloc_tE": False, "alloc_yb": False}
    _post_bufs(C, B)
    C.rotbanks = (1, 2, 3)
    identf, ident = B["identf"], B["ident"]
    tD, s16 = B["tD"], B["s16"]
    W2A2 = alloc([128, D], BF16)
    S.dma("pool", lambda e: e.dma_start(out=W2A2[0:64, :], in_=I["w2"]), writes=["W2A2"])
    S.dma("pool", lambda e: e.dma_start(out=W2A2[64:128, :], in_=I["a2"]), writes=["W2A2"])
    vecs = alloc([128, D])
    S.dma("sp", lambda e: e.dma_start(out=vecs[0:1, :], in_=I["w0"].rearrange("(o c) -> o c", o=1)), writes=["vecs"])
    S.dma("sp", lambda e: e.dma_start(out=vecs[32:33, :], in_=I["a0"].rearrange("(o c) -> o c", o=1)), writes=["vecs"])
    ones = alloc([128, 128])
    S.op("pool", lambda e: e.memset(ones, 1.0), writes=["ones"])
    negcol = alloc([128, 1])
    S.op("pool", lambda e: e.memset(negcol, -CDEC), writes=["negcol"])
    zl = alloc([128, 128], BF16)
    zr = alloc([128, 512], BF16)
    S.op("pool", lambda e: e.memset(zl, 0.0), writes=["zl"])
    S.op("pool", lambda e: e.memset(zr, 0.0), writes=["zr"])

    def tri(name, val, pattern, cm, base):
        m = alloc([128, 128])
        S.op("pool", lambda e: e.memset(m, val), writes=[name])
        S.op("pool", lambda e: e.affine_select(out=m, in_=m, pattern=pattern, compare_op=ALU.is_ge, fill=0.0,
                                               base=base, channel_multiplier=cm), reads=[name], writes=[name])
        return m
    Lincl = tri("Lincl", -CDEC, [[1, 128]], -1, 0)
    Lstr = tri("Lstr", -CDEC, [[1, 128]], -1, -1)
    Ustr = tri("Ustr", -CDEC, [[-1, 128]], 1, -1)
    MlowS = alloc([128, 512])
    S.op("pool", lambda e: e.memset(MlowS, 1.0), writes=["MlowS"])
    for blk in range(4):
        S.op("pool", lambda e, blk=blk: e.affine_select(out=MlowS[:, blk * 128:(blk + 1) * 128], in_=MlowS[:, blk * 128:(blk + 1) * 128],
                                                        pattern=[[-1, 128]], compare_op=ALU.is_ge, fill=0.0, base=-1, channel_multiplier=1),
             reads=["MlowS"], writes=["MlowS"])
    Mask4 = alloc([128, 512])
    S.op("pool", lambda e: e.memset(Mask4, 1.0), writes=["Mask4"])
    for blk in range(4):
        S.op("pool", lambda e, blk=blk: e.affine_select(out=Mask4[:, blk * 128:(blk + 1) * 128], in_=Mask4[:, blk * 128:(blk + 1) * 128],
                                                        pattern=[[1, 128]], compare_op=ALU.is_ge, fill=0.0,
                                                        base=(-1 if blk % 2 == 0 else 0), channel_multiplier=-1),
             reads=["Mask4"], writes=["Mask4"])
    bcs = {}
    for nm, src in (("kkbc", "k_k"), ("kabc", "k_a"), ("rkbc", "r_k")):
        bcs[nm] = alloc([128, D])
        S.dma("sp", lambda e, nm=nm, src=src: e.dma_start(out=bcs[nm], in_=I[src].partition_broadcast(128)), writes=[nm])
    ztb = [alloc([128, 3200]), alloc([128, 3200])]
    sgbt = alloc([128, D])
    sg = alloc([128, D])
    av = alloc([128, D])
    tA = alloc([128, D])
    tB = alloc([128, D])
    tC = alloc([128, D])
    Ea = alloc([128, D])
    B["ybo"], B["ybon"] = Ea, "Ea"
    Eb = alloc([128, D])
    B["tE"], B["tEn"] = Eb, "Eb"
    lT = alloc([128, 128], BF16)
    At, Rt, Bt, Kt, Bh, Kh, Vb = [alloc([128, D], BF16) for _ in range(7)]
    B["ybg"], B["ybgn"] = At, "At"
    ARt = alloc([128, 8, 2, 128], BF16)
    BtT = alloc([128, 8, 128], BF16)
    B["ybT"], B["ybTn"] = BtT, "BtT"
    KtT = alloc([128, 8, 128], BF16)
    WT = KtT
    Am = [alloc([128, 512], BF16) for _ in range(16)]
    Nb = [[alloc([128, 4, 128], BF16) for _ in range(2)] for _ in range(4)]
    Lb = [[alloc([128, 4, 128], BF16) for _ in range(2)] for _ in range(4)]
    L0 = [Lb[hg][1] for hg in range(4)]
    Gbf = [alloc([128, 4, 128], BF16) for _ in range(4)]
    Wall = Rt.rearrange("p (h d) -> p h d", h=16)
    Z32 = tA.rearrange("p (h d) -> p h d", h=16)
    Ub = Bt
    H32 = alloc([128, 8, 64])
    Hbf = [alloc([128, 8, 2, 64], BF16) for _ in range(2)]
    PC = alloc([128, 8])
    yv = Ea
    S.op("pool", lambda e: e.memset(H32, 0.0), writes=["H32"])
    S.op("pool", lambda e: e.memset(Hbf[0], 0.0), writes=["Hbf0"])
    S.op("pool", lambda e: e.memset(Hbf[1], 0.0), writes=["Hbf1"])
    ss16, rn16 = s16[:, 32:48], s16[:, 32:48]
    v3 = lambda ap: ap.rearrange("p (h d) -> p h d", h=16)
    bc = lambda ap: ap.unsqueeze(2).to_broadcast([128, 16, 64])

    tiles = ([NT] if DBG.get('sample', True) else []) + list(range(C.ntiles))
    def ldz(ti):
        S.dma("sp", lambda e, ti=ti: e.dma_start(out=ztb[ti % 2], in_=T["z"][tiles[ti]]), reads=["d_z"], writes=["zt%d" % (ti % 2)])
    if tiles:
        ldz(0)
    for ti, t in enumerate(tiles):
        sample = (t == NT)
        zt, ztn = ztb[ti % 2], "zt%d" % (ti % 2)
        sbon, sbn = s16[:, 48 + 16 * (ti % 2):64 + 16 * (ti % 2)], "sbon%d" % (ti % 2)
        S.mark("L%d" % ti)
        if ti + 1 < len(tiles):
            ldz(ti + 1)
        S.mark("P1_%d" % ti)
        r_, k_, v_, lor = zt[:, 0:1024], zt[:, 1024:2048], zt[:, 2048:3072], zt[:, 3072:3200]
        bank = _nextbank(C)
        pb = C.pbank(bank)
        S.op("pe", lambda e, r_=r_, k_=k_, v_=v_, lor=lor, pb=pb: e.transpose(out=pb[:, 0:128], in_=lor, identity=identf), reads=[ztn, "identf"], writes=["pb%d" % bank])
        S.op("act", lambda e, r_=r_, k_=k_, v_=v_, lor=lor, pb=pb: e.activation(out=lT[0:64, :], in_=pb[0:64, 0:128], func=AF.Tanh), reads=["pb%d" % bank], writes=["lT"])
        S.op("act", lambda e, r_=r_, k_=k_, v_=v_, lor=lor, pb=pb: e.copy(out=lT[64:128, :], in_=pb[64:128, 0:128]), reads=["pb%d" % bank], writes=["lT"])
        for (dst, dn, r0, v0) in ((sg, "sg", 0, 0), (av, "av", 64, 32)):
            for hf in range(2):
                bank = _nextbank(C)
                pb = C.pbank(bank)
                S.op("pe", lambda e, r_=r_, k_=k_, v_=v_, lor=lor, pb=pb, r0=r0, hf=hf: e.matmul(pb, lhsT=lT[r0:r0 + 64, :], rhs=W2A2[r0:r0 + 64, hf * 512:(hf + 1) * 512],
                                                                    start=True, stop=False), reads=["lT", "W2A2"], writes=["pb%d" % bank])
                S.op("pe", lambda e, r_=r_, k_=k_, v_=v_, lor=lor, pb=pb, v0=v0, hf=hf: e.matmul(pb, lhsT=ones[v0:v0 + 1, :], rhs=vecs[v0:v0 + 1, hf * 512:(hf + 1) * 512],
                                                                    start=False, stop=True), reads=["ones", "vecs"], writes=["pb%d" % bank])
                S.op("act", lambda e, r_=r_, k_=k_, v_=v_, lor=lor, pb=pb, dst=dst, hf=hf: e.activation(out=dst[:, hf * 512:(hf + 1) * 512], in_=pb, func=AF.Sigmoid),
                     reads=["pb%d" % bank], writes=[dn])
        if DBG.get('p2b_stop', 99) <= 1:
            continue
        S.mark("P2_%d" % ti)
        S.op("dve", lambda e, r_=r_, k_=k_, v_=v_, lor=lor: e.tensor_tensor(out=tA, in0=k_, in1=bcs["kkbc"], op=ALU.mult), reads=[ztn, "kkbc"], writes=["tA"])
        S.op("dve", lambda e, r_=r_, k_=k_, v_=v_, lor=lor: e.tensor_tensor(out=tB, in0=tA, in1=tA, op=ALU.mult), reads=["tA"], writes=["tB"])
        S.op("dve", lambda e, r_=r_, k_=k_, v_=v_, lor=lor: e.tensor_reduce(out=ss16, in_=v3(tB), axis=AX.X, op=ALU.add), reads=["tB"], writes=["s16n"])
        S.op("act", lambda e, r_=r_, k_=k_, v_=v_, lor=lor: e.activation(out=ss16, in_=ss16, func=AF.Sqrt), reads=["s16n"], writes=["s16n"])
        S.op("dve", lambda e, r_=r_, k_=k_, v_=v_, lor=lor: e.tensor_scalar(out=ss16, in0=ss16, scalar1=1e-12, scalar2=None, op0=ALU.max), reads=["s16n"], writes=["s16n"])
        S.op("dve", lambda e, r_=r_, k_=k_, v_=v_, lor=lor: e.reciprocal(out=ss16, in_=ss16), reads=["s16n"], writes=["s16n"])
        S.op("dve", lambda e, r_=r_, k_=k_, v_=v_, lor=lor: e.tensor_tensor(out=v3(tA), in0=v3(tA), in1=bc(rn16), op=ALU.mult), reads=["tA", "s16n"], writes=["tA"])
        S.op("dve", lambda e, r_=r_, k_=k_, v_=v_, lor=lor: e.scalar_tensor_tensor(out=tB, in0=av, scalar=-1.0, in1=bcs["kabc"], op0=ALU.add, op1=ALU.mult),
             reads=["av", "kabc"], writes=["tB"])
        S.op("dve", lambda e, r_=r_, k_=k_, v_=v_, lor=lor: e.scalar_tensor_tensor(out=tB, in0=tB, scalar=1.0, in1=k_, op0=ALU.add, op1=ALU.mult),
             reads=["tB", ztn], writes=["tB"])
        S.op("dve", lambda e, r_=r_, k_=k_, v_=v_, lor=lor: e.tensor_tensor(out=tC, in0=tA, in1=av, op=ALU.mult), reads=["tA", "av"], writes=["tC"])
        S.mark("P3_%d" % ti)
        S.op("dve", lambda e, r_=r_, k_=k_, v_=v_, lor=lor: e.tensor_tensor(out=tD, in0=r_, in1=tB, op=ALU.mult), reads=[ztn, "tB"], writes=["tD"])
        S.op("dve", lambda e, r_=r_, k_=k_, v_=v_, lor=lor: e.tensor_tensor(out=tD, in0=tD, in1=bcs["rkbc"], op=ALU.mult), reads=["tD", "rkbc"], writes=["tD"])
        S.op("dve", lambda e, r_=r_, k_=k_, v_=v_, lor=lor, sbon=sbon: e.tensor_reduce(out=sbon, in_=v3(tD), axis=AX.X, op=ALU.add), reads=["tD"], writes=[sbn])
        if DBG.get('p2b_stop', 99) <= 2:
            continue
        if sample:
            S.op("act", lambda e, r_=r_, k_=k_, v_=v_, lor=lor: e.activation(out=Ea, in_=sg, func=AF.Exp, scale=-CDEC), reads=["sg"], writes=["Ea"])
            S.op("dve", lambda e, r_=r_, k_=k_, v_=v_, lor=lor: e.tensor_scalar(out=tA, in0=tA, scalar1=-1.0, scalar2=None, op0=ALU.mult), reads=["tA"], writes=["tA"])
            for qi, (src, sn) in enumerate(((r_, ztn), (Ea, "Ea"), (tB, "tB"), (v_, ztn), (tA, "tA"), (tC, "tC"))):
                S.dma("sp", lambda e, r_=r_, k_=k_, v_=v_, lor=lor, qi=qi, src=src: e.dma_start(out=T["s6"][qi], in_=src), reads=[sn], writes=["d_s6"], home=sn)
            S.dma("sp", lambda e, r_=r_, k_=k_, v_=v_, lor=lor, sbon=sbon: e.dma_start(out=T["sextra"][:, 0:16], in_=sbon), reads=[sbn], writes=["d_sx"], home=sbn)
            continue
        def cums(Lm, ln, outs):
            for hf in range(2):
                bank = _nextbank(C)
                pb = C.pbank(bank)
                S.op("pe", lambda e, r_=r_, k_=k_, v_=v_, lor=lor, pb=pb, hf=hf: e.matmul(pb, lhsT=Lm, rhs=sg[:, hf * 512:(hf + 1) * 512], start=True, stop=True),
                     reads=[ln, "sg"], writes=["pb%d" % bank])
                for (dst, dn, sc) in outs:
                    S.op("act", lambda e, r_=r_, k_=k_, v_=v_, lor=lor, pb=pb, dst=dst, sc=sc, hf=hf: e.activation(out=dst[:, hf * 512:(hf + 1) * 512], in_=pb,
                                                                                       func=AF.Exp, scale=sc),
                         reads=["pb%d" % bank], writes=[dn])
        cums(Lincl, "Lincl", ((Ea, "Ea", 1.0), (Eb, "Eb", -1.0)))
        S.op("dve", lambda e, r_=r_, k_=k_, v_=v_, lor=lor: e.tensor_tensor(out=Rt, in0=r_, in1=Ea, op=ALU.mult), reads=[ztn, "Ea"], writes=["Rt"])
        S.op("pool", lambda e, r_=r_, k_=k_, v_=v_, lor=lor: e.tensor_tensor(out=Bt, in0=tC, in1=Eb, op=ALU.mult), reads=["tC", "Eb"], writes=["Bt"])
        S.op("dve", lambda e, r_=r_, k_=k_, v_=v_, lor=lor: e.tensor_tensor(out=Kt, in0=tB, in1=Eb, op=ALU.mult), reads=["tB", "Eb"], writes=["Kt"])
        cums(Lstr, "Lstr", ((Ea, "Ea", 1.0),))
        S.op("dve", lambda e, r_=r_, k_=k_, v_=v_, lor=lor: e.scalar_tensor_tensor(out=At, in0=tA, scalar=-1.0, in1=Ea, op0=ALU.mult, op1=ALU.mult),
             reads=["tA", "Ea"], writes=["At"])
        cums(Ustr, "Ustr", ((Eb, "Eb", 1.0),))
        S.op("pool", lambda e, r_=r_, k_=k_, v_=v_, lor=lor: e.tensor_tensor(out=Bh, in0=tC, in1=Eb, op=ALU.mult), reads=["tC", "Eb"], writes=["Bh"])
        S.op("dve", lambda e, r_=r_, k_=k_, v_=v_, lor=lor: e.tensor_tensor(out=Kh, in0=tB, in1=Eb, op=ALU.mult), reads=["tB", "Eb"], writes=["Kh"])
        S.op("act", lambda e, r_=r_, k_=k_, v_=v_, lor=lor: e.copy(out=Vb, in_=v_), reads=[ztn], writes=["Vb"])
        bank = _nextbank(C)
        pb = C.pbank(bank)
        for p in range(8):
            S.op("pe", lambda e, r_=r_, k_=k_, v_=v_, lor=lor, pb=pb, p=p: e.matmul(pb[:, p:p + 1], lhsT=sg[:, p * 128:(p + 1) * 128], rhs=negcol, start=True, stop=True),
                 reads=["sg", "negcol"], writes=["pb%d" % bank])
        S.op("act", lambda e, r_=r_, k_=k_, v_=v_, lor=lor, pb=pb: e.activation(out=PC, in_=pb[:, 0:8], func=AF.Exp), reads=["pb%d" % bank], writes=["PC"])
        if DBG.get('p2b_stop', 99) <= 3:
            continue
        S.mark("A_%d" % ti)
        pT0 = C.pbank(0, BF16)
        for qi, (src, sn, dst, dn) in enumerate(((At, "At", ARt[:, :, 0, :], "ARt"), (Rt, "Rt", ARt[:, :, 1, :], "ARt"),
                                                 (Bt, "Bt", BtT, "BtT"), (Kt, "Kt", KtT, "KtT"))):
            pTq = C.pbank(4 + qi, BF16)
            for k in range(8):
                S.op("pe", lambda e, r_=r_, k_=k_, v_=v_, lor=lor, k=k, src=src, pTq=pTq: e.transpose(
                    out=pTq[:, k * 128:(k + 1) * 128], in_=src[:, k * 128:(k + 1) * 128], identity=ident),
                    reads=[sn, "ident"], writes=["pb%d" % (4 + qi)])
            S.op("act", lambda e, r_=r_, k_=k_, v_=v_, lor=lor, dst=dst, pTq=pTq: e.copy(out=dst, in_=pTq.rearrange("p (k t) -> p k t", k=8)),
                 reads=["pb%d" % (4 + qi)], writes=[dn])
        if DBG.get('p2b_stop', 99) <= 4:
            continue
        C.rotbanks = (1, 2, 3, 4, 5, 6, 7)
        for h in range(16):
            p, rows = h // 2, slice((h % 2) * 64, (h % 2) * 64 + 64)
            bank = _nextbank(C)
            pb = C.pbank(bank)
            S.op("pe", lambda e, r_=r_, k_=k_, v_=v_, lor=lor, pb=pb, p=p, rows=rows: e.matmul(pb[:, 0:256], lhsT=BtT[rows, p, :],
                                                                  rhs=ARt[rows, p, :, :].rearrange("q a t -> q (a t)"), start=True, stop=True),
                 reads=["BtT", "ARt"], writes=["pb%d" % bank])
            S.op("pe", lambda e, r_=r_, k_=k_, v_=v_, lor=lor, pb=pb, p=p, rows=rows: e.matmul(pb[:, 256:512], lhsT=KtT[rows, p, :],
                                                                  rhs=ARt[rows, p, :, :].rearrange("q a t -> q (a t)"), start=True, stop=True),
                 reads=["KtT", "ARt"], writes=["pb%d" % bank])
            S.op("dve", lambda e, r_=r_, k_=k_, v_=v_, lor=lor, pb=pb, h=h: e.tensor_tensor(out=Am[h], in0=pb, in1=Mask4, op=ALU.mult),
                 reads=["pb%d" % bank, "Mask4"], writes=["Am%d" % h])
        C.rotbanks = (1, 2, 3)
        if DBG.get('p2b_stop', 99) == 45:
            continue
        for half in range(2):
            for i in range(8):
                h = half * 8 + i
                S.op("pe", lambda e, r_=r_, k_=k_, v_=v_, lor=lor, i=i, h=h: e.transpose(out=pT0[:, i * 128:(i + 1) * 128], in_=Am[h][:, 0:128], identity=ident),
                     reads=["Am%d" % h, "ident"], writes=["pb0"])
            for q in range(2):
                hg = half * 2 + q
                S.op("act", lambda e, r_=r_, k_=k_, v_=v_, lor=lor, hg=hg, q=q: e.copy(out=L0[hg].rearrange("p h s -> p (h s)"), in_=pT0[:, q * 512:(q + 1) * 512]),
                     reads=["pb0"], writes=["Lb%d_1" % hg])
        if DBG.get('p2b_stop', 99) <= 5:
            continue
        st_ = []
        for hg in range(4):
            pgb = 4 + hg
            PG = C.pbank(pgb)
            pgn = "pb%d" % pgb
            S.op("pe", lambda e, r_=r_, k_=k_, v_=v_, lor=lor, PG=PG: e.matmul(PG, lhsT=zl, rhs=zr, start=True, stop=False, skip_group_check=True),
                 reads=["zl", "zr"], writes=[pgn])
            for hh in range(4):
                h = hg * 4 + hh
                S.op("pe", lambda e, r_=r_, k_=k_, v_=v_, lor=lor, PG=PG, hh=hh, h=h: e.matmul(PG[:, hh * 128:hh * 128 + 64], lhsT=ident, rhs=At[:, h * 64:(h + 1) * 64],
                                                                  start=False, stop=False, skip_group_check=True),
                     reads=["ident", "At"], writes=[pgn])
                S.op("pe", lambda e, r_=r_, k_=k_, v_=v_, lor=lor, PG=PG, hh=hh, h=h: e.matmul(PG[:, hh * 128 + 64:(hh + 1) * 128], lhsT=Am[h][:, 256:384],
                                                                  rhs=Vb[:, h * 64:(h + 1) * 64], start=False, stop=False, skip_group_check=True),
                     reads=["Am%d" % h, "Vb"], writes=[pgn])
            S.op("dve", lambda e, r_=r_, k_=k_, v_=v_, lor=lor, PG=PG, hg=hg: e.tensor_copy(out=Gbf[hg].rearrange("p h s -> p (h s)"), in_=PG),
                 reads=[pgn], writes=["Gbf%d" % hg])
            st_.append(dict(PG=PG, pgn=pgn,
                            Ncur=[Am[hg * 4 + hh][:, 0:128] for hh in range(4)], Nn=["Am%d" % (hg * 4 + hh) for hh in range(4)],
                            Lcur=[L0[hg][:, hh, :] for hh in range(4)], Ln=["Lb%d_1" % hg] * 4))
        for j in range(7):
            for hg in range(4):
                q = st_[hg]
                PG, pgn, Ncur, Nn_, Lcur, Ln_ = q["PG"], q["pgn"], q["Ncur"], q["Nn"], q["Lcur"], q["Ln"]
                for hh in range(4):
                    S.op("pe", lambda e, r_=r_, k_=k_, v_=v_, lor=lor, PG=PG, hh=hh, nl=Ncur[hh], hg=hg, j=j: e.matmul(
                        PG[:, hh * 128:(hh + 1) * 128], lhsT=nl, rhs=Gbf[hg][:, hh, :], start=False, stop=(j == 6),
                        skip_group_check=True), reads=[Nn_[hh], "Gbf%d" % hg], writes=[pgn])
                if j < 6:
                    bank = _nextbank(C)
                    pb = C.pbank(bank)
                    for hh in range(4):
                        S.op("pe", lambda e, r_=r_, k_=k_, v_=v_, lor=lor, pb=pb, hh=hh, ll=Lcur[hh], nl=Ncur[hh]: e.matmul(
                            pb[:, hh * 128:(hh + 1) * 128], lhsT=ll, rhs=nl, start=True, stop=True),
                            reads=[Ln_[hh], Nn_[hh]], writes=["pb%d" % bank])
                    nb = Nb[hg][j % 2]
                    nbn = "Nb%d_%d" % (hg, j % 2)
                    S.op("act", lambda e, r_=r_, k_=k_, v_=v_, lor=lor, pb=pb, nb=nb: e.copy(out=nb.rearrange("p h s -> p (h s)"), in_=pb),
                         reads=["pb%d" % bank], writes=[nbn])
                    if j < 5:
                        bank = _nextbank(C)
                        pb = C.pbank(bank)
                        for hh in range(4):
                            S.op("pe", lambda e, r_=r_, k_=k_, v_=v_, lor=lor, pb=pb, hh=hh, ll=Lcur[hh], nl=Ncur[hh]: e.matmul(
                                pb[:, hh * 128:(hh + 1) * 128], lhsT=nl, rhs=ll, start=True, stop=True),
                                reads=[Ln_[hh], Nn_[hh]], writes=["pb%d" % bank])
                        lb = Lb[hg][j % 2]
                        lbn = "Lb%d_%d" % (hg, j % 2)
                        S.op("act", lambda e, r_=r_, k_=k_, v_=v_, lor=lor, pb=pb, lb=lb: e.copy(out=lb.rearrange("p h s -> p (h s)"), in_=pb),
                             reads=["pb%d" % bank], writes=[lbn])
                        q["Lcur"] = [lb[:, hh, :] for hh in range(4)]
                        q["Ln"] = [lbn] * 4
                    S.op("dve", lambda e, r_=r_, k_=k_, v_=v_, lor=lor, PG=PG, hg=hg: e.tensor_copy(out=Gbf[hg].rearrange("p h s -> p (h s)"), in_=PG),
                         reads=[pgn], writes=["Gbf%d" % hg])
                    q["Ncur"] = [nb[:, hh, :] for hh in range(4)]
                    q["Nn"] = [nbn] * 4
        for hg in range(4):
            PG, pgn = st_[hg]["PG"], st_[hg]["pgn"]
            PGv = PG.rearrange("p (h s) -> p h s", h=4)
            S.op("act", lambda e, r_=r_, k_=k_, v_=v_, lor=lor, PGv=PGv, hg=hg: e.copy(out=Wall[:, hg * 4:(hg + 1) * 4, :], in_=PGv[:, :, 0:64]),
                 reads=[pgn], writes=["Rt"])
            S.op("dve", lambda e, r_=r_, k_=k_, v_=v_, lor=lor, PGv=PGv, hg=hg: e.tensor_copy(out=Z32[:, hg * 4:(hg + 1) * 4, :], in_=PGv[:, :, 64:128]),
                 reads=[pgn], writes=["tA"])
        if DBG.get('p2b_stop', 99) <= 6:
            continue
        WallF = Wall.rearrange("p h d -> p (h d)")
        for k in range(8):
            S.op("pe", lambda e, r_=r_, k_=k_, v_=v_, lor=lor, k=k: e.transpose(out=pT0[:, k * 128:(k + 1) * 128], in_=WallF[:, k * 128:(k + 1) * 128], identity=ident),
                 reads=["Rt", "ident"], writes=["pb0"])
        S.op("act", lambda e, r_=r_, k_=k_, v_=v_, lor=lor: e.copy(out=WT.rearrange("p k t -> p (k t)"), in_=pT0), reads=["pb0"], writes=["KtT"])
        ho, hn = Hbf[ti % 2], Hbf[(ti + 1) % 2]
        hon, hnn = "Hbf%d" % (ti % 2), "Hbf%d" % ((ti + 1) % 2)
        Z32F = Z32.rearrange("p h d -> p (h d)")
        for hb in range(2):
            bank = _nextbank(C)
            pb = C.pbank(bank)
            for i in range(4):
                p = hb * 4 + i
                S.op("pe", lambda e, r_=r_, k_=k_, v_=v_, lor=lor, pb=pb, i=i, p=p, ho=ho: e.matmul(pb[:, i * 128:(i + 1) * 128], lhsT=WT[:, p, :],
                                                                       rhs=ho[:, p, :, :].rearrange("q a d -> q (a d)"), start=True, stop=True),
                     reads=["KtT", hon], writes=["pb%d" % bank])
            S.op("dve", lambda e, r_=r_, k_=k_, v_=v_, lor=lor, pb=pb, hb=hb: e.tensor_tensor(out=Ub[:, hb * 512:(hb + 1) * 512], in0=pb,
                                                                in1=Z32F[:, hb * 512:(hb + 1) * 512], op=ALU.add),
                 reads=["pb%d" % bank, "tA"], writes=["Bt"])
        if DBG.get('p2b_stop', 99) == 71:
            continue
        S.op("dve", lambda e, r_=r_, k_=k_, v_=v_, lor=lor: e.tensor_tensor(out=H32, in0=H32, in1=PC.unsqueeze(2).to_broadcast([128, 8, 64]), op=ALU.mult),
             reads=["H32", "PC"], writes=["H32"])
        for hb in range(2):
            bank = _nextbank(C)
            pb = C.pbank(bank)
            for i in range(8):
                h = hb * 8 + i
                p = h // 2
                S.op("pe", lambda e, r_=r_, k_=k_, v_=v_, lor=lor, pb=pb, i=i, p=p, h=h: e.matmul(pb[:, i * 64:(i + 1) * 64], lhsT=Bh[:, p * 128:(p + 1) * 128],
                                                                     rhs=Ub[:, h * 64:(h + 1) * 64], start=True, stop=False),
                     reads=["Bh", "Bt"], writes=["pb%d" % bank])
                S.op("pe", lambda e, r_=r_, k_=k_, v_=v_, lor=lor, pb=pb, i=i, p=p, h=h: e.matmul(pb[:, i * 64:(i + 1) * 64], lhsT=Kh[:, p * 128:(p + 1) * 128],
                                                                     rhs=Vb[:, h * 64:(h + 1) * 64], start=False, stop=True),
                     reads=["Kh", "Vb"], writes=["pb%d" % bank])
            pbv = pb.rearrange("q (p a d) -> q p a d", p=4, a=2)
            for hf in range(2):
                S.op("dve", lambda e, r_=r_, k_=k_, v_=v_, lor=lor, pbv=pbv, hf=hf, hb=hb: e.tensor_tensor(
                    out=H32[hf * 64:(hf + 1) * 64, hb * 4:(hb + 1) * 4, :], in0=H32[hf * 64:(hf + 1) * 64, hb * 4:(hb + 1) * 4, :],
                    in1=pbv[hf * 64:(hf + 1) * 64, :, hf, :], op=ALU.add), reads=["pb%d" % bank, "H32"], writes=["H32"])
        if DBG.get('p2b_stop', 99) == 72:
            continue
        for hf in range(2):
            S.op("act", lambda e, r_=r_, k_=k_, v_=v_, lor=lor, hn=hn, hf=hf: e.copy(out=hn[hf * 64:(hf + 1) * 64, :, hf, :], in_=H32[hf * 64:(hf + 1) * 64, :, :]),
                 reads=["H32"], writes=[hnn])
        for hb in range(2):
            bank = _nextbank(C)
            pb = C.pbank(bank)
            for i in range(4):
                p = hb * 4 + i
                S.op("pe", lambda e, r_=r_, k_=k_, v_=v_, lor=lor, pb=pb, i=i, p=p, ho=ho: e.matmul(pb[:, i * 128:(i + 1) * 128], lhsT=ARt[:, p, 1, :],
                                                                       rhs=ho[:, p, :, :].rearrange("q a d -> q (a d)"), start=True, stop=False),
                     reads=["ARt", hon], writes=["pb%d" % bank])
                for hf in range(2):
                    h = 2 * p + hf
                    c0 = i * 128 + hf * 64
                    S.op("pe", lambda e, r_=r_, k_=k_, v_=v_, lor=lor, pb=pb, c0=c0, h=h: e.matmul(pb[:, c0:c0 + 64], lhsT=Am[h][:, 128:256],
                                                                     rhs=Ub[:, h * 64:(h + 1) * 64], start=False, stop=False),
                         reads=["Am%d" % h, "Bt"], writes=["pb%d" % bank])
                    S.op("pe", lambda e, r_=r_, k_=k_, v_=v_, lor=lor, pb=pb, c0=c0, h=h, hf=hf: e.matmul(pb[:, c0:c0 + 64], lhsT=Am[h][:, 384:512],
                                                                            rhs=Vb[:, h * 64:(h + 1) * 64], start=False, stop=(hf == 1)),
                         reads=["Am%d" % h, "Vb"], writes=["pb%d" % bank])
            S.op("act", lambda e, r_=r_, k_=k_, v_=v_, lor=lor, pb=pb, hb=hb: e.copy(out=yv[:, hb * 512:(hb + 1) * 512], in_=pb), reads=["pb%d" % bank], writes=["Ea"])
        if DBG.get('p2b_stop', 99) <= 7:
            continue
        S.mark("PD_%d" % ti)
        S.dma("sp", lambda e, t=t: e.dma_start(out=sgbt, in_=T["sgb"][t]), reads=["d_sgb"], writes=["sgbt"])
        _rwkv_post(C, B, yv, "Ea", v_, ztn, sbon, sgbt, "sgbt", t, mark_pe="PP_%d" % ti, sbon_n=sbn)
        if C.debug:
            S.dma("sp", lambda e, t=t: e.dma_start(out=T["dy"][t], in_=yv), reads=["Ea"], writes=["d_dy"], home="Ea")
    S.mark(None)
    n_ = len(tiles)
    if n_:
        S.replay("L0", "P1_0", "P2_0", "P3_0")
    for ti in range(n_):
        S.replay("A_%d" % ti)
        if ti + 1 < n_:
            S.replay("P1_%d" % (ti + 1))
            S.interleave("PD_%d" % ti, "P2_%d" % (ti + 1))
            S.replay("PP_%d" % ti, "L%d" % (ti + 1), "P3_%d" % (ti + 1))
        else:
            S.replay("PD_%d" % ti, "PP_%d" % ti)
    assert not any(S.caps.values()), [k for k, v in S.caps.items() if v]
    if C.ntiles and DBG.get('p2b_stop', 99) > 8:
        Hout = Eb[0:64, :].rearrange("v (p x) -> v p x", p=8)
        nlast = len(tiles)
        for hb in range(2):
            bank = _nextbank(C)
            pb = C.pbank(bank)
            for i in range(4):
                p = hb * 4 + i
                S.op("pe", lambda e, pb=pb, i=i, p=p: e.transpose(out=pb[0:64, i * 128:(i + 1) * 128], in_=H32[:, p, :], identity=identf),
                     reads=["H32", "identf"], writes=["pb%d" % bank])
            S.op("dve", lambda e, pb=pb, hb=hb: e.tensor_copy(out=Hout[:, hb * 4:(hb + 1) * 4, :].rearrange("v p x -> v (p x)"),
                                                              in_=pb[0:64, :]), reads=["pb%d" % bank], writes=["Eb"])
        S.dma("sp", lambda e: e.dma_start(out=O["pw"].rearrange("h v k -> v h k"),
                                          in_=Hout.rearrange("v p (a k) -> v (p a) k", a=2)), reads=["Eb"], writes=["o_pw"], home="Eb")


def phase2c(C):
    C.rotbanks = (1, 2, 3, 4)
    S, alloc, I, O, T = C.S, C.alloc, C.I, C.O, C.T
    C.reset()
    if not DBG.get('sample', True):
        return
    B = {}
    _post_bufs(C, B)
    Sst = alloc([128, 2, 64, 64])
    tmp = alloc([128, 2, 64, 64])
    vec6 = alloc([128, 6, 8, 128])
    ysc = alloc([128, 8, 128])
    sa = alloc([128, 2, 64])
    yv = alloc([128, D])
    vbuf = alloc([128, D])
    sgbt = alloc([128, D])
    S.dma("sp", lambda e: e.dma_start(out=Sst.rearrange("p a v k -> p (a v k)"),
                                      in_=I["swkv"].rearrange("b (hh a) v k -> (b hh) (a v k)", a=2)), writes=["Sst0", "Sst1"])
    for q in range(6):
        for i in range(8):
            src = T["s6"][q].rearrange("(b i) (hh x) -> i b hh x", i=8, x=128)[i]
            S.dma("sp", lambda e, q=q, i=i, src=src: e.dma_start(out=vec6[:, q, i, :], in_=src),
                  reads=["d_s6"], writes=["vec6"])
    bk = lambda vec: vec.unsqueeze(1).to_broadcast([128, 64, 64])
    bv = lambda vec: vec.unsqueeze(2).to_broadcast([128, 64, 64])
    for i in range(8):
        ops = {0: [], 1: []}
        for a in range(2):
            eng = "dve" if a == 0 else "pool"
            Sv, Tv = Sst[:, a], tmp[:, a]
            sn, tn, san, yn = "Sst%d" % a, "tmp%d" % a, "sa%d" % a, "ysc%d" % a
            sl = slice(a * 64, a * 64 + 64)
            r_, w_, k_, v_, a_, b_ = [vec6[:, q, i, sl] for q in range(6)]
            sav = sa[:, a, :]
            L = ops[a]
            L.append((eng, lambda e, Sv=Sv, Tv=Tv, a_=a_: e.tensor_tensor(out=Tv, in0=Sv, in1=bk(a_), op=ALU.mult), [sn, "vec6"], [tn]))
            L.append(("dve", lambda e, Tv=Tv, sav=sav: e.tensor_reduce(out=sav, in_=Tv, axis=AX.X, op=ALU.add), [tn], [san]))
            L.append((eng, lambda e, Sv=Sv, w_=w_: e.tensor_tensor(out=Sv, in0=Sv, in1=bk(w_), op=ALU.mult), [sn, "vec6"], [sn]))
            L.append((eng, lambda e, Tv=Tv, sav=sav, b_=b_: e.tensor_tensor(out=Tv, in0=bv(sav), in1=bk(b_), op=ALU.mult), [san, "vec6"], [tn]))
            L.append((eng, lambda e, Sv=Sv, Tv=Tv: e.tensor_tensor(out=Sv, in0=Sv, in1=Tv, op=ALU.add), [sn, tn], [sn]))
            L.append((eng, lambda e, Tv=Tv, v_=v_, k_=k_: e.tensor_tensor(out=Tv, in0=bv(v_), in1=bk(k_), op=ALU.mult), ["vec6"], [tn]))
            L.append((eng, lambda e, Sv=Sv, Tv=Tv: e.tensor_tensor(out=Sv, in0=Sv, in1=Tv, op=ALU.add), [sn, tn], [sn]))
            L.append((eng, lambda e, Sv=Sv, Tv=Tv, r_=r_: e.tensor_tensor(out=Tv, in0=Sv, in1=bk(r_), op=ALU.mult), [sn, "vec6"], [tn]))
            L.append(("dve", lambda e, Tv=Tv, i=i, sl=sl: e.tensor_reduce(out=ysc[:, i, sl], in_=Tv, axis=AX.X, op=ALU.add), [tn], [yn]))
        for kk_ in range(9):
            for a in (1, 0):
                eng, fn, rd, wr = ops[a][kk_]
                S.op(eng, fn, reads=rd, writes=wr)
    S.dma("sp", lambda e: e.dma_start(out=O["sw"].rearrange("b (hh a) v k -> (b hh) (a v k)", a=2),
                                      in_=Sst.rearrange("p a v k -> p (a v k)")), reads=["Sst0", "Sst1"], writes=["o_sw"], home="Sst0")
    for i in range(8):
        dst = T["sy"].rearrange("(b i) (hh x) -> i b hh x", i=8, x=128)[i]
        S.dma("sp", lambda e, i=i, dst=dst: e.dma_start(out=dst, in_=ysc[:, i, :]), reads=["ysc0", "ysc1"], writes=["d_sy"], home="ysc0")
    S.dma("sp", lambda e: e.dma_start(out=yv, in_=T["sy"]), reads=["d_sy"], writes=["yv"])
    S.dma("sp", lambda e: e.dma_start(out=vbuf, in_=T["z"][NT][:, 2048:3072]), reads=["d_z"], writes=["vbuf"])
    S.dma("sp", lambda e: e.dma_start(out=sgbt, in_=T["sgb"][NT]), reads=["d_sgb"], writes=["sgbt"])
    sbon = B["s16"][:, 48:64]
    S.dma("sp", lambda e: e.dma_start(out=sbon, in_=T["sextra"][:, 0:16]), reads=["d_sx"], writes=["sbon"])
    _rwkv_post(C, B, yv, "yv", vbuf, "vbuf", sbon, sgbt, "sgbt", NT)


def _scan_setup(C):
    S, alloc, I, O, T = C.S, C.alloc, C.I, C.O, C.T
    Sst = alloc([128, 2, 64, 64])
    tmp = alloc([128, 64, 64])
    vecs = [alloc([128, 6, 128]), alloc([128, 6, 128])]
    ysc = alloc([128, 8, 128])
    sa = alloc([128, 64])
    ops = []
    ops.append(lambda: S.dma("sp", lambda e: e.dma_start(out=Sst.rearrange("p a v k -> p (a v k)"),
                                                         in_=I["swkv"].rearrange("b (hh a) v k -> (b hh) (a v k)", a=2)),
                             writes=["Sst"]))
    bk = lambda vec: vec.unsqueeze(1).to_broadcast([128, 64, 64])
    bv = lambda vec: vec.unsqueeze(2).to_broadcast([128, 64, 64])

    def ldv(i):
        def f():
            for q in range(6):
                src = T["s6"][q].rearrange("(b i) (hh x) -> i b hh x", i=8, x=128)[i]
                S.dma("sp", lambda e, q=q, src=src: e.dma_start(out=vecs[i % 2][:, q, :], in_=src),
                      reads=["d_s6"], writes=["vec%d" % (i % 2)])
        return f
    ops.append(ldv(0))
    for i in range(8):
        if i + 1 < 8:
            ops.append(ldv(i + 1))
        vn = "vec%d" % (i % 2)
        for a in range(2):
            Sv = Sst[:, a]
            sl = slice(a * 64, a * 64 + 64)
            r_, w_, k_, v_, a_, b_ = [vecs[i % 2][:, q, sl] for q in range(6)]
            L = [
                (lambda e, Sv=Sv, a_=a_: e.tensor_tensor(out=tmp, in0=Sv, in1=bk(a_), op=ALU.mult), ["Sst", vn], ["stmp"]),
                (lambda e: e.tensor_reduce(out=sa, in_=tmp, axis=AX.X, op=ALU.add), ["stmp"], ["ssa"]),
                (lambda e, Sv=Sv, w_=w_: e.tensor_tensor(out=Sv, in0=Sv, in1=bk(w_), op=ALU.mult), ["Sst", vn], ["Sst"]),
                (lambda e, b_=b_: e.tensor_tensor(out=tmp, in0=bv(sa), in1=bk(b_), op=ALU.mult), ["ssa", vn], ["stmp"]),
                (lambda e, Sv=Sv: e.tensor_tensor(out=Sv, in0=Sv, in1=tmp, op=ALU.add), ["Sst", "stmp"], ["Sst"]),
                (lambda e, v_=v_, k_=k_: e.tensor_tensor(out=tmp, in0=bv(v_), in1=bk(k_), op=ALU.mult), [vn], ["stmp"]),
                (lambda e, Sv=Sv: e.tensor_tensor(out=Sv, in0=Sv, in1=tmp, op=ALU.add), ["Sst", "stmp"], ["Sst"]),
                (lambda e, Sv=Sv, r_=r_: e.tensor_tensor(out=tmp, in0=Sv, in1=bk(r_), op=ALU.mult), ["Sst", vn], ["stmp"]),
                (lambda e, i=i, sl=sl: e.tensor_reduce(out=ysc[:, i, sl], in_=tmp, axis=AX.X, op=ALU.add), ["stmp"], ["ysc"]),
            ]
            for fn, rd, wr in L:
                ops.append(lambda fn=fn, rd=rd, wr=wr: S.op("dve", fn, reads=rd, writes=wr))

    def fin():
        S.dma("sp", lambda e: e.dma_start(out=O["sw"].rearrange("b (hh a) v k -> (b hh) (a v k)", a=2),
                                          in_=Sst.rearrange("p a v k -> p (a v k)")), reads=["Sst"], writes=["o_sw"], home="Sst")
        for i in range(8):
            dst = T["sy"].rearrange("(b i) (hh x) -> i b hh x", i=8, x=128)[i]
            S.dma("sp", lambda e, i=i, dst=dst: e.dma_start(out=dst, in_=ysc[:, i, :]), reads=["ysc"], writes=["d_sy"], home="ysc")
    ops.append(fin)
    return ops


def phase3a(C):
    phase3(C, tiles=list(range(C.ntiles)), scan=DBG.get('sample', True))


def phase3b(C):
    if DBG.get('sample', True):
        phase3(C, tiles=[NT], scan=False, reuse=("p3a" in C.phases and DBG.get('reuse', True)))


def phase2d(C):
    S, alloc, I, O, T = C.S, C.alloc, C.I, C.O, C.T
    C.rotbanks = (1, 2, 3, 4)
    C.reset()
    if "p3a" in C.phases and DBG.get('reuse', True):
        C.apos = (getattr(C, "p3_persist", 19152) + 63) // 64 * 64
    if not DBG.get('sample', True):
        return
    B = {}
    _post_bufs(C, B)
    yv = alloc([128, D])
    vbuf = alloc([128, D])
    sgbt = alloc([128, D])
    S.dma("sp", lambda e: e.dma_start(out=yv, in_=T["sy"]), reads=["d_sy"], writes=["yv"])
    S.dma("sp", lambda e: e.dma_start(out=vbuf, in_=T["z"][NT][:, 2048:3072]), reads=["d_z"], writes=["vbuf"])
    S.dma("sp", lambda e: e.dma_start(out=sgbt, in_=T["sgb"][NT]), reads=["d_sgb"], writes=["sgbt"])
    sbon = B["s16"][:, 48:64]
    S.dma("sp", lambda e: e.dma_start(out=sbon, in_=T["sextra"][:, 0:16]), reads=["d_sx"], writes=["sbon"])
    _rwkv_post(C, B, yv, "yv", vbuf, "vbuf", sbon, sgbt, "sgbt", NT)


PHASES = {"p1": phase1, "p2a": phase2a, "p2b": phase2b, "p2c": phase2c, "p3": phase3,
          "p3a": phase3a, "p2d": phase2d, "p3b": phase3b}


def _shard_inputs(inp):
    g = lambda k: np.ascontiguousarray(np.asarray(inp[k], dtype=np.float32))
    oh = _t5_bucket_onehot()
    shared = {
        "rel_bias": g("rel_bias"), "onehot": oh, "norm_g": g("norm_g")[0], "w_in": g("w_in")[0],
        "sinks": g("attn_sinks")[0], "mu": g("shift_mu")[0], "w0": g("rwkv_w0")[0], "w2": g("rwkv_w2")[0],
        "a0": g("rwkv_a0")[0], "a2": g("rwkv_a2")[0], "k_k": g("rwkv_k_k")[0], "k_a": g("rwkv_k_a")[0],
        "r_k": g("rwkv_r_k")[0].reshape(-1), "lnx_g": g("lnx_g")[0], "lnx_b": g("lnx_b")[0],
        "w_out_a": g("w_out_a")[0], "w_out_b": g("w_out_b")[0], "w_o": g("w_o")[0], "final_g": g("final_g"),
    }
    xp, xs = g("x_prompt"), g("x_sample")
    ck, cv = g("cache_k_win")[0], g("cache_v_win")[0]
    sw, ssh = g("state_wkv")[0], g("state_shift")[0]
    maps = []
    for c in range(NCORES):
        m = dict(shared)
        b0 = 16 * c
        m["xp"] = xp[c]
        m["xs"] = xs[b0:b0 + 16].reshape(128, D)
        m["ck"] = ck[b0:b0 + 16].reshape(16, 128, 256)
        m["cv"] = cv[b0:b0 + 16].reshape(16, 128, 256)
        m["swkv"] = sw[b0:b0 + 16]
        m["sshift"] = ssh[b0:b0 + 16]
        maps.append(m)
    return maps


_NC_CACHE = {}


def kernel(**inputs):
    if "nc" not in _NC_CACHE:
        _NC_CACHE["nc"] = build()
    nc = _NC_CACHE["nc"]
    maps = _shard_inputs(inputs)
    res = run_bass_kernel_spmd(nc, maps, core_ids=list(range(NCORES)))
    R = res.results
    cat = lambda k: np.stack([np.asarray(r[k]) for r in R])
    y_prompt = cat("yp").reshape(8, 4096, D)
    y_sample = cat("ys").reshape(128, 8, D)
    pk = cat("pk").reshape(1, 8, 128, 4, 64)
    pv = cat("pv").reshape(1, 8, 128, 4, 64)
    pw = cat("pw").reshape(1, 8, 16, 64, 64)
    psh = cat("psh").reshape(1, 8, 3200)
    sk = cat("sk").reshape(1, 128, 128, 4, 64)
    sv = cat("sv").reshape(1, 128, 128, 4, 64)
    sw = cat("sw").reshape(1, 128, 16, 64, 64)
    ssh = cat("ssh").reshape(1, 128, 3200)
    return (y_prompt, y_sample, pk, pv, pw, psh, sk, sv, sw, ssh)
```
